# Optimizing a Trainium2 kernel written in Bass

```python
import math
import jax
import jax.numpy as jnp
from jax import lax
import numpy as np

D_MODEL = 1024
BATCH = 8
SEQ = 2048
DEPTH = 4
DEC_BATCH = 32
DEC_SEQ = 1
PAST_LEN = 16384
PAGE_SIZE = 128

F32 = jnp.float32
N_META = 16
N_MIXERS = 3
N_LRU = (DEPTH + 2) // 3
N_DN = (DEPTH + 1) // 3
N_MLA = DEPTH // 3
RMS_EPS = 1e-6
CONV_W = 4
LRU_WIDTH = D_MODEL
LRU_BLOCKS = 4
LRU_BW = LRU_WIDTH // LRU_BLOCKS
LRU_C = 8.0
DN_HEADS = 8
DN_DK = 128
DN_DV = 128
DN_QKV = DN_HEADS * (2 * DN_DK + DN_DV)
DN_CHUNK = 64
MLA_HEADS = 8
MLA_Q_RANK = 512
MLA_KV_RANK = 256
MLA_NOPE = 128
MLA_ROPE = 64
MLA_V = 128
MLA_SCALE = (MLA_NOPE + MLA_ROPE) ** -0.5
ROPE_THETA = 10000.0
Q_BLOCK = 128
D_FF = ((8 * D_MODEL // 3 + 255) // 256) * 256

kernel_name = 'hybrid_rglru_deltanet_mla_decoder_step'


def rmsnorm(x, g):
    xf = x.astype(F32)
    y = xf * lax.rsqrt(jnp.mean(xf * xf, axis=-1, keepdims=True) + RMS_EPS)
    return (y * g.astype(F32)).astype(x.dtype)


def l2norm(x):
    return x * lax.rsqrt(jnp.sum(x * x, axis=-1, keepdims=True) + 1e-6)


def causal_dwconv(x, buf, w):
    L = x.shape[1]
    xx = jnp.concatenate([buf.astype(x.dtype), x], axis=1)
    y = xx[:, 0:L] * w[0]
    for j in range(1, CONV_W):
        y = y + xx[:, j:j + L] * w[j]
    return y, xx[:, L:]


def rope(x, pos):
    half = MLA_ROPE // 2
    freqs = ROPE_THETA ** (-jnp.arange(half, dtype=F32) / half)
    ang = pos.astype(F32)[:, None] * freqs
    ang = ang.reshape((1, pos.shape[0]) + (1,) * (x.ndim - 3) + (half,))
    c, s = jnp.cos(ang), jnp.sin(ang)
    xf = x.astype(F32)
    x1, x2 = xf[..., :half], xf[..., half:]
    return jnp.concatenate([x1 * c - x2 * s, x1 * s + x2 * c], axis=-1).astype(x.dtype)


def lru_mixer(h, h0, buf, w_in, conv_w, conv_b, w_a, b_a, w_i, b_i, lam, w_out):
    B, L, _ = h.shape
    gx = h @ w_in
    gate = jax.nn.gelu(gx[..., :LRU_WIDTH])
    xc, new_buf = causal_dwconv(gx[..., LRU_WIDTH:], buf, conv_w)
    xc = xc + conv_b
    xb = xc.reshape(B, L, LRU_BLOCKS, LRU_BW)
    r = jax.nn.sigmoid(jnp.einsum('blnc,ncd->blnd', xb, w_a).reshape(B, L, LRU_WIDTH) + b_a)
    i = jax.nn.sigmoid(jnp.einsum('blnc,ncd->blnd', xb, w_i).reshape(B, L, LRU_WIDTH) + b_i)
    log_a = (-LRU_C * jax.nn.softplus(-lam.astype(F32))) * r.astype(F32)
    a = jnp.exp(log_a)
    u = jnp.sqrt(-jnp.expm1(2.0 * log_a)) * (i * xc).astype(F32)
    u = u.at[:, 0].add(a[:, 0] * h0.astype(F32))

    def comb(left, right):
        return (left[0] * right[0], right[0] * left[1] + right[1])

    _, hs = lax.associative_scan(comb, (a, u), axis=1)
    y = (hs.astype(h.dtype) * gate) @ w_out
    return y, hs[:, -1].astype(h.dtype), new_buf


def gated_delta_rule(q, k, v, g, beta, S0, chunk):
    B, L, H, DK = q.shape
    DV = v.shape[-1]
    N, C = L // chunk, chunk

    def blk(t):
        t = t.reshape((B, N, C, H) + t.shape[3:])
        return jnp.moveaxis(t, (1, 3), (0, 2))

    qc = blk(q) * (DK ** -0.5)
    kc, vc, gc, bc = blk(k), blk(v), blk(g), blk(beta)
    G = jnp.cumsum(gc, axis=-1)
    tril = jnp.tril(jnp.ones((C, C), bool))
    strict = jnp.tril(jnp.ones((C, C), bool), -1)
    decay = jnp.exp(jnp.where(tril, G[..., :, None] - G[..., None, :], -jnp.inf))
    kk = jnp.einsum('nbhcd,nbhsd->nbhcs', kc, kc)
    A = jnp.where(strict, bc[..., :, None] * kk * decay, 0.0) + jnp.eye(C, dtype=F32)
    rhs = jnp.concatenate([vc * bc[..., None], kc * (bc * jnp.exp(G))[..., None]], axis=-1)
    sol = lax.linalg.triangular_solve(A, rhs, left_side=True, lower=True, unit_diagonal=True)
    u, w = sol[..., :DV], sol[..., DV:]
    attn = jnp.einsum('nbhcd,nbhsd->nbhcs', qc, kc) * decay
    q_dec = qc * jnp.exp(G)[..., None]
    k_dec = kc * jnp.exp(G[..., -1:] - G)[..., None]
    g_last = G[..., -1]

    def step(S, xs):
        u_n, w_n, qd, kd, at, gl = xs
        delta = u_n - jnp.einsum('bhcd,bhdv->bhcv', w_n, S)
        o = jnp.einsum('bhcd,bhdv->bhcv', qd, S) + jnp.einsum('bhcs,bhsv->bhcv', at, delta)
        S = S * jnp.exp(gl)[..., None, None] + jnp.einsum('bhcd,bhcv->bhdv', kd, delta)
        return S, o

    S, o = lax.scan(step, S0, (u, w, q_dec, k_dec, attn, g_last))
    o = jnp.moveaxis(o, (0, 2), (1, 3)).reshape(B, L, H, DV)
    return S, o


def deltanet_mixer(h, S0, buf, lead, w_in, conv_w, a_log, dt_bias, norm_g, w_out):
    B, L, _ = h.shape
    HV = DN_HEADS * DN_DV
    proj = h @ w_in
    qkv, new_buf = causal_dwconv(proj[..., :DN_QKV], buf, conv_w)
    qkv = jax.nn.silu(qkv).astype(F32)
    z = proj[..., DN_QKV:DN_QKV + HV].reshape(B, L, DN_HEADS, DN_DV).astype(F32)
    b = proj[..., DN_QKV + HV:DN_QKV + HV + DN_HEADS].astype(F32)
    a = proj[..., DN_QKV + HV + DN_HEADS:].astype(F32)
    q = l2norm(qkv[..., :DN_HEADS * DN_DK].reshape(B, L, DN_HEADS, DN_DK))
    k = l2norm(qkv[..., DN_HEADS * DN_DK:2 * DN_HEADS * DN_DK].reshape(B, L, DN_HEADS, DN_DK))
    v = qkv[..., 2 * DN_HEADS * DN_DK:].reshape(B, L, DN_HEADS, DN_DV)
    beta = jax.nn.sigmoid(b)
    g = -jnp.exp(a_log.astype(F32)) * jax.nn.softplus(a + dt_bias.astype(F32))
    S0 = S0.astype(F32)
    if lead > 0:
        S1, o1 = gated_delta_rule(q[:, :lead], k[:, :lead], v[:, :lead], g[:, :lead], beta[:, :lead], S0, lead)
        S, o2 = gated_delta_rule(q[:, lead:], k[:, lead:], v[:, lead:], g[:, lead:], beta[:, lead:], S1, DN_CHUNK)
        o = jnp.concatenate([o1, o2], axis=1)
    else:
        S, o = gated_delta_rule(q, k, v, g, beta, S0, L)
    o = rmsnorm(o, norm_g) * jax.nn.silu(z)
    y = o.reshape(B, L, HV).astype(h.dtype) @ w_out
    return y, S.astype(h.dtype), new_buf


def mla_attend_prompt(q_lat, q_pe, ckv, kpe):
    B, L, H, _ = q_lat.shape
    nblk = -(-L // Q_BLOCK)
    Lp = nblk * Q_BLOCK

    def blocks(t):
        t = jnp.pad(t, ((0, 0), (0, Lp - L), (0, 0), (0, 0)))
        return jnp.swapaxes(t.reshape((B, nblk, Q_BLOCK) + t.shape[2:]), 0, 1)

    kpos = jnp.arange(L)

    def one_block(args):
        ql, qp, start = args
        s = (jnp.einsum('bqhc,bkc->bhqk', ql, ckv) + jnp.einsum('bqhr,bkr->bhqk', qp, kpe)).astype(F32) * MLA_SCALE
        qpos = start + jnp.arange(Q_BLOCK)
        s = jnp.where(kpos[None, :] <= qpos[:, None], s, -jnp.inf)
        p = jax.nn.softmax(s, axis=-1).astype(ckv.dtype)
        return jnp.einsum('bhqk,bkc->bqhc', p, ckv)

    out = lax.map(one_block, (blocks(q_lat), blocks(q_pe), jnp.arange(nblk) * Q_BLOCK))
    return jnp.swapaxes(out, 0, 1).reshape(B, Lp, H, MLA_KV_RANK)[:, :L]


def mla_attend_sample(q_lat, q_pe, ckv, kpe, pool_ckv, pool_kpe, page_table):
    DB, S = q_lat.shape[:2]
    ckv_past = pool_ckv[page_table].reshape(DB, -1, MLA_KV_RANK)
    kpe_past = pool_kpe[page_table].reshape(DB, -1, MLA_ROPE)
    P = ckv_past.shape[1]
    s_past = (jnp.einsum('bqhc,bkc->bhqk', q_lat, ckv_past) + jnp.einsum('bqhr,bkr->bhqk', q_pe, kpe_past)).astype(F32) * MLA_SCALE
    s_new = (jnp.einsum('bqhc,bkc->bhqk', q_lat, ckv) + jnp.einsum('bqhr,bkr->bhqk', q_pe, kpe)).astype(F32) * MLA_SCALE
    s_new = jnp.where(jnp.tril(jnp.ones((S, S), bool)), s_new, -jnp.inf)
    p = jax.nn.softmax(jnp.concatenate([s_past, s_new], axis=-1), axis=-1).astype(ckv.dtype)
    return jnp.einsum('bhqk,bkc->bqhc', p[..., :P], ckv_past) + jnp.einsum('bhqk,bkc->bqhc', p[..., P:], ckv)


def mla_mixer(h, pos0, pool_ckv, pool_kpe, page_table, w_dq, q_norm, w_uq, w_dkv, kv_norm, w_uk, w_uv, w_o):
    B, L, _ = h.shape
    pos = pos0 + jnp.arange(L, dtype=jnp.int32)
    cq = rmsnorm(h @ w_dq, q_norm)
    q = (cq @ w_uq).reshape(B, L, MLA_HEADS, MLA_NOPE + MLA_ROPE)
    q_nope, q_pe = q[..., :MLA_NOPE], rope(q[..., MLA_NOPE:], pos)
    kv = h @ w_dkv
    ckv = rmsnorm(kv[..., :MLA_KV_RANK], kv_norm)
    kpe = rope(kv[..., MLA_KV_RANK:], pos)
    q_lat = jnp.einsum('blhn,chn->blhc', q_nope, w_uk)
    if pool_ckv is None:
        out_lat = mla_attend_prompt(q_lat, q_pe, ckv, kpe)
    else:
        out_lat = mla_attend_sample(q_lat, q_pe, ckv, kpe, pool_ckv, pool_kpe, page_table)
    o = jnp.einsum('blhc,chv->blhv', out_lat, w_uv).reshape(B, L, MLA_HEADS * MLA_V)
    return o @ w_o, ckv, kpe


def swiglu(h, w_gu, w_down):
    gu = h @ w_gu
    return (jax.nn.silu(gu[..., :D_FF]) * gu[..., D_FF:]) @ w_down


def trunk(x, pos0, lead, lru_h, lru_conv, dn_S, dn_conv, pool_ckv, pool_kpe, page_table, P):
    names = ('lru_h', 'lru_conv', 'dn_S', 'dn_conv', 'ckv', 'kpe')
    new = {n: [] for n in names}
    for i in range(DEPTH):
        kind, j = i % N_MIXERS, i // N_MIXERS
        h = rmsnorm(x, P['norm_mix'][i])
        if kind == 0:
            y, hN, cb = lru_mixer(h, lru_h[j], lru_conv[j], P['lru_w_in'][j], P['lru_conv_w'][j], P['lru_conv_b'][j],
                                  P['lru_w_a'][j], P['lru_b_a'][j], P['lru_w_i'][j], P['lru_b_i'][j],
                                  P['lru_lambda'][j], P['lru_w_out'][j])
            new['lru_h'].append(hN)
            new['lru_conv'].append(cb)
        elif kind == 1:
            y, SN, cb = deltanet_mixer(h, dn_S[j], dn_conv[j], lead, P['dn_w_in'][j], P['dn_conv_w'][j],
                                       P['dn_a_log'][j], P['dn_dt_bias'][j], P['dn_norm'][j], P['dn_w_out'][j])
            new['dn_S'].append(SN)
            new['dn_conv'].append(cb)
        else:
            pc = None if pool_ckv is None else pool_ckv[j]
            pk = None if pool_kpe is None else pool_kpe[j]
            y, ckv, kpe = mla_mixer(h, pos0, pc, pk, page_table, P['mla_w_dq'][j], P['mla_q_norm'][j],
                                    P['mla_w_uq'][j], P['mla_w_dkv'][j], P['mla_kv_norm'][j],
                                    P['mla_w_uk'][j], P['mla_w_uv'][j], P['mla_w_o'][j])
            new['ckv'].append(ckv)
            new['kpe'].append(kpe)
        x = x + y
        x = x + swiglu(rmsnorm(x, P['norm_ffn'][i]), P['ffn_w_gu'][i], P['ffn_w_down'][i])
    return rmsnorm(x, P['norm_final']), {n: jnp.stack(new[n]) for n in names}


def setup_inputs(seed: int = 0) -> dict:
    key = jax.random.key(seed)
    keys = jax.random.split(key, 64)
    ctr = [0]

    def nk():
        ctr[0] += 1
        return keys[ctr[0] - 1]

    def nrm(shape, scale=1.0):
        return jax.random.normal(nk(), shape, F32) * scale

    def unif(shape, lo, hi):
        return jax.random.uniform(nk(), shape, F32, lo, hi)

    def gain(shape):
        return 1.0 + nrm(shape, 0.02)

    n_pages = PAST_LEN // PAGE_SIZE
    n_used = DEC_BATCH * n_pages
    n_pool = n_used + n_used // 4
    perm = jax.random.permutation(nk(), n_pool)
    page_table = perm[:n_used].reshape(DEC_BATCH, n_pages).astype(jnp.int32)

    s = unif((N_LRU, LRU_WIDTH), 0.9, 0.999) ** (1.0 / LRU_C)
    lru_lambda = jnp.log(s) - jnp.log1p(-s)
    dt = jnp.exp(unif((N_DN, DN_HEADS), math.log(1e-3), math.log(1e-1)))
    dn_dt_bias = dt + jnp.log(-jnp.expm1(-dt))
    dn_a_log = jnp.log(unif((N_DN, DN_HEADS), 1.0, 16.0))
    D = D_MODEL
    return {
        'x_prompt': nrm((BATCH, SEQ, D)),
        'x_sample': nrm((DEC_BATCH, DEC_SEQ, D)),
        'state_lru_h': nrm((N_LRU, DEC_BATCH, LRU_WIDTH), 0.5),
        'state_lru_conv': nrm((N_LRU, DEC_BATCH, CONV_W - 1, LRU_WIDTH)),
        'state_dn_S': nrm((N_DN, DEC_BATCH, DN_HEADS, DN_DK, DN_DV), DN_DK ** -0.5),
        'state_dn_conv': nrm((N_DN, DEC_BATCH, CONV_W - 1, DN_QKV)),
        'cache_mla_ckv': nrm((N_MLA, n_pool, PAGE_SIZE, MLA_KV_RANK)),
        'cache_mla_kpe': nrm((N_MLA, n_pool, PAGE_SIZE, MLA_ROPE)),
        'page_table': page_table,
        'meta_tokens': nrm((N_META, D)),
        'norm_mix': gain((DEPTH, D)),
        'norm_ffn': gain((DEPTH, D)),
        'norm_final': gain((D,)),
        'lru_w_in': nrm((N_LRU, D, 2 * LRU_WIDTH), D ** -0.5),
        'lru_conv_w': nrm((N_LRU, CONV_W, LRU_WIDTH), CONV_W ** -0.5),
        'lru_conv_b': nrm((N_LRU, LRU_WIDTH), 0.01),
        'lru_w_a': nrm((N_LRU, LRU_BLOCKS, LRU_BW, LRU_BW), LRU_BW ** -0.5),
        'lru_b_a': nrm((N_LRU, LRU_WIDTH), 0.1),
        'lru_w_i': nrm((N_LRU, LRU_BLOCKS, LRU_BW, LRU_BW), LRU_BW ** -0.5),
        'lru_b_i': nrm((N_LRU, LRU_WIDTH), 0.1),
        'lru_lambda': lru_lambda,
        'lru_w_out': nrm((N_LRU, LRU_WIDTH, D), LRU_WIDTH ** -0.5),
        'dn_w_in': nrm((N_DN, D, DN_QKV + DN_HEADS * DN_DV + 2 * DN_HEADS), D ** -0.5),
        'dn_conv_w': nrm((N_DN, CONV_W, DN_QKV), CONV_W ** -0.5),
        'dn_a_log': dn_a_log,
        'dn_dt_bias': dn_dt_bias,
        'dn_norm': gain((N_DN, DN_DV)),
        'dn_w_out': nrm((N_DN, DN_HEADS * DN_DV, D), (DN_HEADS * DN_DV) ** -0.5),
        'mla_w_dq': nrm((N_MLA, D, MLA_Q_RANK), D ** -0.5),
        'mla_q_norm': gain((N_MLA, MLA_Q_RANK)),
        'mla_w_uq': nrm((N_MLA, MLA_Q_RANK, MLA_HEADS * (MLA_NOPE + MLA_ROPE)), MLA_Q_RANK ** -0.5),
        'mla_w_dkv': nrm((N_MLA, D, MLA_KV_RANK + MLA_ROPE), D ** -0.5),
        'mla_kv_norm': gain((N_MLA, MLA_KV_RANK)),
        'mla_w_uk': nrm((N_MLA, MLA_KV_RANK, MLA_HEADS, MLA_NOPE), MLA_KV_RANK ** -0.5),
        'mla_w_uv': nrm((N_MLA, MLA_KV_RANK, MLA_HEADS, MLA_V), MLA_KV_RANK ** -0.5),
        'mla_w_o': nrm((N_MLA, MLA_HEADS * MLA_V, D), (MLA_HEADS * MLA_V) ** -0.5),
        'ffn_w_gu': nrm((DEPTH, D, 2 * D_FF), D ** -0.5),
        'ffn_w_down': nrm((DEPTH, D_FF, D), D_FF ** -0.5),
    }


def reference(x_prompt, x_sample, state_lru_h, state_lru_conv, state_dn_S, state_dn_conv,
              cache_mla_ckv, cache_mla_kpe, page_table, meta_tokens, norm_mix, norm_ffn, norm_final,
              lru_w_in, lru_conv_w, lru_conv_b, lru_w_a, lru_b_a, lru_w_i, lru_b_i, lru_lambda, lru_w_out,
              dn_w_in, dn_conv_w, dn_a_log, dn_dt_bias, dn_norm, dn_w_out,
              mla_w_dq, mla_q_norm, mla_w_uq, mla_w_dkv, mla_kv_norm, mla_w_uk, mla_w_uv, mla_w_o,
              ffn_w_gu, ffn_w_down):
    P = {
        'norm_mix': norm_mix, 'norm_ffn': norm_ffn, 'norm_final': norm_final,
        'lru_w_in': lru_w_in, 'lru_conv_w': lru_conv_w, 'lru_conv_b': lru_conv_b,
        'lru_w_a': lru_w_a, 'lru_b_a': lru_b_a, 'lru_w_i': lru_w_i, 'lru_b_i': lru_b_i,
        'lru_lambda': lru_lambda, 'lru_w_out': lru_w_out,
        'dn_w_in': dn_w_in, 'dn_conv_w': dn_conv_w, 'dn_a_log': dn_a_log, 'dn_dt_bias': dn_dt_bias,
        'dn_norm': dn_norm, 'dn_w_out': dn_w_out,
        'mla_w_dq': mla_w_dq, 'mla_q_norm': mla_q_norm, 'mla_w_uq': mla_w_uq, 'mla_w_dkv': mla_w_dkv,
        'mla_kv_norm': mla_kv_norm, 'mla_w_uk': mla_w_uk, 'mla_w_uv': mla_w_uv, 'mla_w_o': mla_w_o,
        'ffn_w_gu': ffn_w_gu, 'ffn_w_down': ffn_w_down,
    }
    dt = x_prompt.dtype
    B = x_prompt.shape[0]
    meta = jnp.broadcast_to(meta_tokens.astype(dt)[None], (B, N_META, D_MODEL))
    x_full = jnp.concatenate([meta, x_prompt], axis=1)
    z_h = jnp.zeros((N_LRU, B, LRU_WIDTH), dt)
    z_c = jnp.zeros((N_LRU, B, CONV_W - 1, LRU_WIDTH), dt)
    z_S = jnp.zeros((N_DN, B, DN_HEADS, DN_DK, DN_DV), dt)
    z_dc = jnp.zeros((N_DN, B, CONV_W - 1, DN_QKV), dt)
    y_full, pn = trunk(x_full, 0, N_META, z_h, z_c, z_S, z_dc, None, None, None, P)
    y_prompt = y_full[:, N_META:]
    past_len = page_table.shape[1] * cache_mla_ckv.shape[2]
    y_sample, sn = trunk(x_sample, past_len, 0, state_lru_h, state_lru_conv, state_dn_S, state_dn_conv,
                         cache_mla_ckv, cache_mla_kpe, page_table, P)
    return (y_prompt, y_sample,
            pn['lru_h'], pn['lru_conv'], pn['dn_S'], pn['dn_conv'], pn['ckv'], pn['kpe'],
            sn['lru_h'], sn['lru_conv'], sn['dn_S'], sn['dn_conv'], sn['ckv'], sn['kpe'])
```

```python
import bisect
import os
from contextlib import ExitStack

import numpy as np
import concourse.bass as bass
import concourse.mybir as mybir
from concourse.bass_utils import run_bass_kernel_spmd

F32 = mybir.dt.float32
BF16 = mybir.dt.bfloat16
I32 = mybir.dt.int32
AF = mybir.ActivationFunctionType
ALU = mybir.AluOpType
AX = mybir.AxisListType

NCORES = 8
D = 1024
KC = 8
SEQ = 2048
NMETA = 16
TP = SEQ + NMETA
NS = 4
T = TP + NS
DFF = 2816
FC = DFF // 128
NPAGES = 128
PAGE = 128
NPOOL = 5120
EPS = 1e-6
MLA_SCALE = (128 + 64) ** -0.5
TT = [(0, 512), (512, 512), (1024, 512), (1536, 512), (2048, 20)]
WSLOT = 2048


class _Op:
    __slots__ = ("eng", "fn", "deps", "dma_sem", "dma_val", "idx", "milestone", "mval", "waits", "dma_deps")


class _IMap:
    def __init__(self, size):
        self.b = [0, size]
        self.r = [[None, {}]]

    def _split(self, x):
        i = bisect.bisect_left(self.b, x)
        if self.b[i] == x:
            return i
        w, rd = self.r[i - 1]
        self.b.insert(i, x)
        self.r.insert(i, [w, dict(rd)])
        return i

    def read(self, lo, hi, op, key, deps):
        i = self._split(lo)
        j = self._split(hi)
        for k in range(i, j):
            rec = self.r[k]
            if rec[0] is not None:
                deps.add(rec[0])
            rec[1][key] = op

    def write(self, lo, hi, op, deps):
        i = self._split(lo)
        j = self._split(hi)
        for k in range(i, j):
            rec = self.r[k]
            if rec[0] is not None:
                deps.add(rec[0])
            deps.update(rec[1].values())
        self.b[i:j + 1] = [lo, hi]
        self.r[i:j] = [[op, {}]]


class Sched:
    ENGS = ("pe", "act", "dve", "pool", "sp")

    def __init__(self, nc):
        self.nc = nc
        self.ops = {e: [] for e in self.ENGS}
        self.maps = {"SB": _IMap(1 << 20), "PSUM": _IMap(1 << 16)}
        self.dma_cnt = {}
        self.total_sems = set()
        self.bases = {}
        self.sb_ptr = (nc.sbuf_base + 63) // 64 * 64
        self.sb_top = nc.sbuf_top
        self.nalloc = 0

    def sb(self, name, shape, dtype):
        esz = 2 if dtype == BF16 else 4
        n = 1
        for s in shape[1:]:
            n *= s
        nbytes = (n * esz + 63) // 64 * 64
        off = self.sb_ptr
        self.sb_ptr += nbytes
        assert self.sb_ptr <= self.sb_top, f"SBUF overflow at {name}: {self.sb_ptr} > {self.sb_top}"
        self.nalloc += 1
        t = self.nc.alloc_sbuf_tensor_at(f"{name}_{self.nalloc}", list(shape), dtype, offset=off)
        self.bases[t.name] = off
        return t

    def mark(self):
        return self.sb_ptr

    def release(self, m):
        self.sb_ptr = m

    def _range(self, ap):
        sp = str(ap.space)
        if "SB" in sp:
            m = self.maps["SB"]
        elif "PSUM" in sp:
            m = self.maps["PSUM"]
        else:
            return None
        esz = 2 if ap.dtype == BF16 else 4
        pat = ap.ap
        pstride = pat[0][0]
        off = ap.offset % pstride if pstride > 0 else ap.offset
        ext = 1
        for st, cnt in pat[1:]:
            ext += (cnt - 1) * abs(st)
        base = self.bases.get(ap.tensor.name, 0)
        lo = base + off * esz
        hi = lo + ext * esz
        if m is self.maps["PSUM"]:
            lo = lo // 2048 * 2048
            hi = (hi + 2047) // 2048 * 2048
        return m, lo, hi

    def rec(self, eng, fn, reads=(), writes=(), dma_sem=None):
        op = _Op()
        op.eng = eng
        op.fn = fn
        op.dma_sem = dma_sem
        op.milestone = False
        op.mval = 0
        key = eng if dma_sem is None else ("dma", dma_sem)
        deps = set()
        for ap in reads:
            if ap is None or isinstance(ap, (int, float)):
                continue
            r = self._range(ap)
            if r:
                if r[0] is self.maps["PSUM"]:
                    r[0].write(r[1], r[2], op, deps)
                else:
                    r[0].read(r[1], r[2], op, key, deps)
        for ap in writes:
            r = self._range(ap)
            if r:
                r[0].write(r[1], r[2], op, deps)
        deps.discard(op)
        op.deps = []
        op.dma_deps = {}
        for d in deps:
            if d.dma_sem is not None:
                s = d.dma_sem
                v = self.dma_cnt[s]
                if op.dma_deps.get(s, 0) < v:
                    op.dma_deps[s] = v
            else:
                op.deps.append(d)
        if dma_sem is not None:
            self.dma_cnt[dma_sem] = self.dma_cnt.get(dma_sem, 0) + 16
            op.dma_val = self.dma_cnt[dma_sem]
        op.idx = len(self.ops[eng])
        self.ops[eng].append(op)
        return op

    def mm(self, out, lhsT, rhs, start=True, stop=True):
        return self.rec("pe", lambda e: e.matmul(out, lhsT=lhsT, rhs=rhs, start=start, stop=stop),
                        [lhsT, rhs], [out])

    def tr(self, out, in_, ident):
        return self.rec("pe", lambda e: e.transpose(out=out, in_=in_, identity=ident), [in_, ident], [out])

    def act(self, out, in_, func, bias=None, scale=1.0, accum_out=None):
        kw = {}
        if bias is not None:
            kw["bias"] = bias
        if accum_out is not None:
            kw["accum_out"] = accum_out
        w = [out] + ([accum_out] if accum_out is not None else [])
        return self.rec("act", lambda e: e.activation(out=out, in_=in_, func=func, scale=scale, **kw),
                        [in_, bias, scale], w)

    def tt(self, eng, out, in0, in1, op):
        return self.rec(eng, lambda e: e.tensor_tensor(out=out, in0=in0, in1=in1, op=op), [in0, in1], [out])

    def ts(self, eng, out, in0, s1, op0, s2=None, op1=None, accum_out=None):
        kw = {}
        if op1 is not None:
            kw["op1"] = op1
        if accum_out is not None:
            kw["accum_out"] = accum_out
        w = [out] + ([accum_out] if accum_out is not None else [])
        return self.rec(eng, lambda e: e.tensor_scalar(out=out, in0=in0, scalar1=s1, scalar2=s2, op0=op0, **kw),
                        [in0, s1, s2], w)

    def stt(self, out, in0, scalar, in1, op0, op1, eng="dve"):
        return self.rec(eng, lambda e: e.scalar_tensor_tensor(out=out, in0=in0, scalar=scalar, in1=in1,
                                                              op0=op0, op1=op1), [in0, scalar, in1], [out])

    def copy(self, eng, out, in_):
        if eng == "act":
            return self.rec("act", lambda e: e.copy(out=out, in_=in_), [in_], [out])
        return self.rec(eng, lambda e: e.tensor_copy(out=out, in_=in_), [in_], [out])

    def memset(self, eng, ap, val):
        return self.rec(eng, lambda e: e.memset(ap, val), [], [ap])

    def recip(self, out, in_):
        return self.rec("dve", lambda e: e.reciprocal(out=out, in_=in_), [in_], [out])

    def scan(self, out, d0, d1, initial, op0=ALU.mult, op1=ALU.add):
        return self.rec("dve", lambda e: e.tensor_tensor_scan(out=out, data0=d0, data1=d1, initial=initial,
                                                              op0=op0, op1=op1), [d0, d1, initial], [out])

    def reduce(self, out, in_, op, axis=AX.X):
        return self.rec("dve", lambda e: e.tensor_reduce(out=out, in_=in_, axis=axis, op=op), [in_], [out])

    def dma(self, out, in_, sem, eng="sp"):
        return self.rec(eng, lambda e: e.dma_start(out=out, in_=in_), [in_], [out], dma_sem=sem)

    def gather(self, out, in_, idx_ap, sem):
        return self.rec("pool", lambda e: e.indirect_dma_start(
            out=out, out_offset=None, in_=in_, in_offset=bass.IndirectOffsetOnAxis(ap=idx_ap, axis=0)),
            [idx_ap], [out], dma_sem=sem)

    def emit(self):
        nc = self.nc
        ops = self.ops
        for e in self.ENGS:
            seen = {f: -1 for f in self.ENGS}
            seen_dma = {}
            for op in ops[e]:
                keep = {}
                for d in op.deps:
                    f = d.eng
                    if f == e and e in ("pe", "sp"):
                        continue
                    if d.idx > seen[f] and d.idx > keep.get(f, (-1, None))[0]:
                        keep[f] = (d.idx, d)
                op.waits = []
                for f, (i, d) in keep.items():
                    seen[f] = i
                    d.milestone = True
                    op.waits.append(d)
                dw = []
                for s, v in op.dma_deps.items():
                    if s in self.total_sems:
                        v = -1
                    if seen_dma.get(s, 0) < v or v == -1:
                        if v == -1 and seen_dma.get(s, 0) == -1:
                            continue
                        seen_dma[s] = v
                        dw.append((s, v))
                op.dma_deps = dw
        for e in self.ENGS:
            c = 0
            for op in ops[e]:
                if op.milestone:
                    c += 1
                    op.mval = c
        self.nmil = {e: sum(1 for o in ops[e] if o.milestone) for e in self.ENGS}
        with ExitStack() as st:
            esem = {e: st.enter_context(nc.semaphore(f"e_{e}")) for e in self.ENGS}
            dsem = {s: st.enter_context(nc.semaphore(f"d_{s}")) for s in self.dma_cnt}
            block = st.enter_context(nc.Block())

            def run(e, eng):
                for op in ops[e]:
                    for d in op.waits:
                        eng.wait_ge(esem[d.eng], d.mval)
                    for s, v in op.dma_deps:
                        eng.wait_ge(dsem[s], self.dma_cnt[s] if v == -1 else v)
                    ins = op.fn(eng)
                    if op.dma_sem is not None:
                        ins.then_inc(dsem[op.dma_sem], 16)
                    elif op.milestone:
                        ins.then_inc(esem[e], 1)
                if e == "sp":
                    for s, v in self.dma_cnt.items():
                        eng.wait_ge(dsem[s], v)

            @block.tensor
            def _(eng):
                run("pe", eng)

            @block.scalar
            def _(eng):
                run("act", eng)

            @block.vector
            def _(eng):
                run("dve", eng)

            @block.gpsimd
            def _(eng):
                run("pool", eng)

            @block.sync
            def _(eng):
                run("sp", eng)


def _units_proj(W, gf):
    K, N = W.shape
    kc = K // 128
    return np.ascontiguousarray(W.reshape(kc, 128, N // gf, gf).transpose(2, 1, 0, 3).reshape(N // gf, 128, kc * gf))


def _cols(v):
    v = np.asarray(v, np.float32).reshape(-1, 128)
    return np.ascontiguousarray(v.T)


class _ColPack:
    def __init__(self):
        self.parts = []
        self.n = 0
        self.idx = {}

    def add(self, name, arr):
        arr = np.asarray(arr, np.float32)
        assert arr.shape[0] == 128
        self.idx[name] = self.n
        self.parts.append(arr)
        self.n += arr.shape[1]

    def build(self):
        return np.ascontiguousarray(np.concatenate(self.parts, axis=1))


def _prep_shared(inp):
    sh = {}
    cp = _ColPack()
    for i in range(4):
        cp.add(f"nmix{i}", _cols(inp["norm_mix"][i]))
        cp.add(f"nffn{i}", _cols(inp["norm_ffn"][i]))
    cp.add("nfinal", _cols(inp["norm_final"]))
    for j in range(2):
        for k in range(4):
            cp.add(f"lru_cw{j}_{k}", _cols(inp["lru_conv_w"][j, k]))
        cp.add(f"lru_cb{j}", _cols(inp["lru_conv_b"][j]))
        cp.add(f"lru_ba{j}", _cols(inp["lru_b_a"][j]))
        cp.add(f"lru_bi{j}", _cols(inp["lru_b_i"][j]))
        cp.add(f"lru_lam{j}", _cols(inp["lru_lambda"][j]))
        w_in = inp["lru_w_in"][j]
        u = []
        for n in range(4):
            u.append(_units_proj(w_in[:, n * 256:(n + 1) * 256], 256)[0])
            u.append(_units_proj(w_in[:, 1024 + n * 256:1024 + (n + 1) * 256], 256)[0])
        sh[f"lru_win{j}"] = np.stack(u)
        wa, wi = inp["lru_w_a"][j], inp["lru_w_i"][j]
        g = []
        for n in range(4):
            a = _units_proj(wa[n], 256)[0]
            b = _units_proj(wi[n], 256)[0]
            g.append(np.concatenate([a, b], axis=1))
        sh[f"lru_wg{j}"] = np.stack(g)
        wo = inp["lru_w_out"][j]
        sh[f"lru_wout{j}"] = np.stack([_units_proj(wo[n * 256:(n + 1) * 256], 1024)[0] for n in range(4)])
    for i in range(4):
        wgu = inp["ffn_w_gu"][i]
        g = _units_proj(wgu[:, :DFF], 128)
        u = _units_proj(wgu[:, DFF:], 128)
        sh[f"ffn_gu{i}"] = np.ascontiguousarray(
            np.stack([g.reshape(FC, 128, 8, 128), u.reshape(FC, 128, 8, 128)], axis=3).reshape(FC, 128, 2048))
        wd = inp["ffn_w_down"][i]
        hv = []
        for half in range(2):
            hv.append(_units_proj(wd[half * 1408:(half + 1) * 1408], 128))
        sh[f"ffn_dn{i}"] = np.ascontiguousarray(np.stack(hv).reshape(16, 128, 1408))
    _prep_mla(inp, sh, cp)
    _prep_dn(inp, sh, cp)
    sh["cols"] = cp.build()
    sh["_colidx"] = cp.idx
    sh["ones_bf"] = np.ones((128, 128), np.float32)
    sh["ident"] = np.eye(128, dtype=np.float32)
    return sh


def _prep_mla(inp, sh, cp):
    cp.add("mla_qn", _cols(inp["mla_q_norm"][0]))
    cp.add("mla_kvn", _cols(inp["mla_kv_norm"][0]))
    wdkv = inp["mla_w_dkv"][0]
    sh["mla_dkv_c"] = _units_proj(wdkv[:, :256], 256)
    perm = np.concatenate([np.arange(32, 64), np.arange(0, 32)])
    kr = np.concatenate([wdkv[:, 256:320], wdkv[:, 256 + perm]], axis=1)
    sh["mla_dkv_r"] = _units_proj(kr, 128)
    sh["mla_dq"] = _units_proj(inp["mla_w_dq"][0], 256)
    wuq = inp["mla_w_uq"][0].reshape(512, 8, 192)
    wuk = inp["mla_w_uk"][0]
    wuv = inp["mla_w_uv"][0]
    wo = inp["mla_w_o"][0]
    u1, u2 = [], []
    for h in range(8):
        q = np.concatenate([wuq[:, h, :128], wuq[:, h, 128:192], wuq[:, h, 128 + perm]], axis=1)
        a = _units_proj(q, 256)[0]
        b = np.ascontiguousarray(wuk[:, h, :].T)
        u1.append(np.concatenate([a, b], axis=1))
        v = _units_proj(wuv[:, h, :], 128)[0]
        o = wo[h * 128:(h + 1) * 128, :]
        u2.append(np.concatenate([v, o], axis=1))
    sh["mla_u1"] = np.stack(u1)
    sh["mla_u2"] = np.stack(u2)
    half = 32
    freqs = (10000.0 ** (-np.arange(half, dtype=np.float32) / half)).astype(np.float32)
    pos = np.concatenate([np.arange(TP), np.full(NS, NPAGES * PAGE)]).astype(np.float32)
    ang = pos[None, :] * freqs[:, None]
    c, sn = np.cos(ang).astype(np.float32), np.sin(ang).astype(np.float32)
    rope = np.stack([np.concatenate([c, c], axis=0), np.concatenate([-sn, sn], axis=0)], axis=1)
    sh["rope"] = np.ascontiguousarray(rope.astype(np.float32))
    sh["tri"] = np.triu(np.ones((128, 128), np.float32))
    pool = np.concatenate([inp["cache_mla_ckv"][0], inp["cache_mla_kpe"][0]], axis=-1)
    sh["poolkv"] = pool.reshape(NPOOL * 32, 4 * 320)


def _prep_dn(inp, sh, cp):
    w = inp["dn_w_in"][0]
    qk, vz = [], []
    for h in range(8):
        qk.append(_units_proj(np.concatenate([w[:, h * 128:(h + 1) * 128], w[:, 1024 + h * 128:1024 + (h + 1) * 128]], axis=1), 256)[0])
        vz.append(_units_proj(np.concatenate([w[:, 2048 + h * 128:2048 + (h + 1) * 128], w[:, 3072 + h * 128:3072 + (h + 1) * 128]], axis=1), 256)[0])
    sh["dn_qk"] = np.stack(qk)
    sh["dn_vz"] = np.stack(vz)
    sh["dn_ba"] = _units_proj(w[:, 4096:4112], 16)
    for q in range(4):
        cp.add(f"dn_cw{q}", _cols(inp["dn_conv_w"][0, q]))
    cp.add("dn_norm", _cols(inp["dn_norm"][0]))
    pad = np.zeros((128, 2), np.float32)
    pad[:8, 0] = inp["dn_a_log"][0]
    pad[:8, 1] = inp["dn_dt_bias"][0]
    cp.add("dn_ab", pad)
    wo = inp["dn_w_out"][0]
    sh["dn_wo"] = np.ascontiguousarray(wo.reshape(8, 128, 1024))
    sel = np.zeros((8, 8, 128), np.float32)
    for h in range(8):
        sel[h, h, :] = 1.0
    sh["dn_sel"] = sel.reshape(8, 1024)
    mask = np.ones((8, T), np.float32)
    mask[:, 0] = 0.0
    mask[:, 16:TP:64] = 0.0
    mask[:, TP:] = 0.0
    sh["dn_mask"] = mask
    r = np.arange(64)[:, None]
    c = np.arange(64)[None, :]
    mmax = np.where(c < r, 0.0, 30000.0).astype(np.float32)
    mmin = np.where(c >= r, 0.0, -30000.0).astype(np.float32)
    sh["dn_mm"] = np.ascontiguousarray(np.concatenate([mmax, mmin], axis=1))


def _prep_core(inp, c):
    x_full = np.concatenate([inp["meta_tokens"], inp["x_prompt"][c], inp["x_sample"][NS * c:NS * (c + 1), 0]], axis=0)
    pc = {}
    pc["xT"] = np.ascontiguousarray(x_full.reshape(T, KC, 128).transpose(2, 1, 0))
    lh = inp["state_lru_h"][:, NS * c:NS * (c + 1)]
    pc["s_lru_h"] = np.ascontiguousarray(lh.reshape(2, NS, KC, 128).transpose(3, 0, 2, 1))
    lc = inp["state_lru_conv"][:, NS * c:NS * (c + 1)]
    pc["s_lru_conv"] = np.ascontiguousarray(lc.reshape(2, NS, 3, KC, 128).transpose(4, 0, 3, 1, 2))
    pc["s_dn_S"] = np.ascontiguousarray(inp["state_dn_S"][0, NS * c:NS * (c + 1)].reshape(NS * 8, 128, 128))
    dc = inp["state_dn_conv"][0, NS * c:NS * (c + 1)]
    pc["s_dn_conv"] = np.ascontiguousarray(dc.reshape(NS, 3, 24, 128).transpose(3, 2, 0, 1))
    pc["pt"] = np.ascontiguousarray(inp["page_table"][NS * c:NS * (c + 1)].T.astype(np.int32))
    return pc


class Builder:
    def __init__(self, nc, S, shapes, colidx, stop_after=None):
        self.nc = nc
        self.S = S
        self.colidx = colidx
        self.stop_after = stop_after
        self.dram = {}
        for name, shp in shapes.items():
            self.dram[name] = nc.dram_tensor(name, list(shp), I32 if name == "pt" else F32, kind="ExternalInput").ap()
        self.ps = nc.alloc_psum_tensor("ps", [128, 4096], F32)
        self.ps_next = 0
        self.wq = []
        self.wi = 0
        self.outs = {}

    def out(self, name, shape):
        ap = self.nc.dram_tensor(name, list(shape), F32, kind="ExternalOutput").ap()
        self.outs[name] = ap
        return ap

    def bank(self):
        b = self.ps_next
        self.ps_next = (self.ps_next + 1) % 4
        return self.ps[:, b * 512:(b + 1) * 512]

    def col(self, name, k=0, n=1):
        i = self.colidx[name] + k
        return self.cols[:, i:i + n]

    def wload(self, name, u, nel):
        S = self.S
        slot = self.wi % self.nws
        ss = self.wi % self.nst
        self.wi += 1
        stg = self.wstage[ss]
        wb = self.wbf[slot]
        src = self.dram[name][u]
        S.dma(stg[:, 0:nel], src, sem=f"w{ss}")
        S.copy("pool", wb[:, 0:nel], stg[:, 0:nel])
        return wb

    def run_units(self, units, depth=2):
        loaded = []
        n = len(units)
        for i in range(n + depth):
            if i < n:
                nm, u, nel, _ = units[i]
                loaded.append(self.wload(nm, u, nel))
            j = i - depth
            if j >= 0:
                units[j][3](loaded[j])

    def setup(self):
        S = self.S
        nc = self.nc
        ncol = self.dram["cols"].shape[1]
        self.cols = S.sb("cols", [128, ncol], F32)
        S.dma(self.cols[:], self.dram["cols"], sem="init")
        S.total_sems.add("init")
        self.ones_f = S.sb("ones_f", [128, 128], F32)
        self.ident_f = S.sb("ident_f", [128, 128], F32)
        S.dma(self.ones_f[:], self.dram["ones_bf"], sem="init")
        S.dma(self.ident_f[:], self.dram["ident"], sem="init")
        self.ones_b = S.sb("ones_b", [128, 128], BF16)
        self.ident_b = S.sb("ident_b", [128, 128], BF16)
        S.copy("pool", self.ones_b[:], self.ones_f[:])
        S.copy("pool", self.ident_b[:], self.ident_f[:])
        self.x = [S.sb(f"x{k}", [128, T], F32) for k in range(KC)]
        for k in range(KC):
            S.dma(self.x[k][:], self.dram["xT"][:, k, :], sem="init")
        self.xn = [S.sb(f"xn{k}", [128, T], BF16) for k in range(KC)]
        self.nws = 3
        self.nst = 2
        self.wstage = [S.sb(f"wst{i}", [128, WSLOT], F32) for i in range(self.nst)]
        self.wbf = [S.sb(f"wbf{i}", [128, WSLOT], BF16) for i in range(self.nws)]
        self.sq = [S.sb(f"sq{i}", [128, 512], BF16) for i in range(2)]
        self.rstd = [S.sb(f"rstd{i}", [128, 512], F32) for i in range(2)]
        self.small = S.sb("small", [128, 320], F32)
        self.small_n = 0
        self.small_idx = {}

    def small_alloc(self, name, n):
        i = self.small_n
        self.small_idx[name] = (i, n)
        self.small_n += n
        assert self.small_n <= 320
        return self.small[:, i:i + n]

    def rmsnorm_stats(self, ti):
        S = self.S
        t0, n = TT[ti]
        acc = self.bank()
        for k in range(KC):
            sq = self.sq[k % 2]
            S.act(sq[:, 0:n], self.x[k][:, t0:t0 + n], AF.Square)
            S.mm(acc[:, 0:n], self.ones_b[:], sq[:, 0:n], start=(k == 0), stop=(k == KC - 1))
        r = self.rstd[ti % 2]
        S.act(r[:, 0:n], acc[:, 0:n], AF.Sqrt, bias=self.eps_col[:, 0:1], scale=1.0 / D)
        S.recip(r[:, 0:n], r[:, 0:n])
        return r

    def rmsnorm_to_xn(self, gname):
        S = self.S
        for ti, (t0, n) in enumerate(TT):
            r = self.rmsnorm_stats(ti)
            for k in range(KC):
                S.stt(self.xn[k][:, t0:t0 + n], self.x[k][:, t0:t0 + n], self.col(gname, k), r[:, 0:n],
                      ALU.mult, ALU.mult)

    def proj_chunk(self, wb_lhsT, rhs_list, evac):
        S = self.S
        nk = len(rhs_list)
        for ti, (t0, n) in enumerate(TT):
            acc = self.bank()
            for k in range(nk):
                S.mm(acc[:, 0:n], wb_lhsT(k), rhs_list[k][:, t0:t0 + n], start=(k == 0), stop=(k == nk - 1))
            evac(ti, t0, n, acc)

    def ffn(self, li):
        S = self.S
        self.rmsnorm_to_xn(f"nffn{li}")
        m = S.mark()
        h = [S.sb(f"h{j}", [128, T], BF16) for j in range(11)]
        sg = [S.sb(f"sg{j}", [128, 512], BF16) for j in range(2)]
        for half in range(2):
            units = []
            for jj in range(11):
                j = half * 11 + jj

                def fn(wb, jj=jj):
                    w4 = wb[:, 0:2048].rearrange("p (k g f) -> p k g f", k=8, g=2)
                    for ti, (t0, n) in enumerate(TT):
                        pg = self.bank()
                        pu = self.bank()
                        for k in range(KC):
                            S.mm(pg[:, 0:n], w4[:, k, 0, :], self.xn[k][:, t0:t0 + n], start=(k == 0), stop=(k == KC - 1))
                        for k in range(KC):
                            S.mm(pu[:, 0:n], w4[:, k, 1, :], self.xn[k][:, t0:t0 + n], start=(k == 0), stop=(k == KC - 1))
                        s = sg[ti % 2]
                        S.act(s[:, 0:n], pg[:, 0:n], AF.Silu)
                        S.tt("dve", h[jj][:, t0:t0 + n], pu[:, 0:n], s[:, 0:n], ALU.mult)
                units.append((f"ffn_gu{li}", j, 2048, fn))
            for fo in range(KC):
                def fn2(wb, fo=fo):
                    w3 = wb[:, 0:1408].rearrange("p (k f) -> p k f", k=11)

                    def ev(ti, t0, n, acc):
                        S.tt("dve", self.x[fo][:, t0:t0 + n], acc[:, 0:n], self.x[fo][:, t0:t0 + n], ALU.add)
                    self.proj_chunk(lambda k: w3[:, k, :], h, ev)
                units.append((f"ffn_dn{li}", half * 8 + fo, 1408, fn2))
            self.run_units(units)
        S.release(m)

    def lru(self, li, j):
        S = self.S
        self.rmsnorm_to_xn(f"nmix{li}")
        m = S.mark()
        HALF = [(0, 1024, (0, 1)), (1024, T - 1024, (2, 3, 4))]
        HN = T - 1024
        cA = S.sb("cA", [128, 8], F32)
        ncA = S.sb("ncA", [128, 8], F32)
        lam = self.col(f"lru_lam{j}", 0, 8)
        S.act(cA[:], lam, AF.Exp, scale=-1.0)
        S.act(cA[:], cA[:], AF.Ln, bias=self.one_col[:, 0:1])
        S.ts("dve", ncA[:], cA[:], 8.0, ALU.mult)
        S.ts("dve", cA[:], cA[:], -8.0, ALU.mult)
        hg = [S.sb(f"hg{k}", [128, T], BF16) for k in range(2)]
        gate = [S.sb(f"gate{k}", [128, T], BF16) for k in range(2)]
        xx = [S.sb(f"xx{k}", [128, TP + 3], F32) for k in range(2)]
        xs = [S.sb(f"xs{k}", [128, NS, 4], F32) for k in range(2)]
        xcb = [S.sb(f"xcb{k}", [128, T], BF16) for k in range(2)]
        ctmp = S.sb("ctmp", [128, T], F32)
        ra = S.sb("ra", [128, HN], F32)
        ri = S.sb("ri", [128, HN], F32)
        av = S.sb("av", [128, HN], F32)
        tmp = S.sb("tmp", [128, HN], F32)
        carry = S.sb("carry", [128, 1], F32)
        p_h = self.small_alloc(f"p_lru_h{j}", 8)
        p_cv = self.small_alloc(f"p_lru_conv{j}", 24)
        s_h = self.small_alloc(f"s_lru_h{j}", 32)
        s_cv = self.small_alloc(f"s_lru_conv{j}", 96)
        s_cv4 = s_cv.rearrange("p (k b j) -> p k b j", k=8, b=NS)
        s_h3 = s_h.rearrange("p (k b) -> p k b", k=8)
        st_h = S.sb("st_h", [128, 8, NS], F32)
        S.dma(st_h[:], self.dram["s_lru_h"][:, j], sem=f"st{j}")
        st_c = S.sb("st_c", [128, 8, NS, 3], F32)
        S.dma(st_c[:], self.dram["s_lru_conv"][:, j], sem=f"st{j}")
        for k in range(2):
            S.memset("pool", xx[k][:, 0:3], 0.0)

        units = []
        for n in range(4):
            def f_gate(wb, n=n):
                w3 = wb[:, 0:2048].rearrange("p (k f) -> p k f", k=8)
                for c in range(2):
                    def ev(ti, t0, nn, acc, c=c):
                        S.act(gate[c][:, t0:t0 + nn], acc[:, 0:nn], AF.Gelu)
                    self.proj_chunk(lambda k, c=c: w3[:, k, c * 128:(c + 1) * 128], self.xn, ev)
            units.append((f"lru_win{j}", 2 * n, 2048, f_gate))

            def f_x(wb, n=n):
                w3 = wb[:, 0:2048].rearrange("p (k f) -> p k f", k=8)
                for c in range(2):
                    kc = 2 * n + c

                    def ev(ti, t0, nn, acc, c=c, kc=kc):
                        if t0 + nn <= TP:
                            S.copy("act", xx[c][:, 3 + t0:3 + t0 + nn], acc[:, 0:nn])
                        else:
                            npz = TP - t0
                            S.copy("act", xx[c][:, 3 + t0:3 + TP], acc[:, 0:npz])
                            S.copy("act", xs[c][:, :, 3], acc[:, npz:npz + NS])
                    self.proj_chunk(lambda k, c=c: w3[:, k, c * 128:(c + 1) * 128], self.xn, ev)
                    S.copy("pool", xs[c][:, :, 0:3], st_c[:, kc, :, :])
                    S.copy("pool", p_cv[:, kc * 3:(kc + 1) * 3], xx[c][:, TP:TP + 3])
                    S.copy("pool", s_cv4[:, kc, :, :], xs[c][:, :, 1:4])
                    cw = lambda q, kc=kc: self.col(f"lru_cw{j}_{q}", kc)
                    cb = self.col(f"lru_cb{j}", kc)
                    S.ts("dve", ctmp[:, 0:TP], xx[c][:, 0:TP], cw(0), ALU.mult, cb, ALU.add)
                    for q in range(1, 3):
                        S.stt(ctmp[:, 0:TP], xx[c][:, q:q + TP], cw(q), ctmp[:, 0:TP], ALU.mult, ALU.add)
                    S.stt(xcb[c][:, 0:TP], xx[c][:, 3:3 + TP], cw(3), ctmp[:, 0:TP], ALU.mult, ALU.add)
                    S.ts("dve", ctmp[:, TP:T], xs[c][:, :, 0], cw(0), ALU.mult, cb, ALU.add)
                    for q in range(1, 3):
                        S.stt(ctmp[:, TP:T], xs[c][:, :, q], cw(q), ctmp[:, TP:T], ALU.mult, ALU.add)
                    S.stt(xcb[c][:, TP:T], xs[c][:, :, 3], cw(3), ctmp[:, TP:T], ALU.mult, ALU.add)
            units.append((f"lru_win{j}", 2 * n + 1, 2048, f_x))

            def f_g(wb, n=n):
                w4 = wb[:, 0:1024].rearrange("p (g k f) -> p g k f", g=2, k=2)
                for c in range(2):
                    kc = 2 * n + c
                    for (h0, hn, tiles) in HALF:
                        for ti in tiles:
                            t0, nn = TT[ti]
                            pa = self.bank()
                            pi = self.bank()
                            for k in range(2):
                                S.mm(pa[:, 0:nn], w4[:, 0, k, c * 128:(c + 1) * 128], xcb[k][:, t0:t0 + nn],
                                     start=(k == 0), stop=(k == 1))
                            for k in range(2):
                                S.mm(pi[:, 0:nn], w4[:, 1, k, c * 128:(c + 1) * 128], xcb[k][:, t0:t0 + nn],
                                     start=(k == 0), stop=(k == 1))
                            S.act(ra[:, t0 - h0:t0 - h0 + nn], pa[:, 0:nn], AF.Sigmoid, bias=self.col(f"lru_ba{j}", kc))
                            S.act(ri[:, t0 - h0:t0 - h0 + nn], pi[:, 0:nn], AF.Sigmoid, bias=self.col(f"lru_bi{j}", kc))
                        R = slice(0, hn)
                        G = slice(h0, h0 + hn)
                        S.act(av[:, R], ra[:, R], AF.Exp, scale=cA[:, kc:kc + 1])
                        S.act(tmp[:, R], ra[:, R], AF.Tanh, scale=ncA[:, kc:kc + 1])
                        S.tt("pool", ra[:, R], av[:, R], av[:, R], ALU.mult)
                        S.stt(tmp[:, R], ra[:, R], 1.0, tmp[:, R], ALU.add, ALU.mult)
                        S.act(tmp[:, R], tmp[:, R], AF.Sqrt)
                        S.tt("pool", ri[:, R], ri[:, R], xcb[c][:, G], ALU.mult)
                        S.tt("dve", ri[:, R], ri[:, R], tmp[:, R], ALU.mult)
                        if h0 == 0:
                            S.scan(tmp[:, R], av[:, R], ri[:, R], 0.0)
                            S.copy("pool", carry[:], tmp[:, hn - 1:hn])
                        else:
                            npr = TP - h0
                            S.scan(tmp[:, 0:npr], av[:, 0:npr], ri[:, 0:npr], carry[:, 0:1])
                            S.tt("dve", tmp[:, npr:hn], av[:, npr:hn], st_h[:, kc, :], ALU.mult)
                            S.tt("dve", tmp[:, npr:hn], tmp[:, npr:hn], ri[:, npr:hn], ALU.add)
                            S.copy("pool", p_h[:, kc:kc + 1], tmp[:, npr - 1:npr])
                            S.copy("pool", s_h3[:, kc, :], tmp[:, npr:hn])
                        S.tt("dve", hg[c][:, G], tmp[:, R], gate[c][:, G], ALU.mult)
            units.append((f"lru_wg{j}", n, 1024, f_g))

            def f_o(wb, n=n):
                w3 = wb[:, 0:2048].rearrange("p (k f) -> p k f", k=2)
                for fo in range(KC):
                    def ev(ti, t0, nn, acc, fo=fo):
                        S.tt("dve", self.x[fo][:, t0:t0 + nn], acc[:, 0:nn], self.x[fo][:, t0:t0 + nn], ALU.add)
                    self.proj_chunk(lambda k, fo=fo: w3[:, k, fo * 128:(fo + 1) * 128], hg, ev)
            units.append((f"lru_wout{j}", n, 2048, f_o))
        self.run_units(units)
        S.release(m)

    def rbank(self, i):
        return self.ps[:, i * 512:(i + 1) * 512]

    def rope_tile(self, dst, p_raw, p_swp, t0, n, cs, t1, t2):
        S = self.S
        S.dma(cs[:, :, 0:n], self.dram["rope"][:, :, t0:t0 + n], sem="cs")
        S.tt("dve", t1[:, 0:n], p_raw, cs[:, 0, 0:n], ALU.mult)
        S.tt("dve", t2[:, 0:n], p_swp, cs[:, 1, 0:n], ALU.mult)
        S.tt("pool", dst, t1[:, 0:n], t2[:, 0:n], ALU.add)

    def mla(self, li, j):
        S = self.S
        self.rmsnorm_to_xn(f"nmix{li}")
        xn_base = S.bases[self.xn[0].name]
        m0 = S.mark()
        ckvb = [S.sb(f"ckvb{k}", [128, T], BF16) for k in range(2)]
        kpeb = S.sb("kpeb", [64, T], BF16)
        cqb = [S.sb(f"cqb{k}", [128, T], BF16) for k in range(4)]
        qs = S.sb("qs", [128, 3, NS, 8], BF16)
        ols = S.sb("ols", [128, 2, 8, NS], BF16)
        trib = S.sb("trib", [128, 128], BF16)
        trif = S.sb("trif", [128, 128], F32)
        S.dma(trif[:], self.dram["tri"], sem="tri")
        S.copy("pool", trib[:], trif[:])
        cs = S.sb("cs", [64, 2, 512], F32)
        rt1 = S.sb("rt1", [64, 512], F32)
        rt2 = S.sb("rt2", [64, 512], F32)
        m1 = S.mark()
        kpef = S.sb("kpef", [64, T], F32)
        ckvT = [S.sb(f"ckvT{k}", [128, T], F32) for k in range(2)]

        wq = [self.wload("mla_dq", u, 2048) for u in range(2)]
        for ti, (t0, n) in enumerate(TT):
            pb = [self.bank() for _ in range(4)]
            for c4 in range(4):
                w3 = wq[c4 // 2][:, 0:2048].rearrange("p (k f) -> p k f", k=8)
                for k in range(KC):
                    S.mm(pb[c4][:, 0:n], w3[:, k, (c4 % 2) * 128:(c4 % 2 + 1) * 128], self.xn[k][:, t0:t0 + n],
                         start=(k == 0), stop=(k == KC - 1))
            acc = self.rbank(4)
            for c4 in range(4):
                sq = self.sq[c4 % 2]
                S.act(sq[:, 0:n], pb[c4][:, 0:n], AF.Square)
                S.mm(acc[:, 0:n], self.ones_b[:], sq[:, 0:n], start=(c4 == 0), stop=(c4 == 3))
            r = self.rstd[ti % 2]
            S.act(r[:, 0:n], acc[:, 0:n], AF.Sqrt, bias=self.eps_col[:, 0:1], scale=1.0 / 512)
            S.recip(r[:, 0:n], r[:, 0:n])
            for c4 in range(4):
                S.stt(cqb[c4][:, t0:t0 + n], pb[c4][:, 0:n], self.col("mla_qn", c4), r[:, 0:n], ALU.mult, ALU.mult)

        wr = self.wload("mla_dkv_r", 0, 1024)
        wr3 = wr[:, 0:1024].rearrange("p (k f) -> p k f", k=8)
        for ti, (t0, n) in enumerate(TT):
            p1 = self.bank()
            p2 = self.bank()
            for k in range(KC):
                S.mm(p1[0:64, 0:n], wr3[:, k, 0:64], self.xn[k][:, t0:t0 + n], start=(k == 0), stop=(k == KC - 1))
            for k in range(KC):
                S.mm(p2[0:64, 0:n], wr3[:, k, 64:128], self.xn[k][:, t0:t0 + n], start=(k == 0), stop=(k == KC - 1))
            self.rope_tile(kpef[:, t0:t0 + n], p1[0:64, 0:n], p2[0:64, 0:n], t0, n, cs, rt1, rt2)
        S.copy("pool", kpeb[:], kpef[:])

        wc = self.wload("mla_dkv_c", 0, 2048)
        wc3 = wc[:, 0:2048].rearrange("p (k f) -> p k f", k=8)
        for ti, (t0, n) in enumerate(TT):
            pb = [self.bank() for _ in range(2)]
            for c2 in range(2):
                for k in range(KC):
                    S.mm(pb[c2][:, 0:n], wc3[:, k, c2 * 128:(c2 + 1) * 128], self.xn[k][:, t0:t0 + n],
                         start=(k == 0), stop=(k == KC - 1))
            acc = self.rbank(4)
            for c2 in range(2):
                sq = self.sq[c2 % 2]
                S.act(sq[:, 0:n], pb[c2][:, 0:n], AF.Square)
                S.mm(acc[:, 0:n], self.ones_b[:], sq[:, 0:n], start=(c2 == 0), stop=(c2 == 1))
            r = self.rstd[ti % 2]
            S.act(r[:, 0:n], acc[:, 0:n], AF.Sqrt, bias=self.eps_col[:, 0:1], scale=1.0 / 256)
            S.recip(r[:, 0:n], r[:, 0:n])
            for c2 in range(2):
                S.stt(ckvT[c2][:, t0:t0 + n], pb[c2][:, 0:n], self.col("mla_kvn", c2), r[:, 0:n], ALU.mult, ALU.mult)
        for c2 in range(2):
            S.copy("pool", ckvb[c2][:], ckvT[c2][:])

        sv = S.sb_ptr
        S.sb_ptr = xn_base
        vtok = S.sb("vtok", [128, 17, 256], BF16)
        ostg = [S.sb(f"ostg{i}", [128, 320], F32) for i in range(2)]
        qaug = [S.sb(f"qaug{k}", [128, T], BF16) for k in range(3)]
        oh = S.sb("oh", [128, T], BF16)
        vnew = S.sb("vnew", [1, NS, 257], BF16)
        assert S.sb_ptr <= xn_base + 8 * ((T * 2 + 63) // 64 * 64), "xn overlay overflow"
        S.sb_ptr = sv

        okv = self.out("p_kv", [T, 320])
        for bi in range(17 if "T" not in os.environ.get("KSKIP", "") else 0):
            t0 = bi * 128
            n = min(128, T - t0)
            pt_ = self.bank()
            for c2 in range(2):
                S.tr(pt_[0:n, c2 * 128:(c2 + 1) * 128], ckvT[c2][:, t0:t0 + n], self.ident_f[:])
            S.tr(pt_[0:n, 256:320], kpef[:, t0:t0 + n], self.ident_f[0:64, 0:64])
            og = ostg[bi % 2]
            KS = os.environ.get("KSKIP", "")
            if "1" not in KS:
                S.copy("act", og[0:n, :], pt_[0:n, 0:320])
            if "2" not in KS:
                S.copy("dve", vtok[0:n, bi, :], pt_[0:n, 0:256])
            if "3" not in KS:
                S.dma(okv[t0:t0 + n, :], og[0:n, :], sem=f"okv{bi % 2}")
        S.memset("pool", vnew[:], 1.0)
        for b in range(NS if "V" not in os.environ.get("KSKIP", "") else 0):
            pt_ = self.bank()
            for c2 in range(2):
                S.tr(pt_[0:1, c2 * 128:(c2 + 1) * 128], ckvT[c2][:, TP + b:TP + b + 1], self.ident_f[:])
            S.copy("dve", vnew[0:1, b, 0:256], pt_[0:1, 0:256])
        S.release(m1)

        mA = S.mark()
        qn_s = S.sb("qn_s", [128, NS], BF16)
        u1 = []

        def passA(wb, h):
            wq3 = wb[:, 0:1024].rearrange("p (k f) -> p k f", k=4)
            wuk = wb[:, 1024:1280]
            pn = self.bank()
            for k in range(4):
                S.mm(pn[:, 0:NS], wq3[:, k, 0:128], cqb[k][:, TP:T], start=(k == 0), stop=(k == 3))
            S.copy("dve", qn_s[:], pn[:, 0:NS])
            p1 = self.bank()
            p2 = self.bank()
            for k in range(4):
                S.mm(p1[0:64, 0:NS], wq3[:, k, 128:192], cqb[k][:, TP:T], start=(k == 0), stop=(k == 3))
            for k in range(4):
                S.mm(p2[0:64, 0:NS], wq3[:, k, 192:256], cqb[k][:, TP:T], start=(k == 0), stop=(k == 3))
            self.rope_tile(qs[0:64, 2, :, h], p1[0:64, 0:NS], p2[0:64, 0:NS], TP, NS, cs, rt1, rt2)
            for c2 in range(2):
                pl = self.bank()
                S.mm(pl[:, 0:NS], wuk[:, c2 * 128:(c2 + 1) * 128], qn_s[:], start=True, stop=True)
                S.copy("dve", qs[:, c2, :, h], pl[:, 0:NS])
        if "A" not in os.environ.get("KSKIP", ""):
            self.run_units([("mla_u1", h, 1280, (lambda wb, h=h: passA(wb, h))) for h in range(8)])
        S.release(mA)

        if "D" not in os.environ.get("KSKIP", ""):
            self.mla_decode(qs, ols, ckvb, kpeb, vnew)
        else:
            S.memset("pool", ols[:], 0.0)

        mB = S.mark()
        qn = S.sb("qn", [128, T], BF16)
        olat = [S.sb(f"olat{k}", [128, T], BF16) for k in range(2)]
        PT = [S.sb(f"PT{i}", [128, 512], BF16) for i in range(3)]
        rden = S.sb("rden", [128, 512], F32)
        QT = [(0, 512), (512, 512), (1024, 512), (1536, 512), (2048, 16)]

        def head_q(wb, h):
            wq3 = wb[:, 0:1024].rearrange("p (k f) -> p k f", k=4)
            wuk = wb[:, 1024:1280]

            def ev(ti, t0, n, acc):
                S.copy("act", qn[:, t0:t0 + n], acc[:, 0:n])
            self.proj_chunk(lambda k: wq3[:, k, 0:128], cqb, ev)
            for ti, (t0, n) in enumerate(TT):
                p1 = self.bank()
                p2 = self.bank()
                for k in range(4):
                    S.mm(p1[0:64, 0:n], wq3[:, k, 128:192], cqb[k][:, t0:t0 + n], start=(k == 0), stop=(k == 3))
                for k in range(4):
                    S.mm(p2[0:64, 0:n], wq3[:, k, 192:256], cqb[k][:, t0:t0 + n], start=(k == 0), stop=(k == 3))
                self.rope_tile(qaug[2][0:64, t0:t0 + n], p1[0:64, 0:n], p2[0:64, 0:n], t0, n, cs, rt1, rt2)
            for c2 in range(2):
                def ev2(ti, t0, n, acc, c2=c2):
                    S.copy("act", qaug[c2][:, t0:t0 + n], acc[:, 0:n])
                self.proj_chunk(lambda k, c2=c2: wuk[:, c2 * 128:(c2 + 1) * 128], [qn], ev2)
            a0, a1, dn_ = self.rbank(4), self.rbank(5), self.rbank(6)
            pairs = []
            for (q0, qn_) in QT:
                nb = (q0 + qn_ - 1) // 128 + 1
                for jb in range(nb):
                    k0 = jb * 128
                    kn = min(128, TP - k0)
                    qs0 = max(q0, k0)
                    pairs.append(dict(q0=q0, qn=qn_, jb=jb, k0=k0, kn=kn, qs0=qs0, nc=q0 + qn_ - qs0, off=qs0 - q0,
                                      first=(jb == 0), last=(jb == nb - 1), idx=len(pairs)))

            def scores(p):
                kn, nc_, k0, qs0 = p["kn"], p["nc"], p["k0"], p["qs0"]
                sp = self.bank()
                S.mm(sp[0:kn, 0:nc_], ckvb[0][:, k0:k0 + kn], qaug[0][:, qs0:qs0 + nc_], start=True, stop=False)
                S.mm(sp[0:kn, 0:nc_], ckvb[1][:, k0:k0 + kn], qaug[1][:, qs0:qs0 + nc_], start=False, stop=False)
                S.mm(sp[0:kn, 0:nc_], kpeb[0:64, k0:k0 + kn], qaug[2][0:64, qs0:qs0 + nc_], start=False, stop=True)
                pt_ = PT[p["idx"] % len(PT)]
                S.act(pt_[0:kn, 0:nc_], sp[0:kn, 0:nc_], AF.Exp, scale=MLA_SCALE)
                if k0 >= p["q0"]:
                    dnn = min(128, nc_)
                    S.tt("pool", pt_[0:kn, 0:dnn], pt_[0:kn, 0:dnn], trib[0:kn, 0:dnn], ALU.mult)

            def pv(p):
                kn, nc_, off, jb = p["kn"], p["nc"], p["off"], p["jb"]
                pt_ = PT[p["idx"] % len(PT)]
                S.mm(a0[:, off:off + nc_], vtok[0:kn, jb, 0:128], pt_[0:kn, 0:nc_], start=p["first"], stop=p["last"])
                S.mm(a1[:, off:off + nc_], vtok[0:kn, jb, 128:256], pt_[0:kn, 0:nc_], start=p["first"], stop=p["last"])
                S.mm(dn_[:, off:off + nc_], self.ones_b[0:kn, :], pt_[0:kn, 0:nc_], start=p["first"], stop=p["last"])
                if p["last"]:
                    q0, qn_ = p["q0"], p["qn"]
                    S.recip(rden[:, 0:qn_], dn_[:, 0:qn_])
                    S.tt("dve", olat[0][:, q0:q0 + qn_], a0[:, 0:qn_], rden[:, 0:qn_], ALU.mult)
                    S.tt("dve", olat[1][:, q0:q0 + qn_], a1[:, 0:qn_], rden[:, 0:qn_], ALU.mult)
            scores(pairs[0])
            for i, p in enumerate(pairs):
                if i + 1 < len(pairs):
                    scores(pairs[i + 1])
                pv(p)
            for c2 in range(2):
                S.copy("pool", olat[c2][:, TP:T], ols[:, c2, h, :])

        def head_o(wb, h):
            wuv = wb[:, 0:256].rearrange("p (k v) -> p k v", k=2)
            wo = wb[:, 256:1280]

            def ev(ti, t0, n, acc):
                S.copy("act", oh[:, t0:t0 + n], acc[:, 0:n])
            self.proj_chunk(lambda k: wuv[:, k, :], olat, ev)
            for fo in range(KC):
                def ev2(ti, t0, n, acc, fo=fo):
                    S.tt("dve", self.x[fo][:, t0:t0 + n], acc[:, 0:n], self.x[fo][:, t0:t0 + n], ALU.add)
                self.proj_chunk(lambda k, fo=fo: wo[:, fo * 128:(fo + 1) * 128], [oh], ev2)
        units = []
        for h in range(8):
            units.append(("mla_u1", h, 1280, (lambda wb, h=h: head_q(wb, h))))
            units.append(("mla_u2", h, 1280, (lambda wb, h=h: head_o(wb, h))))
        if "B" not in os.environ.get("KSKIP", ""):
            self.run_units(units)
        S.release(m0)

    def mla_decode(self, qs, ols, ckvb, kpeb, vnew):
        S = self.S
        m = S.mark()
        NTK = 4
        NSUB = PAGE // NTK
        NBUF = 4
        ptab = S.sb("ptab", [128, NS], I32)
        S.dma(ptab[:], self.dram["pt"], sem="ptab")
        idx = S.sb("idx", [128, NS, NSUB], I32)
        for b in range(NS):
            for s_ in range(NSUB):
                S.ts("dve", idx[:, b, s_:s_ + 1], ptab[:, b:b + 1], float(NSUB), ALU.mult, float(s_), ALU.add)
        kvs = [S.sb(f"kvs{i}", [128, NTK * 320], F32) for i in range(NBUF)]
        kT = [S.sb(f"kT{i}", [128, 384], BF16) for i in range(2)]
        Vb = [S.sb(f"Vb{i}", [128, NTK, 257], BF16) for i in range(2)]
        for i in range(2):
            S.memset("pool", Vb[i][:], 1.0)
        PTd = [S.sb(f"PTd{i}", [128, NTK * 8], BF16) for i in range(2)]
        pnew = S.sb("pnew", [1, 8], BF16)
        osb = S.sb("osb", [8, 257], F32)
        rd = S.sb("rd", [8, 1], F32)
        onb = S.sb("onb", [8, 256], F32)
        accb = self.rbank(7)
        g = 0
        for b in range(NS):
            for s_ in range(NSUB):
                kv = kvs[g % NBUF]
                vb = Vb[g % 2]
                ptd = PTd[g % 2]
                S.gather(kv[:], self.dram["poolkv"], idx[:, b, s_:s_ + 1], sem=f"kv{g % NBUF}")
                S.copy("act", vb[:, :, 0:256], kv[:].rearrange("p (t c) -> p t c", t=NTK)[:, :, 0:256])
                sp = self.rbank(5 + (g % 2))
                g += 1
                for tt_ in range(NTK):
                    kt = kT[tt_ % 2]
                    pk = self.bank()
                    base = tt_ * 320
                    S.tr(pk[:, 0:128], kv[:, base:base + 128], self.ident_f[:])
                    S.tr(pk[:, 128:256], kv[:, base + 128:base + 256], self.ident_f[:])
                    S.tr(pk[0:64, 256:384], kv[:, base + 256:base + 320], self.ident_f[:])
                    S.copy("dve", kt[:, 0:256], pk[:, 0:256])
                    S.copy("dve", kt[0:64, 256:384], pk[0:64, 256:384])
                    o_ = sp[:, tt_ * 8:(tt_ + 1) * 8]
                    S.mm(o_, kt[:, 0:128], qs[:, 0, b, :], start=True, stop=False)
                    S.mm(o_, kt[:, 128:256], qs[:, 1, b, :], start=False, stop=False)
                    S.mm(o_, kt[0:64, 256:384], qs[0:64, 2, b, :], start=False, stop=True)
                S.act(ptd[:], sp[:, 0:NTK * 8], AF.Exp, scale=MLA_SCALE)
                for tt_ in range(NTK):
                    S.mm(accb[0:8, 0:257], ptd[:, tt_ * 8:(tt_ + 1) * 8], vb[:, tt_, :],
                         start=(s_ == 0 and tt_ == 0), stop=False)
            sp = self.bank()
            S.mm(sp[0:1, 0:8], ckvb[0][:, TP + b:TP + b + 1], qs[:, 0, b, :], start=True, stop=False)
            S.mm(sp[0:1, 0:8], ckvb[1][:, TP + b:TP + b + 1], qs[:, 1, b, :], start=False, stop=False)
            S.mm(sp[0:1, 0:8], kpeb[0:64, TP + b:TP + b + 1], qs[0:64, 2, b, :], start=False, stop=True)
            S.act(pnew[:], sp[0:1, 0:8], AF.Exp, scale=MLA_SCALE)
            S.mm(accb[0:8, 0:257], pnew[:], vnew[0:1, b, :], start=False, stop=True)
            S.copy("dve", osb[:], accb[0:8, 0:257])
            S.recip(rd[:], osb[:, 256:257])
            S.ts("dve", onb[:], osb[:, 0:256], rd[:, 0:1], ALU.mult)
            for c2 in range(2):
                po = self.bank()
                S.tr(po[:, 0:8], onb[:, c2 * 128:(c2 + 1) * 128], self.ident_f[0:8, 0:8])
                S.copy("dve", ols[:, c2, :, b], po[:, 0:8])
        S.release(m)

    def mmf(self, out, lhsT, rhs, start=True, stop=True):
        return self.S.mm(out, lhsT, rhs, start, stop)

    def dn(self, li, j):
        S = self.S
        self.rmsnorm_to_xn(f"nmix{li}")
        m0 = S.mark()
        small2 = S.sb("small2", [128, 360], F32)
        S.memset("pool", small2[:], 0.0)
        p_cv = small2[:, 0:72].rearrange("p (k j) -> p k j", k=24)
        s_cv = small2[:, 72:360].rearrange("p (k b j) -> p k b j", k=24, b=NS)
        oS = self.out("o_dn_S", [8 + NS * 8, 128, 128])
        Gc = S.sb("Gc", [8, T], F32)
        GT = S.sb("GT", [64, 37, 8], F32)
        BT = S.sb("BT", [64, 37, 8], F32)
        sel = S.sb("sel", [8, 128], F32)
        mm_ = S.sb("mm_", [64, 128], F32)
        S.dma(mm_[:], self.dram["dn_mm"], sem="dnc")
        st_c = S.sb("dst_c", [128, 24, NS, 3], F32)
        S.dma(st_c[:], self.dram["s_dn_conv"], sem="dnc")
        chunks = [(0, 16, 4)] + [(16 + 64 * c, 64, 6) for c in range(32)] + [(TP + b, 1, 0) for b in range(NS)]
        xx = S.sb("dxx", [128, TP + 3], F32)
        xs = S.sb("dxs", [128, NS, 4], F32)
        ctmp = S.sb("dctmp", [128, T], F32)
        S.memset("pool", xx[:, 0:3], 0.0)
        qdec = S.sb("qdec", [128, T], BF16)
        qn = S.sb("dqn", [128, T], BF16)
        kn = S.sb("kn", [128, T], BF16)
        vb = S.sb("vb", [128, T], BF16)
        sz = S.sb("sz", [128, T], BF16)
        GB = S.sb("GB", [128, T], F32)
        oT = S.sb("oT", [128, T], BF16)
        Sf = S.sb("Sf", [128, 128], F32)
        Sb_ = S.sb("Sb", [128, 128], BF16)
        eg = self.rstd[1]
        wba = self.wload("dn_ba", 0, 128)
        wba3 = wba[:, 0:128].rearrange("p (k f) -> p k f", k=8)
        Ball = GB[0:8, 0:T]
        graw = ctmp[0:8, 0:T]
        sv_ = S.sb_ptr
        S.sb_ptr = S.bases[qdec.name]
        mrow_t = S.sb("mrow", [8, T], F32)
        S.sb_ptr = sv_
        mrow = mrow_t[:, :]
        S.dma(mrow, self.dram["dn_mask"], sem="dnc")
        nA = S.sb("nA", [8, 1], F32)
        ab = self.col("dn_ab", 0, 2)
        S.act(nA[:], ab[0:8, 0:1], AF.Exp)
        S.ts("dve", nA[:], nA[:], -1.0, ALU.mult)
        for ti, (t0, n) in enumerate(TT):
            pb_, pa_ = self.bank(), self.bank()
            for k in range(KC):
                S.mm(pb_[0:8, 0:n], wba3[:, k, 0:8], self.xn[k][:, t0:t0 + n], start=(k == 0), stop=(k == KC - 1))
            for k in range(KC):
                S.mm(pa_[0:8, 0:n], wba3[:, k, 8:16], self.xn[k][:, t0:t0 + n], start=(k == 0), stop=(k == KC - 1))
            S.act(Ball[:, t0:t0 + n], pb_[0:8, 0:n], AF.Sigmoid)
            S.act(graw[:, t0:t0 + n], pa_[0:8, 0:n], AF.Exp, bias=ab[0:8, 1:2])
            S.act(graw[:, t0:t0 + n], graw[:, t0:t0 + n], AF.Ln, bias=self.one_col[0:8, 0:1])
        S.ts("dve", graw, graw, nA[:, 0:1], ALU.mult)
        S.scan(Gc[:], mrow, graw, 0.0)
        for ci, (t0, C, L) in enumerate(chunks):
            pt_ = self.bank()
            S.tr(pt_[0:C, 0:8], Gc[:, t0:t0 + C], self.ident_f[0:8, 0:8])
            S.tr(pt_[0:C, 8:16], Ball[:, t0:t0 + C], self.ident_f[0:8, 0:8])
            S.copy("dve", GT[0:C, ci, :], pt_[0:C, 0:8])
            S.copy("dve", BT[0:C, ci, :], pt_[0:C, 8:16])
        sv_ = S.sb_ptr
        S.sb_ptr = S.bases[xx.name]
        F1 = S.sb("gF1", [64, 8, 64], F32)
        F2 = S.sb("gF2", [64, 8, 64], F32)
        gbf = lambda nm: S.sb(nm, [64, 8, 64], BF16)
        A1, A2, A3, B1, B2, B3 = (gbf(nm) for nm in ("gA1", "gA2", "gA3", "gB1", "gB2", "gB3"))
        usb = S.sb("usb", [64, 8, 128], F32)
        attnT = S.sb("attnT", [64, 8, 64], BF16)
        wT = S.sb("wT", [128, 8, 64], BF16)
        assert S.sb_ptr <= S.bases[ctmp.name] + T * 4, "group overlay overflow"
        S.sb_ptr = sv_
        Vb_ = S.sb("Vbt", [64, 8, 128], BF16)
        Kb_ = S.sb("Kbt", [64, 8, 128], BF16)
        kdec = S.sb("kdec", [64, 8, 128], BF16)
        delta = S.sb("delta", [64, 128], BF16)
        cols_ = S.sb("ccols", [64, 8, 4], F32)
        egl = S.sb("egl", [128, 8], F32)
        mmax, mmin = mm_[:, 0:64], mm_[:, 64:128]

        def conv_silu(psrc_list, kc, dst_f32):
            for (ti, t0, n, acc) in psrc_list:
                if t0 + n <= TP:
                    S.copy("act", xx[:, 3 + t0:3 + t0 + n], acc[:, 0:n])
                else:
                    npz = TP - t0
                    S.copy("act", xx[:, 3 + t0:3 + TP], acc[:, 0:npz])
                    S.copy("act", xs[:, :, 3], acc[:, npz:npz + NS])

        def conv_finish(kc, dst):
            S.memset("pool", xx[:, 0:3], 0.0)
            S.copy("pool", xs[:, :, 0:3], st_c[:, kc, :, :])
            S.copy("pool", p_cv[:, kc, :], xx[:, TP:TP + 3])
            S.copy("pool", s_cv[:, kc, :, :], xs[:, :, 1:4])
            cw = lambda q: self.col(f"dn_cw{q}", kc)
            S.ts("dve", ctmp[:, 0:TP], xx[:, 0:TP], cw(0), ALU.mult)
            for q in range(1, 4):
                S.stt(ctmp[:, 0:TP], xx[:, q:q + TP], cw(q), ctmp[:, 0:TP], ALU.mult, ALU.add)
            S.ts("dve", ctmp[:, TP:T], xs[:, :, 0], cw(0), ALU.mult)
            for q in range(1, 4):
                S.stt(ctmp[:, TP:T], xs[:, :, q], cw(q), ctmp[:, TP:T], ALU.mult, ALU.add)
            S.act(dst, ctmp[:], AF.Silu)

        def l2n(src, dst_bf, scale):
            for ti, (t0, n) in enumerate(TT):
                sq = self.sq[ti % 2]
                S.act(sq[:, 0:n], src[:, t0:t0 + n], AF.Square)
                acc = self.bank()
                S.mm(acc[:, 0:n], self.ones_b[:], sq[:, 0:n], start=True, stop=True)
                r = self.rstd[ti % 2]
                S.act(r[:, 0:n], acc[:, 0:n], AF.Sqrt, bias=self.eps_col[:, 0:1], scale=1.0 / (scale * scale))
                S.recip(r[:, 0:n], r[:, 0:n])
                S.tt("dve", dst_bf[:, t0:t0 + n], src[:, t0:t0 + n], r[:, 0:n], ALU.mult)

        def proj2(wb, c, kc):
            w3 = wb[:, 0:2048].rearrange("p (k f) -> p k f", k=8)
            lst = []
            for ti, (t0, n) in enumerate(TT):
                acc = self.bank()
                for k in range(KC):
                    S.mm(acc[:, 0:n], w3[:, k, c * 128:(c + 1) * 128], self.xn[k][:, t0:t0 + n], start=(k == 0), stop=(k == KC - 1))
                conv_silu([(ti, t0, n, acc)], kc, None)

        def head_qk(wb, h):
            proj2(wb, 0, h)
            conv_finish(h, ctmp[:])
            l2n(ctmp, qn, 128.0 ** -0.5)
            S.dma(sel[:], self.dram["dn_sel"][:, h * 128:(h + 1) * 128], sem="dnsel")
            for ti, (t0, n) in enumerate(TT):
                acc = self.bank()
                S.mm(acc[:, 0:n], sel[:, :], Gc[:, t0:t0 + n], start=True, stop=True)
                S.copy("dve", GB[:, t0:t0 + n], acc[:, 0:n])
                S.act(eg[:, 0:n], acc[:, 0:n], AF.Exp)
                S.tt("dve", qdec[:, t0:t0 + n], qn[:, t0:t0 + n], eg[:, 0:n], ALU.mult)
            proj2(wb, 1, 8 + h)
            conv_finish(8 + h, ctmp[:])
            l2n(ctmp, kn, 1.0)

        def head_vz(wb, h):
            proj2(wb, 0, 16 + h)
            conv_finish(16 + h, ctmp[:])
            S.copy("pool", vb[:], ctmp[:])
            w3 = wb[:, 0:2048].rearrange("p (k f) -> p k f", k=8)

            def ev(ti, t0, n, acc):
                S.act(sz[:, t0:t0 + n], acc[:, 0:n], AF.Silu)
            self.proj_chunk(lambda k: w3[:, k, 128:256], self.xn, ev)
            S.memset("pool", Sf[:], 0.0)
            S.memset("pool", Sb_[:], 0.0)
            groups = [[0]] + [list(range(1 + 8 * q, 9 + 8 * q)) for q in range(4)] + [[33, 34, 35, 36]]
            for grp in groups:
                ng = len(grp)
                C, L = chunks[grp[0]][1], chunks[grp[0]][2]
                R = slice(0, C)
                pk, pq = self.rbank(0), self.rbank(1)
                ci0 = grp[0]
                tg0 = chunks[ci0][0]
                GR = (R, slice(0, ng), slice(0, C))
                bc = lambda ap: ap.to_broadcast([C, ng, C])
                GBg = GB[0:C, tg0:tg0 + ng * C].rearrange("p (g c) -> p g c", g=ng)
                gcolg = GT[0:C, ci0:ci0 + ng, h:h + 1]
                bcolg = BT[0:C, ci0:ci0 + ng, h:h + 1]
                glastg = GB[0:C, tg0 + C - 1:tg0 + ng * C:C].unsqueeze(2)
                S.act(cols_[R, 0:ng, 0:1], gcolg, AF.Exp)
                S.tt("dve", cols_[R, 0:ng, 1:2], cols_[R, 0:ng, 0:1], bcolg, ALU.mult)
                S.tt("dve", cols_[R, 0:ng, 2:3], glastg, gcolg, ALU.subtract)
                S.act(cols_[R, 0:ng, 2:3], cols_[R, 0:ng, 2:3], AF.Exp)
                S.act(egl[:, 0:ng], GB[:, tg0 + C - 1:tg0 + ng * C:C], AF.Exp)
                for g, ci in enumerate(grp):
                    t0 = chunks[ci][0]
                    cs_ = slice(t0, t0 + C)
                    S.mm(pk[R, g * 64:g * 64 + C], kn[:, cs_], kn[:, cs_], start=True, stop=True)
                    S.mm(pq[R, g * 64:g * 64 + C], kn[:, cs_], qn[:, cs_], start=True, stop=True)
                S.tt("dve", F2[GR], GBg, bc(gcolg), ALU.subtract)
                S.tt("dve", F1[GR], F2[GR], bc(mmax[0:C, 0:C].unsqueeze(1)), ALU.max)
                S.tt("dve", F2[GR], F2[GR], bc(mmin[0:C, 0:C].unsqueeze(1)), ALU.min)
                pk3 = pk[:, 0:512].rearrange("p (g c) -> p g c", g=8)
                pq3 = pq[:, 0:512].rearrange("p (g c) -> p g c", g=8)
                S.act(B1[GR], F1[GR], AF.Exp, scale=-1.0)
                S.act(B2[GR], F2[GR], AF.Exp)
                S.tt("dve", F1[GR], pk3[GR], bc(bcolg), ALU.mult)
                S.stt(F1[GR], F1[GR], -1.0, B1[GR], ALU.mult, ALU.mult)
                S.copy("act", A1[GR], F1[GR])
                S.tt("dve", attnT[GR], pq3[GR], B2[GR], ALU.mult)
                if C > 1:
                    ptr_ = self.rbank(2)
                    ptr3 = ptr_[:, 0:512].rearrange("p (g c) -> p g c", g=8)
                    for g in range(ng):
                        S.tr(ptr_[R, g * 64:g * 64 + C], F1[R, g, 0:C], self.ident_f[0:C, 0:C])
                    S.copy("dve", A2[GR], ptr3[GR])
                else:
                    S.copy("dve", A2[GR], A1[GR])
                S.tt("pool", A3[GR], A2[GR], bc(self.ident_b[0:C, 0:C].unsqueeze(1)), ALU.add)
                P_, PT_, TT_ = A1, A2, A3
                P2, PT2, TT2 = B1, B2, B3
                for lv in range(1, L):
                    pl, plT, pl2 = self.rbank(2), self.rbank(3), self.rbank(4)
                    pl3 = pl[:, 0:512].rearrange("p (g c) -> p g c", g=8)
                    plT3 = plT[:, 0:512].rearrange("p (g c) -> p g c", g=8)
                    pl23 = pl2[:, 0:512].rearrange("p (g c) -> p g c", g=8)
                    for g in range(ng):
                        S.mm(pl[R, g * 64:g * 64 + C], PT_[R, g, 0:C], P_[R, g, 0:C], start=True, stop=True)
                    for g in range(ng):
                        S.mm(plT[R, g * 64:g * 64 + C], P_[R, g, 0:C], PT_[R, g, 0:C], start=True, stop=True)
                    S.copy("dve", P2[GR], pl3[GR])
                    S.copy("act", PT2[GR], plT3[GR])
                    for g in range(ng):
                        S.mm(pl2[R, g * 64:g * 64 + C], P2[R, g, 0:C], TT_[R, g, 0:C], start=True, stop=True)
                    S.tt("dve", TT2[GR], pl23[GR], TT_[GR], ALU.add)
                    P_, P2 = P2, P_
                    PT_, PT2 = PT2, PT_
                    TT_, TT2 = TT2, TT_
                TTbf = TT_
                for half in range((ng + 3) // 4):
                    pkk, pvv = self.rbank(0 + half), self.rbank(2 + half)
                    for g in range(half * 4, min(ng, half * 4 + 4)):
                        ci = grp[g]
                        t0 = chunks[ci][0]
                        cs_ = slice(t0, t0 + C)
                        o0 = (g % 4) * 128
                        S.mm(pkk[R, o0:o0 + 128], kn[:, cs_], self.ident_b[:], start=True, stop=True)
                        S.mm(pvv[R, o0:o0 + 128], vb[:, cs_], self.ident_b[:], start=True, stop=True)
                    h4 = half * 4
                    n4 = min(ng, h4 + 4) - h4
                    pkk3 = pkk[:, 0:512].rearrange("p (g c) -> p g c", g=4)
                    pvv3 = pvv[:, 0:512].rearrange("p (g c) -> p g c", g=4)
                    b4 = lambda ap: ap.to_broadcast([C, n4, 128])
                    S.tt("dve", Kb_[R, h4:h4 + n4, :], pkk3[R, 0:n4, :], b4(cols_[R, h4:h4 + n4, 1:2]), ALU.mult)
                    S.tt("dve", kdec[R, h4:h4 + n4, :], pkk3[R, 0:n4, :], b4(cols_[R, h4:h4 + n4, 2:3]), ALU.mult)
                    S.tt("dve", Vb_[R, h4:h4 + n4, :], pvv3[R, 0:n4, :], b4(BT[0:C, ci0 + h4:ci0 + h4 + n4, h:h + 1]), ALU.mult)
                pw = self.rbank(6)
                for half in range((ng + 3) // 4):
                    pu = self.rbank(4 + half)
                    for g in range(half * 4, min(ng, half * 4 + 4)):
                        o0 = (g % 4) * 128
                        S.mm(pu[R, o0:o0 + 128], TTbf[R, g, 0:C], Vb_[R, g, :], start=True, stop=True)
                    n4 = min(ng, half * 4 + 4) - half * 4
                    S.copy("act", usb[R, half * 4:half * 4 + n4, :],
                           pu[:, 0:512].rearrange("p (g c) -> p g c", g=4)[R, 0:n4, :])
                for g in range(ng):
                    S.mm(pw[:, g * 64:g * 64 + C], Kb_[R, g, :], TTbf[R, g, 0:C], start=True, stop=True)
                S.copy("dve", wT[:, 0:ng, 0:C], pw[:, 0:512].rearrange("p (g c) -> p g c", g=8)[:, 0:ng, 0:C])
                for g, ci in enumerate(grp):
                    t0 = chunks[ci][0]
                    cs_ = slice(t0, t0 + C)
                    sample = t0 >= TP
                    if sample:
                        b = t0 - TP
                        S.dma(Sf[:], self.dram["s_dn_S"][b * 8 + h], sem="dnS")
                        S.copy("dve", Sb_[:], Sf[:])
                    pd, po, ps_ = self.rbank(7), self.rbank(5), self.rbank(3)
                    S.mm(pd[R, 0:128], wT[:, g, 0:C], Sb_[:], start=True, stop=True)
                    S.tt("dve", delta[R, :], usb[R, g, :], pd[R, 0:128], ALU.subtract)
                    S.mm(ps_[:, 0:128], kdec[R, g, :], delta[R, :], start=True, stop=True)
                    S.mm(po[:, 0:C], Sb_[:], qdec[:, cs_], start=True, stop=False)
                    S.mm(po[:, 0:C], delta[R, :], attnT[R, g, 0:C], start=False, stop=True)
                    S.stt(Sf[:], Sf[:], egl[:, g:g + 1], ps_[:, 0:128], ALU.mult, ALU.add)
                    S.copy("act", Sb_[:], Sf[:])
                    S.copy("act", oT[:, cs_], po[:, 0:C])
                    if ci == 32:
                        S.dma(oS[h], Sf[:], sem="oS")
                    if sample:
                        S.dma(oS[8 + (t0 - TP) * 8 + h], Sf[:], sem="oS")

        def head_o(wb, h):
            for ti, (t0, n) in enumerate(TT):
                sq = self.sq[ti % 2]
                S.act(sq[:, 0:n], oT[:, t0:t0 + n], AF.Square)
                acc = self.bank()
                S.mm(acc[:, 0:n], self.ones_b[:], sq[:, 0:n], start=True, stop=True)
                r = self.rstd[0]
                S.act(r[:, 0:n], acc[:, 0:n], AF.Sqrt, bias=self.eps_col[:, 0:1], scale=1.0 / 128)
                S.recip(r[:, 0:n], r[:, 0:n])
                S.stt(eg[:, 0:n], oT[:, t0:t0 + n], self.col("dn_norm", 0), r[:, 0:n], ALU.mult, ALU.mult)
                S.tt("dve", oT[:, t0:t0 + n], eg[:, 0:n], sz[:, t0:t0 + n], ALU.mult)
            for fo in range(KC):
                def ev2(ti, t0, n, acc, fo=fo):
                    S.tt("dve", self.x[fo][:, t0:t0 + n], acc[:, 0:n], self.x[fo][:, t0:t0 + n], ALU.add)
                self.proj_chunk(lambda k, fo=fo: wb[:, fo * 128:(fo + 1) * 128], [oT], ev2)
        units = []
        for h in range(8):
            units.append(("dn_qk", h, 2048, (lambda wb, h=h: head_qk(wb, h))))
            units.append(("dn_vz", h, 2048, (lambda wb, h=h: head_vz(wb, h))))
            units.append(("dn_wo", h, 1024, (lambda wb, h=h: head_o(wb, h))))
        self.run_units(units)
        sm2 = self.out("small2", [128, 360])
        S.dma(sm2[:, :], small2[:], sem="o1")
        S.release(m0)

    def consts(self):
        S = self.S
        self.eps_col = S.sb("eps_col", [128, 1], F32)
        self.one_col = S.sb("one_col", [128, 1], F32)
        S.memset("pool", self.eps_col[:], EPS)
        S.memset("pool", self.one_col[:], 1.0)

    def final(self):
        S = self.S
        yT = self.out("yT", [128, KC, T])
        for ti, (t0, n) in enumerate(TT):
            r = self.rmsnorm_stats(ti)
            for k in range(KC):
                S.stt(self.x[k][:, t0:t0 + n], self.x[k][:, t0:t0 + n], self.col("nfinal", k), r[:, 0:n],
                      ALU.mult, ALU.mult)
        for k in range(KC):
            S.dma(yT[:, k, :], self.x[k][:], sem=f"o{k % 2}")
        sm = self.out("small", [128, 320])
        S.dma(sm[:, :], self.small[:], sem="o0")

    def dump_x(self):
        S = self.S
        dbg = self.out("dbg", [128, KC, T])
        for k in range(KC):
            S.dma(dbg[:, k, :], self.x[k][:], sem=f"o{k % 2}")


def build_program(shapes, colidx, stop_after=None, only=None):
    nc = bass.Bass("TRN2", target_bir_lowering=False)
    S = Sched(nc)
    B = Builder(nc, S, shapes, colidx, stop_after)
    B.setup()
    B.consts()
    S.memset("pool", B.small[:], 0.0)
    layers = [("lru", 0), ("dn", 0), ("mla", 0), ("lru", 1)]
    done = False
    for li, (kind, j) in enumerate(layers):
        if only is not None and li != only:
            continue
        if kind == "lru":
            B.lru(li, j)
        elif kind == "dn":
            B.dn(li, j)
        else:
            B.mla(li, j)
        if stop_after == (li, "mix"):
            done = True
            break
        B.ffn(li)
        if stop_after == (li, "ffn"):
            done = True
            break
    if done:
        B.dump_x()
        sm = B.out("small", [128, 320])
        S.dma(sm[:, :], B.small[:], sem="o0")
    else:
        B.final()
    S.emit()
    return nc, B


_STOP_AFTER = None
_DEBUG = {}


def _run(inputs, stop_after=None, cores=NCORES, only=None, x_override=None):
    inp = {k: np.asarray(v) for k, v in inputs.items()}
    sh = _prep_shared(inp)
    colidx = sh.pop("_colidx")
    per_core = [_prep_core(inp, c) for c in range(cores)]
    shapes = {k: v.shape for k, v in sh.items()}
    shapes.update({k: v.shape for k, v in per_core[0].items()})
    if x_override is not None:
        for c in range(cores):
            per_core[c]["xT"] = np.ascontiguousarray(x_override[c].reshape(T, KC, 128).transpose(2, 1, 0))
    nc, B = build_program(shapes, colidx, stop_after, only)
    in_maps = []
    for c in range(cores):
        m = dict(sh)
        m.update(per_core[c])
        in_maps.append(m)
    res = run_bass_kernel_spmd(nc, in_maps, core_ids=list(range(cores)))
    return res.results, B


def kernel(**inputs):
    results, B = _run(inputs, None)
    f = np.float32
    y_prompt = np.zeros((8, SEQ, D), f)
    y_sample = np.zeros((32, 1, D), f)
    p_lru_h = np.zeros((2, 8, D), f)
    p_lru_conv = np.zeros((2, 8, 3, D), f)
    p_dn_S = np.zeros((1, 8, 8, 128, 128), f)
    p_dn_conv = np.zeros((1, 8, 3, 3072), f)
    p_ckv = np.zeros((1, 8, TP, 256), f)
    p_kpe = np.zeros((1, 8, TP, 64), f)
    s_lru_h = np.zeros((2, 32, D), f)
    s_lru_conv = np.zeros((2, 32, 3, D), f)
    s_dn_S = np.zeros((1, 32, 8, 128, 128), f)
    s_dn_conv = np.zeros((1, 32, 3, 3072), f)
    s_ckv = np.zeros((1, 32, 1, 256), f)
    s_kpe = np.zeros((1, 32, 1, 64), f)
    for c in range(NCORES):
        r = results[c]
        y = r["yT"].transpose(2, 1, 0).reshape(T, D)
        y_prompt[c] = y[NMETA:TP]
        y_sample[NS * c:NS * (c + 1), 0] = y[TP:]
        sm = r["small"]
        for j in range(2):
            i, n = B.small_idx[f"p_lru_h{j}"]
            p_lru_h[j, c] = sm[:, i:i + n].T.reshape(D)
            i, n = B.small_idx[f"p_lru_conv{j}"]
            p_lru_conv[j, c] = sm[:, i:i + n].reshape(128, 8, 3).transpose(2, 1, 0).reshape(3, D)
            i, n = B.small_idx[f"s_lru_h{j}"]
            s_lru_h[j, NS * c:NS * (c + 1)] = sm[:, i:i + n].reshape(128, 8, NS).transpose(2, 1, 0).reshape(NS, D)
            i, n = B.small_idx[f"s_lru_conv{j}"]
            s_lru_conv[j, NS * c:NS * (c + 1)] = sm[:, i:i + n].reshape(128, 8, NS, 3).transpose(2, 3, 1, 0).reshape(NS, 3, D)
        s2 = r["small2"]
        p_dn_conv[0, c] = s2[:, 0:72].reshape(128, 24, 3).transpose(2, 1, 0).reshape(3, 3072)
        s_dn_conv[0, NS * c:NS * (c + 1)] = s2[:, 72:360].reshape(128, 24, NS, 3).transpose(2, 3, 1, 0).reshape(NS, 3, 3072)
        oS = r["o_dn_S"]
        p_dn_S[0, c] = oS[0:8]
        s_dn_S[0, NS * c:NS * (c + 1)] = oS[8:].reshape(NS, 8, 128, 128)
        kv = r["p_kv"]
        p_ckv[0, c] = kv[:TP, :256]
        p_kpe[0, c] = kv[:TP, 256:]
        s_ckv[0, NS * c:NS * (c + 1), 0] = kv[TP:, :256]
        s_kpe[0, NS * c:NS * (c + 1), 0] = kv[TP:, 256:]
    return (y_prompt, y_sample, p_lru_h, p_lru_conv, p_dn_S, p_dn_conv, p_ckv, p_kpe,
            s_lru_h, s_lru_conv, s_dn_S, s_dn_conv, s_ckv, s_kpe)
```

```python
import bisect
import os
from contextlib import ExitStack

import numpy as np
import concourse.bass as bass
import concourse.mybir as mybir
from concourse.bass_utils import run_bass_kernel_spmd

F32 = mybir.dt.float32
BF16 = mybir.dt.bfloat16
I32 = mybir.dt.int32
AF = mybir.ActivationFunctionType
ALU = mybir.AluOpType
AX = mybir.AxisListType

NCORES = 8
D = 1024
KC = 8
SEQ = 2048
NMETA = 16
TP = SEQ + NMETA
NS = 4
T = TP + NS
DFF = 2816
FC = DFF // 128
NPAGES = 128
PAGE = 128
NPOOL = 5120
EPS = 1e-6
MLA_SCALE = (128 + 64) ** -0.5
TT = [(0, 512), (512, 512), (1024, 512), (1536, 512), (2048, 20)]
WSLOT = 2048


class _Op:
    __slots__ = ("eng", "fn", "deps", "dma_sem", "dma_val", "idx", "milestone", "mval", "waits", "dma_deps")


class _IMap:
    def __init__(self, size):
        self.b = [0, size]
        self.r = [[None, {}]]

    def _split(self, x):
        i = bisect.bisect_left(self.b, x)
        if self.b[i] == x:
            return i
        w, rd = self.r[i - 1]
        self.b.insert(i, x)
        self.r.insert(i, [w, dict(rd)])
        return i

    def read(self, lo, hi, op, key, deps):
        i = self._split(lo)
        j = self._split(hi)
        for k in range(i, j):
            rec = self.r[k]
            if rec[0] is not None:
                deps.add(rec[0])
            rec[1][key] = op

    def write(self, lo, hi, op, deps):
        i = self._split(lo)
        j = self._split(hi)
        for k in range(i, j):
            rec = self.r[k]
            if rec[0] is not None:
                deps.add(rec[0])
            deps.update(rec[1].values())
        self.b[i:j + 1] = [lo, hi]
        self.r[i:j] = [[op, {}]]


class Sched:
    ENGS = ("pe", "act", "dve", "pool", "sp")

    def __init__(self, nc):
        self.nc = nc
        self.ops = {e: [] for e in self.ENGS}
        self.maps = {"SB": _IMap(1 << 20), "PSUM": _IMap(1 << 16)}
        self.dma_cnt = {}
        self.total_sems = set()
        self.bases = {}
        self.sb_ptr = (nc.sbuf_base + 63) // 64 * 64
        self.sb_top = nc.sbuf_top
        self.nalloc = 0

    def sb(self, name, shape, dtype):
        esz = 2 if dtype == BF16 else 4
        n = 1
        for s in shape[1:]:
            n *= s
        nbytes = (n * esz + 63) // 64 * 64
        off = self.sb_ptr
        self.sb_ptr += nbytes
        assert self.sb_ptr <= self.sb_top, f"SBUF overflow at {name}: {self.sb_ptr} > {self.sb_top}"
        self.nalloc += 1
        t = self.nc.alloc_sbuf_tensor_at(f"{name}_{self.nalloc}", list(shape), dtype, offset=off)
        self.bases[t.name] = off
        return t

    def mark(self):
        return self.sb_ptr

    def release(self, m):
        self.sb_ptr = m

    def _range(self, ap):
        sp = str(ap.space)
        if "SB" in sp:
            m = self.maps["SB"]
        elif "PSUM" in sp:
            m = self.maps["PSUM"]
        else:
            return None
        esz = 2 if ap.dtype == BF16 else 4
        pat = ap.ap
        pstride = pat[0][0]
        off = ap.offset % pstride if pstride > 0 else ap.offset
        ext = 1
        for st, cnt in pat[1:]:
            ext += (cnt - 1) * abs(st)
        base = self.bases.get(ap.tensor.name, 0)
        lo = base + off * esz
        hi = lo + ext * esz
        if m is self.maps["PSUM"]:
            lo = lo // 2048 * 2048
            hi = (hi + 2047) // 2048 * 2048
        return m, lo, hi

    def rec(self, eng, fn, reads=(), writes=(), dma_sem=None):
        op = _Op()
        op.eng = eng
        op.fn = fn
        op.dma_sem = dma_sem
        op.milestone = False
        op.mval = 0
        key = eng if dma_sem is None else ("dma", dma_sem)
        deps = set()
        for ap in reads:
            if ap is None or isinstance(ap, (int, float)):
                continue
            r = self._range(ap)
            if r:
                if r[0] is self.maps["PSUM"]:
                    r[0].write(r[1], r[2], op, deps)
                else:
                    r[0].read(r[1], r[2], op, key, deps)
        for ap in writes:
            r = self._range(ap)
            if r:
                r[0].write(r[1], r[2], op, deps)
        deps.discard(op)
        op.deps = []
        op.dma_deps = {}
        for d in deps:
            if d.dma_sem is not None:
                s = d.dma_sem
                v = self.dma_cnt[s]
                if op.dma_deps.get(s, 0) < v:
                    op.dma_deps[s] = v
            else:
                op.deps.append(d)
        if dma_sem is not None:
            self.dma_cnt[dma_sem] = self.dma_cnt.get(dma_sem, 0) + 16
            op.dma_val = self.dma_cnt[dma_sem]
        op.idx = len(self.ops[eng])
        self.ops[eng].append(op)
        return op

    def mm(self, out, lhsT, rhs, start=True, stop=True):
        return self.rec("pe", lambda e: e.matmul(out, lhsT=lhsT, rhs=rhs, start=start, stop=stop),
                        [lhsT, rhs], [out])

    def tr(self, out, in_, ident):
        return self.rec("pe", lambda e: e.transpose(out=out, in_=in_, identity=ident), [in_, ident], [out])

    def act(self, out, in_, func, bias=None, scale=1.0, accum_out=None):
        kw = {}
        if bias is not None:
            kw["bias"] = bias
        if accum_out is not None:
            kw["accum_out"] = accum_out
        w = [out] + ([accum_out] if accum_out is not None else [])
        return self.rec("act", lambda e: e.activation(out=out, in_=in_, func=func, scale=scale, **kw),
                        [in_, bias, scale], w)

    def tt(self, eng, out, in0, in1, op):
        return self.rec(eng, lambda e: e.tensor_tensor(out=out, in0=in0, in1=in1, op=op), [in0, in1], [out])

    def ts(self, eng, out, in0, s1, op0, s2=None, op1=None, accum_out=None):
        kw = {}
        if op1 is not None:
            kw["op1"] = op1
        if accum_out is not None:
            kw["accum_out"] = accum_out
        w = [out] + ([accum_out] if accum_out is not None else [])
        return self.rec(eng, lambda e: e.tensor_scalar(out=out, in0=in0, scalar1=s1, scalar2=s2, op0=op0, **kw),
                        [in0, s1, s2], w)

    def stt(self, out, in0, scalar, in1, op0, op1, eng="dve"):
        return self.rec(eng, lambda e: e.scalar_tensor_tensor(out=out, in0=in0, scalar=scalar, in1=in1,
                                                              op0=op0, op1=op1), [in0, scalar, in1], [out])

    def copy(self, eng, out, in_):
        if eng == "act":
            return self.rec("act", lambda e: e.copy(out=out, in_=in_), [in_], [out])
        return self.rec(eng, lambda e: e.tensor_copy(out=out, in_=in_), [in_], [out])

    def memset(self, eng, ap, val):
        return self.rec(eng, lambda e: e.memset(ap, val), [], [ap])

    def recip(self, out, in_):
        return self.rec("dve", lambda e: e.reciprocal(out=out, in_=in_), [in_], [out])

    def scan(self, out, d0, d1, initial, op0=ALU.mult, op1=ALU.add):
        return self.rec("dve", lambda e: e.tensor_tensor_scan(out=out, data0=d0, data1=d1, initial=initial,
                                                              op0=op0, op1=op1), [d0, d1, initial], [out])

    def reduce(self, out, in_, op, axis=AX.X):
        return self.rec("dve", lambda e: e.tensor_reduce(out=out, in_=in_, axis=axis, op=op), [in_], [out])

    def dma(self, out, in_, sem, eng="sp"):
        return self.rec(eng, lambda e: e.dma_start(out=out, in_=in_), [in_], [out], dma_sem=sem)

    def gather(self, out, in_, idx_ap, sem):
        return self.rec("pool", lambda e: e.indirect_dma_start(
            out=out, out_offset=None, in_=in_, in_offset=bass.IndirectOffsetOnAxis(ap=idx_ap, axis=0)),
            [idx_ap], [out], dma_sem=sem)

    def emit(self):
        nc = self.nc
        ops = self.ops
        for e in self.ENGS:
            seen = {f: -1 for f in self.ENGS}
            seen_dma = {}
            for op in ops[e]:
                keep = {}
                for d in op.deps:
                    f = d.eng
                    if f == e and e in ("pe", "sp"):
                        continue
                    if d.idx > seen[f] and d.idx > keep.get(f, (-1, None))[0]:
                        keep[f] = (d.idx, d)
                op.waits = []
                for f, (i, d) in keep.items():
                    seen[f] = i
                    d.milestone = True
                    op.waits.append(d)
                dw = []
                for s, v in op.dma_deps.items():
                    if s in self.total_sems:
                        v = -1
                    if seen_dma.get(s, 0) < v or v == -1:
                        if v == -1 and seen_dma.get(s, 0) == -1:
                            continue
                        seen_dma[s] = v
                        dw.append((s, v))
                op.dma_deps = dw
        for e in self.ENGS:
            c = 0
            for op in ops[e]:
                if op.milestone:
                    c += 1
                    op.mval = c
        self.nmil = {e: sum(1 for o in ops[e] if o.milestone) for e in self.ENGS}
        with ExitStack() as st:
            esem = {e: st.enter_context(nc.semaphore(f"e_{e}")) for e in self.ENGS}
            dsem = {s: st.enter_context(nc.semaphore(f"d_{s}")) for s in self.dma_cnt}
            block = st.enter_context(nc.Block())

            def run(e, eng):
                for op in ops[e]:
                    for d in op.waits:
                        eng.wait_ge(esem[d.eng], d.mval)
                    for s, v in op.dma_deps:
                        eng.wait_ge(dsem[s], self.dma_cnt[s] if v == -1 else v)
                    ins = op.fn(eng)
                    if op.dma_sem is not None:
                        ins.then_inc(dsem[op.dma_sem], 16)
                    elif op.milestone:
                        ins.then_inc(esem[e], 1)
                if e == "sp":
                    for s, v in self.dma_cnt.items():
                        eng.wait_ge(dsem[s], v)

            @block.tensor
            def _(eng):
                run("pe", eng)

            @block.scalar
            def _(eng):
                run("act", eng)

            @block.vector
            def _(eng):
                run("dve", eng)

            @block.gpsimd
            def _(eng):
                run("pool", eng)

            @block.sync
            def _(eng):
                run("sp", eng)


def _units_proj(W, gf):
    K, N = W.shape
    kc = K // 128
    return np.ascontiguousarray(W.reshape(kc, 128, N // gf, gf).transpose(2, 1, 0, 3).reshape(N // gf, 128, kc * gf))


def _cols(v):
    v = np.asarray(v, np.float32).reshape(-1, 128)
    return np.ascontiguousarray(v.T)


class _ColPack:
    def __init__(self):
        self.parts = []
        self.n = 0
        self.idx = {}

    def add(self, name, arr):
        arr = np.asarray(arr, np.float32)
        assert arr.shape[0] == 128
        self.idx[name] = self.n
        self.parts.append(arr)
        self.n += arr.shape[1]

    def build(self):
        return np.ascontiguousarray(np.concatenate(self.parts, axis=1))


def _prep_shared(inp):
    sh = {}
    cp = _ColPack()
    for i in range(4):
        cp.add(f"nmix{i}", _cols(inp["norm_mix"][i]))
        cp.add(f"nffn{i}", _cols(inp["norm_ffn"][i]))
    cp.add("nfinal", _cols(inp["norm_final"]))
    for j in range(2):
        for k in range(4):
            cp.add(f"lru_cw{j}_{k}", _cols(inp["lru_conv_w"][j, k]))
        cp.add(f"lru_cb{j}", _cols(inp["lru_conv_b"][j]))
        cp.add(f"lru_ba{j}", _cols(inp["lru_b_a"][j]))
        cp.add(f"lru_bi{j}", _cols(inp["lru_b_i"][j]))
        cp.add(f"lru_lam{j}", _cols(inp["lru_lambda"][j]))
        w_in = inp["lru_w_in"][j]
        u = []
        for n in range(4):
            u.append(_units_proj(w_in[:, n * 256:(n + 1) * 256], 256)[0])
            u.append(_units_proj(w_in[:, 1024 + n * 256:1024 + (n + 1) * 256], 256)[0])
        sh[f"lru_win{j}"] = np.stack(u)
        wa, wi = inp["lru_w_a"][j], inp["lru_w_i"][j]
        g = []
        for n in range(4):
            a = _units_proj(wa[n], 256)[0]
            b = _units_proj(wi[n], 256)[0]
            g.append(np.concatenate([a, b], axis=1))
        sh[f"lru_wg{j}"] = np.stack(g)
        wo = inp["lru_w_out"][j]
        sh[f"lru_wout{j}"] = np.stack([_units_proj(wo[n * 256:(n + 1) * 256], 1024)[0] for n in range(4)])
    for i in range(4):
        wgu = inp["ffn_w_gu"][i]
        g = _units_proj(wgu[:, :DFF], 128)
        u = _units_proj(wgu[:, DFF:], 128)
        sh[f"ffn_gu{i}"] = np.ascontiguousarray(
            np.stack([g.reshape(FC, 128, 8, 128), u.reshape(FC, 128, 8, 128)], axis=3).reshape(FC, 128, 2048))
        wd = inp["ffn_w_down"][i]
        hv = []
        for half in range(2):
            hv.append(_units_proj(wd[half * 1408:(half + 1) * 1408], 128))
        sh[f"ffn_dn{i}"] = np.ascontiguousarray(np.stack(hv).reshape(16, 128, 1408))
    _prep_mla(inp, sh, cp)
    _prep_dn(inp, sh, cp)
    sh["cols"] = cp.build()
    sh["_colidx"] = cp.idx
    sh["ones_bf"] = np.ones((128, 128), np.float32)
    sh["ident"] = np.eye(128, dtype=np.float32)
    return sh


def _prep_mla(inp, sh, cp):
    cp.add("mla_qn", _cols(inp["mla_q_norm"][0]))
    cp.add("mla_kvn", _cols(inp["mla_kv_norm"][0]))
    wdkv = inp["mla_w_dkv"][0]
    sh["mla_dkv_c"] = _units_proj(wdkv[:, :256], 256)
    perm = np.concatenate([np.arange(32, 64), np.arange(0, 32)])
    kr = np.concatenate([wdkv[:, 256:320], wdkv[:, 256 + perm]], axis=1)
    sh["mla_dkv_r"] = _units_proj(kr, 128)
    sh["mla_dq"] = _units_proj(inp["mla_w_dq"][0], 256)
    wuq = inp["mla_w_uq"][0].reshape(512, 8, 192)
    wuk = inp["mla_w_uk"][0]
    wuv = inp["mla_w_uv"][0]
    wo = inp["mla_w_o"][0]
    u1, u2 = [], []
    for h in range(8):
        q = np.concatenate([wuq[:, h, :128], wuq[:, h, 128:192], wuq[:, h, 128 + perm]], axis=1)
        a = _units_proj(q, 256)[0]
        b = np.ascontiguousarray(wuk[:, h, :].T)
        u1.append(np.concatenate([a, b], axis=1))
        v = _units_proj(wuv[:, h, :], 128)[0]
        o = wo[h * 128:(h + 1) * 128, :]
        u2.append(np.concatenate([v, o], axis=1))
    sh["mla_u1"] = np.stack(u1)
    sh["mla_u2"] = np.stack(u2)
    half = 32
    freqs = (10000.0 ** (-np.arange(half, dtype=np.float32) / half)).astype(np.float32)
    pos = np.concatenate([np.arange(TP), np.full(NS, NPAGES * PAGE)]).astype(np.float32)
    ang = pos[None, :] * freqs[:, None]
    c, sn = np.cos(ang).astype(np.float32), np.sin(ang).astype(np.float32)
    rope = np.stack([np.concatenate([c, c], axis=0), np.concatenate([-sn, sn], axis=0)], axis=1)
    sh["rope"] = np.ascontiguousarray(rope.astype(np.float32))
    sh["tri"] = np.triu(np.ones((128, 128), np.float32))
    pool = np.concatenate([inp["cache_mla_ckv"][0], inp["cache_mla_kpe"][0]], axis=-1)
    sh["poolkv"] = pool.reshape(NPOOL * 32, 4 * 320)


def _prep_dn(inp, sh, cp):
    w = inp["dn_w_in"][0]
    qk, vz = [], []
    for h in range(8):
        qk.append(_units_proj(np.concatenate([w[:, h * 128:(h + 1) * 128], w[:, 1024 + h * 128:1024 + (h + 1) * 128]], axis=1), 256)[0])
        vz.append(_units_proj(np.concatenate([w[:, 2048 + h * 128:2048 + (h + 1) * 128], w[:, 3072 + h * 128:3072 + (h + 1) * 128]], axis=1), 256)[0])
    sh["dn_qk"] = np.stack(qk)
    sh["dn_vz"] = np.stack(vz)
    sh["dn_ba"] = _units_proj(w[:, 4096:4112], 16)
    for q in range(4):
        cp.add(f"dn_cw{q}", _cols(inp["dn_conv_w"][0, q]))
    cp.add("dn_norm", _cols(inp["dn_norm"][0]))
    pad = np.zeros((128, 2), np.float32)
    pad[:8, 0] = inp["dn_a_log"][0]
    pad[:8, 1] = inp["dn_dt_bias"][0]
    cp.add("dn_ab", pad)
    wo = inp["dn_w_out"][0]
    sh["dn_wo"] = np.ascontiguousarray(wo.reshape(8, 128, 1024))
    sel = np.zeros((8, 8, 128), np.float32)
    for h in range(8):
        sel[h, h, :] = 1.0
    sh["dn_sel"] = sel.reshape(8, 1024)
    mask = np.ones((8, T), np.float32)
    mask[:, 0] = 0.0
    mask[:, 16:TP:64] = 0.0
    mask[:, TP:] = 0.0
    sh["dn_mask"] = mask
    r = np.arange(64)[:, None]
    c = np.arange(64)[None, :]
    mmax = np.where(c < r, 0.0, 30000.0).astype(np.float32)
    mmin = np.where(c >= r, 0.0, -30000.0).astype(np.float32)
    sh["dn_mm"] = np.ascontiguousarray(np.concatenate([mmax, mmin], axis=1))


def _prep_core(inp, c):
    x_full = np.concatenate([inp["meta_tokens"], inp["x_prompt"][c], inp["x_sample"][NS * c:NS * (c + 1), 0]], axis=0)
    pc = {}
    pc["xT"] = np.ascontiguousarray(x_full.reshape(T, KC, 128).transpose(2, 1, 0))
    lh = inp["state_lru_h"][:, NS * c:NS * (c + 1)]
    pc["s_lru_h"] = np.ascontiguousarray(lh.reshape(2, NS, KC, 128).transpose(3, 0, 2, 1))
    lc = inp["state_lru_conv"][:, NS * c:NS * (c + 1)]
    pc["s_lru_conv"] = np.ascontiguousarray(lc.reshape(2, NS, 3, KC, 128).transpose(4, 0, 3, 1, 2))
    pc["s_dn_S"] = np.ascontiguousarray(inp["state_dn_S"][0, NS * c:NS * (c + 1)].reshape(NS * 8, 128, 128))
    dc = inp["state_dn_conv"][0, NS * c:NS * (c + 1)]
    pc["s_dn_conv"] = np.ascontiguousarray(dc.reshape(NS, 3, 24, 128).transpose(3, 2, 0, 1))
    pc["pt"] = np.ascontiguousarray(inp["page_table"][NS * c:NS * (c + 1)].T.astype(np.int32))
    return pc


class Builder:
    def __init__(self, nc, S, shapes, colidx, stop_after=None):
        self.nc = nc
        self.S = S
        self.colidx = colidx
        self.stop_after = stop_after
        self.dram = {}
        for name, shp in shapes.items():
            self.dram[name] = nc.dram_tensor(name, list(shp), I32 if name == "pt" else F32, kind="ExternalInput").ap()
        self.ps = nc.alloc_psum_tensor("ps", [128, 4096], F32)
        self.ps_next = 0
        self.wq = []
        self.wi = 0
        self.outs = {}

    def out(self, name, shape):
        ap = self.nc.dram_tensor(name, list(shape), F32, kind="ExternalOutput").ap()
        self.outs[name] = ap
        return ap

    def bank(self):
        b = self.ps_next
        self.ps_next = (self.ps_next + 1) % 4
        return self.ps[:, b * 512:(b + 1) * 512]

    def col(self, name, k=0, n=1):
        i = self.colidx[name] + k
        return self.cols[:, i:i + n]

    def wload(self, name, u, nel):
        S = self.S
        slot = self.wi % self.nws
        ss = self.wi % self.nst
        self.wi += 1
        stg = self.wstage[ss]
        wb = self.wbf[slot]
        src = self.dram[name][u]
        S.dma(stg[:, 0:nel], src, sem=f"w{ss}")
        S.copy("pool", wb[:, 0:nel], stg[:, 0:nel])
        return wb

    def run_units(self, units, depth=2):
        loaded = []
        n = len(units)
        for i in range(n + depth):
            if i < n:
                nm, u, nel, _ = units[i]
                loaded.append(self.wload(nm, u, nel))
            j = i - depth
            if j >= 0:
                units[j][3](loaded[j])

    def setup(self):
        S = self.S
        nc = self.nc
        ncol = self.dram["cols"].shape[1]
        self.cols = S.sb("cols", [128, ncol], F32)
        S.dma(self.cols[:], self.dram["cols"], sem="init")
        S.total_sems.add("init")
        self.ones_f = S.sb("ones_f", [128, 128], F32)
        self.ident_f = S.sb("ident_f", [128, 128], F32)
        S.dma(self.ones_f[:], self.dram["ones_bf"], sem="init")
        S.dma(self.ident_f[:], self.dram["ident"], sem="init")
        self.ones_b = S.sb("ones_b", [128, 128], BF16)
        self.ident_b = S.sb("ident_b", [128, 128], BF16)
        S.copy("pool", self.ones_b[:], self.ones_f[:])
        S.copy("pool", self.ident_b[:], self.ident_f[:])
        self.x = [S.sb(f"x{k}", [128, T], F32) for k in range(KC)]
        for k in range(KC):
            S.dma(self.x[k][:], self.dram["xT"][:, k, :], sem="init")
        self.xn = [S.sb(f"xn{k}", [128, T], BF16) for k in range(KC)]
        self.nws = 3
        self.nst = 2
        self.wstage = [S.sb(f"wst{i}", [128, WSLOT], F32) for i in range(self.nst)]
        self.wbf = [S.sb(f"wbf{i}", [128, WSLOT], BF16) for i in range(self.nws)]
        self.sq = [S.sb(f"sq{i}", [128, 512], BF16) for i in range(2)]
        self.rstd = [S.sb(f"rstd{i}", [128, 512], F32) for i in range(2)]
        self.small = S.sb("small", [128, 320], F32)
        self.small_n = 0
        self.small_idx = {}

    def small_alloc(self, name, n):
        i = self.small_n
        self.small_idx[name] = (i, n)
        self.small_n += n
        assert self.small_n <= 320
        return self.small[:, i:i + n]

    def rmsnorm_stats(self, ti):
        S = self.S
        t0, n = TT[ti]
        acc = self.bank()
        for k in range(KC):
            sq = self.sq[k % 2]
            S.act(sq[:, 0:n], self.x[k][:, t0:t0 + n], AF.Square)
            S.mm(acc[:, 0:n], self.ones_b[:], sq[:, 0:n], start=(k == 0), stop=(k == KC - 1))
        r = self.rstd[ti % 2]
        S.act(r[:, 0:n], acc[:, 0:n], AF.Sqrt, bias=self.eps_col[:, 0:1], scale=1.0 / D)
        S.recip(r[:, 0:n], r[:, 0:n])
        return r

    def rmsnorm_to_xn(self, gname):
        S = self.S
        for ti, (t0, n) in enumerate(TT):
            r = self.rmsnorm_stats(ti)
            for k in range(KC):
                S.stt(self.xn[k][:, t0:t0 + n], self.x[k][:, t0:t0 + n], self.col(gname, k), r[:, 0:n],
                      ALU.mult, ALU.mult)

    def proj_chunk(self, wb_lhsT, rhs_list, evac):
        S = self.S
        nk = len(rhs_list)
        for ti, (t0, n) in enumerate(TT):
            acc = self.bank()
            for k in range(nk):
                S.mm(acc[:, 0:n], wb_lhsT(k), rhs_list[k][:, t0:t0 + n], start=(k == 0), stop=(k == nk - 1))
            evac(ti, t0, n, acc)

    def ffn(self, li):
        S = self.S
        self.rmsnorm_to_xn(f"nffn{li}")
        m = S.mark()
        h = [S.sb(f"h{j}", [128, T], BF16) for j in range(11)]
        sg = [S.sb(f"sg{j}", [128, 512], BF16) for j in range(2)]
        for half in range(2):
            units = []
            for jj in range(11):
                j = half * 11 + jj

                def fn(wb, jj=jj):
                    w4 = wb[:, 0:2048].rearrange("p (k g f) -> p k g f", k=8, g=2)
                    for ti, (t0, n) in enumerate(TT):
                        pg = self.bank()
                        pu = self.bank()
                        for k in range(KC):
                            S.mm(pg[:, 0:n], w4[:, k, 0, :], self.xn[k][:, t0:t0 + n], start=(k == 0), stop=(k == KC - 1))
                        for k in range(KC):
                            S.mm(pu[:, 0:n], w4[:, k, 1, :], self.xn[k][:, t0:t0 + n], start=(k == 0), stop=(k == KC - 1))
                        s = sg[ti % 2]
                        S.act(s[:, 0:n], pg[:, 0:n], AF.Silu)
                        S.tt("dve", h[jj][:, t0:t0 + n], pu[:, 0:n], s[:, 0:n], ALU.mult)
                units.append((f"ffn_gu{li}", j, 2048, fn))
            for fo in range(KC):
                def fn2(wb, fo=fo):
                    w3 = wb[:, 0:1408].rearrange("p (k f) -> p k f", k=11)

                    def ev(ti, t0, n, acc):
                        S.tt("dve", self.x[fo][:, t0:t0 + n], acc[:, 0:n], self.x[fo][:, t0:t0 + n], ALU.add)
                    self.proj_chunk(lambda k: w3[:, k, :], h, ev)
                units.append((f"ffn_dn{li}", half * 8 + fo, 1408, fn2))
            self.run_units(units)
        S.release(m)

    def lru(self, li, j):
        S = self.S
        self.rmsnorm_to_xn(f"nmix{li}")
        m = S.mark()
        HALF = [(0, 1024, (0, 1)), (1024, T - 1024, (2, 3, 4))]
        HN = T - 1024
        cA = S.sb("cA", [128, 8], F32)
        ncA = S.sb("ncA", [128, 8], F32)
        lam = self.col(f"lru_lam{j}", 0, 8)
        S.act(cA[:], lam, AF.Exp, scale=-1.0)
        S.act(cA[:], cA[:], AF.Ln, bias=self.one_col[:, 0:1])
        S.ts("dve", ncA[:], cA[:], 8.0, ALU.mult)
        S.ts("dve", cA[:], cA[:], -8.0, ALU.mult)
        hg = [S.sb(f"hg{k}", [128, T], BF16) for k in range(2)]
        gate = [S.sb(f"gate{k}", [128, T], BF16) for k in range(2)]
        xx = [S.sb(f"xx{k}", [128, TP + 3], F32) for k in range(2)]
        xs = [S.sb(f"xs{k}", [128, NS, 4], F32) for k in range(2)]
        xcb = [S.sb(f"xcb{k}", [128, T], BF16) for k in range(2)]
        ctmp = S.sb("ctmp", [128, T], F32)
        ra = S.sb("ra", [128, HN], F32)
        ri = S.sb("ri", [128, HN], F32)
        av = S.sb("av", [128, HN], F32)
        tmp = S.sb("tmp", [128, HN], F32)
        carry = S.sb("carry", [128, 1], F32)
        p_h = self.small_alloc(f"p_lru_h{j}", 8)
        p_cv = self.small_alloc(f"p_lru_conv{j}", 24)
        s_h = self.small_alloc(f"s_lru_h{j}", 32)
        s_cv = self.small_alloc(f"s_lru_conv{j}", 96)
        s_cv4 = s_cv.rearrange("p (k b j) -> p k b j", k=8, b=NS)
        s_h3 = s_h.rearrange("p (k b) -> p k b", k=8)
        st_h = S.sb("st_h", [128, 8, NS], F32)
        S.dma(st_h[:], self.dram["s_lru_h"][:, j], sem=f"st{j}")
        st_c = S.sb("st_c", [128, 8, NS, 3], F32)
        S.dma(st_c[:], self.dram["s_lru_conv"][:, j], sem=f"st{j}")
        for k in range(2):
            S.memset("pool", xx[k][:, 0:3], 0.0)

        units = []
        for n in range(4):
            def f_gate(wb, n=n):
                w3 = wb[:, 0:2048].rearrange("p (k f) -> p k f", k=8)
                for c in range(2):
                    def ev(ti, t0, nn, acc, c=c):
                        S.act(gate[c][:, t0:t0 + nn], acc[:, 0:nn], AF.Gelu)
                    self.proj_chunk(lambda k, c=c: w3[:, k, c * 128:(c + 1) * 128], self.xn, ev)
            units.append((f"lru_win{j}", 2 * n, 2048, f_gate))

            def f_x(wb, n=n):
                w3 = wb[:, 0:2048].rearrange("p (k f) -> p k f", k=8)
                for c in range(2):
                    kc = 2 * n + c

                    def ev(ti, t0, nn, acc, c=c, kc=kc):
                        if t0 + nn <= TP:
                            S.copy("act", xx[c][:, 3 + t0:3 + t0 + nn], acc[:, 0:nn])
                        else:
                            npz = TP - t0
                            S.copy("act", xx[c][:, 3 + t0:3 + TP], acc[:, 0:npz])
                            S.copy("act", xs[c][:, :, 3], acc[:, npz:npz + NS])
                    self.proj_chunk(lambda k, c=c: w3[:, k, c * 128:(c + 1) * 128], self.xn, ev)
                    S.copy("pool", xs[c][:, :, 0:3], st_c[:, kc, :, :])
                    S.copy("pool", p_cv[:, kc * 3:(kc + 1) * 3], xx[c][:, TP:TP + 3])
                    S.copy("pool", s_cv4[:, kc, :, :], xs[c][:, :, 1:4])
                    cw = lambda q, kc=kc: self.col(f"lru_cw{j}_{q}", kc)
                    cb = self.col(f"lru_cb{j}", kc)
                    S.ts("dve", ctmp[:, 0:TP], xx[c][:, 0:TP], cw(0), ALU.mult, cb, ALU.add)
                    for q in range(1, 3):
                        S.stt(ctmp[:, 0:TP], xx[c][:, q:q + TP], cw(q), ctmp[:, 0:TP], ALU.mult, ALU.add)
                    S.stt(xcb[c][:, 0:TP], xx[c][:, 3:3 + TP], cw(3), ctmp[:, 0:TP], ALU.mult, ALU.add)
                    S.ts("dve", ctmp[:, TP:T], xs[c][:, :, 0], cw(0), ALU.mult, cb, ALU.add)
                    for q in range(1, 3):
                        S.stt(ctmp[:, TP:T], xs[c][:, :, q], cw(q), ctmp[:, TP:T], ALU.mult, ALU.add)
                    S.stt(xcb[c][:, TP:T], xs[c][:, :, 3], cw(3), ctmp[:, TP:T], ALU.mult, ALU.add)
            units.append((f"lru_win{j}", 2 * n + 1, 2048, f_x))

            def f_g(wb, n=n):
                w4 = wb[:, 0:1024].rearrange("p (g k f) -> p g k f", g=2, k=2)
                for c in range(2):
                    kc = 2 * n + c
                    for (h0, hn, tiles) in HALF:
                        for ti in tiles:
                            t0, nn = TT[ti]
                            pa = self.bank()
                            pi = self.bank()
                            for k in range(2):
                                S.mm(pa[:, 0:nn], w4[:, 0, k, c * 128:(c + 1) * 128], xcb[k][:, t0:t0 + nn],
                                     start=(k == 0), stop=(k == 1))
                            for k in range(2):
                                S.mm(pi[:, 0:nn], w4[:, 1, k, c * 128:(c + 1) * 128], xcb[k][:, t0:t0 + nn],
                                     start=(k == 0), stop=(k == 1))
                            S.act(ra[:, t0 - h0:t0 - h0 + nn], pa[:, 0:nn], AF.Sigmoid, bias=self.col(f"lru_ba{j}", kc))
                            S.act(ri[:, t0 - h0:t0 - h0 + nn], pi[:, 0:nn], AF.Sigmoid, bias=self.col(f"lru_bi{j}", kc))
                        R = slice(0, hn)
                        G = slice(h0, h0 + hn)
                        S.act(av[:, R], ra[:, R], AF.Exp, scale=cA[:, kc:kc + 1])
                        S.act(tmp[:, R], ra[:, R], AF.Tanh, scale=ncA[:, kc:kc + 1])
                        S.tt("pool", ra[:, R], av[:, R], av[:, R], ALU.mult)
                        S.stt(tmp[:, R], ra[:, R], 1.0, tmp[:, R], ALU.add, ALU.mult)
                        S.act(tmp[:, R], tmp[:, R], AF.Sqrt)
                        S.tt("pool", ri[:, R], ri[:, R], xcb[c][:, G], ALU.mult)
                        S.tt("dve", ri[:, R], ri[:, R], tmp[:, R], ALU.mult)
                        if h0 == 0:
                            S.scan(tmp[:, R], av[:, R], ri[:, R], 0.0)
                            S.copy("pool", carry[:], tmp[:, hn - 1:hn])
                        else:
                            npr = TP - h0
                            S.scan(tmp[:, 0:npr], av[:, 0:npr], ri[:, 0:npr], carry[:, 0:1])
                            S.tt("dve", tmp[:, npr:hn], av[:, npr:hn], st_h[:, kc, :], ALU.mult)
                            S.tt("dve", tmp[:, npr:hn], tmp[:, npr:hn], ri[:, npr:hn], ALU.add)
                            S.copy("pool", p_h[:, kc:kc + 1], tmp[:, npr - 1:npr])
                            S.copy("pool", s_h3[:, kc, :], tmp[:, npr:hn])
                        S.tt("dve", hg[c][:, G], tmp[:, R], gate[c][:, G], ALU.mult)
            units.append((f"lru_wg{j}", n, 1024, f_g))

            def f_o(wb, n=n):
                w3 = wb[:, 0:2048].rearrange("p (k f) -> p k f", k=2)
                for fo in range(KC):
                    def ev(ti, t0, nn, acc, fo=fo):
                        S.tt("dve", self.x[fo][:, t0:t0 + nn], acc[:, 0:nn], self.x[fo][:, t0:t0 + nn], ALU.add)
                    self.proj_chunk(lambda k, fo=fo: w3[:, k, fo * 128:(fo + 1) * 128], hg, ev)
            units.append((f"lru_wout{j}", n, 2048, f_o))
        self.run_units(units)
        S.release(m)

    def rbank(self, i):
        return self.ps[:, i * 512:(i + 1) * 512]

    def rope_tile(self, dst, p_raw, p_swp, t0, n, cs, t1, t2):
        S = self.S
        S.dma(cs[:, :, 0:n], self.dram["rope"][:, :, t0:t0 + n], sem="cs")
        S.tt("dve", t1[:, 0:n], p_raw, cs[:, 0, 0:n], ALU.mult)
        S.tt("dve", t2[:, 0:n], p_swp, cs[:, 1, 0:n], ALU.mult)
        S.tt("pool", dst, t1[:, 0:n], t2[:, 0:n], ALU.add)

    def mla(self, li, j):
        S = self.S
        self.rmsnorm_to_xn(f"nmix{li}")
        xn_base = S.bases[self.xn[0].name]
        m0 = S.mark()
        ckvb = [S.sb(f"ckvb{k}", [128, T], BF16) for k in range(2)]
        kpeb = S.sb("kpeb", [64, T], BF16)
        cqb = [S.sb(f"cqb{k}", [128, T], BF16) for k in range(4)]
        qs = S.sb("qs", [128, 3, NS, 8], BF16)
        ols = S.sb("ols", [128, 2, 8, NS], BF16)
        trib = S.sb("trib", [128, 128], BF16)
        trif = S.sb("trif", [128, 128], F32)
        S.dma(trif[:], self.dram["tri"], sem="tri")
        S.copy("pool", trib[:], trif[:])
        cs = S.sb("cs", [64, 2, 512], F32)
        rt1 = S.sb("rt1", [64, 512], F32)
        rt2 = S.sb("rt2", [64, 512], F32)
        m1 = S.mark()
        kpef = S.sb("kpef", [64, T], F32)
        ckvT = [S.sb(f"ckvT{k}", [128, T], F32) for k in range(2)]

        wq = [self.wload("mla_dq", u, 2048) for u in range(2)]
        for ti, (t0, n) in enumerate(TT):
            pb = [self.bank() for _ in range(4)]
            for c4 in range(4):
                w3 = wq[c4 // 2][:, 0:2048].rearrange("p (k f) -> p k f", k=8)
                for k in range(KC):
                    S.mm(pb[c4][:, 0:n], w3[:, k, (c4 % 2) * 128:(c4 % 2 + 1) * 128], self.xn[k][:, t0:t0 + n],
                         start=(k == 0), stop=(k == KC - 1))
            acc = self.rbank(4)
            for c4 in range(4):
                sq = self.sq[c4 % 2]
                S.act(sq[:, 0:n], pb[c4][:, 0:n], AF.Square)
                S.mm(acc[:, 0:n], self.ones_b[:], sq[:, 0:n], start=(c4 == 0), stop=(c4 == 3))
            r = self.rstd[ti % 2]
            S.act(r[:, 0:n], acc[:, 0:n], AF.Sqrt, bias=self.eps_col[:, 0:1], scale=1.0 / 512)
            S.recip(r[:, 0:n], r[:, 0:n])
            for c4 in range(4):
                S.stt(cqb[c4][:, t0:t0 + n], pb[c4][:, 0:n], self.col("mla_qn", c4), r[:, 0:n], ALU.mult, ALU.mult)

        wr = self.wload("mla_dkv_r", 0, 1024)
        wr3 = wr[:, 0:1024].rearrange("p (k f) -> p k f", k=8)
        for ti, (t0, n) in enumerate(TT):
            p1 = self.bank()
            p2 = self.bank()
            for k in range(KC):
                S.mm(p1[0:64, 0:n], wr3[:, k, 0:64], self.xn[k][:, t0:t0 + n], start=(k == 0), stop=(k == KC - 1))
            for k in range(KC):
                S.mm(p2[0:64, 0:n], wr3[:, k, 64:128], self.xn[k][:, t0:t0 + n], start=(k == 0), stop=(k == KC - 1))
            self.rope_tile(kpef[:, t0:t0 + n], p1[0:64, 0:n], p2[0:64, 0:n], t0, n, cs, rt1, rt2)
        S.copy("pool", kpeb[:], kpef[:])

        wc = self.wload("mla_dkv_c", 0, 2048)
        wc3 = wc[:, 0:2048].rearrange("p (k f) -> p k f", k=8)
        for ti, (t0, n) in enumerate(TT):
            pb = [self.bank() for _ in range(2)]
            for c2 in range(2):
                for k in range(KC):
                    S.mm(pb[c2][:, 0:n], wc3[:, k, c2 * 128:(c2 + 1) * 128], self.xn[k][:, t0:t0 + n],
                         start=(k == 0), stop=(k == KC - 1))
            acc = self.rbank(4)
            for c2 in range(2):
                sq = self.sq[c2 % 2]
                S.act(sq[:, 0:n], pb[c2][:, 0:n], AF.Square)
                S.mm(acc[:, 0:n], self.ones_b[:], sq[:, 0:n], start=(c2 == 0), stop=(c2 == 1))
            r = self.rstd[ti % 2]
            S.act(r[:, 0:n], acc[:, 0:n], AF.Sqrt, bias=self.eps_col[:, 0:1], scale=1.0 / 256)
            S.recip(r[:, 0:n], r[:, 0:n])
            for c2 in range(2):
                S.stt(ckvT[c2][:, t0:t0 + n], pb[c2][:, 0:n], self.col("mla_kvn", c2), r[:, 0:n], ALU.mult, ALU.mult)
        for c2 in range(2):
            S.copy("pool", ckvb[c2][:], ckvT[c2][:])

        sv = S.sb_ptr
        S.sb_ptr = xn_base
        vtok = S.sb("vtok", [128, 17, 256], BF16)
        ostg = [S.sb(f"ostg{i}", [128, 320], F32) for i in range(2)]
        qaug = [S.sb(f"qaug{k}", [128, T], BF16) for k in range(3)]
        oh = S.sb("oh", [128, T], BF16)
        vnew = S.sb("vnew", [1, NS, 257], BF16)
        assert S.sb_ptr <= xn_base + 8 * ((T * 2 + 63) // 64 * 64), "xn overlay overflow"
        S.sb_ptr = sv

        okv = self.out("p_kv", [T, 320])
        for bi in range(17 if "T" not in os.environ.get("KSKIP", "") else 0):
            t0 = bi * 128
            n = min(128, T - t0)
            pt_ = self.bank()
            for c2 in range(2):
                S.tr(pt_[0:n, c2 * 128:(c2 + 1) * 128], ckvT[c2][:, t0:t0 + n], self.ident_f[:])
            S.tr(pt_[0:n, 256:320], kpef[:, t0:t0 + n], self.ident_f[0:64, 0:64])
            og = ostg[bi % 2]
            KS = os.environ.get("KSKIP", "")
            if "1" not in KS:
                S.copy("act", og[0:n, :], pt_[0:n, 0:320])
            if "2" not in KS:
                S.copy("dve", vtok[0:n, bi, :], pt_[0:n, 0:256])
            if "3" not in KS:
                S.dma(okv[t0:t0 + n, :], og[0:n, :], sem=f"okv{bi % 2}")
        S.memset("pool", vnew[:], 1.0)
        for b in range(NS if "V" not in os.environ.get("KSKIP", "") else 0):
            pt_ = self.bank()
            for c2 in range(2):
                S.tr(pt_[0:1, c2 * 128:(c2 + 1) * 128], ckvT[c2][:, TP + b:TP + b + 1], self.ident_f[:])
            S.copy("dve", vnew[0:1, b, 0:256], pt_[0:1, 0:256])
        S.release(m1)

        mA = S.mark()
        qn_s = S.sb("qn_s", [128, NS], BF16)
        u1 = []

        def passA(wb, h):
            wq3 = wb[:, 0:1024].rearrange("p (k f) -> p k f", k=4)
            wuk = wb[:, 1024:1280]
            pn = self.bank()
            for k in range(4):
                S.mm(pn[:, 0:NS], wq3[:, k, 0:128], cqb[k][:, TP:T], start=(k == 0), stop=(k == 3))
            S.copy("dve", qn_s[:], pn[:, 0:NS])
            p1 = self.bank()
            p2 = self.bank()
            for k in range(4):
                S.mm(p1[0:64, 0:NS], wq3[:, k, 128:192], cqb[k][:, TP:T], start=(k == 0), stop=(k == 3))
            for k in range(4):
                S.mm(p2[0:64, 0:NS], wq3[:, k, 192:256], cqb[k][:, TP:T], start=(k == 0), stop=(k == 3))
            self.rope_tile(qs[0:64, 2, :, h], p1[0:64, 0:NS], p2[0:64, 0:NS], TP, NS, cs, rt1, rt2)
            for c2 in range(2):
                pl = self.bank()
                S.mm(pl[:, 0:NS], wuk[:, c2 * 128:(c2 + 1) * 128], qn_s[:], start=True, stop=True)
                S.copy("dve", qs[:, c2, :, h], pl[:, 0:NS])
        if "A" not in os.environ.get("KSKIP", ""):
            self.run_units([("mla_u1", h, 1280, (lambda wb, h=h: passA(wb, h))) for h in range(8)])
        S.release(mA)

        if "D" not in os.environ.get("KSKIP", ""):
            self.mla_decode(qs, ols, ckvb, kpeb, vnew)
        else:
            S.memset("pool", ols[:], 0.0)

        mB = S.mark()
        qn = S.sb("qn", [128, T], BF16)
        olat = [S.sb(f"olat{k}", [128, T], BF16) for k in range(2)]
        PT = [S.sb(f"PT{i}", [128, 512], BF16) for i in range(3)]
        rden = S.sb("rden", [128, 512], F32)
        QT = [(0, 512), (512, 512), (1024, 512), (1536, 512), (2048, 16)]

        def head_q(wb, h):
            wq3 = wb[:, 0:1024].rearrange("p (k f) -> p k f", k=4)
            wuk = wb[:, 1024:1280]

            def ev(ti, t0, n, acc):
                S.copy("act", qn[:, t0:t0 + n], acc[:, 0:n])
            self.proj_chunk(lambda k: wq3[:, k, 0:128], cqb, ev)
            for ti, (t0, n) in enumerate(TT):
                p1 = self.bank()
                p2 = self.bank()
                for k in range(4):
                    S.mm(p1[0:64, 0:n], wq3[:, k, 128:192], cqb[k][:, t0:t0 + n], start=(k == 0), stop=(k == 3))
                for k in range(4):
                    S.mm(p2[0:64, 0:n], wq3[:, k, 192:256], cqb[k][:, t0:t0 + n], start=(k == 0), stop=(k == 3))
                self.rope_tile(qaug[2][0:64, t0:t0 + n], p1[0:64, 0:n], p2[0:64, 0:n], t0, n, cs, rt1, rt2)
            for c2 in range(2):
                def ev2(ti, t0, n, acc, c2=c2):
                    S.copy("act", qaug[c2][:, t0:t0 + n], acc[:, 0:n])
                self.proj_chunk(lambda k, c2=c2: wuk[:, c2 * 128:(c2 + 1) * 128], [qn], ev2)
            a0, a1, dn_ = self.rbank(4), self.rbank(5), self.rbank(6)
            pairs = []
            for (q0, qn_) in QT:
                nb = (q0 + qn_ - 1) // 128 + 1
                for jb in range(nb):
                    k0 = jb * 128
                    kn = min(128, TP - k0)
                    qs0 = max(q0, k0)
                    pairs.append(dict(q0=q0, qn=qn_, jb=jb, k0=k0, kn=kn, qs0=qs0, nc=q0 + qn_ - qs0, off=qs0 - q0,
                                      first=(jb == 0), last=(jb == nb - 1), idx=len(pairs)))

            def scores(p):
                kn, nc_, k0, qs0 = p["kn"], p["nc"], p["k0"], p["qs0"]
                sp = self.bank()
                S.mm(sp[0:kn, 0:nc_], ckvb[0][:, k0:k0 + kn], qaug[0][:, qs0:qs0 + nc_], start=True, stop=False)
                S.mm(sp[0:kn, 0:nc_], ckvb[1][:, k0:k0 + kn], qaug[1][:, qs0:qs0 + nc_], start=False, stop=False)
                S.mm(sp[0:kn, 0:nc_], kpeb[0:64, k0:k0 + kn], qaug[2][0:64, qs0:qs0 + nc_], start=False, stop=True)
                pt_ = PT[p["idx"] % len(PT)]
                S.act(pt_[0:kn, 0:nc_], sp[0:kn, 0:nc_], AF.Exp, scale=MLA_SCALE)
                if k0 >= p["q0"]:
                    dnn = min(128, nc_)
                    S.tt("pool", pt_[0:kn, 0:dnn], pt_[0:kn, 0:dnn], trib[0:kn, 0:dnn], ALU.mult)

            def pv(p):
                kn, nc_, off, jb = p["kn"], p["nc"], p["off"], p["jb"]
                pt_ = PT[p["idx"] % len(PT)]
                S.mm(a0[:, off:off + nc_], vtok[0:kn, jb, 0:128], pt_[0:kn, 0:nc_], start=p["first"], stop=p["last"])
                S.mm(a1[:, off:off + nc_], vtok[0:kn, jb, 128:256], pt_[0:kn, 0:nc_], start=p["first"], stop=p["last"])
                S.mm(dn_[:, off:off + nc_], self.ones_b[0:kn, :], pt_[0:kn, 0:nc_], start=p["first"], stop=p["last"])
                if p["last"]:
                    q0, qn_ = p["q0"], p["qn"]
                    S.recip(rden[:, 0:qn_], dn_[:, 0:qn_])
                    S.tt("dve", olat[0][:, q0:q0 + qn_], a0[:, 0:qn_], rden[:, 0:qn_], ALU.mult)
                    S.tt("dve", olat[1][:, q0:q0 + qn_], a1[:, 0:qn_], rden[:, 0:qn_], ALU.mult)
            scores(pairs[0])
            for i, p in enumerate(pairs):
                if i + 1 < len(pairs):
                    scores(pairs[i + 1])
                pv(p)
            for c2 in range(2):
                S.copy("pool", olat[c2][:, TP:T], ols[:, c2, h, :])

        def head_o(wb, h):
            wuv = wb[:, 0:256].rearrange("p (k v) -> p k v", k=2)
            wo = wb[:, 256:1280]

            def ev(ti, t0, n, acc):
                S.copy("act", oh[:, t0:t0 + n], acc[:, 0:n])
            self.proj_chunk(lambda k: wuv[:, k, :], olat, ev)
            for fo in range(KC):
                def ev2(ti, t0, n, acc, fo=fo):
                    S.tt("dve", self.x[fo][:, t0:t0 + n], acc[:, 0:n], self.x[fo][:, t0:t0 + n], ALU.add)
                self.proj_chunk(lambda k, fo=fo: wo[:, fo * 128:(fo + 1) * 128], [oh], ev2)
        units = []
        for h in range(8):
            units.append(("mla_u1", h, 1280, (lambda wb, h=h: head_q(wb, h))))
            units.append(("mla_u2", h, 1280, (lambda wb, h=h: head_o(wb, h))))
        if "B" not in os.environ.get("KSKIP", ""):
            self.run_units(units)
        S.release(m0)

    def mla_decode(self, qs, ols, ckvb, kpeb, vnew):
        S = self.S
        m = S.mark()
        NTK = 4
        NSUB = PAGE // NTK
        NBUF = 4
        ptab = S.sb("ptab", [128, NS], I32)
        S.dma(ptab[:], self.dram["pt"], sem="ptab")
        idx = S.sb("idx", [128, NS, NSUB], I32)
        for b in range(NS):
            for s_ in range(NSUB):
                S.ts("dve", idx[:, b, s_:s_ + 1], ptab[:, b:b + 1], float(NSUB), ALU.mult, float(s_), ALU.add)
        kvs = [S.sb(f"kvs{i}", [128, NTK * 320], F32) for i in range(NBUF)]
        kT = [S.sb(f"kT{i}", [128, 384], BF16) for i in range(2)]
        Vb = [S.sb(f"Vb{i}", [128, NTK, 257], BF16) for i in range(2)]
        for i in range(2):
            S.memset("pool", Vb[i][:], 1.0)
        PTd = [S.sb(f"PTd{i}", [128, NTK * 8], BF16) for i in range(2)]
        pnew = S.sb("pnew", [1, 8], BF16)
        osb = S.sb("osb", [8, 257], F32)
        rd = S.sb("rd", [8, 1], F32)
        onb = S.sb("onb", [8, 256], F32)
        accb = self.rbank(7)
        toks = []
        g = 0
        for b in range(NS):
            for s_ in range(NSUB):
                for tt_ in range(NTK):
                    toks.append((b, s_, tt_, g))
                g += 1
        pks = {}

        def start_chunk(b, s_, g):
            kv = kvs[g % NBUF]
            S.gather(kv[:], self.dram["poolkv"], idx[:, b, s_:s_ + 1], sem=f"kv{g % NBUF}")

        def vcast(g):
            kv = kvs[g % NBUF]
            S.copy("act", Vb[g % 2][:, :, 0:256], kv[:].rearrange("p (t c) -> p t c", t=NTK)[:, :, 0:256])

        def transposes(i):
            b, s_, tt_, g = toks[i]
            kv = kvs[g % NBUF]
            pk = self.bank()
            base = tt_ * 320
            S.tr(pk[:, 0:128], kv[:, base:base + 128], self.ident_f[:])
            S.tr(pk[:, 128:256], kv[:, base + 128:base + 256], self.ident_f[:])
            S.tr(pk[0:64, 256:384], kv[:, base + 256:base + 320], self.ident_f[:])
            kt = kT[i % 2]
            S.copy("dve", kt[:, 0:256], pk[:, 0:256])
            S.copy("dve", kt[0:64, 256:384], pk[0:64, 256:384])

        def qk(i):
            b, s_, tt_, g = toks[i]
            kt = kT[i % 2]
            sp = self.rbank(5 + (g % 2))
            o_ = sp[:, tt_ * 8:(tt_ + 1) * 8]
            S.mm(o_, kt[:, 0:128], qs[:, 0, b, :], start=True, stop=False)
            S.mm(o_, kt[:, 128:256], qs[:, 1, b, :], start=False, stop=False)
            S.mm(o_, kt[0:64, 256:384], qs[0:64, 2, b, :], start=False, stop=True)
            if tt_ == NTK - 1:
                S.act(PTd[g % 2][:], sp[:, 0:NTK * 8], AF.Exp, scale=MLA_SCALE)

        def pv(b, s_, g):
            for tt_ in range(NTK):
                S.mm(accb[0:8, 0:257], PTd[g % 2][:, tt_ * 8:(tt_ + 1) * 8], Vb[g % 2][:, tt_, :],
                     start=(s_ == 0 and tt_ == 0), stop=False)

        def finish(b):
            sp = self.bank()
            S.mm(sp[0:1, 0:8], ckvb[0][:, TP + b:TP + b + 1], qs[:, 0, b, :], start=True, stop=False)
            S.mm(sp[0:1, 0:8], ckvb[1][:, TP + b:TP + b + 1], qs[:, 1, b, :], start=False, stop=False)
            S.mm(sp[0:1, 0:8], kpeb[0:64, TP + b:TP + b + 1], qs[0:64, 2, b, :], start=False, stop=True)
            S.act(pnew[:], sp[0:1, 0:8], AF.Exp, scale=MLA_SCALE)
            S.mm(accb[0:8, 0:257], pnew[:], vnew[0:1, b, :], start=False, stop=True)
            S.copy("dve", osb[:], accb[0:8, 0:257])
            S.recip(rd[:], osb[:, 256:257])
            S.ts("dve", onb[:], osb[:, 0:256], rd[:, 0:1], ALU.mult)
            for c2 in range(2):
                po = self.bank()
                S.tr(po[:, 0:8], onb[:, c2 * 128:(c2 + 1) * 128], self.ident_f[0:8, 0:8])
                S.copy("dve", ols[:, c2, :, b], po[:, 0:8])

        n = len(toks)
        started = 0
        nchunks = NS * NSUB

        def ensure_started(upto):
            nonlocal started
            while started <= min(upto, nchunks - 1):
                bb, ss = divmod(started, NSUB)
                start_chunk(bb, ss, started)
                started += 1
        ensure_started(NBUF - 2)
        transposes(0)
        pending_pv = None
        for i in range(n):
            b, s_, tt_, g = toks[i]
            if tt_ == 0:
                ensure_started(g + NBUF - 2)
                vcast(g)
            if i + 1 < n:
                transposes(i + 1)
            qk(i)
            if tt_ == NTK - 1:
                if pending_pv is not None:
                    pv(*pending_pv)
                    if pending_pv[1] == NSUB - 1:
                        finish(pending_pv[0])
                pending_pv = (b, s_, g)
        pv(*pending_pv)
        finish(pending_pv[0])
        S.release(m)

    def mmf(self, out, lhsT, rhs, start=True, stop=True):
        return self.S.mm(out, lhsT, rhs, start, stop)

    def dn(self, li, j):
        S = self.S
        self.rmsnorm_to_xn(f"nmix{li}")
        m0 = S.mark()
        small2 = S.sb("small2", [128, 360], F32)
        S.memset("pool", small2[:], 0.0)
        p_cv = small2[:, 0:72].rearrange("p (k j) -> p k j", k=24)
        s_cv = small2[:, 72:360].rearrange("p (k b j) -> p k b j", k=24, b=NS)
        oS = self.out("o_dn_S", [8 + NS * 8, 128, 128])
        Gc = S.sb("Gc", [8, T], F32)
        GT = S.sb("GT", [64, 37, 8], F32)
        BT = S.sb("BT", [64, 37, 8], F32)
        sel = S.sb("sel", [8, 128], F32)
        mm_ = S.sb("mm_", [64, 128], F32)
        S.dma(mm_[:], self.dram["dn_mm"], sem="dnc")
        st_c = S.sb("dst_c", [128, 24, NS, 3], F32)
        S.dma(st_c[:], self.dram["s_dn_conv"], sem="dnc")
        chunks = [(0, 16, 4)] + [(16 + 64 * c, 64, 6) for c in range(32)] + [(TP + b, 1, 0) for b in range(NS)]
        xx = S.sb("dxx", [128, TP + 3], F32)
        xs = S.sb("dxs", [128, NS, 4], F32)
        ctmp = S.sb("dctmp", [128, T], F32)
        S.memset("pool", xx[:, 0:3], 0.0)
        qdec = S.sb("qdec", [128, T], BF16)
        qn = S.sb("dqn", [128, T], BF16)
        kn = S.sb("kn", [128, T], BF16)
        vb = S.sb("vb", [128, T], BF16)
        sz = S.sb("sz", [128, T], BF16)
        GB = S.sb("GB", [128, T], F32)
        oT = S.sb("oT", [128, T], BF16)
        Sf = S.sb("Sf", [128, 128], F32)
        Sb_ = S.sb("Sb", [128, 128], BF16)
        eg = self.rstd[1]
        wba = self.wload("dn_ba", 0, 128)
        wba3 = wba[:, 0:128].rearrange("p (k f) -> p k f", k=8)
        Ball = GB[0:8, 0:T]
        graw = ctmp[0:8, 0:T]
        sv_ = S.sb_ptr
        S.sb_ptr = S.bases[qdec.name]
        mrow_t = S.sb("mrow", [8, T], F32)
        S.sb_ptr = sv_
        mrow = mrow_t[:, :]
        S.dma(mrow, self.dram["dn_mask"], sem="dnc")
        nA = S.sb("nA", [8, 1], F32)
        ab = self.col("dn_ab", 0, 2)
        S.act(nA[:], ab[0:8, 0:1], AF.Exp)
        S.ts("dve", nA[:], nA[:], -1.0, ALU.mult)
        for ti, (t0, n) in enumerate(TT):
            pb_, pa_ = self.bank(), self.bank()
            for k in range(KC):
                S.mm(pb_[0:8, 0:n], wba3[:, k, 0:8], self.xn[k][:, t0:t0 + n], start=(k == 0), stop=(k == KC - 1))
            for k in range(KC):
                S.mm(pa_[0:8, 0:n], wba3[:, k, 8:16], self.xn[k][:, t0:t0 + n], start=(k == 0), stop=(k == KC - 1))
            S.act(Ball[:, t0:t0 + n], pb_[0:8, 0:n], AF.Sigmoid)
            S.act(graw[:, t0:t0 + n], pa_[0:8, 0:n], AF.Exp, bias=ab[0:8, 1:2])
            S.act(graw[:, t0:t0 + n], graw[:, t0:t0 + n], AF.Ln, bias=self.one_col[0:8, 0:1])
        S.ts("dve", graw, graw, nA[:, 0:1], ALU.mult)
        S.scan(Gc[:], mrow, graw, 0.0)
        for ci, (t0, C, L) in enumerate(chunks):
            pt_ = self.bank()
            S.tr(pt_[0:C, 0:8], Gc[:, t0:t0 + C], self.ident_f[0:8, 0:8])
            S.tr(pt_[0:C, 8:16], Ball[:, t0:t0 + C], self.ident_f[0:8, 0:8])
            S.copy("dve", GT[0:C, ci, :], pt_[0:C, 0:8])
            S.copy("dve", BT[0:C, ci, :], pt_[0:C, 8:16])
        sv_ = S.sb_ptr
        S.sb_ptr = S.bases[xx.name]
        F1 = S.sb("gF1", [64, 8, 64], F32)
        F2 = S.sb("gF2", [64, 8, 64], F32)
        gbf = lambda nm: S.sb(nm, [64, 8, 64], BF16)
        A1, A2, A3, B1, B2, B3 = (gbf(nm) for nm in ("gA1", "gA2", "gA3", "gB1", "gB2", "gB3"))
        usb = S.sb("usb", [64, 8, 128], F32)
        attnT = S.sb("attnT", [64, 8, 64], BF16)
        wT = S.sb("wT", [128, 8, 64], BF16)
        assert S.sb_ptr <= S.bases[ctmp.name] + T * 4, "group overlay overflow"
        S.sb_ptr = sv_
        Vb_ = S.sb("Vbt", [64, 8, 128], BF16)
        Kb_ = S.sb("Kbt", [64, 8, 128], BF16)
        kdec = S.sb("kdec", [64, 8, 128], BF16)
        delta = S.sb("delta", [64, 128], BF16)
        cols_ = S.sb("ccols", [64, 8, 4], F32)
        egl = S.sb("egl", [128, 8], F32)
        mmax, mmin = mm_[:, 0:64], mm_[:, 64:128]

        def conv_silu(psrc_list, kc, dst_f32):
            for (ti, t0, n, acc) in psrc_list:
                if t0 + n <= TP:
                    S.copy("act", xx[:, 3 + t0:3 + t0 + n], acc[:, 0:n])
                else:
                    npz = TP - t0
                    S.copy("act", xx[:, 3 + t0:3 + TP], acc[:, 0:npz])
                    S.copy("act", xs[:, :, 3], acc[:, npz:npz + NS])

        def conv_finish(kc, dst):
            S.memset("pool", xx[:, 0:3], 0.0)
            S.copy("pool", xs[:, :, 0:3], st_c[:, kc, :, :])
            S.copy("pool", p_cv[:, kc, :], xx[:, TP:TP + 3])
            S.copy("pool", s_cv[:, kc, :, :], xs[:, :, 1:4])
            cw = lambda q: self.col(f"dn_cw{q}", kc)
            S.ts("dve", ctmp[:, 0:TP], xx[:, 0:TP], cw(0), ALU.mult)
            for q in range(1, 4):
                S.stt(ctmp[:, 0:TP], xx[:, q:q + TP], cw(q), ctmp[:, 0:TP], ALU.mult, ALU.add)
            S.ts("dve", ctmp[:, TP:T], xs[:, :, 0], cw(0), ALU.mult)
            for q in range(1, 4):
                S.stt(ctmp[:, TP:T], xs[:, :, q], cw(q), ctmp[:, TP:T], ALU.mult, ALU.add)
            S.act(dst, ctmp[:], AF.Silu)

        def l2n(src, dst_bf, scale):
            for ti, (t0, n) in enumerate(TT):
                sq = self.sq[ti % 2]
                S.act(sq[:, 0:n], src[:, t0:t0 + n], AF.Square)
                acc = self.bank()
                S.mm(acc[:, 0:n], self.ones_b[:], sq[:, 0:n], start=True, stop=True)
                r = self.rstd[ti % 2]
                S.act(r[:, 0:n], acc[:, 0:n], AF.Sqrt, bias=self.eps_col[:, 0:1], scale=1.0 / (scale * scale))
                S.recip(r[:, 0:n], r[:, 0:n])
                S.tt("dve", dst_bf[:, t0:t0 + n], src[:, t0:t0 + n], r[:, 0:n], ALU.mult)

        def proj2(wb, c, kc):
            w3 = wb[:, 0:2048].rearrange("p (k f) -> p k f", k=8)
            lst = []
            for ti, (t0, n) in enumerate(TT):
                acc = self.bank()
                for k in range(KC):
                    S.mm(acc[:, 0:n], w3[:, k, c * 128:(c + 1) * 128], self.xn[k][:, t0:t0 + n], start=(k == 0), stop=(k == KC - 1))
                conv_silu([(ti, t0, n, acc)], kc, None)

        def head_qk(wb, h):
            proj2(wb, 0, h)
            conv_finish(h, ctmp[:])
            l2n(ctmp, qn, 128.0 ** -0.5)
            S.dma(sel[:], self.dram["dn_sel"][:, h * 128:(h + 1) * 128], sem="dnsel")
            for ti, (t0, n) in enumerate(TT):
                acc = self.bank()
                S.mm(acc[:, 0:n], sel[:, :], Gc[:, t0:t0 + n], start=True, stop=True)
                S.copy("dve", GB[:, t0:t0 + n], acc[:, 0:n])
                S.act(eg[:, 0:n], acc[:, 0:n], AF.Exp)
                S.tt("dve", qdec[:, t0:t0 + n], qn[:, t0:t0 + n], eg[:, 0:n], ALU.mult)
            proj2(wb, 1, 8 + h)
            conv_finish(8 + h, ctmp[:])
            l2n(ctmp, kn, 1.0)

        def head_vz(wb, h):
            proj2(wb, 0, 16 + h)
            conv_finish(16 + h, ctmp[:])
            S.copy("pool", vb[:], ctmp[:])
            w3 = wb[:, 0:2048].rearrange("p (k f) -> p k f", k=8)

            def ev(ti, t0, n, acc):
                S.act(sz[:, t0:t0 + n], acc[:, 0:n], AF.Silu)
            self.proj_chunk(lambda k: w3[:, k, 128:256], self.xn, ev)
            S.memset("pool", Sf[:], 0.0)
            S.memset("pool", Sb_[:], 0.0)
            groups = [[0]] + [list(range(1 + 8 * q, 9 + 8 * q)) for q in range(4)] + [[33, 34, 35, 36]]
            for grp in groups:
                ng = len(grp)
                C, L = chunks[grp[0]][1], chunks[grp[0]][2]
                R = slice(0, C)
                pk, pq = self.rbank(0), self.rbank(1)
                ci0 = grp[0]
                tg0 = chunks[ci0][0]
                GR = (R, slice(0, ng), slice(0, C))
                bc = lambda ap: ap.to_broadcast([C, ng, C])
                GBg = GB[0:C, tg0:tg0 + ng * C].rearrange("p (g c) -> p g c", g=ng)
                gcolg = GT[0:C, ci0:ci0 + ng, h:h + 1]
                bcolg = BT[0:C, ci0:ci0 + ng, h:h + 1]
                glastg = GB[0:C, tg0 + C - 1:tg0 + ng * C:C].unsqueeze(2)
                S.act(cols_[R, 0:ng, 0:1], gcolg, AF.Exp)
                S.tt("dve", cols_[R, 0:ng, 1:2], cols_[R, 0:ng, 0:1], bcolg, ALU.mult)
                S.tt("dve", cols_[R, 0:ng, 2:3], glastg, gcolg, ALU.subtract)
                S.act(cols_[R, 0:ng, 2:3], cols_[R, 0:ng, 2:3], AF.Exp)
                S.act(egl[:, 0:ng], GB[:, tg0 + C - 1:tg0 + ng * C:C], AF.Exp)
                for g, ci in enumerate(grp):
                    t0 = chunks[ci][0]
                    cs_ = slice(t0, t0 + C)
                    S.mm(pk[R, g * 64:g * 64 + C], kn[:, cs_], kn[:, cs_], start=True, stop=True)
                    S.mm(pq[R, g * 64:g * 64 + C], kn[:, cs_], qn[:, cs_], start=True, stop=True)
                S.tt("dve", F2[GR], GBg, bc(gcolg), ALU.subtract)
                S.tt("dve", F1[GR], F2[GR], bc(mmax[0:C, 0:C].unsqueeze(1)), ALU.max)
                S.tt("dve", F2[GR], F2[GR], bc(mmin[0:C, 0:C].unsqueeze(1)), ALU.min)
                pk3 = pk[:, 0:512].rearrange("p (g c) -> p g c", g=8)
                pq3 = pq[:, 0:512].rearrange("p (g c) -> p g c", g=8)
                S.act(B1[GR], F1[GR], AF.Exp, scale=-1.0)
                S.act(B2[GR], F2[GR], AF.Exp)
                S.tt("dve", F1[GR], pk3[GR], bc(bcolg), ALU.mult)
                S.stt(F1[GR], F1[GR], -1.0, B1[GR], ALU.mult, ALU.mult)
                S.copy("act", A1[GR], F1[GR])
                S.tt("dve", attnT[GR], pq3[GR], B2[GR], ALU.mult)
                if C > 1:
                    ptr_ = self.rbank(2)
                    ptr3 = ptr_[:, 0:512].rearrange("p (g c) -> p g c", g=8)
                    for g in range(ng):
                        S.tr(ptr_[R, g * 64:g * 64 + C], F1[R, g, 0:C], self.ident_f[0:C, 0:C])
                    S.copy("dve", A2[GR], ptr3[GR])
                else:
                    S.copy("dve", A2[GR], A1[GR])
                S.tt("pool", A3[GR], A2[GR], bc(self.ident_b[0:C, 0:C].unsqueeze(1)), ALU.add)
                P_, PT_, TT_ = A1, A2, A3
                P2, PT2, TT2 = B1, B2, B3
                for lv in range(1, L):
                    pl, plT, pl2 = self.rbank(2), self.rbank(3), self.rbank(4)
                    pl3 = pl[:, 0:512].rearrange("p (g c) -> p g c", g=8)
                    plT3 = plT[:, 0:512].rearrange("p (g c) -> p g c", g=8)
                    pl23 = pl2[:, 0:512].rearrange("p (g c) -> p g c", g=8)
                    for g in range(ng):
                        S.mm(pl[R, g * 64:g * 64 + C], PT_[R, g, 0:C], P_[R, g, 0:C], start=True, stop=True)
                    for g in range(ng):
                        S.mm(plT[R, g * 64:g * 64 + C], P_[R, g, 0:C], PT_[R, g, 0:C], start=True, stop=True)
                    S.copy("dve", P2[GR], pl3[GR])
                    S.copy("act", PT2[GR], plT3[GR])
                    for g in range(ng):
                        S.mm(pl2[R, g * 64:g * 64 + C], P2[R, g, 0:C], TT_[R, g, 0:C], start=True, stop=True)
                    S.tt("dve", TT2[GR], pl23[GR], TT_[GR], ALU.add)
                    P_, P2 = P2, P_
                    PT_, PT2 = PT2, PT_
                    TT_, TT2 = TT2, TT_
                TTbf = TT_
                for half in range((ng + 3) // 4):
                    pkk, pvv = self.rbank(0 + half), self.rbank(2 + half)
                    for g in range(half * 4, min(ng, half * 4 + 4)):
                        ci = grp[g]
                        t0 = chunks[ci][0]
                        cs_ = slice(t0, t0 + C)
                        o0 = (g % 4) * 128
                        S.mm(pkk[R, o0:o0 + 128], kn[:, cs_], self.ident_b[:], start=True, stop=True)
                        S.mm(pvv[R, o0:o0 + 128], vb[:, cs_], self.ident_b[:], start=True, stop=True)
                    h4 = half * 4
                    n4 = min(ng, h4 + 4) - h4
                    pkk3 = pkk[:, 0:512].rearrange("p (g c) -> p g c", g=4)
                    pvv3 = pvv[:, 0:512].rearrange("p (g c) -> p g c", g=4)
                    b4 = lambda ap: ap.to_broadcast([C, n4, 128])
                    S.tt("dve", Kb_[R, h4:h4 + n4, :], pkk3[R, 0:n4, :], b4(cols_[R, h4:h4 + n4, 1:2]), ALU.mult)
                    S.tt("dve", kdec[R, h4:h4 + n4, :], pkk3[R, 0:n4, :], b4(cols_[R, h4:h4 + n4, 2:3]), ALU.mult)
                    S.tt("dve", Vb_[R, h4:h4 + n4, :], pvv3[R, 0:n4, :], b4(BT[0:C, ci0 + h4:ci0 + h4 + n4, h:h + 1]), ALU.mult)
                pw = self.rbank(6)
                for half in range((ng + 3) // 4):
                    pu = self.rbank(4 + half)
                    for g in range(half * 4, min(ng, half * 4 + 4)):
                        o0 = (g % 4) * 128
                        S.mm(pu[R, o0:o0 + 128], TTbf[R, g, 0:C], Vb_[R, g, :], start=True, stop=True)
                    n4 = min(ng, half * 4 + 4) - half * 4
                    S.copy("act", usb[R, half * 4:half * 4 + n4, :],
                           pu[:, 0:512].rearrange("p (g c) -> p g c", g=4)[R, 0:n4, :])
                for g in range(ng):
                    S.mm(pw[:, g * 64:g * 64 + C], Kb_[R, g, :], TTbf[R, g, 0:C], start=True, stop=True)
                S.copy("dve", wT[:, 0:ng, 0:C], pw[:, 0:512].rearrange("p (g c) -> p g c", g=8)[:, 0:ng, 0:C])
                for g, ci in enumerate(grp):
                    t0 = chunks[ci][0]
                    cs_ = slice(t0, t0 + C)
                    sample = t0 >= TP
                    if sample:
                        b = t0 - TP
                        S.dma(Sf[:], self.dram["s_dn_S"][b * 8 + h], sem="dnS")
                        S.copy("dve", Sb_[:], Sf[:])
                    HV = [slice(0, 64), slice(64, 128)]
                    pdh = [self.rbank(7), self.rbank(6)]
                    psh = [self.rbank(3), self.rbank(2)]
                    poh = [self.rbank(5), self.rbank(4)]
                    for hf in range(2):
                        S.mm(pdh[hf][R, 0:64], wT[:, g, 0:C], Sb_[:, HV[hf]], start=True, stop=True)
                    for hf in range(2):
                        S.tt("dve", delta[R, HV[hf]], usb[R, g, HV[hf]], pdh[hf][R, 0:64], ALU.subtract)
                    for hf in range(2):
                        S.mm(psh[hf][:, 0:64], kdec[R, g, :], delta[R, HV[hf]], start=True, stop=True)
                        S.mm(poh[hf][HV[hf], 0:C], Sb_[:, HV[hf]], qdec[:, cs_], start=True, stop=False)
                        S.mm(poh[hf][HV[hf], 0:C], delta[R, HV[hf]], attnT[R, g, 0:C], start=False, stop=True)
                    for hf in range(2):
                        S.stt(Sb_[:, HV[hf]], Sf[:, HV[hf]], egl[:, g:g + 1], psh[hf][:, 0:64], ALU.mult, ALU.add)
                    for hf in range(2):
                        S.stt(Sf[:, HV[hf]], Sf[:, HV[hf]], egl[:, g:g + 1], psh[hf][:, 0:64], ALU.mult, ALU.add)
                    for hf in range(2):
                        S.copy("act", oT[HV[hf], cs_], poh[hf][HV[hf], 0:C])
                    if ci == 32:
                        S.dma(oS[h], Sf[:], sem="oS")
                    if sample:
                        S.dma(oS[8 + (t0 - TP) * 8 + h], Sf[:], sem="oS")

        def head_o(wb, h):
            for ti, (t0, n) in enumerate(TT):
                sq = self.sq[ti % 2]
                S.act(sq[:, 0:n], oT[:, t0:t0 + n], AF.Square)
                acc = self.bank()
                S.mm(acc[:, 0:n], self.ones_b[:], sq[:, 0:n], start=True, stop=True)
                r = self.rstd[0]
                S.act(r[:, 0:n], acc[:, 0:n], AF.Sqrt, bias=self.eps_col[:, 0:1], scale=1.0 / 128)
                S.recip(r[:, 0:n], r[:, 0:n])
                S.stt(eg[:, 0:n], oT[:, t0:t0 + n], self.col("dn_norm", 0), r[:, 0:n], ALU.mult, ALU.mult)
                S.tt("dve", oT[:, t0:t0 + n], eg[:, 0:n], sz[:, t0:t0 + n], ALU.mult)
            for fo in range(KC):
                def ev2(ti, t0, n, acc, fo=fo):
                    S.tt("dve", self.x[fo][:, t0:t0 + n], acc[:, 0:n], self.x[fo][:, t0:t0 + n], ALU.add)
                self.proj_chunk(lambda k, fo=fo: wb[:, fo * 128:(fo + 1) * 128], [oT], ev2)
        units = []
        for h in range(8):
            units.append(("dn_qk", h, 2048, (lambda wb, h=h: head_qk(wb, h))))
            units.append(("dn_vz", h, 2048, (lambda wb, h=h: head_vz(wb, h))))
            units.append(("dn_wo", h, 1024, (lambda wb, h=h: head_o(wb, h))))
        self.run_units(units)
        sm2 = self.out("small2", [128, 360])
        S.dma(sm2[:, :], small2[:], sem="o1")
        S.release(m0)

    def consts(self):
        S = self.S
        self.eps_col = S.sb("eps_col", [128, 1], F32)
        self.one_col = S.sb("one_col", [128, 1], F32)
        S.memset("pool", self.eps_col[:], EPS)
        S.memset("pool", self.one_col[:], 1.0)

    def final(self):
        S = self.S
        yT = self.out("yT", [128, KC, T])
        for ti, (t0, n) in enumerate(TT):
            r = self.rmsnorm_stats(ti)
            for k in range(KC):
                S.stt(self.x[k][:, t0:t0 + n], self.x[k][:, t0:t0 + n], self.col("nfinal", k), r[:, 0:n],
                      ALU.mult, ALU.mult)
        for k in range(KC):
            S.dma(yT[:, k, :], self.x[k][:], sem=f"o{k % 2}")
        sm = self.out("small", [128, 320])
        S.dma(sm[:, :], self.small[:], sem="o0")

    def dump_x(self):
        S = self.S
        dbg = self.out("dbg", [128, KC, T])
        for k in range(KC):
            S.dma(dbg[:, k, :], self.x[k][:], sem=f"o{k % 2}")


def build_program(shapes, colidx, stop_after=None, only=None):
    nc = bass.Bass("TRN2", target_bir_lowering=False)
    S = Sched(nc)
    B = Builder(nc, S, shapes, colidx, stop_after)
    B.setup()
    B.consts()
    S.memset("pool", B.small[:], 0.0)
    layers = [("lru", 0), ("dn", 0), ("mla", 0), ("lru", 1)]
    done = False
    for li, (kind, j) in enumerate(layers):
        if only is not None and li != only:
            continue
        if kind == "lru":
            B.lru(li, j)
        elif kind == "dn":
            B.dn(li, j)
        else:
            B.mla(li, j)
        if stop_after == (li, "mix"):
            done = True
            break
        B.ffn(li)
        if stop_after == (li, "ffn"):
            done = True
            break
    if done:
        B.dump_x()
        sm = B.out("small", [128, 320])
        S.dma(sm[:, :], B.small[:], sem="o0")
    else:
        B.final()
    S.emit()
    return nc, B


_STOP_AFTER = None
_DEBUG = {}


def _run(inputs, stop_after=None, cores=NCORES, only=None, x_override=None):
    inp = {k: np.asarray(v) for k, v in inputs.items()}
    sh = _prep_shared(inp)
    colidx = sh.pop("_colidx")
    per_core = [_prep_core(inp, c) for c in range(cores)]
    shapes = {k: v.shape for k, v in sh.items()}
    shapes.update({k: v.shape for k, v in per_core[0].items()})
    if x_override is not None:
        for c in range(cores):
            per_core[c]["xT"] = np.ascontiguousarray(x_override[c].reshape(T, KC, 128).transpose(2, 1, 0))
    nc, B = build_program(shapes, colidx, stop_after, only)
    in_maps = []
    for c in range(cores):
        m = dict(sh)
        m.update(per_core[c])
        in_maps.append(m)
    res = run_bass_kernel_spmd(nc, in_maps, core_ids=list(range(cores)))
    return res.results, B


def kernel(**inputs):
    results, B = _run(inputs, None)
    f = np.float32
    y_prompt = np.zeros((8, SEQ, D), f)
    y_sample = np.zeros((32, 1, D), f)
    p_lru_h = np.zeros((2, 8, D), f)
    p_lru_conv = np.zeros((2, 8, 3, D), f)
    p_dn_S = np.zeros((1, 8, 8, 128, 128), f)
    p_dn_conv = np.zeros((1, 8, 3, 3072), f)
    p_ckv = np.zeros((1, 8, TP, 256), f)
    p_kpe = np.zeros((1, 8, TP, 64), f)
    s_lru_h = np.zeros((2, 32, D), f)
    s_lru_conv = np.zeros((2, 32, 3, D), f)
    s_dn_S = np.zeros((1, 32, 8, 128, 128), f)
    s_dn_conv = np.zeros((1, 32, 3, 3072), f)
    s_ckv = np.zeros((1, 32, 1, 256), f)
    s_kpe = np.zeros((1, 32, 1, 64), f)
    for c in range(NCORES):
        r = results[c]
        y = r["yT"].transpose(2, 1, 0).reshape(T, D)
        y_prompt[c] = y[NMETA:TP]
        y_sample[NS * c:NS * (c + 1), 0] = y[TP:]
        sm = r["small"]
        for j in range(2):
            i, n = B.small_idx[f"p_lru_h{j}"]
            p_lru_h[j, c] = sm[:, i:i + n].T.reshape(D)
            i, n = B.small_idx[f"p_lru_conv{j}"]
            p_lru_conv[j, c] = sm[:, i:i + n].reshape(128, 8, 3).transpose(2, 1, 0).reshape(3, D)
            i, n = B.small_idx[f"s_lru_h{j}"]
            s_lru_h[j, NS * c:NS * (c + 1)] = sm[:, i:i + n].reshape(128, 8, NS).transpose(2, 1, 0).reshape(NS, D)
            i, n = B.small_idx[f"s_lru_conv{j}"]
            s_lru_conv[j, NS * c:NS * (c + 1)] = sm[:, i:i + n].reshape(128, 8, NS, 3).transpose(2, 3, 1, 0).reshape(NS, 3, D)
        s2 = r["small2"]
        p_dn_conv[0, c] = s2[:, 0:72].reshape(128, 24, 3).transpose(2, 1, 0).reshape(3, 3072)
        s_dn_conv[0, NS * c:NS * (c + 1)] = s2[:, 72:360].reshape(128, 24, NS, 3).transpose(2, 3, 1, 0).reshape(NS, 3, 3072)
        oS = r["o_dn_S"]
        p_dn_S[0, c] = oS[0:8]
        s_dn_S[0, NS * c:NS * (c + 1)] = oS[8:].reshape(NS, 8, 128, 128)
        kv = r["p_kv"]
        p_ckv[0, c] = kv[:TP, :256]
        p_kpe[0, c] = kv[:TP, 256:]
        s_ckv[0, NS * c:NS * (c + 1), 0] = kv[TP:, :256]
        s_kpe[0, NS * c:NS * (c + 1), 0] = kv[TP:, 256:]
    return (y_prompt, y_sample, p_lru_h, p_lru_conv, p_dn_S, p_dn_conv, p_ckv, p_kpe,
            s_lru_h, s_lru_conv, s_dn_S, s_dn_conv, s_ckv, s_kpe)
```

```python
import bisect
import os
from contextlib import ExitStack

import numpy as np
import concourse.bass as bass
import concourse.mybir as mybir
from concourse.bass_utils import run_bass_kernel_spmd

F32 = mybir.dt.float32
BF16 = mybir.dt.bfloat16
I32 = mybir.dt.int32
AF = mybir.ActivationFunctionType
ALU = mybir.AluOpType
AX = mybir.AxisListType

NCORES = 8
D = 1024
KC = 8
SEQ = 2048
NMETA = 16
TP = SEQ + NMETA
NS = 4
T = TP + NS
DFF = 2816
FC = DFF // 128
NPAGES = 128
PAGE = 128
NPOOL = 5120
EPS = 1e-6
MLA_SCALE = (128 + 64) ** -0.5
TT = [(0, 512), (512, 512), (1024, 512), (1536, 512), (2048, 20)]
WSLOT = 2048


class _Op:
    __slots__ = ("eng", "fn", "deps", "dma_sem", "dma_val", "idx", "milestone", "mval", "waits", "dma_deps")


class _IMap:
    def __init__(self, size):
        self.b = [0, size]
        self.r = [[None, {}]]

    def _split(self, x):
        i = bisect.bisect_left(self.b, x)
        if self.b[i] == x:
            return i
        w, rd = self.r[i - 1]
        self.b.insert(i, x)
        self.r.insert(i, [w, dict(rd)])
        return i

    def read(self, lo, hi, op, key, deps):
        i = self._split(lo)
        j = self._split(hi)
        for k in range(i, j):
            rec = self.r[k]
            if rec[0] is not None:
                deps.add(rec[0])
            rec[1][key] = op

    def write(self, lo, hi, op, deps):
        i = self._split(lo)
        j = self._split(hi)
        for k in range(i, j):
            rec = self.r[k]
            if rec[0] is not None:
                deps.add(rec[0])
            deps.update(rec[1].values())
        self.b[i:j + 1] = [lo, hi]
        self.r[i:j] = [[op, {}]]


class Sched:
    ENGS = ("pe", "act", "dve", "pool", "sp")

    def __init__(self, nc):
        self.nc = nc
        self.ops = {e: [] for e in self.ENGS}
        self.maps = {"SB": _IMap(1 << 20), "PSUM": _IMap(1 << 16)}
        self.dma_cnt = {}
        self.total_sems = set()
        self.bases = {}
        self.sb_ptr = (nc.sbuf_base + 63) // 64 * 64
        self.sb_top = nc.sbuf_top
        self.nalloc = 0

    def sb(self, name, shape, dtype):
        esz = 2 if dtype == BF16 else 4
        n = 1
        for s in shape[1:]:
            n *= s
        nbytes = (n * esz + 63) // 64 * 64
        off = self.sb_ptr
        self.sb_ptr += nbytes
        assert self.sb_ptr <= self.sb_top, f"SBUF overflow at {name}: {self.sb_ptr} > {self.sb_top}"
        self.nalloc += 1
        t = self.nc.alloc_sbuf_tensor_at(f"{name}_{self.nalloc}", list(shape), dtype, offset=off)
        self.bases[t.name] = off
        return t

    def mark(self):
        return self.sb_ptr

    def release(self, m):
        self.sb_ptr = m

    def _range(self, ap):
        sp = str(ap.space)
        if "SB" in sp:
            m = self.maps["SB"]
        elif "PSUM" in sp:
            m = self.maps["PSUM"]
        else:
            return None
        esz = 2 if ap.dtype == BF16 else 4
        pat = ap.ap
        pstride = pat[0][0]
        off = ap.offset % pstride if pstride > 0 else ap.offset
        ext = 1
        for st, cnt in pat[1:]:
            ext += (cnt - 1) * abs(st)
        base = self.bases.get(ap.tensor.name, 0)
        lo = base + off * esz
        hi = lo + ext * esz
        if m is self.maps["PSUM"]:
            lo = lo // 2048 * 2048
            hi = (hi + 2047) // 2048 * 2048
        return m, lo, hi

    def rec(self, eng, fn, reads=(), writes=(), dma_sem=None):
        op = _Op()
        op.eng = eng
        op.fn = fn
        op.dma_sem = dma_sem
        op.milestone = False
        op.mval = 0
        key = eng if dma_sem is None else ("dma", dma_sem)
        deps = set()
        raw = set()
        for ap in reads:
            if ap is None or isinstance(ap, (int, float)):
                continue
            r = self._range(ap)
            if r:
                if r[0] is self.maps["PSUM"]:
                    r[0].write(r[1], r[2], op, raw)
                else:
                    r[0].read(r[1], r[2], op, key, raw)
        for ap in writes:
            r = self._range(ap)
            if r:
                r[0].write(r[1], r[2], op, deps)
        if dma_sem is None:
            deps = {d for d in deps if not (d.eng == eng and d.dma_sem is None)}
        deps |= raw
        deps.discard(op)
        op.deps = []
        op.dma_deps = {}
        for d in deps:
            if d.dma_sem is not None:
                s = d.dma_sem
                v = self.dma_cnt[s]
                if op.dma_deps.get(s, 0) < v:
                    op.dma_deps[s] = v
            else:
                op.deps.append(d)
        if dma_sem is not None:
            self.dma_cnt[dma_sem] = self.dma_cnt.get(dma_sem, 0) + 16
            op.dma_val = self.dma_cnt[dma_sem]
        op.idx = len(self.ops[eng])
        self.ops[eng].append(op)
        return op

    def mm(self, out, lhsT, rhs, start=True, stop=True):
        return self.rec("pe", lambda e: e.matmul(out, lhsT=lhsT, rhs=rhs, start=start, stop=stop),
                        [lhsT, rhs], [out])

    def tr(self, out, in_, ident):
        return self.rec("pe", lambda e: e.transpose(out=out, in_=in_, identity=ident), [in_, ident], [out])

    def act(self, out, in_, func, bias=None, scale=1.0, accum_out=None):
        kw = {}
        if bias is not None:
            kw["bias"] = bias
        if accum_out is not None:
            kw["accum_out"] = accum_out
        w = [out] + ([accum_out] if accum_out is not None else [])
        return self.rec("act", lambda e: e.activation(out=out, in_=in_, func=func, scale=scale, **kw),
                        [in_, bias, scale], w)

    def tt(self, eng, out, in0, in1, op):
        return self.rec(eng, lambda e: e.tensor_tensor(out=out, in0=in0, in1=in1, op=op), [in0, in1], [out])

    def ts(self, eng, out, in0, s1, op0, s2=None, op1=None, accum_out=None):
        kw = {}
        if op1 is not None:
            kw["op1"] = op1
        if accum_out is not None:
            kw["accum_out"] = accum_out
        w = [out] + ([accum_out] if accum_out is not None else [])
        return self.rec(eng, lambda e: e.tensor_scalar(out=out, in0=in0, scalar1=s1, scalar2=s2, op0=op0, **kw),
                        [in0, s1, s2], w)

    def stt(self, out, in0, scalar, in1, op0, op1, eng="dve"):
        return self.rec(eng, lambda e: e.scalar_tensor_tensor(out=out, in0=in0, scalar=scalar, in1=in1,
                                                              op0=op0, op1=op1), [in0, scalar, in1], [out])

    def copy(self, eng, out, in_):
        if eng == "act":
            return self.rec("act", lambda e: e.copy(out=out, in_=in_), [in_], [out])
        return self.rec(eng, lambda e: e.tensor_copy(out=out, in_=in_), [in_], [out])

    def memset(self, eng, ap, val):
        return self.rec(eng, lambda e: e.memset(ap, val), [], [ap])

    def recip(self, out, in_):
        return self.rec("dve", lambda e: e.reciprocal(out=out, in_=in_), [in_], [out])

    def scan(self, out, d0, d1, initial, op0=ALU.mult, op1=ALU.add):
        return self.rec("dve", lambda e: e.tensor_tensor_scan(out=out, data0=d0, data1=d1, initial=initial,
                                                              op0=op0, op1=op1), [d0, d1, initial], [out])

    def reduce(self, out, in_, op, axis=AX.X):
        return self.rec("dve", lambda e: e.tensor_reduce(out=out, in_=in_, axis=axis, op=op), [in_], [out])

    def dma(self, out, in_, sem, eng="sp"):
        return self.rec(eng, lambda e: e.dma_start(out=out, in_=in_), [in_], [out], dma_sem=sem)

    def gather(self, out, in_, idx_ap, sem):
        return self.rec("pool", lambda e: e.indirect_dma_start(
            out=out, out_offset=None, in_=in_, in_offset=bass.IndirectOffsetOnAxis(ap=idx_ap, axis=0)),
            [idx_ap], [out], dma_sem=sem)

    def emit(self):
        nc = self.nc
        ops = self.ops
        for e in self.ENGS:
            seen = {f: -1 for f in self.ENGS}
            seen_dma = {}
            for op in ops[e]:
                keep = {}
                for d in op.deps:
                    f = d.eng
                    if f == e and e in ("pe", "sp"):
                        continue
                    if d.idx > seen[f] and d.idx > keep.get(f, (-1, None))[0]:
                        keep[f] = (d.idx, d)
                op.waits = []
                for f, (i, d) in keep.items():
                    seen[f] = i
                    d.milestone = True
                    op.waits.append(d)
                dw = []
                for s, v in op.dma_deps.items():
                    if s in self.total_sems:
                        v = -1
                    if seen_dma.get(s, 0) < v or v == -1:
                        if v == -1 and seen_dma.get(s, 0) == -1:
                            continue
                        seen_dma[s] = v
                        dw.append((s, v))
                op.dma_deps = dw
        for e in self.ENGS:
            c = 0
            for op in ops[e]:
                if op.milestone:
                    c += 1
                    op.mval = c
        self.nmil = {e: sum(1 for o in ops[e] if o.milestone) for e in self.ENGS}
        with ExitStack() as st:
            esem = {e: st.enter_context(nc.semaphore(f"e_{e}")) for e in self.ENGS}
            dsem = {s: st.enter_context(nc.semaphore(f"d_{s}")) for s in self.dma_cnt}
            block = st.enter_context(nc.Block())

            def run(e, eng):
                for op in ops[e]:
                    for d in op.waits:
                        eng.wait_ge(esem[d.eng], d.mval)
                    for s, v in op.dma_deps:
                        eng.wait_ge(dsem[s], self.dma_cnt[s] if v == -1 else v)
                    ins = op.fn(eng)
                    if op.dma_sem is not None:
                        ins.then_inc(dsem[op.dma_sem], 16)
                    elif op.milestone:
                        ins.then_inc(esem[e], 1)
                if e == "sp":
                    for s, v in self.dma_cnt.items():
                        eng.wait_ge(dsem[s], v)

            @block.tensor
            def _(eng):
                run("pe", eng)

            @block.scalar
            def _(eng):
                run("act", eng)

            @block.vector
            def _(eng):
                run("dve", eng)

            @block.gpsimd
            def _(eng):
                run("pool", eng)

            @block.sync
            def _(eng):
                run("sp", eng)


def _units_proj(W, gf):
    K, N = W.shape
    kc = K // 128
    return np.ascontiguousarray(W.reshape(kc, 128, N // gf, gf).transpose(2, 1, 0, 3).reshape(N // gf, 128, kc * gf))


def _cols(v):
    v = np.asarray(v, np.float32).reshape(-1, 128)
    return np.ascontiguousarray(v.T)


class _ColPack:
    def __init__(self):
        self.parts = []
        self.n = 0
        self.idx = {}

    def add(self, name, arr):
        arr = np.asarray(arr, np.float32)
        assert arr.shape[0] == 128
        self.idx[name] = self.n
        self.parts.append(arr)
        self.n += arr.shape[1]

    def build(self):
        return np.ascontiguousarray(np.concatenate(self.parts, axis=1))


def _prep_shared(inp):
    sh = {}
    cp = _ColPack()
    for i in range(4):
        cp.add(f"nmix{i}", _cols(inp["norm_mix"][i]))
        cp.add(f"nffn{i}", _cols(inp["norm_ffn"][i]))
    cp.add("nfinal", _cols(inp["norm_final"]))
    for j in range(2):
        for k in range(4):
            cp.add(f"lru_cw{j}_{k}", _cols(inp["lru_conv_w"][j, k]))
        cp.add(f"lru_cb{j}", _cols(inp["lru_conv_b"][j]))
        cp.add(f"lru_ba{j}", _cols(inp["lru_b_a"][j]))
        cp.add(f"lru_bi{j}", _cols(inp["lru_b_i"][j]))
        cp.add(f"lru_lam{j}", _cols(inp["lru_lambda"][j]))
        w_in = inp["lru_w_in"][j]
        u = []
        for n in range(4):
            u.append(_units_proj(w_in[:, n * 256:(n + 1) * 256], 256)[0])
            u.append(_units_proj(w_in[:, 1024 + n * 256:1024 + (n + 1) * 256], 256)[0])
        sh[f"lru_win{j}"] = np.stack(u)
        wa, wi = inp["lru_w_a"][j], inp["lru_w_i"][j]
        g = []
        for n in range(4):
            a = _units_proj(wa[n], 256)[0]
            b = _units_proj(wi[n], 256)[0]
            g.append(np.concatenate([a, b], axis=1))
        sh[f"lru_wg{j}"] = np.stack(g)
        wo = inp["lru_w_out"][j]
        sh[f"lru_wout{j}"] = np.stack([_units_proj(wo[n * 256:(n + 1) * 256], 1024)[0] for n in range(4)])
    for i in range(4):
        wgu = inp["ffn_w_gu"][i]
        g = _units_proj(wgu[:, :DFF], 128)
        u = _units_proj(wgu[:, DFF:], 128)
        sh[f"ffn_gu{i}"] = np.ascontiguousarray(
            np.stack([g.reshape(FC, 128, 8, 128), u.reshape(FC, 128, 8, 128)], axis=3).reshape(FC, 128, 2048))
        wd = inp["ffn_w_down"][i]
        hv = []
        for half in range(2):
            hv.append(_units_proj(wd[half * 1408:(half + 1) * 1408], 128))
        sh[f"ffn_dn{i}"] = np.ascontiguousarray(np.stack(hv).reshape(16, 128, 1408))
    _prep_mla(inp, sh, cp)
    _prep_dn(inp, sh, cp)
    sh["cols"] = cp.build()
    sh["_colidx"] = cp.idx
    sh["ones_bf"] = np.ones((128, 128), np.float32)
    sh["ident"] = np.eye(128, dtype=np.float32)
    return sh


def _prep_mla(inp, sh, cp):
    cp.add("mla_qn", _cols(inp["mla_q_norm"][0]))
    cp.add("mla_kvn", _cols(inp["mla_kv_norm"][0]))
    wdkv = inp["mla_w_dkv"][0]
    sh["mla_dkv_c"] = _units_proj(wdkv[:, :256], 256)
    perm = np.concatenate([np.arange(32, 64), np.arange(0, 32)])
    kr = np.concatenate([wdkv[:, 256:320], wdkv[:, 256 + perm]], axis=1)
    sh["mla_dkv_r"] = _units_proj(kr, 128)
    sh["mla_dq"] = _units_proj(inp["mla_w_dq"][0], 256)
    wuq = inp["mla_w_uq"][0].reshape(512, 8, 192)
    wuk = inp["mla_w_uk"][0]
    wuv = inp["mla_w_uv"][0]
    wo = inp["mla_w_o"][0]
    u1, u2 = [], []
    for h in range(8):
        q = np.concatenate([wuq[:, h, :128], wuq[:, h, 128:192], wuq[:, h, 128 + perm]], axis=1)
        a = _units_proj(q, 256)[0]
        b = np.ascontiguousarray(wuk[:, h, :].T)
        u1.append(np.concatenate([a, b], axis=1))
        v = _units_proj(wuv[:, h, :], 128)[0]
        o = wo[h * 128:(h + 1) * 128, :]
        u2.append(np.concatenate([v, o], axis=1))
    sh["mla_u1"] = np.stack(u1)
    sh["mla_u2"] = np.stack(u2)
    half = 32
    freqs = (10000.0 ** (-np.arange(half, dtype=np.float32) / half)).astype(np.float32)
    pos = np.concatenate([np.arange(TP), np.full(NS, NPAGES * PAGE)]).astype(np.float32)
    ang = pos[None, :] * freqs[:, None]
    c, sn = np.cos(ang).astype(np.float32), np.sin(ang).astype(np.float32)
    rope = np.stack([np.concatenate([c, c], axis=0), np.concatenate([-sn, sn], axis=0)], axis=1)
    sh["rope"] = np.ascontiguousarray(rope.astype(np.float32))
    sh["tri"] = np.triu(np.ones((128, 128), np.float32))
    pool = np.concatenate([inp["cache_mla_ckv"][0], inp["cache_mla_kpe"][0]], axis=-1)
    sh["poolkv"] = pool.reshape(NPOOL * 32, 4 * 320)


def _prep_dn(inp, sh, cp):
    w = inp["dn_w_in"][0]
    qk, vz = [], []
    for h in range(8):
        qk.append(_units_proj(np.concatenate([w[:, h * 128:(h + 1) * 128], w[:, 1024 + h * 128:1024 + (h + 1) * 128]], axis=1), 256)[0])
        vz.append(_units_proj(np.concatenate([w[:, 2048 + h * 128:2048 + (h + 1) * 128], w[:, 3072 + h * 128:3072 + (h + 1) * 128]], axis=1), 256)[0])
    sh["dn_qk"] = np.stack(qk)
    sh["dn_vz"] = np.stack(vz)
    sh["dn_ba"] = _units_proj(w[:, 4096:4112], 16)
    for q in range(4):
        cp.add(f"dn_cw{q}", _cols(inp["dn_conv_w"][0, q]))
    cp.add("dn_norm", _cols(inp["dn_norm"][0]))
    pad = np.zeros((128, 2), np.float32)
    pad[:8, 0] = inp["dn_a_log"][0]
    pad[:8, 1] = inp["dn_dt_bias"][0]
    cp.add("dn_ab", pad)
    wo = inp["dn_w_out"][0]
    sh["dn_wo"] = np.ascontiguousarray(wo.reshape(8, 128, 1024))
    sel = np.zeros((8, 8, 128), np.float32)
    for h in range(8):
        sel[h, h, :] = 1.0
    sh["dn_sel"] = sel.reshape(8, 1024)
    mask = np.ones((8, T), np.float32)
    mask[:, 0] = 0.0
    mask[:, 16:TP:64] = 0.0
    mask[:, TP:] = 0.0
    sh["dn_mask"] = mask
    r = np.arange(64)[:, None]
    c = np.arange(64)[None, :]
    mmax = np.where(c < r, 0.0, 30000.0).astype(np.float32)
    mmin = np.where(c >= r, 0.0, -30000.0).astype(np.float32)
    sh["dn_mm"] = np.ascontiguousarray(np.concatenate([mmax, mmin], axis=1))


def _prep_core(inp, c):
    x_full = np.concatenate([inp["meta_tokens"], inp["x_prompt"][c], inp["x_sample"][NS * c:NS * (c + 1), 0]], axis=0)
    pc = {}
    pc["xT"] = np.ascontiguousarray(x_full.reshape(T, KC, 128).transpose(2, 1, 0))
    lh = inp["state_lru_h"][:, NS * c:NS * (c + 1)]
    pc["s_lru_h"] = np.ascontiguousarray(lh.reshape(2, NS, KC, 128).transpose(3, 0, 2, 1))
    lc = inp["state_lru_conv"][:, NS * c:NS * (c + 1)]
    pc["s_lru_conv"] = np.ascontiguousarray(lc.reshape(2, NS, 3, KC, 128).transpose(4, 0, 3, 1, 2))
    pc["s_dn_S"] = np.ascontiguousarray(inp["state_dn_S"][0, NS * c:NS * (c + 1)].reshape(NS * 8, 128, 128))
    dc = inp["state_dn_conv"][0, NS * c:NS * (c + 1)]
    pc["s_dn_conv"] = np.ascontiguousarray(dc.reshape(NS, 3, 24, 128).transpose(3, 2, 0, 1))
    pc["pt"] = np.ascontiguousarray(inp["page_table"][NS * c:NS * (c + 1)].T.astype(np.int32))
    return pc


class Builder:
    def __init__(self, nc, S, shapes, colidx, stop_after=None):
        self.nc = nc
        self.S = S
        self.colidx = colidx
        self.stop_after = stop_after
        self.dram = {}
        for name, shp in shapes.items():
            self.dram[name] = nc.dram_tensor(name, list(shp), I32 if name == "pt" else F32, kind="ExternalInput").ap()
        self.ps = nc.alloc_psum_tensor("ps", [128, 4096], F32)
        self.ps_next = 0
        self.wq = []
        self.wi = 0
        self.outs = {}

    def out(self, name, shape):
        ap = self.nc.dram_tensor(name, list(shape), F32, kind="ExternalOutput").ap()
        self.outs[name] = ap
        return ap

    def bank(self):
        b = self.ps_next
        self.ps_next = (self.ps_next + 1) % 4
        return self.ps[:, b * 512:(b + 1) * 512]

    def col(self, name, k=0, n=1):
        i = self.colidx[name] + k
        return self.cols[:, i:i + n]

    def wload(self, name, u, nel):
        S = self.S
        slot = self.wi % self.nws
        ss = self.wi % self.nst
        self.wi += 1
        stg = self.wstage[ss]
        wb = self.wbf[slot]
        src = self.dram[name][u]
        S.dma(stg[:, 0:nel], src, sem=f"w{ss}")
        S.copy("pool", wb[:, 0:nel], stg[:, 0:nel])
        return wb

    def run_units(self, units, depth=2):
        loaded = []
        n = len(units)
        for i in range(n + depth):
            if i < n:
                nm, u, nel, _ = units[i]
                loaded.append(self.wload(nm, u, nel))
            j = i - depth
            if j >= 0:
                units[j][3](loaded[j])

    def setup(self):
        S = self.S
        nc = self.nc
        ncol = self.dram["cols"].shape[1]
        self.cols = S.sb("cols", [128, ncol], F32)
        S.dma(self.cols[:], self.dram["cols"], sem="init")
        S.total_sems.add("init")
        self.ones_f = S.sb("ones_f", [128, 128], F32)
        self.ident_f = S.sb("ident_f", [128, 128], F32)
        S.dma(self.ones_f[:], self.dram["ones_bf"], sem="init")
        S.dma(self.ident_f[:], self.dram["ident"], sem="init")
        self.ones_b = S.sb("ones_b", [128, 128], BF16)
        self.ident_b = S.sb("ident_b", [128, 128], BF16)
        S.copy("pool", self.ones_b[:], self.ones_f[:])
        S.copy("pool", self.ident_b[:], self.ident_f[:])
        self.x = [S.sb(f"x{k}", [128, T], F32) for k in range(KC)]
        for k in range(KC):
            S.dma(self.x[k][:], self.dram["xT"][:, k, :], sem="init")
        self.xn = [S.sb(f"xn{k}", [128, T], BF16) for k in range(KC)]
        self.nws = 3
        self.nst = 2
        self.wstage = [S.sb(f"wst{i}", [128, WSLOT], F32) for i in range(self.nst)]
        self.wbf = [S.sb(f"wbf{i}", [128, WSLOT], BF16) for i in range(self.nws)]
        self.sq = [S.sb(f"sq{i}", [128, 512], BF16) for i in range(2)]
        self.rstd = [S.sb(f"rstd{i}", [128, 512], F32) for i in range(2)]
        self.small = S.sb("small", [128, 320], F32)
        self.small_n = 0
        self.small_idx = {}

    def small_alloc(self, name, n):
        i = self.small_n
        self.small_idx[name] = (i, n)
        self.small_n += n
        assert self.small_n <= 320
        return self.small[:, i:i + n]

    def rmsnorm_stats(self, ti):
        S = self.S
        t0, n = TT[ti]
        acc = self.bank()
        for k in range(KC):
            sq = self.sq[k % 2]
            S.act(sq[:, 0:n], self.x[k][:, t0:t0 + n], AF.Square)
            S.mm(acc[:, 0:n], self.ones_b[:], sq[:, 0:n], start=(k == 0), stop=(k == KC - 1))
        r = self.rstd[ti % 2]
        S.act(r[:, 0:n], acc[:, 0:n], AF.Sqrt, bias=self.eps_col[:, 0:1], scale=1.0 / D)
        S.recip(r[:, 0:n], r[:, 0:n])
        return r

    def rmsnorm_to_xn(self, gname):
        S = self.S
        for ti, (t0, n) in enumerate(TT):
            r = self.rmsnorm_stats(ti)
            for k in range(KC):
                S.stt(self.xn[k][:, t0:t0 + n], self.x[k][:, t0:t0 + n], self.col(gname, k), r[:, 0:n],
                      ALU.mult, ALU.mult)

    def proj_chunk(self, wb_lhsT, rhs_list, evac):
        S = self.S
        nk = len(rhs_list)
        for ti, (t0, n) in enumerate(TT):
            acc = self.bank()
            for k in range(nk):
                S.mm(acc[:, 0:n], wb_lhsT(k), rhs_list[k][:, t0:t0 + n], start=(k == 0), stop=(k == nk - 1))
            evac(ti, t0, n, acc)

    def ffn(self, li):
        S = self.S
        self.rmsnorm_to_xn(f"nffn{li}")
        m = S.mark()
        h = [S.sb(f"h{j}", [128, T], BF16) for j in range(11)]
        sg = [S.sb(f"sg{j}", [128, 512], BF16) for j in range(2)]
        for half in range(2):
            units = []
            for jj in range(11):
                j = half * 11 + jj

                def fn(wb, jj=jj):
                    w4 = wb[:, 0:2048].rearrange("p (k g f) -> p k g f", k=8, g=2)
                    for ti, (t0, n) in enumerate(TT):
                        pg = self.bank()
                        pu = self.bank()
                        for k in range(KC):
                            S.mm(pg[:, 0:n], w4[:, k, 0, :], self.xn[k][:, t0:t0 + n], start=(k == 0), stop=(k == KC - 1))
                        for k in range(KC):
                            S.mm(pu[:, 0:n], w4[:, k, 1, :], self.xn[k][:, t0:t0 + n], start=(k == 0), stop=(k == KC - 1))
                        s = sg[ti % 2]
                        S.act(s[:, 0:n], pg[:, 0:n], AF.Silu)
                        S.tt("dve", h[jj][:, t0:t0 + n], pu[:, 0:n], s[:, 0:n], ALU.mult)
                units.append((f"ffn_gu{li}", j, 2048, fn))
            for fo in range(KC):
                def fn2(wb, fo=fo):
                    w3 = wb[:, 0:1408].rearrange("p (k f) -> p k f", k=11)

                    def ev(ti, t0, n, acc):
                        S.tt("dve", self.x[fo][:, t0:t0 + n], acc[:, 0:n], self.x[fo][:, t0:t0 + n], ALU.add)
                    self.proj_chunk(lambda k: w3[:, k, :], h, ev)
                units.append((f"ffn_dn{li}", half * 8 + fo, 1408, fn2))
            self.run_units(units)
        S.release(m)

    def lru(self, li, j):
        S = self.S
        self.rmsnorm_to_xn(f"nmix{li}")
        m = S.mark()
        HALF = [(0, 1024, (0, 1)), (1024, T - 1024, (2, 3, 4))]
        HN = T - 1024
        cA = S.sb("cA", [128, 8], F32)
        ncA = S.sb("ncA", [128, 8], F32)
        lam = self.col(f"lru_lam{j}", 0, 8)
        S.act(cA[:], lam, AF.Exp, scale=-1.0)
        S.act(cA[:], cA[:], AF.Ln, bias=self.one_col[:, 0:1])
        S.ts("dve", ncA[:], cA[:], 8.0, ALU.mult)
        S.ts("dve", cA[:], cA[:], -8.0, ALU.mult)
        hg = [S.sb(f"hg{k}", [128, T], BF16) for k in range(2)]
        gate = [S.sb(f"gate{k}", [128, T], BF16) for k in range(2)]
        xx = [S.sb(f"xx{k}", [128, TP + 3], F32) for k in range(2)]
        xs = [S.sb(f"xs{k}", [128, NS, 4], F32) for k in range(2)]
        xcb = [S.sb(f"xcb{k}", [128, T], BF16) for k in range(2)]
        ctmp = S.sb("ctmp", [128, T], F32)
        ra = S.sb("ra", [128, HN], F32)
        ri = S.sb("ri", [128, HN], F32)
        av = S.sb("av", [128, HN], F32)
        tmp = S.sb("tmp", [128, HN], F32)
        carry = S.sb("carry", [128, 1], F32)
        p_h = self.small_alloc(f"p_lru_h{j}", 8)
        p_cv = self.small_alloc(f"p_lru_conv{j}", 24)
        s_h = self.small_alloc(f"s_lru_h{j}", 32)
        s_cv = self.small_alloc(f"s_lru_conv{j}", 96)
        s_cv4 = s_cv.rearrange("p (k b j) -> p k b j", k=8, b=NS)
        s_h3 = s_h.rearrange("p (k b) -> p k b", k=8)
        st_h = S.sb("st_h", [128, 8, NS], F32)
        S.dma(st_h[:], self.dram["s_lru_h"][:, j], sem=f"st{j}")
        st_c = S.sb("st_c", [128, 8, NS, 3], F32)
        S.dma(st_c[:], self.dram["s_lru_conv"][:, j], sem=f"st{j}")
        for k in range(2):
            S.memset("pool", xx[k][:, 0:3], 0.0)

        units = []
        for n in range(4):
            def f_gate(wb, n=n):
                w3 = wb[:, 0:2048].rearrange("p (k f) -> p k f", k=8)
                for c in range(2):
                    def ev(ti, t0, nn, acc, c=c):
                        S.act(gate[c][:, t0:t0 + nn], acc[:, 0:nn], AF.Gelu)
                    self.proj_chunk(lambda k, c=c: w3[:, k, c * 128:(c + 1) * 128], self.xn, ev)
            units.append((f"lru_win{j}", 2 * n, 2048, f_gate))

            def f_x(wb, n=n):
                w3 = wb[:, 0:2048].rearrange("p (k f) -> p k f", k=8)
                for c in range(2):
                    kc = 2 * n + c

                    def ev(ti, t0, nn, acc, c=c, kc=kc):
                        if t0 + nn <= TP:
                            S.copy("act", xx[c][:, 3 + t0:3 + t0 + nn], acc[:, 0:nn])
                        else:
                            npz = TP - t0
                            S.copy("act", xx[c][:, 3 + t0:3 + TP], acc[:, 0:npz])
                            S.copy("act", xs[c][:, :, 3], acc[:, npz:npz + NS])
                    self.proj_chunk(lambda k, c=c: w3[:, k, c * 128:(c + 1) * 128], self.xn, ev)
                    S.copy("pool", xs[c][:, :, 0:3], st_c[:, kc, :, :])
                    S.copy("pool", p_cv[:, kc * 3:(kc + 1) * 3], xx[c][:, TP:TP + 3])
                    S.copy("pool", s_cv4[:, kc, :, :], xs[c][:, :, 1:4])
                    cw = lambda q, kc=kc: self.col(f"lru_cw{j}_{q}", kc)
                    cb = self.col(f"lru_cb{j}", kc)
                    S.ts("dve", ctmp[:, 0:TP], xx[c][:, 0:TP], cw(0), ALU.mult, cb, ALU.add)
                    for q in range(1, 3):
                        S.stt(ctmp[:, 0:TP], xx[c][:, q:q + TP], cw(q), ctmp[:, 0:TP], ALU.mult, ALU.add)
                    S.stt(xcb[c][:, 0:TP], xx[c][:, 3:3 + TP], cw(3), ctmp[:, 0:TP], ALU.mult, ALU.add)
                    S.ts("dve", ctmp[:, TP:T], xs[c][:, :, 0], cw(0), ALU.mult, cb, ALU.add)
                    for q in range(1, 3):
                        S.stt(ctmp[:, TP:T], xs[c][:, :, q], cw(q), ctmp[:, TP:T], ALU.mult, ALU.add)
                    S.stt(xcb[c][:, TP:T], xs[c][:, :, 3], cw(3), ctmp[:, TP:T], ALU.mult, ALU.add)
            units.append((f"lru_win{j}", 2 * n + 1, 2048, f_x))

            def f_g(wb, n=n):
                w4 = wb[:, 0:1024].rearrange("p (g k f) -> p g k f", g=2, k=2)
                for c in range(2):
                    kc = 2 * n + c
                    for (h0, hn, tiles) in HALF:
                        for ti in tiles:
                            t0, nn = TT[ti]
                            pa = self.bank()
                            pi = self.bank()
                            for k in range(2):
                                S.mm(pa[:, 0:nn], w4[:, 0, k, c * 128:(c + 1) * 128], xcb[k][:, t0:t0 + nn],
                                     start=(k == 0), stop=(k == 1))
                            for k in range(2):
                                S.mm(pi[:, 0:nn], w4[:, 1, k, c * 128:(c + 1) * 128], xcb[k][:, t0:t0 + nn],
                                     start=(k == 0), stop=(k == 1))
                            S.act(ra[:, t0 - h0:t0 - h0 + nn], pa[:, 0:nn], AF.Sigmoid, bias=self.col(f"lru_ba{j}", kc))
                            S.act(ri[:, t0 - h0:t0 - h0 + nn], pi[:, 0:nn], AF.Sigmoid, bias=self.col(f"lru_bi{j}", kc))
                        R = slice(0, hn)
                        G = slice(h0, h0 + hn)
                        S.act(av[:, R], ra[:, R], AF.Exp, scale=cA[:, kc:kc + 1])
                        S.act(tmp[:, R], ra[:, R], AF.Tanh, scale=ncA[:, kc:kc + 1])
                        S.tt("pool", ra[:, R], av[:, R], av[:, R], ALU.mult)
                        S.stt(tmp[:, R], ra[:, R], 1.0, tmp[:, R], ALU.add, ALU.mult)
                        S.act(tmp[:, R], tmp[:, R], AF.Sqrt)
                        S.tt("pool", ri[:, R], ri[:, R], xcb[c][:, G], ALU.mult)
                        S.tt("dve", ri[:, R], ri[:, R], tmp[:, R], ALU.mult)
                        if h0 == 0:
                            S.scan(tmp[:, R], av[:, R], ri[:, R], 0.0)
                            S.copy("pool", carry[:], tmp[:, hn - 1:hn])
                        else:
                            npr = TP - h0
                            S.scan(tmp[:, 0:npr], av[:, 0:npr], ri[:, 0:npr], carry[:, 0:1])
                            S.tt("dve", tmp[:, npr:hn], av[:, npr:hn], st_h[:, kc, :], ALU.mult)
                            S.tt("dve", tmp[:, npr:hn], tmp[:, npr:hn], ri[:, npr:hn], ALU.add)
                            S.copy("pool", p_h[:, kc:kc + 1], tmp[:, npr - 1:npr])
                            S.copy("pool", s_h3[:, kc, :], tmp[:, npr:hn])
                        S.tt("dve", hg[c][:, G], tmp[:, R], gate[c][:, G], ALU.mult)
            units.append((f"lru_wg{j}", n, 1024, f_g))

            def f_o(wb, n=n):
                w3 = wb[:, 0:2048].rearrange("p (k f) -> p k f", k=2)
                for fo in range(KC):
                    def ev(ti, t0, nn, acc, fo=fo):
                        S.tt("dve", self.x[fo][:, t0:t0 + nn], acc[:, 0:nn], self.x[fo][:, t0:t0 + nn], ALU.add)
                    self.proj_chunk(lambda k, fo=fo: w3[:, k, fo * 128:(fo + 1) * 128], hg, ev)
            units.append((f"lru_wout{j}", n, 2048, f_o))
        self.run_units(units)
        S.release(m)

    def rbank(self, i):
        return self.ps[:, i * 512:(i + 1) * 512]

    def rope_tile(self, dst, p_raw, p_swp, t0, n, cs, t1, t2):
        S = self.S
        S.dma(cs[:, :, 0:n], self.dram["rope"][:, :, t0:t0 + n], sem="cs")
        S.tt("dve", t1[:, 0:n], p_raw, cs[:, 0, 0:n], ALU.mult)
        S.tt("dve", t2[:, 0:n], p_swp, cs[:, 1, 0:n], ALU.mult)
        S.tt("pool", dst, t1[:, 0:n], t2[:, 0:n], ALU.add)

    def mla(self, li, j):
        S = self.S
        self.rmsnorm_to_xn(f"nmix{li}")
        xn_base = S.bases[self.xn[0].name]
        m0 = S.mark()
        ckvb = [S.sb(f"ckvb{k}", [128, T], BF16) for k in range(2)]
        kpeb = S.sb("kpeb", [64, T], BF16)
        cqb = [S.sb(f"cqb{k}", [128, T], BF16) for k in range(4)]
        qs = S.sb("qs", [128, 3, NS, 8], BF16)
        ols = S.sb("ols", [128, 2, 8, NS], BF16)
        trib = S.sb("trib", [128, 128], BF16)
        trif = S.sb("trif", [128, 128], F32)
        S.dma(trif[:], self.dram["tri"], sem="tri")
        S.copy("pool", trib[:], trif[:])
        cs = S.sb("cs", [64, 2, 512], F32)
        rt1 = S.sb("rt1", [64, 512], F32)
        rt2 = S.sb("rt2", [64, 512], F32)
        m1 = S.mark()
        kpef = S.sb("kpef", [64, T], F32)
        ckvT = [S.sb(f"ckvT{k}", [128, T], F32) for k in range(2)]

        wq = [self.wload("mla_dq", u, 2048) for u in range(2)]
        for ti, (t0, n) in enumerate(TT):
            pb = [self.bank() for _ in range(4)]
            for c4 in range(4):
                w3 = wq[c4 // 2][:, 0:2048].rearrange("p (k f) -> p k f", k=8)
                for k in range(KC):
                    S.mm(pb[c4][:, 0:n], w3[:, k, (c4 % 2) * 128:(c4 % 2 + 1) * 128], self.xn[k][:, t0:t0 + n],
                         start=(k == 0), stop=(k == KC - 1))
            acc = self.rbank(4)
            for c4 in range(4):
                sq = self.sq[c4 % 2]
                S.act(sq[:, 0:n], pb[c4][:, 0:n], AF.Square)
                S.mm(acc[:, 0:n], self.ones_b[:], sq[:, 0:n], start=(c4 == 0), stop=(c4 == 3))
            r = self.rstd[ti % 2]
            S.act(r[:, 0:n], acc[:, 0:n], AF.Sqrt, bias=self.eps_col[:, 0:1], scale=1.0 / 512)
            S.recip(r[:, 0:n], r[:, 0:n])
            for c4 in range(4):
                S.stt(cqb[c4][:, t0:t0 + n], pb[c4][:, 0:n], self.col("mla_qn", c4), r[:, 0:n], ALU.mult, ALU.mult)

        wr = self.wload("mla_dkv_r", 0, 1024)
        wr3 = wr[:, 0:1024].rearrange("p (k f) -> p k f", k=8)
        for ti, (t0, n) in enumerate(TT):
            p1 = self.bank()
            p2 = self.bank()
            for k in range(KC):
                S.mm(p1[0:64, 0:n], wr3[:, k, 0:64], self.xn[k][:, t0:t0 + n], start=(k == 0), stop=(k == KC - 1))
            for k in range(KC):
                S.mm(p2[0:64, 0:n], wr3[:, k, 64:128], self.xn[k][:, t0:t0 + n], start=(k == 0), stop=(k == KC - 1))
            self.rope_tile(kpef[:, t0:t0 + n], p1[0:64, 0:n], p2[0:64, 0:n], t0, n, cs, rt1, rt2)
        S.copy("pool", kpeb[:], kpef[:])

        wc = self.wload("mla_dkv_c", 0, 2048)
        wc3 = wc[:, 0:2048].rearrange("p (k f) -> p k f", k=8)
        for ti, (t0, n) in enumerate(TT):
            pb = [self.bank() for _ in range(2)]
            for c2 in range(2):
                for k in range(KC):
                    S.mm(pb[c2][:, 0:n], wc3[:, k, c2 * 128:(c2 + 1) * 128], self.xn[k][:, t0:t0 + n],
                         start=(k == 0), stop=(k == KC - 1))
            acc = self.rbank(4)
            for c2 in range(2):
                sq = self.sq[c2 % 2]
                S.act(sq[:, 0:n], pb[c2][:, 0:n], AF.Square)
                S.mm(acc[:, 0:n], self.ones_b[:], sq[:, 0:n], start=(c2 == 0), stop=(c2 == 1))
            r = self.rstd[ti % 2]
            S.act(r[:, 0:n], acc[:, 0:n], AF.Sqrt, bias=self.eps_col[:, 0:1], scale=1.0 / 256)
            S.recip(r[:, 0:n], r[:, 0:n])
            for c2 in range(2):
                S.stt(ckvT[c2][:, t0:t0 + n], pb[c2][:, 0:n], self.col("mla_kvn", c2), r[:, 0:n], ALU.mult, ALU.mult)
        for c2 in range(2):
            S.copy("pool", ckvb[c2][:], ckvT[c2][:])

        sv = S.sb_ptr
        S.sb_ptr = xn_base
        vtok = S.sb("vtok", [128, 17, 256], BF16)
        ostg = [S.sb(f"ostg{i}", [128, 320], F32) for i in range(2)]
        qaug = [S.sb(f"qaug{k}", [128, T], BF16) for k in range(3)]
        oh = S.sb("oh", [128, T], BF16)
        vnew = S.sb("vnew", [1, NS, 257], BF16)
        assert S.sb_ptr <= xn_base + 8 * ((T * 2 + 63) // 64 * 64), "xn overlay overflow"
        S.sb_ptr = sv

        okv = self.out("p_kv", [T, 320])
        for bi in range(17 if "T" not in os.environ.get("KSKIP", "") else 0):
            t0 = bi * 128
            n = min(128, T - t0)
            pt_ = self.bank()
            for c2 in range(2):
                S.tr(pt_[0:n, c2 * 128:(c2 + 1) * 128], ckvT[c2][:, t0:t0 + n], self.ident_f[:])
            S.tr(pt_[0:n, 256:320], kpef[:, t0:t0 + n], self.ident_f[0:64, 0:64])
            og = ostg[bi % 2]
            KS = os.environ.get("KSKIP", "")
            if "1" not in KS:
                S.copy("act", og[0:n, :], pt_[0:n, 0:320])
            if "2" not in KS:
                S.copy("dve", vtok[0:n, bi, :], pt_[0:n, 0:256])
            if "3" not in KS:
                S.dma(okv[t0:t0 + n, :], og[0:n, :], sem=f"okv{bi % 2}")
        S.memset("pool", vnew[:], 1.0)
        for b in range(NS if "V" not in os.environ.get("KSKIP", "") else 0):
            pt_ = self.bank()
            for c2 in range(2):
                S.tr(pt_[0:1, c2 * 128:(c2 + 1) * 128], ckvT[c2][:, TP + b:TP + b + 1], self.ident_f[:])
            S.copy("dve", vnew[0:1, b, 0:256], pt_[0:1, 0:256])
        S.release(m1)

        mA = S.mark()
        qn_s = S.sb("qn_s", [128, NS], BF16)
        u1 = []

        def passA(wb, h):
            wq3 = wb[:, 0:1024].rearrange("p (k f) -> p k f", k=4)
            wuk = wb[:, 1024:1280]
            pn = self.bank()
            for k in range(4):
                S.mm(pn[:, 0:NS], wq3[:, k, 0:128], cqb[k][:, TP:T], start=(k == 0), stop=(k == 3))
            S.copy("dve", qn_s[:], pn[:, 0:NS])
            p1 = self.bank()
            p2 = self.bank()
            for k in range(4):
                S.mm(p1[0:64, 0:NS], wq3[:, k, 128:192], cqb[k][:, TP:T], start=(k == 0), stop=(k == 3))
            for k in range(4):
                S.mm(p2[0:64, 0:NS], wq3[:, k, 192:256], cqb[k][:, TP:T], start=(k == 0), stop=(k == 3))
            self.rope_tile(qs[0:64, 2, :, h], p1[0:64, 0:NS], p2[0:64, 0:NS], TP, NS, cs, rt1, rt2)
            for c2 in range(2):
                pl = self.bank()
                S.mm(pl[:, 0:NS], wuk[:, c2 * 128:(c2 + 1) * 128], qn_s[:], start=True, stop=True)
                S.copy("dve", qs[:, c2, :, h], pl[:, 0:NS])
        if "A" not in os.environ.get("KSKIP", ""):
            self.run_units([("mla_u1", h, 1280, (lambda wb, h=h: passA(wb, h))) for h in range(8)])
        S.release(mA)

        if "D" not in os.environ.get("KSKIP", ""):
            self.mla_decode(qs, ols, ckvb, kpeb, vnew)
        else:
            S.memset("pool", ols[:], 0.0)

        mB = S.mark()
        qn = S.sb("qn", [128, T], BF16)
        olat = [S.sb(f"olat{k}", [128, T], BF16) for k in range(2)]
        PT = [S.sb(f"PT{i}", [128, 512], BF16) for i in range(3)]
        rden = S.sb("rden", [128, 512], F32)
        QT = [(0, 512), (512, 512), (1024, 512), (1536, 512), (2048, 16)]

        def head_q(wb, h):
            wq3 = wb[:, 0:1024].rearrange("p (k f) -> p k f", k=4)
            wuk = wb[:, 1024:1280]

            def ev(ti, t0, n, acc):
                S.copy("act", qn[:, t0:t0 + n], acc[:, 0:n])
            self.proj_chunk(lambda k: wq3[:, k, 0:128], cqb, ev)
            for ti, (t0, n) in enumerate(TT):
                p1 = self.bank()
                p2 = self.bank()
                for k in range(4):
                    S.mm(p1[0:64, 0:n], wq3[:, k, 128:192], cqb[k][:, t0:t0 + n], start=(k == 0), stop=(k == 3))
                for k in range(4):
                    S.mm(p2[0:64, 0:n], wq3[:, k, 192:256], cqb[k][:, t0:t0 + n], start=(k == 0), stop=(k == 3))
                self.rope_tile(qaug[2][0:64, t0:t0 + n], p1[0:64, 0:n], p2[0:64, 0:n], t0, n, cs, rt1, rt2)
            for c2 in range(2):
                def ev2(ti, t0, n, acc, c2=c2):
                    S.copy("act", qaug[c2][:, t0:t0 + n], acc[:, 0:n])
                self.proj_chunk(lambda k, c2=c2: wuk[:, c2 * 128:(c2 + 1) * 128], [qn], ev2)
            a0, a1, dn_ = self.rbank(4), self.rbank(5), self.rbank(6)
            pairs = []
            for (q0, qn_) in QT:
                nb = (q0 + qn_ - 1) // 128 + 1
                for jb in range(nb):
                    k0 = jb * 128
                    kn = min(128, TP - k0)
                    qs0 = max(q0, k0)
                    pairs.append(dict(q0=q0, qn=qn_, jb=jb, k0=k0, kn=kn, qs0=qs0, nc=q0 + qn_ - qs0, off=qs0 - q0,
                                      first=(jb == 0), last=(jb == nb - 1), idx=len(pairs)))

            def scores(p):
                kn, nc_, k0, qs0 = p["kn"], p["nc"], p["k0"], p["qs0"]
                sp = self.bank()
                S.mm(sp[0:kn, 0:nc_], ckvb[0][:, k0:k0 + kn], qaug[0][:, qs0:qs0 + nc_], start=True, stop=False)
                S.mm(sp[0:kn, 0:nc_], ckvb[1][:, k0:k0 + kn], qaug[1][:, qs0:qs0 + nc_], start=False, stop=False)
                S.mm(sp[0:kn, 0:nc_], kpeb[0:64, k0:k0 + kn], qaug[2][0:64, qs0:qs0 + nc_], start=False, stop=True)
                pt_ = PT[p["idx"] % len(PT)]
                S.act(pt_[0:kn, 0:nc_], sp[0:kn, 0:nc_], AF.Exp, scale=MLA_SCALE)
                if k0 >= p["q0"]:
                    dnn = min(128, nc_)
                    S.tt("pool", pt_[0:kn, 0:dnn], pt_[0:kn, 0:dnn], trib[0:kn, 0:dnn], ALU.mult)

            def pv(p):
                kn, nc_, off, jb = p["kn"], p["nc"], p["off"], p["jb"]
                pt_ = PT[p["idx"] % len(PT)]
                S.mm(a0[:, off:off + nc_], vtok[0:kn, jb, 0:128], pt_[0:kn, 0:nc_], start=p["first"], stop=p["last"])
                S.mm(a1[:, off:off + nc_], vtok[0:kn, jb, 128:256], pt_[0:kn, 0:nc_], start=p["first"], stop=p["last"])
                S.mm(dn_[:, off:off + nc_], self.ones_b[0:kn, :], pt_[0:kn, 0:nc_], start=p["first"], stop=p["last"])
                if p["last"]:
                    q0, qn_ = p["q0"], p["qn"]
                    S.recip(rden[:, 0:qn_], dn_[:, 0:qn_])
                    S.tt("dve", olat[0][:, q0:q0 + qn_], a0[:, 0:qn_], rden[:, 0:qn_], ALU.mult)
                    S.tt("dve", olat[1][:, q0:q0 + qn_], a1[:, 0:qn_], rden[:, 0:qn_], ALU.mult)
            scores(pairs[0])
            for i, p in enumerate(pairs):
                if i + 1 < len(pairs):
                    scores(pairs[i + 1])
                pv(p)
            for c2 in range(2):
                S.copy("pool", olat[c2][:, TP:T], ols[:, c2, h, :])

        def head_o(wb, h):
            wuv = wb[:, 0:256].rearrange("p (k v) -> p k v", k=2)
            wo = wb[:, 256:1280]

            def ev(ti, t0, n, acc):
                S.copy("act", oh[:, t0:t0 + n], acc[:, 0:n])
            self.proj_chunk(lambda k: wuv[:, k, :], olat, ev)
            for fo in range(KC):
                def ev2(ti, t0, n, acc, fo=fo):
                    S.tt("dve", self.x[fo][:, t0:t0 + n], acc[:, 0:n], self.x[fo][:, t0:t0 + n], ALU.add)
                self.proj_chunk(lambda k, fo=fo: wo[:, fo * 128:(fo + 1) * 128], [oh], ev2)
        units = []
        for h in range(8):
            units.append(("mla_u1", h, 1280, (lambda wb, h=h: head_q(wb, h))))
            units.append(("mla_u2", h, 1280, (lambda wb, h=h: head_o(wb, h))))
        if "B" not in os.environ.get("KSKIP", ""):
            self.run_units(units)
        S.release(m0)

    def mla_decode(self, qs, ols, ckvb, kpeb, vnew):
        S = self.S
        m = S.mark()
        NTK = 4
        NSUB = PAGE // NTK
        NBUF = 4
        ptab = S.sb("ptab", [128, NS], I32)
        S.dma(ptab[:], self.dram["pt"], sem="ptab")
        idx = S.sb("idx", [128, NS, NSUB], I32)
        for b in range(NS):
            for s_ in range(NSUB):
                S.ts("dve", idx[:, b, s_:s_ + 1], ptab[:, b:b + 1], float(NSUB), ALU.mult, float(s_), ALU.add)
        kvs = [S.sb(f"kvs{i}", [128, NTK * 320], F32) for i in range(NBUF)]
        kT = [S.sb(f"kT{i}", [128, 384], BF16) for i in range(2)]
        Vb = [S.sb(f"Vb{i}", [128, NTK, 257], BF16) for i in range(2)]
        for i in range(2):
            S.memset("pool", Vb[i][:], 1.0)
        PTd = [S.sb(f"PTd{i}", [128, NTK * 8], BF16) for i in range(2)]
        pnew = S.sb("pnew", [1, 8], BF16)
        osb = S.sb("osb", [8, 257], F32)
        rd = S.sb("rd", [8, 1], F32)
        onb = S.sb("onb", [8, 256], F32)
        accb = self.rbank(7)
        toks = []
        g = 0
        for b in range(NS):
            for s_ in range(NSUB):
                for tt_ in range(NTK):
                    toks.append((b, s_, tt_, g))
                g += 1
        pks = {}

        def start_chunk(b, s_, g):
            kv = kvs[g % NBUF]
            S.gather(kv[:], self.dram["poolkv"], idx[:, b, s_:s_ + 1], sem=f"kv{g % NBUF}")

        def vcast(g):
            kv = kvs[g % NBUF]
            S.copy("act", Vb[g % 2][:, :, 0:256], kv[:].rearrange("p (t c) -> p t c", t=NTK)[:, :, 0:256])

        def transposes(i):
            b, s_, tt_, g = toks[i]
            kv = kvs[g % NBUF]
            pk = self.bank()
            base = tt_ * 320
            S.tr(pk[:, 0:128], kv[:, base:base + 128], self.ident_f[:])
            S.tr(pk[:, 128:256], kv[:, base + 128:base + 256], self.ident_f[:])
            S.tr(pk[0:64, 256:384], kv[:, base + 256:base + 320], self.ident_f[:])
            kt = kT[i % 2]
            S.copy("dve", kt[:, 0:256], pk[:, 0:256])
            S.copy("dve", kt[0:64, 256:384], pk[0:64, 256:384])

        def qk(i):
            b, s_, tt_, g = toks[i]
            kt = kT[i % 2]
            sp = self.rbank(5 + (g % 2))
            o_ = sp[:, tt_ * 8:(tt_ + 1) * 8]
            S.mm(o_, kt[:, 0:128], qs[:, 0, b, :], start=True, stop=False)
            S.mm(o_, kt[:, 128:256], qs[:, 1, b, :], start=False, stop=False)
            S.mm(o_, kt[0:64, 256:384], qs[0:64, 2, b, :], start=False, stop=True)
            if tt_ == NTK - 1:
                S.act(PTd[g % 2][:], sp[:, 0:NTK * 8], AF.Exp, scale=MLA_SCALE)

        def pv(b, s_, g):
            for tt_ in range(NTK):
                S.mm(accb[0:8, 0:257], PTd[g % 2][:, tt_ * 8:(tt_ + 1) * 8], Vb[g % 2][:, tt_, :],
                     start=(s_ == 0 and tt_ == 0), stop=False)

        def finish(b):
            sp = self.bank()
            S.mm(sp[0:1, 0:8], ckvb[0][:, TP + b:TP + b + 1], qs[:, 0, b, :], start=True, stop=False)
            S.mm(sp[0:1, 0:8], ckvb[1][:, TP + b:TP + b + 1], qs[:, 1, b, :], start=False, stop=False)
            S.mm(sp[0:1, 0:8], kpeb[0:64, TP + b:TP + b + 1], qs[0:64, 2, b, :], start=False, stop=True)
            S.act(pnew[:], sp[0:1, 0:8], AF.Exp, scale=MLA_SCALE)
            S.mm(accb[0:8, 0:257], pnew[:], vnew[0:1, b, :], start=False, stop=True)
            S.copy("dve", osb[:], accb[0:8, 0:257])
            S.recip(rd[:], osb[:, 256:257])
            S.ts("dve", onb[:], osb[:, 0:256], rd[:, 0:1], ALU.mult)
            for c2 in range(2):
                po = self.bank()
                S.tr(po[:, 0:8], onb[:, c2 * 128:(c2 + 1) * 128], self.ident_f[0:8, 0:8])
                S.copy("dve", ols[:, c2, :, b], po[:, 0:8])

        n = len(toks)
        started = 0
        nchunks = NS * NSUB

        def ensure_started(upto):
            nonlocal started
            while started <= min(upto, nchunks - 1):
                bb, ss = divmod(started, NSUB)
                start_chunk(bb, ss, started)
                started += 1
        ensure_started(NBUF - 2)
        transposes(0)
        pending_pv = None
        for i in range(n):
            b, s_, tt_, g = toks[i]
            if tt_ == 0:
                ensure_started(g + NBUF - 2)
                vcast(g)
            if i + 1 < n:
                transposes(i + 1)
            qk(i)
            if tt_ == NTK - 1:
                if pending_pv is not None:
                    pv(*pending_pv)
                    if pending_pv[1] == NSUB - 1:
                        finish(pending_pv[0])
                pending_pv = (b, s_, g)
        pv(*pending_pv)
        finish(pending_pv[0])
        S.release(m)

    def mmf(self, out, lhsT, rhs, start=True, stop=True):
        return self.S.mm(out, lhsT, rhs, start, stop)

    def dn(self, li, j):
        S = self.S
        self.rmsnorm_to_xn(f"nmix{li}")
        m0 = S.mark()
        small2 = S.sb("small2", [128, 360], F32)
        S.memset("pool", small2[:], 0.0)
        p_cv = small2[:, 0:72].rearrange("p (k j) -> p k j", k=24)
        s_cv = small2[:, 72:360].rearrange("p (k b j) -> p k b j", k=24, b=NS)
        oS = self.out("o_dn_S", [8 + NS * 8, 128, 128])
        Gc = S.sb("Gc", [8, T], F32)
        GT = S.sb("GT", [64, 37, 8], F32)
        BT = S.sb("BT", [64, 37, 8], F32)
        sel = S.sb("sel", [8, 128], F32)
        mm_ = S.sb("mm_", [64, 128], F32)
        S.dma(mm_[:], self.dram["dn_mm"], sem="dnc")
        st_c = S.sb("dst_c", [128, 24, NS, 3], F32)
        S.dma(st_c[:], self.dram["s_dn_conv"], sem="dnc")
        chunks = [(0, 16, 4)] + [(16 + 64 * c, 64, 6) for c in range(32)] + [(TP + b, 1, 0) for b in range(NS)]
        xx = S.sb("dxx", [128, TP + 3], F32)
        xs = S.sb("dxs", [128, NS, 4], F32)
        ctmp = S.sb("dctmp", [128, T], F32)
        S.memset("pool", xx[:, 0:3], 0.0)
        qdec = S.sb("qdec", [128, T], BF16)
        qn = S.sb("dqn", [128, T], BF16)
        kn = S.sb("kn", [128, T], BF16)
        vb = S.sb("vb", [128, T], BF16)
        sz = S.sb("sz", [128, T], BF16)
        GB = S.sb("GB", [128, T], F32)
        oT = S.sb("oT", [128, T], BF16)
        Sf = S.sb("Sf", [128, 128], F32)
        Sb_ = S.sb("Sb", [128, 128], BF16)
        eg = self.rstd[1]
        wba = self.wload("dn_ba", 0, 128)
        wba3 = wba[:, 0:128].rearrange("p (k f) -> p k f", k=8)
        Ball = GB[0:8, 0:T]
        graw = ctmp[0:8, 0:T]
        sv_ = S.sb_ptr
        S.sb_ptr = S.bases[qdec.name]
        mrow_t = S.sb("mrow", [8, T], F32)
        S.sb_ptr = sv_
        mrow = mrow_t[:, :]
        S.dma(mrow, self.dram["dn_mask"], sem="dnc")
        nA = S.sb("nA", [8, 1], F32)
        ab = self.col("dn_ab", 0, 2)
        S.act(nA[:], ab[0:8, 0:1], AF.Exp)
        S.ts("dve", nA[:], nA[:], -1.0, ALU.mult)
        for ti, (t0, n) in enumerate(TT):
            pb_, pa_ = self.bank(), self.bank()
            for k in range(KC):
                S.mm(pb_[0:8, 0:n], wba3[:, k, 0:8], self.xn[k][:, t0:t0 + n], start=(k == 0), stop=(k == KC - 1))
            for k in range(KC):
                S.mm(pa_[0:8, 0:n], wba3[:, k, 8:16], self.xn[k][:, t0:t0 + n], start=(k == 0), stop=(k == KC - 1))
            S.act(Ball[:, t0:t0 + n], pb_[0:8, 0:n], AF.Sigmoid)
            S.act(graw[:, t0:t0 + n], pa_[0:8, 0:n], AF.Exp, bias=ab[0:8, 1:2])
            S.act(graw[:, t0:t0 + n], graw[:, t0:t0 + n], AF.Ln, bias=self.one_col[0:8, 0:1])
        S.ts("dve", graw, graw, nA[:, 0:1], ALU.mult)
        S.scan(Gc[:], mrow, graw, 0.0)
        for ci, (t0, C, L) in enumerate(chunks):
            pt_ = self.bank()
            S.tr(pt_[0:C, 0:8], Gc[:, t0:t0 + C], self.ident_f[0:8, 0:8])
            S.tr(pt_[0:C, 8:16], Ball[:, t0:t0 + C], self.ident_f[0:8, 0:8])
            S.copy("dve", GT[0:C, ci, :], pt_[0:C, 0:8])
            S.copy("dve", BT[0:C, ci, :], pt_[0:C, 8:16])
        sv_ = S.sb_ptr
        S.sb_ptr = S.bases[xx.name]
        F1 = S.sb("gF1", [64, 8, 64], F32)
        F2 = S.sb("gF2", [64, 8, 64], F32)
        gbf = lambda nm: S.sb(nm, [64, 8, 64], BF16)
        A1, A2, A3, B1, B2, B3 = (gbf(nm) for nm in ("gA1", "gA2", "gA3", "gB1", "gB2", "gB3"))
        usb = S.sb("usb", [64, 8, 128], F32)
        attnT = S.sb("attnT", [64, 8, 64], BF16)
        wT = S.sb("wT", [128, 8, 64], BF16)
        assert S.sb_ptr <= S.bases[ctmp.name] + T * 4, "group overlay overflow"
        S.sb_ptr = sv_
        Vb_ = S.sb("Vbt", [64, 8, 128], BF16)
        Kb_ = S.sb("Kbt", [64, 8, 128], BF16)
        kdec = S.sb("kdec", [64, 8, 128], BF16)
        delta = S.sb("delta", [64, 128], BF16)
        cols_ = S.sb("ccols", [64, 8, 4], F32)
        egl = S.sb("egl", [128, 8], F32)
        mmax, mmin = mm_[:, 0:64], mm_[:, 64:128]

        def conv_silu(psrc_list, kc, dst_f32):
            for (ti, t0, n, acc) in psrc_list:
                if t0 + n <= TP:
                    S.copy("act", xx[:, 3 + t0:3 + t0 + n], acc[:, 0:n])
                else:
                    npz = TP - t0
                    S.copy("act", xx[:, 3 + t0:3 + TP], acc[:, 0:npz])
                    S.copy("act", xs[:, :, 3], acc[:, npz:npz + NS])

        def conv_finish(kc, dst):
            S.memset("pool", xx[:, 0:3], 0.0)
            S.copy("pool", xs[:, :, 0:3], st_c[:, kc, :, :])
            S.copy("pool", p_cv[:, kc, :], xx[:, TP:TP + 3])
            S.copy("pool", s_cv[:, kc, :, :], xs[:, :, 1:4])
            cw = lambda q: self.col(f"dn_cw{q}", kc)
            S.ts("dve", ctmp[:, 0:TP], xx[:, 0:TP], cw(0), ALU.mult)
            for q in range(1, 4):
                S.stt(ctmp[:, 0:TP], xx[:, q:q + TP], cw(q), ctmp[:, 0:TP], ALU.mult, ALU.add)
            S.ts("dve", ctmp[:, TP:T], xs[:, :, 0], cw(0), ALU.mult)
            for q in range(1, 4):
                S.stt(ctmp[:, TP:T], xs[:, :, q], cw(q), ctmp[:, TP:T], ALU.mult, ALU.add)
            S.act(dst, ctmp[:], AF.Silu)

        def l2n(src, dst_bf, scale):
            for ti, (t0, n) in enumerate(TT):
                sq = self.sq[ti % 2]
                S.act(sq[:, 0:n], src[:, t0:t0 + n], AF.Square)
                acc = self.bank()
                S.mm(acc[:, 0:n], self.ones_b[:], sq[:, 0:n], start=True, stop=True)
                r = self.rstd[ti % 2]
                S.act(r[:, 0:n], acc[:, 0:n], AF.Sqrt, bias=self.eps_col[:, 0:1], scale=1.0 / (scale * scale))
                S.recip(r[:, 0:n], r[:, 0:n])
                S.tt("dve", dst_bf[:, t0:t0 + n], src[:, t0:t0 + n], r[:, 0:n], ALU.mult)

        def proj2(wb, c, kc):
            w3 = wb[:, 0:2048].rearrange("p (k f) -> p k f", k=8)
            lst = []
            for ti, (t0, n) in enumerate(TT):
                acc = self.bank()
                for k in range(KC):
                    S.mm(acc[:, 0:n], w3[:, k, c * 128:(c + 1) * 128], self.xn[k][:, t0:t0 + n], start=(k == 0), stop=(k == KC - 1))
                conv_silu([(ti, t0, n, acc)], kc, None)

        def head_qk(wb, h):
            proj2(wb, 0, h)
            conv_finish(h, ctmp[:])
            l2n(ctmp, qn, 128.0 ** -0.5)
            S.dma(sel[:], self.dram["dn_sel"][:, h * 128:(h + 1) * 128], sem="dnsel")
            for ti, (t0, n) in enumerate(TT):
                acc = self.bank()
                S.mm(acc[:, 0:n], sel[:, :], Gc[:, t0:t0 + n], start=True, stop=True)
                S.copy("dve", GB[:, t0:t0 + n], acc[:, 0:n])
                S.act(eg[:, 0:n], acc[:, 0:n], AF.Exp)
                S.tt("dve", qdec[:, t0:t0 + n], qn[:, t0:t0 + n], eg[:, 0:n], ALU.mult)
            proj2(wb, 1, 8 + h)
            conv_finish(8 + h, ctmp[:])
            l2n(ctmp, kn, 1.0)

        def head_vz(wb, h):
            proj2(wb, 0, 16 + h)
            conv_finish(16 + h, ctmp[:])
            S.copy("pool", vb[:], ctmp[:])
            w3 = wb[:, 0:2048].rearrange("p (k f) -> p k f", k=8)

            def ev(ti, t0, n, acc):
                S.act(sz[:, t0:t0 + n], acc[:, 0:n], AF.Silu)
            self.proj_chunk(lambda k: w3[:, k, 128:256], self.xn, ev)
            S.memset("pool", Sf[:], 0.0)
            S.memset("pool", Sb_[:], 0.0)
            groups = [[0]] + [list(range(1 + 8 * q, 9 + 8 * q)) for q in range(4)] + [[33, 34, 35, 36]]
            for grp in groups:
                ng = len(grp)
                C, L = chunks[grp[0]][1], chunks[grp[0]][2]
                R = slice(0, C)
                pk, pq = self.rbank(0), self.rbank(1)
                ci0 = grp[0]
                tg0 = chunks[ci0][0]
                GR = (R, slice(0, ng), slice(0, C))
                bc = lambda ap: ap.to_broadcast([C, ng, C])
                GBg = GB[0:C, tg0:tg0 + ng * C].rearrange("p (g c) -> p g c", g=ng)
                gcolg = GT[0:C, ci0:ci0 + ng, h:h + 1]
                bcolg = BT[0:C, ci0:ci0 + ng, h:h + 1]
                glastg = GB[0:C, tg0 + C - 1:tg0 + ng * C:C].unsqueeze(2)
                S.act(cols_[R, 0:ng, 0:1], gcolg, AF.Exp)
                S.tt("dve", cols_[R, 0:ng, 1:2], cols_[R, 0:ng, 0:1], bcolg, ALU.mult)
                S.tt("dve", cols_[R, 0:ng, 2:3], glastg, gcolg, ALU.subtract)
                S.act(cols_[R, 0:ng, 2:3], cols_[R, 0:ng, 2:3], AF.Exp)
                S.act(egl[:, 0:ng], GB[:, tg0 + C - 1:tg0 + ng * C:C], AF.Exp)
                for g, ci in enumerate(grp):
                    t0 = chunks[ci][0]
                    cs_ = slice(t0, t0 + C)
                    S.mm(pk[R, g * 64:g * 64 + C], kn[:, cs_], kn[:, cs_], start=True, stop=True)
                    S.mm(pq[R, g * 64:g * 64 + C], kn[:, cs_], qn[:, cs_], start=True, stop=True)
                S.tt("dve", F2[GR], GBg, bc(gcolg), ALU.subtract)
                S.tt("dve", F1[GR], F2[GR], bc(mmax[0:C, 0:C].unsqueeze(1)), ALU.max)
                S.tt("dve", F2[GR], F2[GR], bc(mmin[0:C, 0:C].unsqueeze(1)), ALU.min)
                pk3 = pk[:, 0:512].rearrange("p (g c) -> p g c", g=8)
                pq3 = pq[:, 0:512].rearrange("p (g c) -> p g c", g=8)
                S.act(B1[GR], F1[GR], AF.Exp, scale=-1.0)
                S.act(B2[GR], F2[GR], AF.Exp)
                S.tt("dve", F1[GR], pk3[GR], bc(bcolg), ALU.mult)
                S.stt(F1[GR], F1[GR], -1.0, B1[GR], ALU.mult, ALU.mult)
                S.copy("act", A1[GR], F1[GR])
                S.tt("dve", attnT[GR], pq3[GR], B2[GR], ALU.mult)
                if C > 1:
                    ptr_ = self.rbank(2)
                    ptr3 = ptr_[:, 0:512].rearrange("p (g c) -> p g c", g=8)
                    for g in range(ng):
                        S.tr(ptr_[R, g * 64:g * 64 + C], F1[R, g, 0:C], self.ident_f[0:C, 0:C])
                    S.copy("dve", A2[GR], ptr3[GR])
                else:
                    S.copy("dve", A2[GR], A1[GR])
                S.tt("pool", A3[GR], A2[GR], bc(self.ident_b[0:C, 0:C].unsqueeze(1)), ALU.add)
                P_, PT_, TT_ = A1, A2, A3
                P2, PT2, TT2 = B1, B2, B3
                for lv in range(1, L):
                    pl, plT, pl2 = self.rbank(2), self.rbank(3), self.rbank(4)
                    pl3 = pl[:, 0:512].rearrange("p (g c) -> p g c", g=8)
                    plT3 = plT[:, 0:512].rearrange("p (g c) -> p g c", g=8)
                    pl23 = pl2[:, 0:512].rearrange("p (g c) -> p g c", g=8)
                    for g in range(ng):
                        S.mm(pl[R, g * 64:g * 64 + C], PT_[R, g, 0:C], P_[R, g, 0:C], start=True, stop=True)
                    for g in range(ng):
                        S.mm(plT[R, g * 64:g * 64 + C], P_[R, g, 0:C], PT_[R, g, 0:C], start=True, stop=True)
                    S.copy("dve", P2[GR], pl3[GR])
                    S.copy("act", PT2[GR], plT3[GR])
                    for g in range(ng):
                        S.mm(pl2[R, g * 64:g * 64 + C], P2[R, g, 0:C], TT_[R, g, 0:C], start=True, stop=True)
                    S.tt("dve", TT2[GR], pl23[GR], TT_[GR], ALU.add)
                    P_, P2 = P2, P_
                    PT_, PT2 = PT2, PT_
                    TT_, TT2 = TT2, TT_
                TTbf = TT_
                for half in range((ng + 3) // 4):
                    pkk, pvv = self.rbank(0 + half), self.rbank(2 + half)
                    for g in range(half * 4, min(ng, half * 4 + 4)):
                        ci = grp[g]
                        t0 = chunks[ci][0]
                        cs_ = slice(t0, t0 + C)
                        o0 = (g % 4) * 128
                        S.mm(pkk[R, o0:o0 + 128], kn[:, cs_], self.ident_b[:], start=True, stop=True)
                        S.mm(pvv[R, o0:o0 + 128], vb[:, cs_], self.ident_b[:], start=True, stop=True)
                    h4 = half * 4
                    n4 = min(ng, h4 + 4) - h4
                    pkk3 = pkk[:, 0:512].rearrange("p (g c) -> p g c", g=4)
                    pvv3 = pvv[:, 0:512].rearrange("p (g c) -> p g c", g=4)
                    b4 = lambda ap: ap.to_broadcast([C, n4, 128])
                    S.tt("dve", Kb_[R, h4:h4 + n4, :], pkk3[R, 0:n4, :], b4(cols_[R, h4:h4 + n4, 1:2]), ALU.mult)
                    S.tt("dve", kdec[R, h4:h4 + n4, :], pkk3[R, 0:n4, :], b4(cols_[R, h4:h4 + n4, 2:3]), ALU.mult)
                    S.tt("dve", Vb_[R, h4:h4 + n4, :], pvv3[R, 0:n4, :], b4(BT[0:C, ci0 + h4:ci0 + h4 + n4, h:h + 1]), ALU.mult)
                pw = self.rbank(6)
                for half in range((ng + 3) // 4):
                    pu = self.rbank(4 + half)
                    for g in range(half * 4, min(ng, half * 4 + 4)):
                        o0 = (g % 4) * 128
                        S.mm(pu[R, o0:o0 + 128], TTbf[R, g, 0:C], Vb_[R, g, :], start=True, stop=True)
                    n4 = min(ng, half * 4 + 4) - half * 4
                    S.copy("act", usb[R, half * 4:half * 4 + n4, :],
                           pu[:, 0:512].rearrange("p (g c) -> p g c", g=4)[R, 0:n4, :])
                for g in range(ng):
                    S.mm(pw[:, g * 64:g * 64 + C], Kb_[R, g, :], TTbf[R, g, 0:C], start=True, stop=True)
                S.copy("dve", wT[:, 0:ng, 0:C], pw[:, 0:512].rearrange("p (g c) -> p g c", g=8)[:, 0:ng, 0:C])
                for g, ci in enumerate(grp):
                    t0 = chunks[ci][0]
                    cs_ = slice(t0, t0 + C)
                    sample = t0 >= TP
                    if sample:
                        b = t0 - TP
                        S.dma(Sf[:], self.dram["s_dn_S"][b * 8 + h], sem="dnS")
                        S.copy("dve", Sb_[:], Sf[:])
                    pd, po, ps_ = self.rbank(7), self.rbank(5), self.rbank(3)
                    S.mm(pd[R, 0:128], wT[:, g, 0:C], Sb_[:], start=True, stop=True)
                    S.tt("dve", delta[R, :], usb[R, g, :], pd[R, 0:128], ALU.subtract)
                    S.mm(ps_[:, 0:128], kdec[R, g, :], delta[R, :], start=True, stop=True)
                    S.mm(po[:, 0:C], Sb_[:], qdec[:, cs_], start=True, stop=False)
                    S.mm(po[:, 0:C], delta[R, :], attnT[R, g, 0:C], start=False, stop=True)
                    S.stt(Sf[:], Sf[:], egl[:, g:g + 1], ps_[:, 0:128], ALU.mult, ALU.add)
                    S.copy("act", Sb_[:], Sf[:])
                    S.copy("act", oT[:, cs_], po[:, 0:C])
                    if ci == 32:
                        S.dma(oS[h], Sf[:], sem="oS")
                    if sample:
                        S.dma(oS[8 + (t0 - TP) * 8 + h], Sf[:], sem="oS")

        def head_o(wb, h):
            for ti, (t0, n) in enumerate(TT):
                sq = self.sq[ti % 2]
                S.act(sq[:, 0:n], oT[:, t0:t0 + n], AF.Square)
                acc = self.bank()
                S.mm(acc[:, 0:n], self.ones_b[:], sq[:, 0:n], start=True, stop=True)
                r = self.rstd[0]
                S.act(r[:, 0:n], acc[:, 0:n], AF.Sqrt, bias=self.eps_col[:, 0:1], scale=1.0 / 128)
                S.recip(r[:, 0:n], r[:, 0:n])
                S.stt(eg[:, 0:n], oT[:, t0:t0 + n], self.col("dn_norm", 0), r[:, 0:n], ALU.mult, ALU.mult)
                S.tt("dve", oT[:, t0:t0 + n], eg[:, 0:n], sz[:, t0:t0 + n], ALU.mult)
            for fo in range(KC):
                def ev2(ti, t0, n, acc, fo=fo):
                    S.tt("dve", self.x[fo][:, t0:t0 + n], acc[:, 0:n], self.x[fo][:, t0:t0 + n], ALU.add)
                self.proj_chunk(lambda k, fo=fo: wb[:, fo * 128:(fo + 1) * 128], [oT], ev2)
        units = []
        for h in range(8):
            units.append(("dn_qk", h, 2048, (lambda wb, h=h: head_qk(wb, h))))
            units.append(("dn_vz", h, 2048, (lambda wb, h=h: head_vz(wb, h))))
            units.append(("dn_wo", h, 1024, (lambda wb, h=h: head_o(wb, h))))
        self.run_units(units)
        sm2 = self.out("small2", [128, 360])
        S.dma(sm2[:, :], small2[:], sem="o1")
        S.release(m0)

    def consts(self):
        S = self.S
        self.eps_col = S.sb("eps_col", [128, 1], F32)
        self.one_col = S.sb("one_col", [128, 1], F32)
        S.memset("pool", self.eps_col[:], EPS)
        S.memset("pool", self.one_col[:], 1.0)

    def final(self):
        S = self.S
        yT = self.out("yT", [128, KC, T])
        for ti, (t0, n) in enumerate(TT):
            r = self.rmsnorm_stats(ti)
            for k in range(KC):
                S.stt(self.x[k][:, t0:t0 + n], self.x[k][:, t0:t0 + n], self.col("nfinal", k), r[:, 0:n],
                      ALU.mult, ALU.mult)
        for k in range(KC):
            S.dma(yT[:, k, :], self.x[k][:], sem=f"o{k % 2}")
        sm = self.out("small", [128, 320])
        S.dma(sm[:, :], self.small[:], sem="o0")

    def dump_x(self):
        S = self.S
        dbg = self.out("dbg", [128, KC, T])
        for k in range(KC):
            S.dma(dbg[:, k, :], self.x[k][:], sem=f"o{k % 2}")


def build_program(shapes, colidx, stop_after=None, only=None):
    nc = bass.Bass("TRN2", target_bir_lowering=False)
    S = Sched(nc)
    B = Builder(nc, S, shapes, colidx, stop_after)
    B.setup()
    B.consts()
    S.memset("pool", B.small[:], 0.0)
    layers = [("lru", 0), ("dn", 0), ("mla", 0), ("lru", 1)]
    done = False
    for li, (kind, j) in enumerate(layers):
        if only is not None and li != only:
            continue
        if kind == "lru":
            B.lru(li, j)
        elif kind == "dn":
            B.dn(li, j)
        else:
            B.mla(li, j)
        if stop_after == (li, "mix"):
            done = True
            break
        B.ffn(li)
        if stop_after == (li, "ffn"):
            done = True
            break
    if done:
        B.dump_x()
        sm = B.out("small", [128, 320])
        S.dma(sm[:, :], B.small[:], sem="o0")
    else:
        B.final()
    S.emit()
    return nc, B


_STOP_AFTER = None
_DEBUG = {}


def _run(inputs, stop_after=None, cores=NCORES, only=None, x_override=None):
    inp = {k: np.asarray(v) for k, v in inputs.items()}
    sh = _prep_shared(inp)
    colidx = sh.pop("_colidx")
    per_core = [_prep_core(inp, c) for c in range(cores)]
    shapes = {k: v.shape for k, v in sh.items()}
    shapes.update({k: v.shape for k, v in per_core[0].items()})
    if x_override is not None:
        for c in range(cores):
            per_core[c]["xT"] = np.ascontiguousarray(x_override[c].reshape(T, KC, 128).transpose(2, 1, 0))
    nc, B = build_program(shapes, colidx, stop_after, only)
    in_maps = []
    for c in range(cores):
        m = dict(sh)
        m.update(per_core[c])
        in_maps.append(m)
    res = run_bass_kernel_spmd(nc, in_maps, core_ids=list(range(cores)))
    return res.results, B


def kernel(**inputs):
    results, B = _run(inputs, None)
    f = np.float32
    y_prompt = np.zeros((8, SEQ, D), f)
    y_sample = np.zeros((32, 1, D), f)
    p_lru_h = np.zeros((2, 8, D), f)
    p_lru_conv = np.zeros((2, 8, 3, D), f)
    p_dn_S = np.zeros((1, 8, 8, 128, 128), f)
    p_dn_conv = np.zeros((1, 8, 3, 3072), f)
    p_ckv = np.zeros((1, 8, TP, 256), f)
    p_kpe = np.zeros((1, 8, TP, 64), f)
    s_lru_h = np.zeros((2, 32, D), f)
    s_lru_conv = np.zeros((2, 32, 3, D), f)
    s_dn_S = np.zeros((1, 32, 8, 128, 128), f)
    s_dn_conv = np.zeros((1, 32, 3, 3072), f)
    s_ckv = np.zeros((1, 32, 1, 256), f)
    s_kpe = np.zeros((1, 32, 1, 64), f)
    for c in range(NCORES):
        r = results[c]
        y = r["yT"].transpose(2, 1, 0).reshape(T, D)
        y_prompt[c] = y[NMETA:TP]
        y_sample[NS * c:NS * (c + 1), 0] = y[TP:]
        sm = r["small"]
        for j in range(2):
            i, n = B.small_idx[f"p_lru_h{j}"]
            p_lru_h[j, c] = sm[:, i:i + n].T.reshape(D)
            i, n = B.small_idx[f"p_lru_conv{j}"]
            p_lru_conv[j, c] = sm[:, i:i + n].reshape(128, 8, 3).transpose(2, 1, 0).reshape(3, D)
            i, n = B.small_idx[f"s_lru_h{j}"]
            s_lru_h[j, NS * c:NS * (c + 1)] = sm[:, i:i + n].reshape(128, 8, NS).transpose(2, 1, 0).reshape(NS, D)
            i, n = B.small_idx[f"s_lru_conv{j}"]
            s_lru_conv[j, NS * c:NS * (c + 1)] = sm[:, i:i + n].reshape(128, 8, NS, 3).transpose(2, 3, 1, 0).reshape(NS, 3, D)
        s2 = r["small2"]
        p_dn_conv[0, c] = s2[:, 0:72].reshape(128, 24, 3).transpose(2, 1, 0).reshape(3, 3072)
        s_dn_conv[0, NS * c:NS * (c + 1)] = s2[:, 72:360].reshape(128, 24, NS, 3).transpose(2, 3, 1, 0).reshape(NS, 3, 3072)
        oS = r["o_dn_S"]
        p_dn_S[0, c] = oS[0:8]
        s_dn_S[0, NS * c:NS * (c + 1)] = oS[8:].reshape(NS, 8, 128, 128)
        kv = r["p_kv"]
        p_ckv[0, c] = kv[:TP, :256]
        p_kpe[0, c] = kv[:TP, 256:]
        s_ckv[0, NS * c:NS * (c + 1), 0] = kv[TP:, :256]
        s_kpe[0, NS * c:NS * (c + 1), 0] = kv[TP:, 256:]
    return (y_prompt, y_sample, p_lru_h, p_lru_conv, p_dn_S, p_dn_conv, p_ckv, p_kpe,
            s_lru_h, s_lru_conv, s_dn_S, s_dn_conv, s_ckv, s_kpe)
```

```python
import bisect
import os
from contextlib import ExitStack

import numpy as np
import concourse.bass as bass
import concourse.mybir as mybir
from concourse.bass_utils import run_bass_kernel_spmd

F32 = mybir.dt.float32
BF16 = mybir.dt.bfloat16
I32 = mybir.dt.int32
AF = mybir.ActivationFunctionType
ALU = mybir.AluOpType
AX = mybir.AxisListType

NCORES = 8
D = 1024
KC = 8
SEQ = 2048
NMETA = 16
TP = SEQ + NMETA
NS = 4
T = TP + NS
DFF = 2816
FC = DFF // 128
NPAGES = 128
PAGE = 128
NPOOL = 5120
EPS = 1e-6
MLA_SCALE = (128 + 64) ** -0.5
TT = [(0, 512), (512, 512), (1024, 512), (1536, 512), (2048, 20)]
WSLOT = 2048


class _Op:
    __slots__ = ("eng", "fn", "deps", "dma_sem", "dma_val", "idx", "milestone", "mval", "waits", "dma_deps")


class _IMap:
    def __init__(self, size):
        self.b = [0, size]
        self.r = [[None, {}]]

    def _split(self, x):
        i = bisect.bisect_left(self.b, x)
        if self.b[i] == x:
            return i
        w, rd = self.r[i - 1]
        self.b.insert(i, x)
        self.r.insert(i, [w, dict(rd)])
        return i

    def read(self, lo, hi, op, key, deps):
        i = self._split(lo)
        j = self._split(hi)
        for k in range(i, j):
            rec = self.r[k]
            if rec[0] is not None:
                deps.add(rec[0])
            rec[1][key] = op

    def write(self, lo, hi, op, deps):
        i = self._split(lo)
        j = self._split(hi)
        for k in range(i, j):
            rec = self.r[k]
            if rec[0] is not None:
                deps.add(rec[0])
            deps.update(rec[1].values())
        self.b[i:j + 1] = [lo, hi]
        self.r[i:j] = [[op, {}]]


class Sched:
    ENGS = ("pe", "act", "dve", "pool", "sp")

    def __init__(self, nc):
        self.nc = nc
        self.ops = {e: [] for e in self.ENGS}
        self.maps = {"SB": _IMap(1 << 20), "PSUM": _IMap(1 << 16)}
        self.dma_cnt = {}
        self.total_sems = set()
        self.bases = {}
        self.sb_ptr = (nc.sbuf_base + 63) // 64 * 64
        self.sb_top = nc.sbuf_top
        self.nalloc = 0

    def sb(self, name, shape, dtype):
        esz = 2 if dtype == BF16 else 4
        n = 1
        for s in shape[1:]:
            n *= s
        nbytes = (n * esz + 63) // 64 * 64
        off = self.sb_ptr
        self.sb_ptr += nbytes
        assert self.sb_ptr <= self.sb_top, f"SBUF overflow at {name}: {self.sb_ptr} > {self.sb_top}"
        self.nalloc += 1
        t = self.nc.alloc_sbuf_tensor_at(f"{name}_{self.nalloc}", list(shape), dtype, offset=off)
        self.bases[t.name] = off
        return t

    def mark(self):
        return self.sb_ptr

    def release(self, m):
        self.sb_ptr = m

    def _range(self, ap):
        sp = str(ap.space)
        if "SB" in sp:
            m = self.maps["SB"]
        elif "PSUM" in sp:
            m = self.maps["PSUM"]
        else:
            return None
        esz = 2 if ap.dtype == BF16 else 4
        pat = ap.ap
        pstride = pat[0][0]
        off = ap.offset % pstride if pstride > 0 else ap.offset
        ext = 1
        for st, cnt in pat[1:]:
            ext += (cnt - 1) * abs(st)
        base = self.bases.get(ap.tensor.name, 0)
        lo = base + off * esz
        hi = lo + ext * esz
        if m is self.maps["PSUM"]:
            lo = lo // 2048 * 2048
            hi = (hi + 2047) // 2048 * 2048
        return m, lo, hi

    def rec(self, eng, fn, reads=(), writes=(), dma_sem=None):
        op = _Op()
        op.eng = eng
        op.fn = fn
        op.dma_sem = dma_sem
        op.milestone = False
        op.mval = 0
        key = eng if dma_sem is None else ("dma", dma_sem)
        deps = set()
        for ap in reads:
            if ap is None or isinstance(ap, (int, float)):
                continue
            r = self._range(ap)
            if r:
                if r[0] is self.maps["PSUM"]:
                    r[0].write(r[1], r[2], op, deps)
                else:
                    r[0].read(r[1], r[2], op, key, deps)
        for ap in writes:
            r = self._range(ap)
            if r:
                r[0].write(r[1], r[2], op, deps)
        deps.discard(op)
        op.deps = []
        op.dma_deps = {}
        for d in deps:
            if d.dma_sem is not None:
                s = d.dma_sem
                v = self.dma_cnt[s]
                if op.dma_deps.get(s, 0) < v:
                    op.dma_deps[s] = v
            else:
                op.deps.append(d)
        if dma_sem is not None:
            self.dma_cnt[dma_sem] = self.dma_cnt.get(dma_sem, 0) + 16
            op.dma_val = self.dma_cnt[dma_sem]
        op.idx = len(self.ops[eng])
        self.ops[eng].append(op)
        return op

    def mm(self, out, lhsT, rhs, start=True, stop=True):
        return self.rec("pe", lambda e: e.matmul(out, lhsT=lhsT, rhs=rhs, start=start, stop=stop),
                        [lhsT, rhs], [out])

    def tr(self, out, in_, ident):
        return self.rec("pe", lambda e: e.transpose(out=out, in_=in_, identity=ident), [in_, ident], [out])

    def act(self, out, in_, func, bias=None, scale=1.0, accum_out=None):
        kw = {}
        if bias is not None:
            kw["bias"] = bias
        if accum_out is not None:
            kw["accum_out"] = accum_out
        w = [out] + ([accum_out] if accum_out is not None else [])
        return self.rec("act", lambda e: e.activation(out=out, in_=in_, func=func, scale=scale, **kw),
                        [in_, bias, scale], w)

    def tt(self, eng, out, in0, in1, op):
        return self.rec(eng, lambda e: e.tensor_tensor(out=out, in0=in0, in1=in1, op=op), [in0, in1], [out])

    def ts(self, eng, out, in0, s1, op0, s2=None, op1=None, accum_out=None):
        kw = {}
        if op1 is not None:
            kw["op1"] = op1
        if accum_out is not None:
            kw["accum_out"] = accum_out
        w = [out] + ([accum_out] if accum_out is not None else [])
        return self.rec(eng, lambda e: e.tensor_scalar(out=out, in0=in0, scalar1=s1, scalar2=s2, op0=op0, **kw),
                        [in0, s1, s2], w)

    def stt(self, out, in0, scalar, in1, op0, op1, eng="dve"):
        return self.rec(eng, lambda e: e.scalar_tensor_tensor(out=out, in0=in0, scalar=scalar, in1=in1,
                                                              op0=op0, op1=op1), [in0, scalar, in1], [out])

    def copy(self, eng, out, in_):
        if eng == "act":
            return self.rec("act", lambda e: e.copy(out=out, in_=in_), [in_], [out])
        return self.rec(eng, lambda e: e.tensor_copy(out=out, in_=in_), [in_], [out])

    def memset(self, eng, ap, val):
        return self.rec(eng, lambda e: e.memset(ap, val), [], [ap])

    def recip(self, out, in_):
        return self.rec("dve", lambda e: e.reciprocal(out=out, in_=in_), [in_], [out])

    def scan(self, out, d0, d1, initial, op0=ALU.mult, op1=ALU.add):
        return self.rec("dve", lambda e: e.tensor_tensor_scan(out=out, data0=d0, data1=d1, initial=initial,
                                                              op0=op0, op1=op1), [d0, d1, initial], [out])

    def reduce(self, out, in_, op, axis=AX.X):
        return self.rec("dve", lambda e: e.tensor_reduce(out=out, in_=in_, axis=axis, op=op), [in_], [out])

    def dma(self, out, in_, sem, eng="sp"):
        return self.rec(eng, lambda e: e.dma_start(out=out, in_=in_), [in_], [out], dma_sem=sem)

    def gather(self, out, in_, idx_ap, sem):
        return self.rec("pool", lambda e: e.indirect_dma_start(
            out=out, out_offset=None, in_=in_, in_offset=bass.IndirectOffsetOnAxis(ap=idx_ap, axis=0)),
            [idx_ap], [out], dma_sem=sem)

    def emit(self):
        nc = self.nc
        ops = self.ops
        for e in self.ENGS:
            seen = {f: -1 for f in self.ENGS}
            seen_dma = {}
            for op in ops[e]:
                keep = {}
                for d in op.deps:
                    f = d.eng
                    if f == e and e in ("pe", "sp"):
                        continue
                    if d.idx > seen[f] and d.idx > keep.get(f, (-1, None))[0]:
                        keep[f] = (d.idx, d)
                op.waits = []
                for f, (i, d) in keep.items():
                    seen[f] = i
                    d.milestone = True
                    op.waits.append(d)
                dw = []
                for s, v in op.dma_deps.items():
                    if s in self.total_sems:
                        v = -1
                    if seen_dma.get(s, 0) < v or v == -1:
                        if v == -1 and seen_dma.get(s, 0) == -1:
                            continue
                        seen_dma[s] = v
                        dw.append((s, v))
                op.dma_deps = dw
        for e in self.ENGS:
            c = 0
            for op in ops[e]:
                if op.milestone:
                    c += 1
                    op.mval = c
        self.nmil = {e: sum(1 for o in ops[e] if o.milestone) for e in self.ENGS}
        with ExitStack() as st:
            esem = {e: st.enter_context(nc.semaphore(f"e_{e}")) for e in self.ENGS}
            dsem = {s: st.enter_context(nc.semaphore(f"d_{s}")) for s in self.dma_cnt}
            block = st.enter_context(nc.Block())

            def run(e, eng):
                for op in ops[e]:
                    for d in op.waits:
                        eng.wait_ge(esem[d.eng], d.mval)
                    for s, v in op.dma_deps:
                        eng.wait_ge(dsem[s], self.dma_cnt[s] if v == -1 else v)
                    ins = op.fn(eng)
                    if op.dma_sem is not None:
                        ins.then_inc(dsem[op.dma_sem], 16)
                    elif op.milestone:
                        ins.then_inc(esem[e], 1)
                if e == "sp":
                    for s, v in self.dma_cnt.items():
                        eng.wait_ge(dsem[s], v)

            @block.tensor
            def _(eng):
                run("pe", eng)

            @block.scalar
            def _(eng):
                run("act", eng)

            @block.vector
            def _(eng):
                run("dve", eng)

            @block.gpsimd
            def _(eng):
                run("pool", eng)

            @block.sync
            def _(eng):
                run("sp", eng)


def _units_proj(W, gf):
    K, N = W.shape
    kc = K // 128
    return np.ascontiguousarray(W.reshape(kc, 128, N // gf, gf).transpose(2, 1, 0, 3).reshape(N // gf, 128, kc * gf))


def _cols(v):
    v = np.asarray(v, np.float32).reshape(-1, 128)
    return np.ascontiguousarray(v.T)


class _ColPack:
    def __init__(self):
        self.parts = []
        self.n = 0
        self.idx = {}

    def add(self, name, arr):
        arr = np.asarray(arr, np.float32)
        assert arr.shape[0] == 128
        self.idx[name] = self.n
        self.parts.append(arr)
        self.n += arr.shape[1]

    def build(self):
        return np.ascontiguousarray(np.concatenate(self.parts, axis=1))


def _prep_shared(inp):
    sh = {}
    cp = _ColPack()
    for i in range(4):
        cp.add(f"nmix{i}", _cols(inp["norm_mix"][i]))
        cp.add(f"nffn{i}", _cols(inp["norm_ffn"][i]))
    cp.add("nfinal", _cols(inp["norm_final"]))
    for j in range(2):
        for k in range(4):
            cp.add(f"lru_cw{j}_{k}", _cols(inp["lru_conv_w"][j, k]))
        cp.add(f"lru_cb{j}", _cols(inp["lru_conv_b"][j]))
        cp.add(f"lru_ba{j}", _cols(inp["lru_b_a"][j]))
        cp.add(f"lru_bi{j}", _cols(inp["lru_b_i"][j]))
        cp.add(f"lru_lam{j}", _cols(inp["lru_lambda"][j]))
        w_in = inp["lru_w_in"][j]
        u = []
        for n in range(4):
            u.append(_units_proj(w_in[:, n * 256:(n + 1) * 256], 256)[0])
            u.append(_units_proj(w_in[:, 1024 + n * 256:1024 + (n + 1) * 256], 256)[0])
        sh[f"lru_win{j}"] = np.stack(u)
        wa, wi = inp["lru_w_a"][j], inp["lru_w_i"][j]
        g = []
        for n in range(4):
            a = _units_proj(wa[n], 256)[0]
            b = _units_proj(wi[n], 256)[0]
            g.append(np.concatenate([a, b], axis=1))
        sh[f"lru_wg{j}"] = np.stack(g)
        wo = inp["lru_w_out"][j]
        sh[f"lru_wout{j}"] = np.stack([_units_proj(wo[n * 256:(n + 1) * 256], 1024)[0] for n in range(4)])
    for i in range(4):
        wgu = inp["ffn_w_gu"][i]
        g = _units_proj(wgu[:, :DFF], 128)
        u = _units_proj(wgu[:, DFF:], 128)
        sh[f"ffn_gu{i}"] = np.ascontiguousarray(
            np.stack([g.reshape(FC, 128, 8, 128), u.reshape(FC, 128, 8, 128)], axis=3).reshape(FC, 128, 2048))
        wd = inp["ffn_w_down"][i]
        hv = []
        for half in range(2):
            hv.append(_units_proj(wd[half * 1408:(half + 1) * 1408], 128))
        sh[f"ffn_dn{i}"] = np.ascontiguousarray(np.stack(hv).reshape(16, 128, 1408))
    _prep_mla(inp, sh, cp)
    _prep_dn(inp, sh, cp)
    sh["cols"] = cp.build()
    sh["_colidx"] = cp.idx
    sh["ones_bf"] = np.ones((128, 128), np.float32)
    sh["ident"] = np.eye(128, dtype=np.float32)
    return sh


def _prep_mla(inp, sh, cp):
    cp.add("mla_qn", _cols(inp["mla_q_norm"][0]))
    cp.add("mla_kvn", _cols(inp["mla_kv_norm"][0]))
    wdkv = inp["mla_w_dkv"][0]
    sh["mla_dkv_c"] = _units_proj(wdkv[:, :256], 256)
    perm = np.concatenate([np.arange(32, 64), np.arange(0, 32)])
    kr = np.concatenate([wdkv[:, 256:320], wdkv[:, 256 + perm]], axis=1)
    sh["mla_dkv_r"] = _units_proj(kr, 128)
    sh["mla_dq"] = _units_proj(inp["mla_w_dq"][0], 256)
    wuq = inp["mla_w_uq"][0].reshape(512, 8, 192)
    wuk = inp["mla_w_uk"][0]
    wuv = inp["mla_w_uv"][0]
    wo = inp["mla_w_o"][0]
    u1, u2 = [], []
    for h in range(8):
        q = np.concatenate([wuq[:, h, :128], wuq[:, h, 128:192], wuq[:, h, 128 + perm]], axis=1)
        a = _units_proj(q, 256)[0]
        b = np.ascontiguousarray(wuk[:, h, :].T)
        u1.append(np.concatenate([a, b], axis=1))
        v = _units_proj(wuv[:, h, :], 128)[0]
        o = wo[h * 128:(h + 1) * 128, :]
        u2.append(np.concatenate([v, o], axis=1))
    sh["mla_u1"] = np.stack(u1)
    sh["mla_u2"] = np.stack(u2)
    half = 32
    freqs = (10000.0 ** (-np.arange(half, dtype=np.float32) / half)).astype(np.float32)
    pos = np.concatenate([np.arange(TP), np.full(NS, NPAGES * PAGE)]).astype(np.float32)
    ang = pos[None, :] * freqs[:, None]
    c, sn = np.cos(ang).astype(np.float32), np.sin(ang).astype(np.float32)
    rope = np.stack([np.concatenate([c, c], axis=0), np.concatenate([-sn, sn], axis=0)], axis=1)
    sh["rope"] = np.ascontiguousarray(rope.astype(np.float32))
    sh["tri"] = np.triu(np.ones((128, 128), np.float32))
    pool = np.concatenate([inp["cache_mla_ckv"][0], inp["cache_mla_kpe"][0]], axis=-1)
    sh["poolkv"] = pool.reshape(NPOOL * 32, 4 * 320)


def _prep_dn(inp, sh, cp):
    w = inp["dn_w_in"][0]
    qk, vz = [], []
    for h in range(8):
        qk.append(_units_proj(np.concatenate([w[:, h * 128:(h + 1) * 128], w[:, 1024 + h * 128:1024 + (h + 1) * 128]], axis=1), 256)[0])
        vz.append(_units_proj(np.concatenate([w[:, 2048 + h * 128:2048 + (h + 1) * 128], w[:, 3072 + h * 128:3072 + (h + 1) * 128]], axis=1), 256)[0])
    sh["dn_qk"] = np.stack(qk)
    sh["dn_vz"] = np.stack(vz)
    sh["dn_ba"] = _units_proj(w[:, 4096:4112], 16)
    for q in range(4):
        cp.add(f"dn_cw{q}", _cols(inp["dn_conv_w"][0, q]))
    cp.add("dn_norm", _cols(inp["dn_norm"][0]))
    pad = np.zeros((128, 2), np.float32)
    pad[:8, 0] = inp["dn_a_log"][0]
    pad[:8, 1] = inp["dn_dt_bias"][0]
    cp.add("dn_ab", pad)
    wo = inp["dn_w_out"][0]
    sh["dn_wo"] = np.ascontiguousarray(wo.reshape(8, 128, 1024))
    sel = np.zeros((8, 8, 128), np.float32)
    for h in range(8):
        sel[h, h, :] = 1.0
    sh["dn_sel"] = sel.reshape(8, 1024)
    mask = np.ones((8, T), np.float32)
    mask[:, 0] = 0.0
    mask[:, 16:TP:64] = 0.0
    mask[:, TP:] = 0.0
    sh["dn_mask"] = mask
    r = np.arange(64)[:, None]
    c = np.arange(64)[None, :]
    mmax = np.where(c < r, 0.0, 30000.0).astype(np.float32)
    mmin = np.where(c >= r, 0.0, -30000.0).astype(np.float32)
    sh["dn_mm"] = np.ascontiguousarray(np.concatenate([mmax, mmin], axis=1))


def _prep_core(inp, c):
    x_full = np.concatenate([inp["meta_tokens"], inp["x_prompt"][c], inp["x_sample"][NS * c:NS * (c + 1), 0]], axis=0)
    pc = {}
    pc["xT"] = np.ascontiguousarray(x_full.reshape(T, KC, 128).transpose(2, 1, 0))
    lh = inp["state_lru_h"][:, NS * c:NS * (c + 1)]
    pc["s_lru_h"] = np.ascontiguousarray(lh.reshape(2, NS, KC, 128).transpose(3, 0, 2, 1))
    lc = inp["state_lru_conv"][:, NS * c:NS * (c + 1)]
    pc["s_lru_conv"] = np.ascontiguousarray(lc.reshape(2, NS, 3, KC, 128).transpose(4, 0, 3, 1, 2))
    pc["s_dn_S"] = np.ascontiguousarray(inp["state_dn_S"][0, NS * c:NS * (c + 1)].reshape(NS * 8, 128, 128))
    dc = inp["state_dn_conv"][0, NS * c:NS * (c + 1)]
    pc["s_dn_conv"] = np.ascontiguousarray(dc.reshape(NS, 3, 24, 128).transpose(3, 2, 0, 1))
    pc["pt"] = np.ascontiguousarray(inp["page_table"][NS * c:NS * (c + 1)].T.astype(np.int32))
    return pc


class Builder:
    def __init__(self, nc, S, shapes, colidx, stop_after=None):
        self.nc = nc
        self.S = S
        self.colidx = colidx
        self.stop_after = stop_after
        self.dram = {}
        for name, shp in shapes.items():
            self.dram[name] = nc.dram_tensor(name, list(shp), I32 if name == "pt" else F32, kind="ExternalInput").ap()
        self.ps = nc.alloc_psum_tensor("ps", [128, 4096], F32)
        self.ps_next = 0
        self.wq = []
        self.wi = 0
        self.outs = {}

    def out(self, name, shape):
        ap = self.nc.dram_tensor(name, list(shape), F32, kind="ExternalOutput").ap()
        self.outs[name] = ap
        return ap

    def bank(self):
        b = self.ps_next
        self.ps_next = (self.ps_next + 1) % 4
        return self.ps[:, b * 512:(b + 1) * 512]

    def col(self, name, k=0, n=1):
        i = self.colidx[name] + k
        return self.cols[:, i:i + n]

    def wload(self, name, u, nel):
        S = self.S
        slot = self.wi % self.nws
        ss = self.wi % self.nst
        self.wi += 1
        stg = self.wstage[ss]
        wb = self.wbf[slot]
        src = self.dram[name][u]
        S.dma(stg[:, 0:nel], src, sem=f"w{ss}")
        S.copy("pool", wb[:, 0:nel], stg[:, 0:nel])
        return wb

    def run_units(self, units, depth=2):
        loaded = []
        n = len(units)
        for i in range(n + depth):
            if i < n:
                nm, u, nel, _ = units[i]
                loaded.append(self.wload(nm, u, nel))
            j = i - depth
            if j >= 0:
                units[j][3](loaded[j])

    def setup(self):
        S = self.S
        nc = self.nc
        ncol = self.dram["cols"].shape[1]
        self.cols = S.sb("cols", [128, ncol], F32)
        S.dma(self.cols[:], self.dram["cols"], sem="init")
        S.total_sems.add("init")
        self.ones_f = S.sb("ones_f", [128, 128], F32)
        self.ident_f = S.sb("ident_f", [128, 128], F32)
        S.dma(self.ones_f[:], self.dram["ones_bf"], sem="init")
        S.dma(self.ident_f[:], self.dram["ident"], sem="init")
        self.ones_b = S.sb("ones_b", [128, 128], BF16)
        self.ident_b = S.sb("ident_b", [128, 128], BF16)
        S.copy("pool", self.ones_b[:], self.ones_f[:])
        S.copy("pool", self.ident_b[:], self.ident_f[:])
        self.x = [S.sb(f"x{k}", [128, T], F32) for k in range(KC)]
        for k in range(KC):
            S.dma(self.x[k][:], self.dram["xT"][:, k, :], sem="init")
        self.xn = [S.sb(f"xn{k}", [128, T], BF16) for k in range(KC)]
        self.nws = 3
        self.nst = 2
        self.wstage = [S.sb(f"wst{i}", [128, WSLOT], F32) for i in range(self.nst)]
        self.wbf = [S.sb(f"wbf{i}", [128, WSLOT], BF16) for i in range(self.nws)]
        self.sq = [S.sb(f"sq{i}", [128, 512], BF16) for i in range(2)]
        self.rstd = [S.sb(f"rstd{i}", [128, 512], F32) for i in range(2)]
        self.small = S.sb("small", [128, 320], F32)
        self.small_n = 0
        self.small_idx = {}

    def small_alloc(self, name, n):
        i = self.small_n
        self.small_idx[name] = (i, n)
        self.small_n += n
        assert self.small_n <= 320
        return self.small[:, i:i + n]

    def rmsnorm_stats(self, ti):
        S = self.S
        t0, n = TT[ti]
        acc = self.bank()
        for k in range(KC):
            sq = self.sq[k % 2]
            S.act(sq[:, 0:n], self.x[k][:, t0:t0 + n], AF.Square)
            S.mm(acc[:, 0:n], self.ones_b[:], sq[:, 0:n], start=(k == 0), stop=(k == KC - 1))
        r = self.rstd[ti % 2]
        S.act(r[:, 0:n], acc[:, 0:n], AF.Sqrt, bias=self.eps_col[:, 0:1], scale=1.0 / D)
        S.recip(r[:, 0:n], r[:, 0:n])
        return r

    def rmsnorm_to_xn(self, gname):
        S = self.S
        for ti, (t0, n) in enumerate(TT):
            r = self.rmsnorm_stats(ti)
            for k in range(KC):
                S.stt(self.xn[k][:, t0:t0 + n], self.x[k][:, t0:t0 + n], self.col(gname, k), r[:, 0:n],
                      ALU.mult, ALU.mult)

    def proj_chunk(self, wb_lhsT, rhs_list, evac):
        S = self.S
        nk = len(rhs_list)
        for ti, (t0, n) in enumerate(TT):
            acc = self.bank()
            for k in range(nk):
                S.mm(acc[:, 0:n], wb_lhsT(k), rhs_list[k][:, t0:t0 + n], start=(k == 0), stop=(k == nk - 1))
            evac(ti, t0, n, acc)

    def ffn(self, li):
        S = self.S
        self.rmsnorm_to_xn(f"nffn{li}")
        m = S.mark()
        h = [S.sb(f"h{j}", [128, T], BF16) for j in range(11)]
        sg = [S.sb(f"sg{j}", [128, 512], BF16) for j in range(2)]
        for half in range(2):
            units = []
            for jj in range(11):
                j = half * 11 + jj

                def fn(wb, jj=jj):
                    w4 = wb[:, 0:2048].rearrange("p (k g f) -> p k g f", k=8, g=2)
                    for ti, (t0, n) in enumerate(TT):
                        pg = self.bank()
                        pu = self.bank()
                        for k in range(KC):
                            S.mm(pg[:, 0:n], w4[:, k, 0, :], self.xn[k][:, t0:t0 + n], start=(k == 0), stop=(k == KC - 1))
                        for k in range(KC):
                            S.mm(pu[:, 0:n], w4[:, k, 1, :], self.xn[k][:, t0:t0 + n], start=(k == 0), stop=(k == KC - 1))
                        s = sg[ti % 2]
                        S.act(s[:, 0:n], pg[:, 0:n], AF.Silu)
                        S.tt("dve", h[jj][:, t0:t0 + n], pu[:, 0:n], s[:, 0:n], ALU.mult)
                units.append((f"ffn_gu{li}", j, 2048, fn))
            for fo in range(KC):
                def fn2(wb, fo=fo):
                    w3 = wb[:, 0:1408].rearrange("p (k f) -> p k f", k=11)

                    def ev(ti, t0, n, acc):
                        S.tt("dve", self.x[fo][:, t0:t0 + n], acc[:, 0:n], self.x[fo][:, t0:t0 + n], ALU.add)
                    self.proj_chunk(lambda k: w3[:, k, :], h, ev)
                units.append((f"ffn_dn{li}", half * 8 + fo, 1408, fn2))
            self.run_units(units)
        S.release(m)

    def lru(self, li, j):
        S = self.S
        self.rmsnorm_to_xn(f"nmix{li}")
        m = S.mark()
        HALF = [(0, 1024, (0, 1)), (1024, T - 1024, (2, 3, 4))]
        HN = T - 1024
        cA = S.sb("cA", [128, 8], F32)
        ncA = S.sb("ncA", [128, 8], F32)
        lam = self.col(f"lru_lam{j}", 0, 8)
        S.act(cA[:], lam, AF.Exp, scale=-1.0)
        S.act(cA[:], cA[:], AF.Ln, bias=self.one_col[:, 0:1])
        S.ts("dve", ncA[:], cA[:], 8.0, ALU.mult)
        S.ts("dve", cA[:], cA[:], -8.0, ALU.mult)
        hg = [S.sb(f"hg{k}", [128, T], BF16) for k in range(2)]
        gate = [S.sb(f"gate{k}", [128, T], BF16) for k in range(2)]
        xx = [S.sb(f"xx{k}", [128, TP + 3], F32) for k in range(2)]
        xs = [S.sb(f"xs{k}", [128, NS, 4], F32) for k in range(2)]
        xcb = [S.sb(f"xcb{k}", [128, T], BF16) for k in range(2)]
        ctmp = S.sb("ctmp", [128, T], F32)
        ra = S.sb("ra", [128, HN], F32)
        ri = S.sb("ri", [128, HN], F32)
        av = S.sb("av", [128, HN], F32)
        tmp = S.sb("tmp", [128, HN], F32)
        carry = S.sb("carry", [128, 1], F32)
        p_h = self.small_alloc(f"p_lru_h{j}", 8)
        p_cv = self.small_alloc(f"p_lru_conv{j}", 24)
        s_h = self.small_alloc(f"s_lru_h{j}", 32)
        s_cv = self.small_alloc(f"s_lru_conv{j}", 96)
        s_cv4 = s_cv.rearrange("p (k b j) -> p k b j", k=8, b=NS)
        s_h3 = s_h.rearrange("p (k b) -> p k b", k=8)
        st_h = S.sb("st_h", [128, 8, NS], F32)
        S.dma(st_h[:], self.dram["s_lru_h"][:, j], sem=f"st{j}")
        st_c = S.sb("st_c", [128, 8, NS, 3], F32)
        S.dma(st_c[:], self.dram["s_lru_conv"][:, j], sem=f"st{j}")
        for k in range(2):
            S.memset("pool", xx[k][:, 0:3], 0.0)

        units = []
        for n in range(4):
            def f_gate(wb, n=n):
                w3 = wb[:, 0:2048].rearrange("p (k f) -> p k f", k=8)
                for c in range(2):
                    def ev(ti, t0, nn, acc, c=c):
                        S.act(gate[c][:, t0:t0 + nn], acc[:, 0:nn], AF.Gelu)
                    self.proj_chunk(lambda k, c=c: w3[:, k, c * 128:(c + 1) * 128], self.xn, ev)
            units.append((f"lru_win{j}", 2 * n, 2048, f_gate))

            def f_x(wb, n=n):
                w3 = wb[:, 0:2048].rearrange("p (k f) -> p k f", k=8)
                for c in range(2):
                    kc = 2 * n + c

                    def ev(ti, t0, nn, acc, c=c, kc=kc):
                        if t0 + nn <= TP:
                            S.copy("act", xx[c][:, 3 + t0:3 + t0 + nn], acc[:, 0:nn])
                        else:
                            npz = TP - t0
                            S.copy("act", xx[c][:, 3 + t0:3 + TP], acc[:, 0:npz])
                            S.copy("act", xs[c][:, :, 3], acc[:, npz:npz + NS])
                    self.proj_chunk(lambda k, c=c: w3[:, k, c * 128:(c + 1) * 128], self.xn, ev)
                    S.copy("pool", xs[c][:, :, 0:3], st_c[:, kc, :, :])
                    S.copy("pool", p_cv[:, kc * 3:(kc + 1) * 3], xx[c][:, TP:TP + 3])
                    S.copy("pool", s_cv4[:, kc, :, :], xs[c][:, :, 1:4])
                    cw = lambda q, kc=kc: self.col(f"lru_cw{j}_{q}", kc)
                    cb = self.col(f"lru_cb{j}", kc)
                    S.ts("dve", ctmp[:, 0:TP], xx[c][:, 0:TP], cw(0), ALU.mult, cb, ALU.add)
                    for q in range(1, 3):
                        S.stt(ctmp[:, 0:TP], xx[c][:, q:q + TP], cw(q), ctmp[:, 0:TP], ALU.mult, ALU.add)
                    S.stt(xcb[c][:, 0:TP], xx[c][:, 3:3 + TP], cw(3), ctmp[:, 0:TP], ALU.mult, ALU.add)
                    S.ts("dve", ctmp[:, TP:T], xs[c][:, :, 0], cw(0), ALU.mult, cb, ALU.add)
                    for q in range(1, 3):
                        S.stt(ctmp[:, TP:T], xs[c][:, :, q], cw(q), ctmp[:, TP:T], ALU.mult, ALU.add)
                    S.stt(xcb[c][:, TP:T], xs[c][:, :, 3], cw(3), ctmp[:, TP:T], ALU.mult, ALU.add)
            units.append((f"lru_win{j}", 2 * n + 1, 2048, f_x))

            def f_g(wb, n=n):
                w4 = wb[:, 0:1024].rearrange("p (g k f) -> p g k f", g=2, k=2)
                for c in range(2):
                    kc = 2 * n + c
                    for (h0, hn, tiles) in HALF:
                        for ti in tiles:
                            t0, nn = TT[ti]
                            pa = self.bank()
                            pi = self.bank()
                            for k in range(2):
                                S.mm(pa[:, 0:nn], w4[:, 0, k, c * 128:(c + 1) * 128], xcb[k][:, t0:t0 + nn],
                                     start=(k == 0), stop=(k == 1))
                            for k in range(2):
                                S.mm(pi[:, 0:nn], w4[:, 1, k, c * 128:(c + 1) * 128], xcb[k][:, t0:t0 + nn],
                                     start=(k == 0), stop=(k == 1))
                            S.act(ra[:, t0 - h0:t0 - h0 + nn], pa[:, 0:nn], AF.Sigmoid, bias=self.col(f"lru_ba{j}", kc))
                            S.act(ri[:, t0 - h0:t0 - h0 + nn], pi[:, 0:nn], AF.Sigmoid, bias=self.col(f"lru_bi{j}", kc))
                        R = slice(0, hn)
                        G = slice(h0, h0 + hn)
                        S.act(av[:, R], ra[:, R], AF.Exp, scale=cA[:, kc:kc + 1])
                        S.act(tmp[:, R], ra[:, R], AF.Tanh, scale=ncA[:, kc:kc + 1])
                        S.tt("pool", ra[:, R], av[:, R], av[:, R], ALU.mult)
                        S.stt(tmp[:, R], ra[:, R], 1.0, tmp[:, R], ALU.add, ALU.mult)
                        S.act(tmp[:, R], tmp[:, R], AF.Sqrt)
                        S.tt("pool", ri[:, R], ri[:, R], xcb[c][:, G], ALU.mult)
                        S.tt("dve", ri[:, R], ri[:, R], tmp[:, R], ALU.mult)
                        if h0 == 0:
                            S.scan(tmp[:, R], av[:, R], ri[:, R], 0.0)
                            S.copy("pool", carry[:], tmp[:, hn - 1:hn])
                        else:
                            npr = TP - h0
                            S.scan(tmp[:, 0:npr], av[:, 0:npr], ri[:, 0:npr], carry[:, 0:1])
                            S.tt("dve", tmp[:, npr:hn], av[:, npr:hn], st_h[:, kc, :], ALU.mult)
                            S.tt("dve", tmp[:, npr:hn], tmp[:, npr:hn], ri[:, npr:hn], ALU.add)
                            S.copy("pool", p_h[:, kc:kc + 1], tmp[:, npr - 1:npr])
                            S.copy("pool", s_h3[:, kc, :], tmp[:, npr:hn])
                        S.tt("dve", hg[c][:, G], tmp[:, R], gate[c][:, G], ALU.mult)
            units.append((f"lru_wg{j}", n, 1024, f_g))

            def f_o(wb, n=n):
                w3 = wb[:, 0:2048].rearrange("p (k f) -> p k f", k=2)
                for fo in range(KC):
                    def ev(ti, t0, nn, acc, fo=fo):
                        S.tt("dve", self.x[fo][:, t0:t0 + nn], acc[:, 0:nn], self.x[fo][:, t0:t0 + nn], ALU.add)
                    self.proj_chunk(lambda k, fo=fo: w3[:, k, fo * 128:(fo + 1) * 128], hg, ev)
            units.append((f"lru_wout{j}", n, 2048, f_o))
        self.run_units(units)
        S.release(m)

    def rbank(self, i):
        return self.ps[:, i * 512:(i + 1) * 512]

    def rope_tile(self, dst, p_raw, p_swp, t0, n, cs, t1, t2):
        S = self.S
        S.dma(cs[:, :, 0:n], self.dram["rope"][:, :, t0:t0 + n], sem="cs")
        S.tt("dve", t1[:, 0:n], p_raw, cs[:, 0, 0:n], ALU.mult)
        S.tt("dve", t2[:, 0:n], p_swp, cs[:, 1, 0:n], ALU.mult)
        S.tt("pool", dst, t1[:, 0:n], t2[:, 0:n], ALU.add)

    def mla(self, li, j):
        S = self.S
        self.rmsnorm_to_xn(f"nmix{li}")
        xn_base = S.bases[self.xn[0].name]
        m0 = S.mark()
        ckvb = [S.sb(f"ckvb{k}", [128, T], BF16) for k in range(2)]
        kpeb = S.sb("kpeb", [64, T], BF16)
        cqb = [S.sb(f"cqb{k}", [128, T], BF16) for k in range(4)]
        qs = S.sb("qs", [128, 3, NS, 8], BF16)
        ols = S.sb("ols", [128, 2, 8, NS], BF16)
        trib = S.sb("trib", [128, 128], BF16)
        trif = S.sb("trif", [128, 128], F32)
        S.dma(trif[:], self.dram["tri"], sem="tri")
        S.copy("pool", trib[:], trif[:])
        cs = S.sb("cs", [64, 2, 512], F32)
        rt1 = S.sb("rt1", [64, 512], F32)
        rt2 = S.sb("rt2", [64, 512], F32)
        m1 = S.mark()
        kpef = S.sb("kpef", [64, T], F32)
        ckvT = [S.sb(f"ckvT{k}", [128, T], F32) for k in range(2)]

        wq = [self.wload("mla_dq", u, 2048) for u in range(2)]
        for ti, (t0, n) in enumerate(TT):
            pb = [self.bank() for _ in range(4)]
            for c4 in range(4):
                w3 = wq[c4 // 2][:, 0:2048].rearrange("p (k f) -> p k f", k=8)
                for k in range(KC):
                    S.mm(pb[c4][:, 0:n], w3[:, k, (c4 % 2) * 128:(c4 % 2 + 1) * 128], self.xn[k][:, t0:t0 + n],
                         start=(k == 0), stop=(k == KC - 1))
            acc = self.rbank(4)
            for c4 in range(4):
                sq = self.sq[c4 % 2]
                S.act(sq[:, 0:n], pb[c4][:, 0:n], AF.Square)
                S.mm(acc[:, 0:n], self.ones_b[:], sq[:, 0:n], start=(c4 == 0), stop=(c4 == 3))
            r = self.rstd[ti % 2]
            S.act(r[:, 0:n], acc[:, 0:n], AF.Sqrt, bias=self.eps_col[:, 0:1], scale=1.0 / 512)
            S.recip(r[:, 0:n], r[:, 0:n])
            for c4 in range(4):
                S.stt(cqb[c4][:, t0:t0 + n], pb[c4][:, 0:n], self.col("mla_qn", c4), r[:, 0:n], ALU.mult, ALU.mult)

        wr = self.wload("mla_dkv_r", 0, 1024)
        wr3 = wr[:, 0:1024].rearrange("p (k f) -> p k f", k=8)
        for ti, (t0, n) in enumerate(TT):
            p1 = self.bank()
            p2 = self.bank()
            for k in range(KC):
                S.mm(p1[0:64, 0:n], wr3[:, k, 0:64], self.xn[k][:, t0:t0 + n], start=(k == 0), stop=(k == KC - 1))
            for k in range(KC):
                S.mm(p2[0:64, 0:n], wr3[:, k, 64:128], self.xn[k][:, t0:t0 + n], start=(k == 0), stop=(k == KC - 1))
            self.rope_tile(kpef[:, t0:t0 + n], p1[0:64, 0:n], p2[0:64, 0:n], t0, n, cs, rt1, rt2)
        S.copy("pool", kpeb[:], kpef[:])

        wc = self.wload("mla_dkv_c", 0, 2048)
        wc3 = wc[:, 0:2048].rearrange("p (k f) -> p k f", k=8)
        for ti, (t0, n) in enumerate(TT):
            pb = [self.bank() for _ in range(2)]
            for c2 in range(2):
                for k in range(KC):
                    S.mm(pb[c2][:, 0:n], wc3[:, k, c2 * 128:(c2 + 1) * 128], self.xn[k][:, t0:t0 + n],
                         start=(k == 0), stop=(k == KC - 1))
            acc = self.rbank(4)
            for c2 in range(2):
                sq = self.sq[c2 % 2]
                S.act(sq[:, 0:n], pb[c2][:, 0:n], AF.Square)
                S.mm(acc[:, 0:n], self.ones_b[:], sq[:, 0:n], start=(c2 == 0), stop=(c2 == 1))
            r = self.rstd[ti % 2]
            S.act(r[:, 0:n], acc[:, 0:n], AF.Sqrt, bias=self.eps_col[:, 0:1], scale=1.0 / 256)
            S.recip(r[:, 0:n], r[:, 0:n])
            for c2 in range(2):
                S.stt(ckvT[c2][:, t0:t0 + n], pb[c2][:, 0:n], self.col("mla_kvn", c2), r[:, 0:n], ALU.mult, ALU.mult)
        for c2 in range(2):
            S.copy("pool", ckvb[c2][:], ckvT[c2][:])

        sv = S.sb_ptr
        S.sb_ptr = xn_base
        vtok = S.sb("vtok", [128, 17, 256], BF16)
        ostg = [S.sb(f"ostg{i}", [128, 320], F32) for i in range(2)]
        qaug = [S.sb(f"qaug{k}", [128, T], BF16) for k in range(3)]
        oh = S.sb("oh", [128, T], BF16)
        vnew = S.sb("vnew", [1, NS, 257], BF16)
        assert S.sb_ptr <= xn_base + 8 * ((T * 2 + 63) // 64 * 64), "xn overlay overflow"
        S.sb_ptr = sv

        okv = self.out("p_kv", [T, 320])
        for bi in range(17 if "T" not in os.environ.get("KSKIP", "") else 0):
            t0 = bi * 128
            n = min(128, T - t0)
            pt_ = self.bank()
            for c2 in range(2):
                S.tr(pt_[0:n, c2 * 128:(c2 + 1) * 128], ckvT[c2][:, t0:t0 + n], self.ident_f[:])
            S.tr(pt_[0:n, 256:320], kpef[:, t0:t0 + n], self.ident_f[0:64, 0:64])
            og = ostg[bi % 2]
            KS = os.environ.get("KSKIP", "")
            if "1" not in KS:
                S.copy("act", og[0:n, :], pt_[0:n, 0:320])
            if "2" not in KS:
                S.copy("dve", vtok[0:n, bi, :], pt_[0:n, 0:256])
            if "3" not in KS:
                S.dma(okv[t0:t0 + n, :], og[0:n, :], sem=f"okv{bi % 2}")
        S.memset("pool", vnew[:], 1.0)
        for b in range(NS if "V" not in os.environ.get("KSKIP", "") else 0):
            pt_ = self.bank()
            for c2 in range(2):
                S.tr(pt_[0:1, c2 * 128:(c2 + 1) * 128], ckvT[c2][:, TP + b:TP + b + 1], self.ident_f[:])
            S.copy("dve", vnew[0:1, b, 0:256], pt_[0:1, 0:256])
        S.release(m1)

        mA = S.mark()
        qn_s = S.sb("qn_s", [128, NS], BF16)
        u1 = []

        def passA(wb, h):
            wq3 = wb[:, 0:1024].rearrange("p (k f) -> p k f", k=4)
            wuk = wb[:, 1024:1280]
            pn = self.bank()
            for k in range(4):
                S.mm(pn[:, 0:NS], wq3[:, k, 0:128], cqb[k][:, TP:T], start=(k == 0), stop=(k == 3))
            S.copy("dve", qn_s[:], pn[:, 0:NS])
            p1 = self.bank()
            p2 = self.bank()
            for k in range(4):
                S.mm(p1[0:64, 0:NS], wq3[:, k, 128:192], cqb[k][:, TP:T], start=(k == 0), stop=(k == 3))
            for k in range(4):
                S.mm(p2[0:64, 0:NS], wq3[:, k, 192:256], cqb[k][:, TP:T], start=(k == 0), stop=(k == 3))
            self.rope_tile(qs[0:64, 2, :, h], p1[0:64, 0:NS], p2[0:64, 0:NS], TP, NS, cs, rt1, rt2)
            for c2 in range(2):
                pl = self.bank()
                S.mm(pl[:, 0:NS], wuk[:, c2 * 128:(c2 + 1) * 128], qn_s[:], start=True, stop=True)
                S.copy("dve", qs[:, c2, :, h], pl[:, 0:NS])
        if "A" not in os.environ.get("KSKIP", ""):
            self.run_units([("mla_u1", h, 1280, (lambda wb, h=h: passA(wb, h))) for h in range(8)])
        S.release(mA)

        if "D" not in os.environ.get("KSKIP", ""):
            self.mla_decode(qs, ols, ckvb, kpeb, vnew)
        else:
            S.memset("pool", ols[:], 0.0)

        mB = S.mark()
        qn = S.sb("qn", [128, T], BF16)
        olat = [S.sb(f"olat{k}", [128, T], BF16) for k in range(2)]
        PT = [S.sb(f"PT{i}", [128, 512], BF16) for i in range(3)]
        rden = S.sb("rden", [128, 512], F32)
        QT = [(0, 512), (512, 512), (1024, 512), (1536, 512), (2048, 16)]

        def head_q(wb, h):
            wq3 = wb[:, 0:1024].rearrange("p (k f) -> p k f", k=4)
            wuk = wb[:, 1024:1280]

            def ev(ti, t0, n, acc):
                S.copy("act", qn[:, t0:t0 + n], acc[:, 0:n])
            self.proj_chunk(lambda k: wq3[:, k, 0:128], cqb, ev)
            for ti, (t0, n) in enumerate(TT):
                p1 = self.bank()
                p2 = self.bank()
                for k in range(4):
                    S.mm(p1[0:64, 0:n], wq3[:, k, 128:192], cqb[k][:, t0:t0 + n], start=(k == 0), stop=(k == 3))
                for k in range(4):
                    S.mm(p2[0:64, 0:n], wq3[:, k, 192:256], cqb[k][:, t0:t0 + n], start=(k == 0), stop=(k == 3))
                self.rope_tile(qaug[2][0:64, t0:t0 + n], p1[0:64, 0:n], p2[0:64, 0:n], t0, n, cs, rt1, rt2)
            for c2 in range(2):
                def ev2(ti, t0, n, acc, c2=c2):
                    S.copy("act", qaug[c2][:, t0:t0 + n], acc[:, 0:n])
                self.proj_chunk(lambda k, c2=c2: wuk[:, c2 * 128:(c2 + 1) * 128], [qn], ev2)
            a0, a1, dn_ = self.rbank(4), self.rbank(5), self.rbank(6)
            pairs = []
            for (q0, qn_) in QT:
                nb = (q0 + qn_ - 1) // 128 + 1
                for jb in range(nb):
                    k0 = jb * 128
                    kn = min(128, TP - k0)
                    qs0 = max(q0, k0)
                    pairs.append(dict(q0=q0, qn=qn_, jb=jb, k0=k0, kn=kn, qs0=qs0, nc=q0 + qn_ - qs0, off=qs0 - q0,
                                      first=(jb == 0), last=(jb == nb - 1), idx=len(pairs)))

            def scores(p):
                kn, nc_, k0, qs0 = p["kn"], p["nc"], p["k0"], p["qs0"]
                sp = self.bank()
                S.mm(sp[0:kn, 0:nc_], ckvb[0][:, k0:k0 + kn], qaug[0][:, qs0:qs0 + nc_], start=True, stop=False)
                S.mm(sp[0:kn, 0:nc_], ckvb[1][:, k0:k0 + kn], qaug[1][:, qs0:qs0 + nc_], start=False, stop=False)
                S.mm(sp[0:kn, 0:nc_], kpeb[0:64, k0:k0 + kn], qaug[2][0:64, qs0:qs0 + nc_], start=False, stop=True)
                pt_ = PT[p["idx"] % len(PT)]
                S.act(pt_[0:kn, 0:nc_], sp[0:kn, 0:nc_], AF.Exp, scale=MLA_SCALE)
                if k0 >= p["q0"]:
                    dnn = min(128, nc_)
                    S.tt("pool", pt_[0:kn, 0:dnn], pt_[0:kn, 0:dnn], trib[0:kn, 0:dnn], ALU.mult)

            def pv(p):
                kn, nc_, off, jb = p["kn"], p["nc"], p["off"], p["jb"]
                pt_ = PT[p["idx"] % len(PT)]
                S.mm(a0[:, off:off + nc_], vtok[0:kn, jb, 0:128], pt_[0:kn, 0:nc_], start=p["first"], stop=p["last"])
                S.mm(a1[:, off:off + nc_], vtok[0:kn, jb, 128:256], pt_[0:kn, 0:nc_], start=p["first"], stop=p["last"])
                S.mm(dn_[:, off:off + nc_], self.ones_b[0:kn, :], pt_[0:kn, 0:nc_], start=p["first"], stop=p["last"])
                if p["last"]:
                    q0, qn_ = p["q0"], p["qn"]
                    S.recip(rden[:, 0:qn_], dn_[:, 0:qn_])
                    S.tt("dve", olat[0][:, q0:q0 + qn_], a0[:, 0:qn_], rden[:, 0:qn_], ALU.mult)
                    S.tt("dve", olat[1][:, q0:q0 + qn_], a1[:, 0:qn_], rden[:, 0:qn_], ALU.mult)
            scores(pairs[0])
            for i, p in enumerate(pairs):
                if i + 1 < len(pairs):
                    scores(pairs[i + 1])
                pv(p)
            for c2 in range(2):
                S.copy("pool", olat[c2][:, TP:T], ols[:, c2, h, :])

        def head_o(wb, h):
            wuv = wb[:, 0:256].rearrange("p (k v) -> p k v", k=2)
            wo = wb[:, 256:1280]

            def ev(ti, t0, n, acc):
                S.copy("act", oh[:, t0:t0 + n], acc[:, 0:n])
            self.proj_chunk(lambda k: wuv[:, k, :], olat, ev)
            for fo in range(KC):
                def ev2(ti, t0, n, acc, fo=fo):
                    S.tt("dve", self.x[fo][:, t0:t0 + n], acc[:, 0:n], self.x[fo][:, t0:t0 + n], ALU.add)
                self.proj_chunk(lambda k, fo=fo: wo[:, fo * 128:(fo + 1) * 128], [oh], ev2)
        units = []
        for h in range(8):
            units.append(("mla_u1", h, 1280, (lambda wb, h=h: head_q(wb, h))))
            units.append(("mla_u2", h, 1280, (lambda wb, h=h: head_o(wb, h))))
        if "B" not in os.environ.get("KSKIP", ""):
            self.run_units(units)
        S.release(m0)

    def mla_decode(self, qs, ols, ckvb, kpeb, vnew):
        S = self.S
        m = S.mark()
        NTK = 4
        NSUB = PAGE // NTK
        NBUF = 4
        ptab = S.sb("ptab", [128, NS], I32)
        S.dma(ptab[:], self.dram["pt"], sem="ptab")
        idx = S.sb("idx", [128, NS, NSUB], I32)
        for b in range(NS):
            for s_ in range(NSUB):
                S.ts("dve", idx[:, b, s_:s_ + 1], ptab[:, b:b + 1], float(NSUB), ALU.mult, float(s_), ALU.add)
        kvs = [S.sb(f"kvs{i}", [128, NTK * 320], F32) for i in range(NBUF)]
        kT = [S.sb(f"kT{i}", [128, 384], BF16) for i in range(2)]
        Vb = [S.sb(f"Vb{i}", [128, NTK, 257], BF16) for i in range(2)]
        for i in range(2):
            S.memset("pool", Vb[i][:], 1.0)
        PTd = [S.sb(f"PTd{i}", [128, NTK * 8], BF16) for i in range(2)]
        pnew = S.sb("pnew", [1, 8], BF16)
        osb = S.sb("osb", [8, 257], F32)
        rd = S.sb("rd", [8, 1], F32)
        onb = S.sb("onb", [8, 256], F32)
        accb = self.rbank(7)
        toks = []
        g = 0
        for b in range(NS):
            for s_ in range(NSUB):
                for tt_ in range(NTK):
                    toks.append((b, s_, tt_, g))
                g += 1
        pks = {}

        def start_chunk(b, s_, g):
            kv = kvs[g % NBUF]
            S.gather(kv[:], self.dram["poolkv"], idx[:, b, s_:s_ + 1], sem=f"kv{g % NBUF}")

        def vcast(g):
            kv = kvs[g % NBUF]
            S.copy("act", Vb[g % 2][:, :, 0:256], kv[:].rearrange("p (t c) -> p t c", t=NTK)[:, :, 0:256])

        def transposes(i):
            b, s_, tt_, g = toks[i]
            kv = kvs[g % NBUF]
            pk = self.bank()
            base = tt_ * 320
            S.tr(pk[:, 0:128], kv[:, base:base + 128], self.ident_f[:])
            S.tr(pk[:, 128:256], kv[:, base + 128:base + 256], self.ident_f[:])
            S.tr(pk[0:64, 256:384], kv[:, base + 256:base + 320], self.ident_f[:])
            kt = kT[i % 2]
            S.copy("dve", kt[:, 0:256], pk[:, 0:256])
            S.copy("dve", kt[0:64, 256:384], pk[0:64, 256:384])

        def qk(i):
            b, s_, tt_, g = toks[i]
            kt = kT[i % 2]
            sp = self.rbank(5 + (g % 2))
            o_ = sp[:, tt_ * 8:(tt_ + 1) * 8]
            S.mm(o_, kt[:, 0:128], qs[:, 0, b, :], start=True, stop=False)
            S.mm(o_, kt[:, 128:256], qs[:, 1, b, :], start=False, stop=False)
            S.mm(o_, kt[0:64, 256:384], qs[0:64, 2, b, :], start=False, stop=True)
            if tt_ == NTK - 1:
                S.act(PTd[g % 2][:], sp[:, 0:NTK * 8], AF.Exp, scale=MLA_SCALE)

        def pv(b, s_, g):
            for tt_ in range(NTK):
                S.mm(accb[0:8, 0:257], PTd[g % 2][:, tt_ * 8:(tt_ + 1) * 8], Vb[g % 2][:, tt_, :],
                     start=(s_ == 0 and tt_ == 0), stop=False)

        def finish(b):
            sp = self.bank()
            S.mm(sp[0:1, 0:8], ckvb[0][:, TP + b:TP + b + 1], qs[:, 0, b, :], start=True, stop=False)
            S.mm(sp[0:1, 0:8], ckvb[1][:, TP + b:TP + b + 1], qs[:, 1, b, :], start=False, stop=False)
            S.mm(sp[0:1, 0:8], kpeb[0:64, TP + b:TP + b + 1], qs[0:64, 2, b, :], start=False, stop=True)
            S.act(pnew[:], sp[0:1, 0:8], AF.Exp, scale=MLA_SCALE)
            S.mm(accb[0:8, 0:257], pnew[:], vnew[0:1, b, :], start=False, stop=True)
            S.copy("dve", osb[:], accb[0:8, 0:257])
            S.recip(rd[:], osb[:, 256:257])
            S.ts("dve", onb[:], osb[:, 0:256], rd[:, 0:1], ALU.mult)
            for c2 in range(2):
                po = self.bank()
                S.tr(po[:, 0:8], onb[:, c2 * 128:(c2 + 1) * 128], self.ident_f[0:8, 0:8])
                S.copy("dve", ols[:, c2, :, b], po[:, 0:8])

        n = len(toks)
        started = 0
        nchunks = NS * NSUB

        def ensure_started(upto):
            nonlocal started
            while started <= min(upto, nchunks - 1):
                bb, ss = divmod(started, NSUB)
                start_chunk(bb, ss, started)
                started += 1
        ensure_started(NBUF - 2)
        transposes(0)
        pending_pv = None
        for i in range(n):
            b, s_, tt_, g = toks[i]
            if tt_ == 0:
                ensure_started(g + NBUF - 2)
                vcast(g)
            if i + 1 < n:
                transposes(i + 1)
            qk(i)
            if tt_ == NTK - 1:
                if pending_pv is not None:
                    pv(*pending_pv)
                    if pending_pv[1] == NSUB - 1:
                        finish(pending_pv[0])
                pending_pv = (b, s_, g)
        pv(*pending_pv)
        finish(pending_pv[0])
        S.release(m)

    def mmf(self, out, lhsT, rhs, start=True, stop=True):
        return self.S.mm(out, lhsT, rhs, start, stop)

    def dn(self, li, j):
        S = self.S
        self.rmsnorm_to_xn(f"nmix{li}")
        m0 = S.mark()
        small2 = S.sb("small2", [128, 360], F32)
        S.memset("pool", small2[:], 0.0)
        p_cv = small2[:, 0:72].rearrange("p (k j) -> p k j", k=24)
        s_cv = small2[:, 72:360].rearrange("p (k b j) -> p k b j", k=24, b=NS)
        oS = self.out("o_dn_S", [8 + NS * 8, 128, 128])
        Gc = S.sb("Gc", [8, T], F32)
        GT = S.sb("GT", [64, 37, 8], F32)
        BT = S.sb("BT", [64, 37, 8], F32)
        sel = S.sb("sel", [8, 128], F32)
        mm_ = S.sb("mm_", [64, 128], F32)
        S.dma(mm_[:], self.dram["dn_mm"], sem="dnc")
        st_c = S.sb("dst_c", [128, 24, NS, 3], F32)
        S.dma(st_c[:], self.dram["s_dn_conv"], sem="dnc")
        chunks = [(0, 16, 4)] + [(16 + 64 * c, 64, 6) for c in range(32)] + [(TP + b, 1, 0) for b in range(NS)]
        xx = S.sb("dxx", [128, TP + 3], F32)
        xs = S.sb("dxs", [128, NS, 4], F32)
        ctmp = S.sb("dctmp", [128, T], F32)
        S.memset("pool", xx[:, 0:3], 0.0)
        qdec = S.sb("qdec", [128, T], BF16)
        qn = S.sb("dqn", [128, T], BF16)
        kn = S.sb("kn", [128, T], BF16)
        vb = S.sb("vb", [128, T], BF16)
        sz = S.sb("sz", [128, T], BF16)
        GB = S.sb("GB", [128, T], F32)
        oT = S.sb("oT", [128, T], BF16)
        Sf = S.sb("Sf", [128, 128], F32)
        Sb_ = S.sb("Sb", [128, 128], BF16)
        eg = self.rstd[1]
        wba = self.wload("dn_ba", 0, 128)
        wba3 = wba[:, 0:128].rearrange("p (k f) -> p k f", k=8)
        Ball = GB[0:8, 0:T]
        graw = ctmp[0:8, 0:T]
        sv_ = S.sb_ptr
        S.sb_ptr = S.bases[qdec.name]
        mrow_t = S.sb("mrow", [8, T], F32)
        S.sb_ptr = sv_
        mrow = mrow_t[:, :]
        S.dma(mrow, self.dram["dn_mask"], sem="dnc")
        nA = S.sb("nA", [8, 1], F32)
        ab = self.col("dn_ab", 0, 2)
        S.act(nA[:], ab[0:8, 0:1], AF.Exp)
        S.ts("dve", nA[:], nA[:], -1.0, ALU.mult)
        for ti, (t0, n) in enumerate(TT):
            pb_, pa_ = self.bank(), self.bank()
            for k in range(KC):
                S.mm(pb_[0:8, 0:n], wba3[:, k, 0:8], self.xn[k][:, t0:t0 + n], start=(k == 0), stop=(k == KC - 1))
            for k in range(KC):
                S.mm(pa_[0:8, 0:n], wba3[:, k, 8:16], self.xn[k][:, t0:t0 + n], start=(k == 0), stop=(k == KC - 1))
            S.act(Ball[:, t0:t0 + n], pb_[0:8, 0:n], AF.Sigmoid)
            S.act(graw[:, t0:t0 + n], pa_[0:8, 0:n], AF.Exp, bias=ab[0:8, 1:2])
            S.act(graw[:, t0:t0 + n], graw[:, t0:t0 + n], AF.Ln, bias=self.one_col[0:8, 0:1])
        S.ts("dve", graw, graw, nA[:, 0:1], ALU.mult)
        S.scan(Gc[:], mrow, graw, 0.0)
        for ci, (t0, C, L) in enumerate(chunks):
            pt_ = self.bank()
            S.tr(pt_[0:C, 0:8], Gc[:, t0:t0 + C], self.ident_f[0:8, 0:8])
            S.tr(pt_[0:C, 8:16], Ball[:, t0:t0 + C], self.ident_f[0:8, 0:8])
            S.copy("dve", GT[0:C, ci, :], pt_[0:C, 0:8])
            S.copy("dve", BT[0:C, ci, :], pt_[0:C, 8:16])
        sv_ = S.sb_ptr
        S.sb_ptr = S.bases[xx.name]
        F1 = S.sb("gF1", [64, 8, 64], F32)
        F2 = S.sb("gF2", [64, 8, 64], F32)
        gbf = lambda nm: S.sb(nm, [64, 8, 64], BF16)
        A1, A2, A3, B1, B2, B3 = (gbf(nm) for nm in ("gA1", "gA2", "gA3", "gB1", "gB2", "gB3"))
        usb = S.sb("usb", [64, 8, 128], F32)
        attnT = S.sb("attnT", [64, 8, 64], BF16)
        wT = S.sb("wT", [128, 8, 64], BF16)
        assert S.sb_ptr <= S.bases[ctmp.name] + T * 4, "group overlay overflow"
        S.sb_ptr = sv_
        Vb_ = S.sb("Vbt", [64, 8, 128], BF16)
        Kb_ = S.sb("Kbt", [64, 8, 128], BF16)
        kdec = S.sb("kdec", [64, 8, 128], BF16)
        delta = S.sb("delta", [64, 128], BF16)
        cols_ = S.sb("ccols", [64, 8, 4], F32)
        egl = S.sb("egl", [128, 8], F32)
        mmax, mmin = mm_[:, 0:64], mm_[:, 64:128]

        def conv_silu(psrc_list, kc, dst_f32):
            for (ti, t0, n, acc) in psrc_list:
                if t0 + n <= TP:
                    S.copy("act", xx[:, 3 + t0:3 + t0 + n], acc[:, 0:n])
                else:
                    npz = TP - t0
                    S.copy("act", xx[:, 3 + t0:3 + TP], acc[:, 0:npz])
                    S.copy("act", xs[:, :, 3], acc[:, npz:npz + NS])

        def conv_finish(kc, dst):
            S.memset("pool", xx[:, 0:3], 0.0)
            S.copy("pool", xs[:, :, 0:3], st_c[:, kc, :, :])
            S.copy("pool", p_cv[:, kc, :], xx[:, TP:TP + 3])
            S.copy("pool", s_cv[:, kc, :, :], xs[:, :, 1:4])
            cw = lambda q: self.col(f"dn_cw{q}", kc)
            S.ts("dve", ctmp[:, 0:TP], xx[:, 0:TP], cw(0), ALU.mult)
            for q in range(1, 4):
                S.stt(ctmp[:, 0:TP], xx[:, q:q + TP], cw(q), ctmp[:, 0:TP], ALU.mult, ALU.add)
            S.ts("dve", ctmp[:, TP:T], xs[:, :, 0], cw(0), ALU.mult)
            for q in range(1, 4):
                S.stt(ctmp[:, TP:T], xs[:, :, q], cw(q), ctmp[:, TP:T], ALU.mult, ALU.add)
            S.act(dst, ctmp[:], AF.Silu)

        def l2n(src, dst_bf, scale):
            for ti, (t0, n) in enumerate(TT):
                sq = self.sq[ti % 2]
                S.act(sq[:, 0:n], src[:, t0:t0 + n], AF.Square)
                acc = self.bank()
                S.mm(acc[:, 0:n], self.ones_b[:], sq[:, 0:n], start=True, stop=True)
                r = self.rstd[ti % 2]
                S.act(r[:, 0:n], acc[:, 0:n], AF.Sqrt, bias=self.eps_col[:, 0:1], scale=1.0 / (scale * scale))
                S.recip(r[:, 0:n], r[:, 0:n])
                S.tt("pool", dst_bf[:, t0:t0 + n], src[:, t0:t0 + n], r[:, 0:n], ALU.mult)

        def proj2(wb, c, kc):
            w3 = wb[:, 0:2048].rearrange("p (k f) -> p k f", k=8)
            lst = []
            for ti, (t0, n) in enumerate(TT):
                acc = self.bank()
                for k in range(KC):
                    S.mm(acc[:, 0:n], w3[:, k, c * 128:(c + 1) * 128], self.xn[k][:, t0:t0 + n], start=(k == 0), stop=(k == KC - 1))
                conv_silu([(ti, t0, n, acc)], kc, None)

        def head_qk(wb, h):
            proj2(wb, 0, h)
            conv_finish(h, ctmp[:])
            l2n(ctmp, qn, 128.0 ** -0.5)
            S.dma(sel[:], self.dram["dn_sel"][:, h * 128:(h + 1) * 128], sem="dnsel")
            for ti, (t0, n) in enumerate(TT):
                acc = self.bank()
                S.mm(acc[:, 0:n], sel[:, :], Gc[:, t0:t0 + n], start=True, stop=True)
                S.copy("act", GB[:, t0:t0 + n], acc[:, 0:n])
                S.act(eg[:, 0:n], acc[:, 0:n], AF.Exp)
                S.tt("pool", qdec[:, t0:t0 + n], qn[:, t0:t0 + n], eg[:, 0:n], ALU.mult)
            proj2(wb, 1, 8 + h)
            conv_finish(8 + h, ctmp[:])
            l2n(ctmp, kn, 1.0)

        def head_vz(wb, h):
            proj2(wb, 0, 16 + h)
            conv_finish(16 + h, ctmp[:])
            S.copy("pool", vb[:], ctmp[:])
            w3 = wb[:, 0:2048].rearrange("p (k f) -> p k f", k=8)

            def ev(ti, t0, n, acc):
                S.act(sz[:, t0:t0 + n], acc[:, 0:n], AF.Silu)
            self.proj_chunk(lambda k: w3[:, k, 128:256], self.xn, ev)
            S.memset("pool", Sf[:], 0.0)
            S.memset("pool", Sb_[:], 0.0)
            groups = [[0]] + [list(range(1 + 8 * q, 9 + 8 * q)) for q in range(4)] + [[33, 34, 35, 36]]
            for grp in groups:
                ng = len(grp)
                C, L = chunks[grp[0]][1], chunks[grp[0]][2]
                R = slice(0, C)
                pk, pq = self.rbank(0), self.rbank(1)
                ci0 = grp[0]
                tg0 = chunks[ci0][0]
                GR = (R, slice(0, ng), slice(0, C))
                bc = lambda ap: ap.to_broadcast([C, ng, C])
                GBg = GB[0:C, tg0:tg0 + ng * C].rearrange("p (g c) -> p g c", g=ng)
                gcolg = GT[0:C, ci0:ci0 + ng, h:h + 1]
                bcolg = BT[0:C, ci0:ci0 + ng, h:h + 1]
                glastg = GB[0:C, tg0 + C - 1:tg0 + ng * C:C].unsqueeze(2)
                S.act(cols_[R, 0:ng, 0:1], gcolg, AF.Exp)
                S.tt("dve", cols_[R, 0:ng, 1:2], cols_[R, 0:ng, 0:1], bcolg, ALU.mult)
                S.tt("dve", cols_[R, 0:ng, 2:3], glastg, gcolg, ALU.subtract)
                S.act(cols_[R, 0:ng, 2:3], cols_[R, 0:ng, 2:3], AF.Exp)
                S.act(egl[:, 0:ng], GB[:, tg0 + C - 1:tg0 + ng * C:C], AF.Exp)
                for g, ci in enumerate(grp):
                    t0 = chunks[ci][0]
                    cs_ = slice(t0, t0 + C)
                    S.mm(pk[R, g * 64:g * 64 + C], kn[:, cs_], kn[:, cs_], start=True, stop=True)
                    S.mm(pq[R, g * 64:g * 64 + C], kn[:, cs_], qn[:, cs_], start=True, stop=True)
                S.tt("dve", F2[GR], GBg, bc(gcolg), ALU.subtract)
                S.tt("dve", F1[GR], F2[GR], bc(mmax[0:C, 0:C].unsqueeze(1)), ALU.max)
                S.tt("dve", F2[GR], F2[GR], bc(mmin[0:C, 0:C].unsqueeze(1)), ALU.min)
                pk3 = pk[:, 0:512].rearrange("p (g c) -> p g c", g=8)
                pq3 = pq[:, 0:512].rearrange("p (g c) -> p g c", g=8)
                S.act(B1[GR], F1[GR], AF.Exp, scale=-1.0)
                S.act(B2[GR], F2[GR], AF.Exp)
                S.tt("dve", F1[GR], pk3[GR], bc(bcolg), ALU.mult)
                S.stt(F1[GR], F1[GR], -1.0, B1[GR], ALU.mult, ALU.mult)
                S.copy("act", A1[GR], F1[GR])
                S.tt("dve", attnT[GR], pq3[GR], B2[GR], ALU.mult)
                if C > 1:
                    ptr_ = self.rbank(2)
                    ptr3 = ptr_[:, 0:512].rearrange("p (g c) -> p g c", g=8)
                    for g in range(ng):
                        S.tr(ptr_[R, g * 64:g * 64 + C], F1[R, g, 0:C], self.ident_f[0:C, 0:C])
                    S.copy("dve", A2[GR], ptr3[GR])
                else:
                    S.copy("dve", A2[GR], A1[GR])
                S.tt("pool", A3[GR], A2[GR], bc(self.ident_b[0:C, 0:C].unsqueeze(1)), ALU.add)
                P_, PT_, TT_ = A1, A2, A3
                P2, PT2, TT2 = B1, B2, B3
                for lv in range(1, L):
                    pl, plT, pl2 = self.rbank(2), self.rbank(3), self.rbank(4)
                    pl3 = pl[:, 0:512].rearrange("p (g c) -> p g c", g=8)
                    plT3 = plT[:, 0:512].rearrange("p (g c) -> p g c", g=8)
                    pl23 = pl2[:, 0:512].rearrange("p (g c) -> p g c", g=8)
                    for g in range(ng):
                        S.mm(pl[R, g * 64:g * 64 + C], PT_[R, g, 0:C], P_[R, g, 0:C], start=True, stop=True)
                    for g in range(ng):
                        S.mm(plT[R, g * 64:g * 64 + C], P_[R, g, 0:C], PT_[R, g, 0:C], start=True, stop=True)
                    S.copy("dve", P2[GR], pl3[GR])
                    S.copy("act", PT2[GR], plT3[GR])
                    for g in range(ng):
                        S.mm(pl2[R, g * 64:g * 64 + C], P2[R, g, 0:C], TT_[R, g, 0:C], start=True, stop=True)
                    S.tt("dve", TT2[GR], pl23[GR], TT_[GR], ALU.add)
                    P_, P2 = P2, P_
                    PT_, PT2 = PT2, PT_
                    TT_, TT2 = TT2, TT_
                TTbf = TT_
                for half in range((ng + 3) // 4):
                    pkk, pvv = self.rbank(0 + half), self.rbank(2 + half)
                    for g in range(half * 4, min(ng, half * 4 + 4)):
                        ci = grp[g]
                        t0 = chunks[ci][0]
                        cs_ = slice(t0, t0 + C)
                        o0 = (g % 4) * 128
                        S.mm(pkk[R, o0:o0 + 128], kn[:, cs_], self.ident_b[:], start=True, stop=True)
                        S.mm(pvv[R, o0:o0 + 128], vb[:, cs_], self.ident_b[:], start=True, stop=True)
                    h4 = half * 4
                    n4 = min(ng, h4 + 4) - h4
                    pkk3 = pkk[:, 0:512].rearrange("p (g c) -> p g c", g=4)
                    pvv3 = pvv[:, 0:512].rearrange("p (g c) -> p g c", g=4)
                    b4 = lambda ap: ap.to_broadcast([C, n4, 128])
                    S.tt("dve", Kb_[R, h4:h4 + n4, :], pkk3[R, 0:n4, :], b4(cols_[R, h4:h4 + n4, 1:2]), ALU.mult)
                    S.tt("dve", kdec[R, h4:h4 + n4, :], pkk3[R, 0:n4, :], b4(cols_[R, h4:h4 + n4, 2:3]), ALU.mult)
                    S.tt("dve", Vb_[R, h4:h4 + n4, :], pvv3[R, 0:n4, :], b4(BT[0:C, ci0 + h4:ci0 + h4 + n4, h:h + 1]), ALU.mult)
                pw = self.rbank(6)
                for half in range((ng + 3) // 4):
                    pu = self.rbank(4 + half)
                    for g in range(half * 4, min(ng, half * 4 + 4)):
                        o0 = (g % 4) * 128
                        S.mm(pu[R, o0:o0 + 128], TTbf[R, g, 0:C], Vb_[R, g, :], start=True, stop=True)
                    n4 = min(ng, half * 4 + 4) - half * 4
                    S.copy("act", usb[R, half * 4:half * 4 + n4, :],
                           pu[:, 0:512].rearrange("p (g c) -> p g c", g=4)[R, 0:n4, :])
                for g in range(ng):
                    S.mm(pw[:, g * 64:g * 64 + C], Kb_[R, g, :], TTbf[R, g, 0:C], start=True, stop=True)
                S.copy("dve", wT[:, 0:ng, 0:C], pw[:, 0:512].rearrange("p (g c) -> p g c", g=8)[:, 0:ng, 0:C])
                for g, ci in enumerate(grp):
                    t0 = chunks[ci][0]
                    cs_ = slice(t0, t0 + C)
                    sample = t0 >= TP
                    if sample:
                        b = t0 - TP
                        S.dma(Sf[:], self.dram["s_dn_S"][b * 8 + h], sem="dnS")
                        S.copy("dve", Sb_[:], Sf[:])
                    pd, po, ps_ = self.rbank(7), self.rbank(5), self.rbank(3)
                    S.mm(pd[R, 0:128], wT[:, g, 0:C], Sb_[:], start=True, stop=True)
                    S.tt("dve", delta[R, :], usb[R, g, :], pd[R, 0:128], ALU.subtract)
                    S.mm(ps_[:, 0:128], kdec[R, g, :], delta[R, :], start=True, stop=True)
                    S.mm(po[:, 0:C], Sb_[:], qdec[:, cs_], start=True, stop=False)
                    S.mm(po[:, 0:C], delta[R, :], attnT[R, g, 0:C], start=False, stop=True)
                    S.stt(Sf[:], Sf[:], egl[:, g:g + 1], ps_[:, 0:128], ALU.mult, ALU.add)
                    S.copy("act", Sb_[:], Sf[:])
                    S.copy("act", oT[:, cs_], po[:, 0:C])
                    if ci == 32:
                        S.dma(oS[h], Sf[:], sem="oS")
                    if sample:
                        S.dma(oS[8 + (t0 - TP) * 8 + h], Sf[:], sem="oS")

        def head_o(wb, h):
            for ti, (t0, n) in enumerate(TT):
                sq = self.sq[ti % 2]
                S.act(sq[:, 0:n], oT[:, t0:t0 + n], AF.Square)
                acc = self.bank()
                S.mm(acc[:, 0:n], self.ones_b[:], sq[:, 0:n], start=True, stop=True)
                r = self.rstd[0]
                S.act(r[:, 0:n], acc[:, 0:n], AF.Sqrt, bias=self.eps_col[:, 0:1], scale=1.0 / 128)
                S.recip(r[:, 0:n], r[:, 0:n])
                S.stt(eg[:, 0:n], oT[:, t0:t0 + n], self.col("dn_norm", 0), r[:, 0:n], ALU.mult, ALU.mult)
                S.tt("pool", oT[:, t0:t0 + n], eg[:, 0:n], sz[:, t0:t0 + n], ALU.mult)
            for fo in range(KC):
                def ev2(ti, t0, n, acc, fo=fo):
                    S.tt("dve", self.x[fo][:, t0:t0 + n], acc[:, 0:n], self.x[fo][:, t0:t0 + n], ALU.add)
                self.proj_chunk(lambda k, fo=fo: wb[:, fo * 128:(fo + 1) * 128], [oT], ev2)
        units = []
        for h in range(8):
            units.append(("dn_qk", h, 2048, (lambda wb, h=h: head_qk(wb, h))))
            units.append(("dn_vz", h, 2048, (lambda wb, h=h: head_vz(wb, h))))
            units.append(("dn_wo", h, 1024, (lambda wb, h=h: head_o(wb, h))))
        self.run_units(units)
        sm2 = self.out("small2", [128, 360])
        S.dma(sm2[:, :], small2[:], sem="o1")
        S.release(m0)

    def consts(self):
        S = self.S
        self.eps_col = S.sb("eps_col", [128, 1], F32)
        self.one_col = S.sb("one_col", [128, 1], F32)
        S.memset("pool", self.eps_col[:], EPS)
        S.memset("pool", self.one_col[:], 1.0)

    def final(self):
        S = self.S
        yT = self.out("yT", [128, KC, T])
        for ti, (t0, n) in enumerate(TT):
            r = self.rmsnorm_stats(ti)
            for k in range(KC):
                S.stt(self.x[k][:, t0:t0 + n], self.x[k][:, t0:t0 + n], self.col("nfinal", k), r[:, 0:n],
                      ALU.mult, ALU.mult)
        for k in range(KC):
            S.dma(yT[:, k, :], self.x[k][:], sem=f"o{k % 2}")
        sm = self.out("small", [128, 320])
        S.dma(sm[:, :], self.small[:], sem="o0")

    def dump_x(self):
        S = self.S
        dbg = self.out("dbg", [128, KC, T])
        for k in range(KC):
            S.dma(dbg[:, k, :], self.x[k][:], sem=f"o{k % 2}")


def build_program(shapes, colidx, stop_after=None, only=None):
    nc = bass.Bass("TRN2", target_bir_lowering=False)
    S = Sched(nc)
    B = Builder(nc, S, shapes, colidx, stop_after)
    B.setup()
    B.consts()
    S.memset("pool", B.small[:], 0.0)
    layers = [("lru", 0), ("dn", 0), ("mla", 0), ("lru", 1)]
    done = False
    for li, (kind, j) in enumerate(layers):
        if only is not None and li != only:
            continue
        if kind == "lru":
            B.lru(li, j)
        elif kind == "dn":
            B.dn(li, j)
        else:
            B.mla(li, j)
        if stop_after == (li, "mix"):
            done = True
            break
        B.ffn(li)
        if stop_after == (li, "ffn"):
            done = True
            break
    if done:
        B.dump_x()
        sm = B.out("small", [128, 320])
        S.dma(sm[:, :], B.small[:], sem="o0")
    else:
        B.final()
    S.emit()
    return nc, B


_STOP_AFTER = None
_DEBUG = {}


def _run(inputs, stop_after=None, cores=NCORES, only=None, x_override=None):
    inp = {k: np.asarray(v) for k, v in inputs.items()}
    sh = _prep_shared(inp)
    colidx = sh.pop("_colidx")
    per_core = [_prep_core(inp, c) for c in range(cores)]
    shapes = {k: v.shape for k, v in sh.items()}
    shapes.update({k: v.shape for k, v in per_core[0].items()})
    if x_override is not None:
        for c in range(cores):
            per_core[c]["xT"] = np.ascontiguousarray(x_override[c].reshape(T, KC, 128).transpose(2, 1, 0))
    nc, B = build_program(shapes, colidx, stop_after, only)
    in_maps = []
    for c in range(cores):
        m = dict(sh)
        m.update(per_core[c])
        in_maps.append(m)
    res = run_bass_kernel_spmd(nc, in_maps, core_ids=list(range(cores)))
    return res.results, B


def kernel(**inputs):
    results, B = _run(inputs, None)
    f = np.float32
    y_prompt = np.zeros((8, SEQ, D), f)
    y_sample = np.zeros((32, 1, D), f)
    p_lru_h = np.zeros((2, 8, D), f)
    p_lru_conv = np.zeros((2, 8, 3, D), f)
    p_dn_S = np.zeros((1, 8, 8, 128, 128), f)
    p_dn_conv = np.zeros((1, 8, 3, 3072), f)
    p_ckv = np.zeros((1, 8, TP, 256), f)
    p_kpe = np.zeros((1, 8, TP, 64), f)
    s_lru_h = np.zeros((2, 32, D), f)
    s_lru_conv = np.zeros((2, 32, 3, D), f)
    s_dn_S = np.zeros((1, 32, 8, 128, 128), f)
    s_dn_conv = np.zeros((1, 32, 3, 3072), f)
    s_ckv = np.zeros((1, 32, 1, 256), f)
    s_kpe = np.zeros((1, 32, 1, 64), f)
    for c in range(NCORES):
        r = results[c]
        y = r["yT"].transpose(2, 1, 0).reshape(T, D)
        y_prompt[c] = y[NMETA:TP]
        y_sample[NS * c:NS * (c + 1), 0] = y[TP:]
        sm = r["small"]
        for j in range(2):
            i, n = B.small_idx[f"p_lru_h{j}"]
            p_lru_h[j, c] = sm[:, i:i + n].T.reshape(D)
            i, n = B.small_idx[f"p_lru_conv{j}"]
            p_lru_conv[j, c] = sm[:, i:i + n].reshape(128, 8, 3).transpose(2, 1, 0).reshape(3, D)
            i, n = B.small_idx[f"s_lru_h{j}"]
            s_lru_h[j, NS * c:NS * (c + 1)] = sm[:, i:i + n].reshape(128, 8, NS).transpose(2, 1, 0).reshape(NS, D)
            i, n = B.small_idx[f"s_lru_conv{j}"]
            s_lru_conv[j, NS * c:NS * (c + 1)] = sm[:, i:i + n].reshape(128, 8, NS, 3).transpose(2, 3, 1, 0).reshape(NS, 3, D)
        s2 = r["small2"]
        p_dn_conv[0, c] = s2[:, 0:72].reshape(128, 24, 3).transpose(2, 1, 0).reshape(3, 3072)
        s_dn_conv[0, NS * c:NS * (c + 1)] = s2[:, 72:360].reshape(128, 24, NS, 3).transpose(2, 3, 1, 0).reshape(NS, 3, 3072)
        oS = r["o_dn_S"]
        p_dn_S[0, c] = oS[0:8]
        s_dn_S[0, NS * c:NS * (c + 1)] = oS[8:].reshape(NS, 8, 128, 128)
        kv = r["p_kv"]
        p_ckv[0, c] = kv[:TP, :256]
        p_kpe[0, c] = kv[:TP, 256:]
        s_ckv[0, NS * c:NS * (c + 1), 0] = kv[TP:, :256]
        s_kpe[0, NS * c:NS * (c + 1), 0] = kv[TP:, 256:]
    return (y_prompt, y_sample, p_lru_h, p_lru_conv, p_dn_S, p_dn_conv, p_ckv, p_kpe,
            s_lru_h, s_lru_conv, s_dn_S, s_dn_conv, s_ckv, s_kpe)
```

```python
import bisect
import os
from contextlib import ExitStack

import numpy as np
import concourse.bass as bass
import concourse.mybir as mybir
from concourse.bass_utils import run_bass_kernel_spmd

F32 = mybir.dt.float32
BF16 = mybir.dt.bfloat16
I32 = mybir.dt.int32
AF = mybir.ActivationFunctionType
ALU = mybir.AluOpType
AX = mybir.AxisListType

NCORES = 8
D = 1024
KC = 8
SEQ = 2048
NMETA = 16
TP = SEQ + NMETA
NS = 4
T = TP + NS
DFF = 2816
FC = DFF // 128
NPAGES = 128
PAGE = 128
NPOOL = 5120
EPS = 1e-6
MLA_SCALE = (128 + 64) ** -0.5
TT = [(0, 512), (512, 512), (1024, 512), (1536, 512), (2048, 20)]
WSLOT = 2048


class _Op:
    __slots__ = ("eng", "fn", "deps", "dma_sem", "dma_val", "idx", "milestone", "mval", "waits", "dma_deps")


class _IMap:
    def __init__(self, size):
        self.b = [0, size]
        self.r = [[None, {}]]

    def _split(self, x):
        i = bisect.bisect_left(self.b, x)
        if self.b[i] == x:
            return i
        w, rd = self.r[i - 1]
        self.b.insert(i, x)
        self.r.insert(i, [w, dict(rd)])
        return i

    def read(self, lo, hi, op, key, deps):
        i = self._split(lo)
        j = self._split(hi)
        for k in range(i, j):
            rec = self.r[k]
            if rec[0] is not None:
                deps.add(rec[0])
            rec[1][key] = op

    def write(self, lo, hi, op, deps):
        i = self._split(lo)
        j = self._split(hi)
        for k in range(i, j):
            rec = self.r[k]
            if rec[0] is not None:
                deps.add(rec[0])
            deps.update(rec[1].values())
        self.b[i:j + 1] = [lo, hi]
        self.r[i:j] = [[op, {}]]


class Sched:
    ENGS = ("pe", "act", "dve", "pool", "sp")

    def __init__(self, nc):
        self.nc = nc
        self.ops = {e: [] for e in self.ENGS}
        self.maps = {"SB": _IMap(1 << 20), "PSUM": _IMap(1 << 16)}
        self.dma_cnt = {}
        self.total_sems = set()
        self.bases = {}
        self.sb_ptr = (nc.sbuf_base + 63) // 64 * 64
        self.sb_top = nc.sbuf_top
        self.nalloc = 0

    def sb(self, name, shape, dtype):
        esz = 2 if dtype == BF16 else 4
        n = 1
        for s in shape[1:]:
            n *= s
        nbytes = (n * esz + 63) // 64 * 64
        off = self.sb_ptr
        self.sb_ptr += nbytes
        assert self.sb_ptr <= self.sb_top, f"SBUF overflow at {name}: {self.sb_ptr} > {self.sb_top}"
        self.nalloc += 1
        t = self.nc.alloc_sbuf_tensor_at(f"{name}_{self.nalloc}", list(shape), dtype, offset=off)
        self.bases[t.name] = off
        return t

    def mark(self):
        return self.sb_ptr

    def release(self, m):
        self.sb_ptr = m

    def _range(self, ap):
        sp = str(ap.space)
        if "SB" in sp:
            m = self.maps["SB"]
        elif "PSUM" in sp:
            m = self.maps["PSUM"]
        else:
            return None
        esz = 2 if ap.dtype == BF16 else 4
        pat = ap.ap
        pstride = pat[0][0]
        off = ap.offset % pstride if pstride > 0 else ap.offset
        ext = 1
        for st, cnt in pat[1:]:
            ext += (cnt - 1) * abs(st)
        base = self.bases.get(ap.tensor.name, 0)
        lo = base + off * esz
        hi = lo + ext * esz
        if m is self.maps["PSUM"]:
            lo = lo // 2048 * 2048
            hi = (hi + 2047) // 2048 * 2048
        return m, lo, hi

    def rec(self, eng, fn, reads=(), writes=(), dma_sem=None):
        op = _Op()
        op.eng = eng
        op.fn = fn
        op.dma_sem = dma_sem
        op.milestone = False
        op.mval = 0
        key = eng if dma_sem is None else ("dma", dma_sem)
        deps = set()
        for ap in reads:
            if ap is None or isinstance(ap, (int, float)):
                continue
            r = self._range(ap)
            if r:
                if r[0] is self.maps["PSUM"]:
                    r[0].write(r[1], r[2], op, deps)
                else:
                    r[0].read(r[1], r[2], op, key, deps)
        for ap in writes:
            r = self._range(ap)
            if r:
                r[0].write(r[1], r[2], op, deps)
        deps.discard(op)
        op.deps = []
        op.dma_deps = {}
        for d in deps:
            if d.dma_sem is not None:
                s = d.dma_sem
                v = self.dma_cnt[s]
                if op.dma_deps.get(s, 0) < v:
                    op.dma_deps[s] = v
            else:
                op.deps.append(d)
        if dma_sem is not None:
            self.dma_cnt[dma_sem] = self.dma_cnt.get(dma_sem, 0) + 16
            op.dma_val = self.dma_cnt[dma_sem]
        op.idx = len(self.ops[eng])
        self.ops[eng].append(op)
        return op

    def mm(self, out, lhsT, rhs, start=True, stop=True):
        return self.rec("pe", lambda e: e.matmul(out, lhsT=lhsT, rhs=rhs, start=start, stop=stop),
                        [lhsT, rhs], [out])

    def tr(self, out, in_, ident):
        return self.rec("pe", lambda e: e.transpose(out=out, in_=in_, identity=ident), [in_, ident], [out])

    def act(self, out, in_, func, bias=None, scale=1.0, accum_out=None):
        kw = {}
        if bias is not None:
            kw["bias"] = bias
        if accum_out is not None:
            kw["accum_out"] = accum_out
        w = [out] + ([accum_out] if accum_out is not None else [])
        return self.rec("act", lambda e: e.activation(out=out, in_=in_, func=func, scale=scale, **kw),
                        [in_, bias, scale], w)

    def tt(self, eng, out, in0, in1, op):
        return self.rec(eng, lambda e: e.tensor_tensor(out=out, in0=in0, in1=in1, op=op), [in0, in1], [out])

    def ts(self, eng, out, in0, s1, op0, s2=None, op1=None, accum_out=None):
        kw = {}
        if op1 is not None:
            kw["op1"] = op1
        if accum_out is not None:
            kw["accum_out"] = accum_out
        w = [out] + ([accum_out] if accum_out is not None else [])
        return self.rec(eng, lambda e: e.tensor_scalar(out=out, in0=in0, scalar1=s1, scalar2=s2, op0=op0, **kw),
                        [in0, s1, s2], w)

    def stt(self, out, in0, scalar, in1, op0, op1, eng="dve"):
        return self.rec(eng, lambda e: e.scalar_tensor_tensor(out=out, in0=in0, scalar=scalar, in1=in1,
                                                              op0=op0, op1=op1), [in0, scalar, in1], [out])

    def copy(self, eng, out, in_):
        if eng == "act":
            return self.rec("act", lambda e: e.copy(out=out, in_=in_), [in_], [out])
        return self.rec(eng, lambda e: e.tensor_copy(out=out, in_=in_), [in_], [out])

    def memset(self, eng, ap, val):
        return self.rec(eng, lambda e: e.memset(ap, val), [], [ap])

    def recip(self, out, in_):
        return self.rec("dve", lambda e: e.reciprocal(out=out, in_=in_), [in_], [out])

    def scan(self, out, d0, d1, initial, op0=ALU.mult, op1=ALU.add):
        return self.rec("dve", lambda e: e.tensor_tensor_scan(out=out, data0=d0, data1=d1, initial=initial,
                                                              op0=op0, op1=op1), [d0, d1, initial], [out])

    def reduce(self, out, in_, op, axis=AX.X):
        return self.rec("dve", lambda e: e.tensor_reduce(out=out, in_=in_, axis=axis, op=op), [in_], [out])

    def dma(self, out, in_, sem, eng="sp"):
        return self.rec(eng, lambda e: e.dma_start(out=out, in_=in_), [in_], [out], dma_sem=sem)

    def gather(self, out, in_, idx_ap, sem):
        return self.rec("pool", lambda e: e.indirect_dma_start(
            out=out, out_offset=None, in_=in_, in_offset=bass.IndirectOffsetOnAxis(ap=idx_ap, axis=0)),
            [idx_ap], [out], dma_sem=sem)

    def emit(self):
        nc = self.nc
        ops = self.ops
        for e in self.ENGS:
            seen = {f: -1 for f in self.ENGS}
            seen_dma = {}
            for op in ops[e]:
                keep = {}
                for d in op.deps:
                    f = d.eng
                    if f == e and e in ("pe", "sp"):
                        continue
                    if d.idx > seen[f] and d.idx > keep.get(f, (-1, None))[0]:
                        keep[f] = (d.idx, d)
                op.waits = []
                for f, (i, d) in keep.items():
                    seen[f] = i
                    d.milestone = True
                    op.waits.append(d)
                dw = []
                for s, v in op.dma_deps.items():
                    if s in self.total_sems:
                        v = -1
                    if seen_dma.get(s, 0) < v or v == -1:
                        if v == -1 and seen_dma.get(s, 0) == -1:
                            continue
                        seen_dma[s] = v
                        dw.append((s, v))
                op.dma_deps = dw
        for e in self.ENGS:
            c = 0
            for op in ops[e]:
                if op.milestone:
                    c += 1
                    op.mval = c
        self.nmil = {e: sum(1 for o in ops[e] if o.milestone) for e in self.ENGS}
        with ExitStack() as st:
            esem = {e: st.enter_context(nc.semaphore(f"e_{e}")) for e in self.ENGS}
            dsem = {s: st.enter_context(nc.semaphore(f"d_{s}")) for s in self.dma_cnt}
            block = st.enter_context(nc.Block())

            def run(e, eng):
                for op in ops[e]:
                    for d in op.waits:
                        eng.wait_ge(esem[d.eng], d.mval)
                    for s, v in op.dma_deps:
                        eng.wait_ge(dsem[s], self.dma_cnt[s] if v == -1 else v)
                    ins = op.fn(eng)
                    if op.dma_sem is not None:
                        ins.then_inc(dsem[op.dma_sem], 16)
                    elif op.milestone:
                        ins.then_inc(esem[e], 1)
                if e == "sp":
                    for s, v in self.dma_cnt.items():
                        eng.wait_ge(dsem[s], v)

            @block.tensor
            def _(eng):
                run("pe", eng)

            @block.scalar
            def _(eng):
                run("act", eng)

            @block.vector
            def _(eng):
                run("dve", eng)

            @block.gpsimd
            def _(eng):
                run("pool", eng)

            @block.sync
            def _(eng):
                run("sp", eng)


def _units_proj(W, gf):
    K, N = W.shape
    kc = K // 128
    return np.ascontiguousarray(W.reshape(kc, 128, N // gf, gf).transpose(2, 1, 0, 3).reshape(N // gf, 128, kc * gf))


def _cols(v):
    v = np.asarray(v, np.float32).reshape(-1, 128)
    return np.ascontiguousarray(v.T)


class _ColPack:
    def __init__(self):
        self.parts = []
        self.n = 0
        self.idx = {}

    def add(self, name, arr):
        arr = np.asarray(arr, np.float32)
        assert arr.shape[0] == 128
        self.idx[name] = self.n
        self.parts.append(arr)
        self.n += arr.shape[1]

    def build(self):
        return np.ascontiguousarray(np.concatenate(self.parts, axis=1))


def _prep_shared(inp):
    sh = {}
    cp = _ColPack()
    for i in range(4):
        cp.add(f"nmix{i}", _cols(inp["norm_mix"][i]))
        cp.add(f"nffn{i}", _cols(inp["norm_ffn"][i]))
    cp.add("nfinal", _cols(inp["norm_final"]))
    for j in range(2):
        for k in range(4):
            cp.add(f"lru_cw{j}_{k}", _cols(inp["lru_conv_w"][j, k]))
        cp.add(f"lru_cb{j}", _cols(inp["lru_conv_b"][j]))
        cp.add(f"lru_ba{j}", _cols(inp["lru_b_a"][j]))
        cp.add(f"lru_bi{j}", _cols(inp["lru_b_i"][j]))
        cp.add(f"lru_lam{j}", _cols(inp["lru_lambda"][j]))
        w_in = inp["lru_w_in"][j]
        u = []
        for n in range(4):
            u.append(_units_proj(w_in[:, n * 256:(n + 1) * 256], 256)[0])
            u.append(_units_proj(w_in[:, 1024 + n * 256:1024 + (n + 1) * 256], 256)[0])
        sh[f"lru_win{j}"] = np.stack(u)
        wa, wi = inp["lru_w_a"][j], inp["lru_w_i"][j]
        g = []
        for n in range(4):
            a = _units_proj(wa[n], 256)[0]
            b = _units_proj(wi[n], 256)[0]
            g.append(np.concatenate([a, b], axis=1))
        sh[f"lru_wg{j}"] = np.stack(g)
        wo = inp["lru_w_out"][j]
        sh[f"lru_wout{j}"] = np.stack([_units_proj(wo[n * 256:(n + 1) * 256], 1024)[0] for n in range(4)])
    for i in range(4):
        wgu = inp["ffn_w_gu"][i]
        g = _units_proj(wgu[:, :DFF], 128)
        u = _units_proj(wgu[:, DFF:], 128)
        sh[f"ffn_gu{i}"] = np.ascontiguousarray(
            np.stack([g.reshape(FC, 128, 8, 128), u.reshape(FC, 128, 8, 128)], axis=3).reshape(FC, 128, 2048))
        wd = inp["ffn_w_down"][i]
        hv = []
        for half in range(2):
            hv.append(_units_proj(wd[half * 1408:(half + 1) * 1408], 128))
        sh[f"ffn_dn{i}"] = np.ascontiguousarray(np.stack(hv).reshape(16, 128, 1408))
    _prep_mla(inp, sh, cp)
    _prep_dn(inp, sh, cp)
    sh["cols"] = cp.build()
    sh["_colidx"] = cp.idx
    sh["ones_bf"] = np.ones((128, 128), np.float32)
    sh["ident"] = np.eye(128, dtype=np.float32)
    return sh


def _prep_mla(inp, sh, cp):
    cp.add("mla_qn", _cols(inp["mla_q_norm"][0]))
    cp.add("mla_kvn", _cols(inp["mla_kv_norm"][0]))
    wdkv = inp["mla_w_dkv"][0]
    sh["mla_dkv_c"] = _units_proj(wdkv[:, :256], 256)
    perm = np.concatenate([np.arange(32, 64), np.arange(0, 32)])
    kr = np.concatenate([wdkv[:, 256:320], wdkv[:, 256 + perm]], axis=1)
    sh["mla_dkv_r"] = _units_proj(kr, 128)
    sh["mla_dq"] = _units_proj(inp["mla_w_dq"][0], 256)
    wuq = inp["mla_w_uq"][0].reshape(512, 8, 192)
    wuk = inp["mla_w_uk"][0]
    wuv = inp["mla_w_uv"][0]
    wo = inp["mla_w_o"][0]
    u1, u2 = [], []
    for h in range(8):
        q = np.concatenate([wuq[:, h, :128], wuq[:, h, 128:192], wuq[:, h, 128 + perm]], axis=1)
        a = _units_proj(q, 256)[0]
        b = np.ascontiguousarray(wuk[:, h, :].T)
        u1.append(np.concatenate([a, b], axis=1))
        v = _units_proj(wuv[:, h, :], 128)[0]
        o = wo[h * 128:(h + 1) * 128, :]
        u2.append(np.concatenate([v, o], axis=1))
    sh["mla_u1"] = np.stack(u1)
    sh["mla_u2"] = np.stack(u2)
    half = 32
    freqs = (10000.0 ** (-np.arange(half, dtype=np.float32) / half)).astype(np.float32)
    pos = np.concatenate([np.arange(TP), np.full(NS, NPAGES * PAGE)]).astype(np.float32)
    ang = pos[None, :] * freqs[:, None]
    c, sn = np.cos(ang).astype(np.float32), np.sin(ang).astype(np.float32)
    rope = np.stack([np.concatenate([c, c], axis=0), np.concatenate([-sn, sn], axis=0)], axis=1)
    sh["rope"] = np.ascontiguousarray(rope.astype(np.float32))
    sh["tri"] = np.triu(np.ones((128, 128), np.float32))
    pool = np.concatenate([inp["cache_mla_ckv"][0], inp["cache_mla_kpe"][0]], axis=-1)
    sh["poolkv"] = pool.reshape(NPOOL * 32, 4 * 320)


def _prep_dn(inp, sh, cp):
    w = inp["dn_w_in"][0]
    qk, vz = [], []
    for h in range(8):
        qk.append(_units_proj(np.concatenate([w[:, h * 128:(h + 1) * 128], w[:, 1024 + h * 128:1024 + (h + 1) * 128]], axis=1), 256)[0])
        vz.append(_units_proj(np.concatenate([w[:, 2048 + h * 128:2048 + (h + 1) * 128], w[:, 3072 + h * 128:3072 + (h + 1) * 128]], axis=1), 256)[0])
    sh["dn_qk"] = np.stack(qk)
    sh["dn_vz"] = np.stack(vz)
    sh["dn_ba"] = _units_proj(w[:, 4096:4112], 16)
    for q in range(4):
        cp.add(f"dn_cw{q}", _cols(inp["dn_conv_w"][0, q]))
    cp.add("dn_norm", _cols(inp["dn_norm"][0]))
    pad = np.zeros((128, 2), np.float32)
    pad[:8, 0] = inp["dn_a_log"][0]
    pad[:8, 1] = inp["dn_dt_bias"][0]
    cp.add("dn_ab", pad)
    wo = inp["dn_w_out"][0]
    sh["dn_wo"] = np.ascontiguousarray(wo.reshape(8, 128, 1024))
    sel = np.zeros((8, 8, 128), np.float32)
    for h in range(8):
        sel[h, h, :] = 1.0
    sh["dn_sel"] = sel.reshape(8, 1024)
    mask = np.ones((8, T), np.float32)
    mask[:, 0] = 0.0
    mask[:, 16:TP:64] = 0.0
    mask[:, TP:] = 0.0
    sh["dn_mask"] = mask
    r = np.arange(64)[:, None]
    c = np.arange(64)[None, :]
    mmax = np.where(c < r, 0.0, 30000.0).astype(np.float32)
    mmin = np.where(c >= r, 0.0, -30000.0).astype(np.float32)
    sh["dn_mm"] = np.ascontiguousarray(np.concatenate([mmax, mmin], axis=1))


def _prep_core(inp, c):
    x_full = np.concatenate([inp["meta_tokens"], inp["x_prompt"][c], inp["x_sample"][NS * c:NS * (c + 1), 0]], axis=0)
    pc = {}
    pc["xT"] = np.ascontiguousarray(x_full.reshape(T, KC, 128).transpose(2, 1, 0))
    lh = inp["state_lru_h"][:, NS * c:NS * (c + 1)]
    pc["s_lru_h"] = np.ascontiguousarray(lh.reshape(2, NS, KC, 128).transpose(3, 0, 2, 1))
    lc = inp["state_lru_conv"][:, NS * c:NS * (c + 1)]
    pc["s_lru_conv"] = np.ascontiguousarray(lc.reshape(2, NS, 3, KC, 128).transpose(4, 0, 3, 1, 2))
    pc["s_dn_S"] = np.ascontiguousarray(inp["state_dn_S"][0, NS * c:NS * (c + 1)].reshape(NS * 8, 128, 128))
    dc = inp["state_dn_conv"][0, NS * c:NS * (c + 1)]
    pc["s_dn_conv"] = np.ascontiguousarray(dc.reshape(NS, 3, 24, 128).transpose(3, 2, 0, 1))
    pc["pt"] = np.ascontiguousarray(inp["page_table"][NS * c:NS * (c + 1)].T.astype(np.int32))
    return pc


class Builder:
    def __init__(self, nc, S, shapes, colidx, stop_after=None):
        self.nc = nc
        self.S = S
        self.colidx = colidx
        self.stop_after = stop_after
        self.dram = {}
        for name, shp in shapes.items():
            self.dram[name] = nc.dram_tensor(name, list(shp), I32 if name == "pt" else F32, kind="ExternalInput").ap()
        self.ps = nc.alloc_psum_tensor("ps", [128, 4096], F32)
        self.ps_next = 0
        self.wq = []
        self.wi = 0
        self.outs = {}

    def out(self, name, shape):
        ap = self.nc.dram_tensor(name, list(shape), F32, kind="ExternalOutput").ap()
        self.outs[name] = ap
        return ap

    def bank(self):
        b = self.ps_next
        self.ps_next = (self.ps_next + 1) % 4
        return self.ps[:, b * 512:(b + 1) * 512]

    def col(self, name, k=0, n=1):
        i = self.colidx[name] + k
        return self.cols[:, i:i + n]

    def wload(self, name, u, nel):
        S = self.S
        slot = self.wi % self.nws
        ss = self.wi % self.nst
        self.wi += 1
        stg = self.wstage[ss]
        wb = self.wbf[slot]
        src = self.dram[name][u]
        S.dma(stg[:, 0:nel], src, sem=f"w{ss}")
        S.copy("pool", wb[:, 0:nel], stg[:, 0:nel])
        return wb

    def run_units(self, units, depth=2):
        loaded = []
        n = len(units)
        for i in range(n + depth):
            if i < n:
                nm, u, nel, _ = units[i]
                loaded.append(self.wload(nm, u, nel))
            j = i - depth
            if j >= 0:
                units[j][3](loaded[j])

    def setup(self):
        S = self.S
        nc = self.nc
        ncol = self.dram["cols"].shape[1]
        self.cols = S.sb("cols", [128, ncol], F32)
        S.dma(self.cols[:], self.dram["cols"], sem="init")
        S.total_sems.add("init")
        self.ones_f = S.sb("ones_f", [128, 128], F32)
        self.ident_f = S.sb("ident_f", [128, 128], F32)
        S.dma(self.ones_f[:], self.dram["ones_bf"], sem="init")
        S.dma(self.ident_f[:], self.dram["ident"], sem="init")
        self.ones_b = S.sb("ones_b", [128, 128], BF16)
        self.ident_b = S.sb("ident_b", [128, 128], BF16)
        S.copy("pool", self.ones_b[:], self.ones_f[:])
        S.copy("pool", self.ident_b[:], self.ident_f[:])
        self.x = [S.sb(f"x{k}", [128, T], F32) for k in range(KC)]
        for k in range(KC):
            S.dma(self.x[k][:], self.dram["xT"][:, k, :], sem="init")
        self.xn = [S.sb(f"xn{k}", [128, T], BF16) for k in range(KC)]
        self.nws = 3
        self.nst = 1
        self.wstage = [S.sb(f"wst{i}", [128, WSLOT], F32) for i in range(self.nst)]
        self.wbf = [S.sb(f"wbf{i}", [128, WSLOT], BF16) for i in range(self.nws)]
        self.sq = [S.sb(f"sq{i}", [128, 512], BF16) for i in range(2)]
        self.rstd = [S.sb(f"rstd{i}", [128, 512], F32) for i in range(2)]
        self.small = S.sb("small", [128, 320], F32)
        self.small_n = 0
        self.small_idx = {}

    def small_alloc(self, name, n):
        i = self.small_n
        self.small_idx[name] = (i, n)
        self.small_n += n
        assert self.small_n <= 320
        return self.small[:, i:i + n]

    def rmsnorm_stats(self, ti):
        S = self.S
        t0, n = TT[ti]
        acc = self.bank()
        for k in range(KC):
            sq = self.sq[k % 2]
            S.act(sq[:, 0:n], self.x[k][:, t0:t0 + n], AF.Square)
            S.mm(acc[:, 0:n], self.ones_b[:], sq[:, 0:n], start=(k == 0), stop=(k == KC - 1))
        r = self.rstd[ti % 2]
        S.act(r[:, 0:n], acc[:, 0:n], AF.Sqrt, bias=self.eps_col[:, 0:1], scale=1.0 / D)
        S.recip(r[:, 0:n], r[:, 0:n])
        return r

    def rmsnorm_to_xn(self, gname):
        S = self.S
        for ti, (t0, n) in enumerate(TT):
            r = self.rmsnorm_stats(ti)
            for k in range(KC):
                S.stt(self.xn[k][:, t0:t0 + n], self.x[k][:, t0:t0 + n], self.col(gname, k), r[:, 0:n],
                      ALU.mult, ALU.mult)

    def proj_chunk(self, wb_lhsT, rhs_list, evac):
        S = self.S
        nk = len(rhs_list)
        for ti, (t0, n) in enumerate(TT):
            acc = self.bank()
            for k in range(nk):
                S.mm(acc[:, 0:n], wb_lhsT(k), rhs_list[k][:, t0:t0 + n], start=(k == 0), stop=(k == nk - 1))
            evac(ti, t0, n, acc)

    def ffn(self, li):
        S = self.S
        self.rmsnorm_to_xn(f"nffn{li}")
        m = S.mark()
        h = [S.sb(f"h{j}", [128, T], BF16) for j in range(11)]
        sg = [S.sb(f"sg{j}", [128, 512], BF16) for j in range(2)]
        for half in range(2):
            units = []
            for jj in range(11):
                j = half * 11 + jj

                def fn(wb, jj=jj):
                    w4 = wb[:, 0:2048].rearrange("p (k g f) -> p k g f", k=8, g=2)
                    for ti, (t0, n) in enumerate(TT):
                        pg = self.bank()
                        pu = self.bank()
                        for k in range(KC):
                            S.mm(pg[:, 0:n], w4[:, k, 0, :], self.xn[k][:, t0:t0 + n], start=(k == 0), stop=(k == KC - 1))
                        for k in range(KC):
                            S.mm(pu[:, 0:n], w4[:, k, 1, :], self.xn[k][:, t0:t0 + n], start=(k == 0), stop=(k == KC - 1))
                        s = sg[ti % 2]
                        S.act(s[:, 0:n], pg[:, 0:n], AF.Silu)
                        S.tt("dve", h[jj][:, t0:t0 + n], pu[:, 0:n], s[:, 0:n], ALU.mult)
                units.append((f"ffn_gu{li}", j, 2048, fn))
            for fo in range(KC):
                def fn2(wb, fo=fo):
                    w3 = wb[:, 0:1408].rearrange("p (k f) -> p k f", k=11)

                    def ev(ti, t0, n, acc):
                        S.tt("dve", self.x[fo][:, t0:t0 + n], acc[:, 0:n], self.x[fo][:, t0:t0 + n], ALU.add)
                    self.proj_chunk(lambda k: w3[:, k, :], h, ev)
                units.append((f"ffn_dn{li}", half * 8 + fo, 1408, fn2))
            self.run_units(units)
        S.release(m)

    def lru(self, li, j):
        S = self.S
        self.rmsnorm_to_xn(f"nmix{li}")
        m = S.mark()
        HALF = [(0, 1024, (0, 1)), (1024, T - 1024, (2, 3, 4))]
        HN = T - 1024
        cA = S.sb("cA", [128, 8], F32)
        ncA = S.sb("ncA", [128, 8], F32)
        lam = self.col(f"lru_lam{j}", 0, 8)
        S.act(cA[:], lam, AF.Exp, scale=-1.0)
        S.act(cA[:], cA[:], AF.Ln, bias=self.one_col[:, 0:1])
        S.ts("dve", ncA[:], cA[:], 8.0, ALU.mult)
        S.ts("dve", cA[:], cA[:], -8.0, ALU.mult)
        hg = [S.sb(f"hg{k}", [128, T], BF16) for k in range(2)]
        gate = [S.sb(f"gate{k}", [128, T], BF16) for k in range(2)]
        xx = [S.sb(f"xx{k}", [128, TP + 3], F32) for k in range(2)]
        xs = [S.sb(f"xs{k}", [128, NS, 4], F32) for k in range(2)]
        xcb = [S.sb(f"xcb{k}", [128, T], BF16) for k in range(2)]
        ctmp = S.sb("ctmp", [128, T], F32)
        ra = S.sb("ra", [128, HN], F32)
        ri = S.sb("ri", [128, HN], F32)
        av = S.sb("av", [128, HN], F32)
        tmp = S.sb("tmp", [128, HN], F32)
        carry = S.sb("carry", [128, 1], F32)
        p_h = self.small_alloc(f"p_lru_h{j}", 8)
        p_cv = self.small_alloc(f"p_lru_conv{j}", 24)
        s_h = self.small_alloc(f"s_lru_h{j}", 32)
        s_cv = self.small_alloc(f"s_lru_conv{j}", 96)
        s_cv4 = s_cv.rearrange("p (k b j) -> p k b j", k=8, b=NS)
        s_h3 = s_h.rearrange("p (k b) -> p k b", k=8)
        st_h = S.sb("st_h", [128, 8, NS], F32)
        S.dma(st_h[:], self.dram["s_lru_h"][:, j], sem=f"st{j}")
        st_c = S.sb("st_c", [128, 8, NS, 3], F32)
        S.dma(st_c[:], self.dram["s_lru_conv"][:, j], sem=f"st{j}")
        for k in range(2):
            S.memset("pool", xx[k][:, 0:3], 0.0)

        units = []
        for n in range(4):
            def f_gate(wb, n=n):
                w3 = wb[:, 0:2048].rearrange("p (k f) -> p k f", k=8)
                for c in range(2):
                    def ev(ti, t0, nn, acc, c=c):
                        S.act(gate[c][:, t0:t0 + nn], acc[:, 0:nn], AF.Gelu)
                    self.proj_chunk(lambda k, c=c: w3[:, k, c * 128:(c + 1) * 128], self.xn, ev)
            units.append((f"lru_win{j}", 2 * n, 2048, f_gate))

            def f_x(wb, n=n):
                w3 = wb[:, 0:2048].rearrange("p (k f) -> p k f", k=8)
                for c in range(2):
                    kc = 2 * n + c

                    def ev(ti, t0, nn, acc, c=c, kc=kc):
                        if t0 + nn <= TP:
                            S.copy("act", xx[c][:, 3 + t0:3 + t0 + nn], acc[:, 0:nn])
                        else:
                            npz = TP - t0
                            S.copy("act", xx[c][:, 3 + t0:3 + TP], acc[:, 0:npz])
                            S.copy("act", xs[c][:, :, 3], acc[:, npz:npz + NS])
                    self.proj_chunk(lambda k, c=c: w3[:, k, c * 128:(c + 1) * 128], self.xn, ev)
                    S.copy("pool", xs[c][:, :, 0:3], st_c[:, kc, :, :])
                    S.copy("pool", p_cv[:, kc * 3:(kc + 1) * 3], xx[c][:, TP:TP + 3])
                    S.copy("pool", s_cv4[:, kc, :, :], xs[c][:, :, 1:4])
                    cw = lambda q, kc=kc: self.col(f"lru_cw{j}_{q}", kc)
                    cb = self.col(f"lru_cb{j}", kc)
                    S.ts("dve", ctmp[:, 0:TP], xx[c][:, 0:TP], cw(0), ALU.mult, cb, ALU.add)
                    for q in range(1, 3):
                        S.stt(ctmp[:, 0:TP], xx[c][:, q:q + TP], cw(q), ctmp[:, 0:TP], ALU.mult, ALU.add)
                    S.stt(xcb[c][:, 0:TP], xx[c][:, 3:3 + TP], cw(3), ctmp[:, 0:TP], ALU.mult, ALU.add)
                    S.ts("dve", ctmp[:, TP:T], xs[c][:, :, 0], cw(0), ALU.mult, cb, ALU.add)
                    for q in range(1, 3):
                        S.stt(ctmp[:, TP:T], xs[c][:, :, q], cw(q), ctmp[:, TP:T], ALU.mult, ALU.add)
                    S.stt(xcb[c][:, TP:T], xs[c][:, :, 3], cw(3), ctmp[:, TP:T], ALU.mult, ALU.add)
            units.append((f"lru_win{j}", 2 * n + 1, 2048, f_x))

            def f_g(wb, n=n):
                w4 = wb[:, 0:1024].rearrange("p (g k f) -> p g k f", g=2, k=2)
                for c in range(2):
                    kc = 2 * n + c
                    for (h0, hn, tiles) in HALF:
                        for ti in tiles:
                            t0, nn = TT[ti]
                            pa = self.bank()
                            pi = self.bank()
                            for k in range(2):
                                S.mm(pa[:, 0:nn], w4[:, 0, k, c * 128:(c + 1) * 128], xcb[k][:, t0:t0 + nn],
                                     start=(k == 0), stop=(k == 1))
                            for k in range(2):
                                S.mm(pi[:, 0:nn], w4[:, 1, k, c * 128:(c + 1) * 128], xcb[k][:, t0:t0 + nn],
                                     start=(k == 0), stop=(k == 1))
                            S.act(ra[:, t0 - h0:t0 - h0 + nn], pa[:, 0:nn], AF.Sigmoid, bias=self.col(f"lru_ba{j}", kc))
                            S.act(ri[:, t0 - h0:t0 - h0 + nn], pi[:, 0:nn], AF.Sigmoid, bias=self.col(f"lru_bi{j}", kc))
                        R = slice(0, hn)
                        G = slice(h0, h0 + hn)
                        S.act(av[:, R], ra[:, R], AF.Exp, scale=cA[:, kc:kc + 1])
                        S.act(tmp[:, R], ra[:, R], AF.Tanh, scale=ncA[:, kc:kc + 1])
                        S.tt("pool", ra[:, R], av[:, R], av[:, R], ALU.mult)
                        S.stt(tmp[:, R], ra[:, R], 1.0, tmp[:, R], ALU.add, ALU.mult)
                        S.act(tmp[:, R], tmp[:, R], AF.Sqrt)
                        S.tt("pool", ri[:, R], ri[:, R], xcb[c][:, G], ALU.mult)
                        S.tt("dve", ri[:, R], ri[:, R], tmp[:, R], ALU.mult)
                        if h0 == 0:
                            S.scan(tmp[:, R], av[:, R], ri[:, R], 0.0)
                            S.copy("pool", carry[:], tmp[:, hn - 1:hn])
                        else:
                            npr = TP - h0
                            S.scan(tmp[:, 0:npr], av[:, 0:npr], ri[:, 0:npr], carry[:, 0:1])
                            S.tt("dve", tmp[:, npr:hn], av[:, npr:hn], st_h[:, kc, :], ALU.mult)
                            S.tt("dve", tmp[:, npr:hn], tmp[:, npr:hn], ri[:, npr:hn], ALU.add)
                            S.copy("pool", p_h[:, kc:kc + 1], tmp[:, npr - 1:npr])
                            S.copy("pool", s_h3[:, kc, :], tmp[:, npr:hn])
                        S.tt("dve", hg[c][:, G], tmp[:, R], gate[c][:, G], ALU.mult)
            units.append((f"lru_wg{j}", n, 1024, f_g))

            def f_o(wb, n=n):
                w3 = wb[:, 0:2048].rearrange("p (k f) -> p k f", k=2)
                for fo in range(KC):
                    def ev(ti, t0, nn, acc, fo=fo):
                        S.tt("dve", self.x[fo][:, t0:t0 + nn], acc[:, 0:nn], self.x[fo][:, t0:t0 + nn], ALU.add)
                    self.proj_chunk(lambda k, fo=fo: w3[:, k, fo * 128:(fo + 1) * 128], hg, ev)
            units.append((f"lru_wout{j}", n, 2048, f_o))
        self.run_units(units)
        S.release(m)

    def rbank(self, i):
        return self.ps[:, i * 512:(i + 1) * 512]

    def rope_tile(self, dst, p_raw, p_swp, t0, n, cs, t1, t2):
        S = self.S
        S.dma(cs[:, :, 0:n], self.dram["rope"][:, :, t0:t0 + n], sem="cs")
        S.tt("dve", t1[:, 0:n], p_raw, cs[:, 0, 0:n], ALU.mult)
        S.tt("dve", t2[:, 0:n], p_swp, cs[:, 1, 0:n], ALU.mult)
        S.tt("pool", dst, t1[:, 0:n], t2[:, 0:n], ALU.add)

    def mla(self, li, j):
        S = self.S
        self.rmsnorm_to_xn(f"nmix{li}")
        xn_base = S.bases[self.xn[0].name]
        m0 = S.mark()
        ckvb = [S.sb(f"ckvb{k}", [128, T], BF16) for k in range(2)]
        kpeb = S.sb("kpeb", [64, T], BF16)
        cqb = [S.sb(f"cqb{k}", [128, T], BF16) for k in range(4)]
        qs = S.sb("qs", [128, 3, NS, 8], BF16)
        ols = S.sb("ols", [128, 2, 8, NS], BF16)
        trib = S.sb("trib", [128, 128], BF16)
        trif = S.sb("trif", [128, 128], F32)
        S.dma(trif[:], self.dram["tri"], sem="tri")
        S.copy("pool", trib[:], trif[:])
        cs = S.sb("cs", [64, 2, 512], F32)
        rt1 = S.sb("rt1", [64, 512], F32)
        rt2 = S.sb("rt2", [64, 512], F32)
        m1 = S.mark()
        kpef = S.sb("kpef", [64, T], F32)
        ckvT = [S.sb(f"ckvT{k}", [128, T], F32) for k in range(2)]

        wq = [self.wload("mla_dq", u, 2048) for u in range(2)]
        for ti, (t0, n) in enumerate(TT):
            pb = [self.bank() for _ in range(4)]
            for c4 in range(4):
                w3 = wq[c4 // 2][:, 0:2048].rearrange("p (k f) -> p k f", k=8)
                for k in range(KC):
                    S.mm(pb[c4][:, 0:n], w3[:, k, (c4 % 2) * 128:(c4 % 2 + 1) * 128], self.xn[k][:, t0:t0 + n],
                         start=(k == 0), stop=(k == KC - 1))
            acc = self.rbank(4)
            for c4 in range(4):
                sq = self.sq[c4 % 2]
                S.act(sq[:, 0:n], pb[c4][:, 0:n], AF.Square)
                S.mm(acc[:, 0:n], self.ones_b[:], sq[:, 0:n], start=(c4 == 0), stop=(c4 == 3))
            r = self.rstd[ti % 2]
            S.act(r[:, 0:n], acc[:, 0:n], AF.Sqrt, bias=self.eps_col[:, 0:1], scale=1.0 / 512)
            S.recip(r[:, 0:n], r[:, 0:n])
            for c4 in range(4):
                S.stt(cqb[c4][:, t0:t0 + n], pb[c4][:, 0:n], self.col("mla_qn", c4), r[:, 0:n], ALU.mult, ALU.mult)

        wr = self.wload("mla_dkv_r", 0, 1024)
        wr3 = wr[:, 0:1024].rearrange("p (k f) -> p k f", k=8)
        for ti, (t0, n) in enumerate(TT):
            p1 = self.bank()
            p2 = self.bank()
            for k in range(KC):
                S.mm(p1[0:64, 0:n], wr3[:, k, 0:64], self.xn[k][:, t0:t0 + n], start=(k == 0), stop=(k == KC - 1))
            for k in range(KC):
                S.mm(p2[0:64, 0:n], wr3[:, k, 64:128], self.xn[k][:, t0:t0 + n], start=(k == 0), stop=(k == KC - 1))
            self.rope_tile(kpef[:, t0:t0 + n], p1[0:64, 0:n], p2[0:64, 0:n], t0, n, cs, rt1, rt2)
        S.copy("pool", kpeb[:], kpef[:])

        wc = self.wload("mla_dkv_c", 0, 2048)
        wc3 = wc[:, 0:2048].rearrange("p (k f) -> p k f", k=8)
        for ti, (t0, n) in enumerate(TT):
            pb = [self.bank() for _ in range(2)]
            for c2 in range(2):
                for k in range(KC):
                    S.mm(pb[c2][:, 0:n], wc3[:, k, c2 * 128:(c2 + 1) * 128], self.xn[k][:, t0:t0 + n],
                         start=(k == 0), stop=(k == KC - 1))
            acc = self.rbank(4)
            for c2 in range(2):
                sq = self.sq[c2 % 2]
                S.act(sq[:, 0:n], pb[c2][:, 0:n], AF.Square)
                S.mm(acc[:, 0:n], self.ones_b[:], sq[:, 0:n], start=(c2 == 0), stop=(c2 == 1))
            r = self.rstd[ti % 2]
            S.act(r[:, 0:n], acc[:, 0:n], AF.Sqrt, bias=self.eps_col[:, 0:1], scale=1.0 / 256)
            S.recip(r[:, 0:n], r[:, 0:n])
            for c2 in range(2):
                S.stt(ckvT[c2][:, t0:t0 + n], pb[c2][:, 0:n], self.col("mla_kvn", c2), r[:, 0:n], ALU.mult, ALU.mult)
        for c2 in range(2):
            S.copy("pool", ckvb[c2][:], ckvT[c2][:])

        sv = S.sb_ptr
        S.sb_ptr = xn_base
        vtok = S.sb("vtok", [128, 17, 256], BF16)
        ostg = [S.sb(f"ostg{i}", [128, 320], F32) for i in range(2)]
        qaug = [S.sb(f"qaug{k}", [128, T], BF16) for k in range(3)]
        oh = S.sb("oh", [128, T], BF16)
        vnew = S.sb("vnew", [1, NS, 257], BF16)
        assert S.sb_ptr <= xn_base + 8 * ((T * 2 + 63) // 64 * 64), "xn overlay overflow"
        S.sb_ptr = sv

        okv = self.out("p_kv", [T, 320])
        for bi in range(17 if "T" not in os.environ.get("KSKIP", "") else 0):
            t0 = bi * 128
            n = min(128, T - t0)
            pt_ = self.bank()
            for c2 in range(2):
                S.tr(pt_[0:n, c2 * 128:(c2 + 1) * 128], ckvT[c2][:, t0:t0 + n], self.ident_f[:])
            S.tr(pt_[0:n, 256:320], kpef[:, t0:t0 + n], self.ident_f[0:64, 0:64])
            og = ostg[bi % 2]
            KS = os.environ.get("KSKIP", "")
            if "1" not in KS:
                S.copy("act", og[0:n, :], pt_[0:n, 0:320])
            if "2" not in KS:
                S.copy("dve", vtok[0:n, bi, :], pt_[0:n, 0:256])
            if "3" not in KS:
                S.dma(okv[t0:t0 + n, :], og[0:n, :], sem=f"okv{bi % 2}")
        S.memset("pool", vnew[:], 1.0)
        for b in range(NS if "V" not in os.environ.get("KSKIP", "") else 0):
            pt_ = self.bank()
            for c2 in range(2):
                S.tr(pt_[0:1, c2 * 128:(c2 + 1) * 128], ckvT[c2][:, TP + b:TP + b + 1], self.ident_f[:])
            S.copy("dve", vnew[0:1, b, 0:256], pt_[0:1, 0:256])
        S.release(m1)

        mA = S.mark()
        qn_s = S.sb("qn_s", [128, NS], BF16)
        u1 = []

        def passA(wb, h):
            wq3 = wb[:, 0:1024].rearrange("p (k f) -> p k f", k=4)
            wuk = wb[:, 1024:1280]
            pn = self.bank()
            for k in range(4):
                S.mm(pn[:, 0:NS], wq3[:, k, 0:128], cqb[k][:, TP:T], start=(k == 0), stop=(k == 3))
            S.copy("dve", qn_s[:], pn[:, 0:NS])
            p1 = self.bank()
            p2 = self.bank()
            for k in range(4):
                S.mm(p1[0:64, 0:NS], wq3[:, k, 128:192], cqb[k][:, TP:T], start=(k == 0), stop=(k == 3))
            for k in range(4):
                S.mm(p2[0:64, 0:NS], wq3[:, k, 192:256], cqb[k][:, TP:T], start=(k == 0), stop=(k == 3))
            self.rope_tile(qs[0:64, 2, :, h], p1[0:64, 0:NS], p2[0:64, 0:NS], TP, NS, cs, rt1, rt2)
            for c2 in range(2):
                pl = self.bank()
                S.mm(pl[:, 0:NS], wuk[:, c2 * 128:(c2 + 1) * 128], qn_s[:], start=True, stop=True)
                S.copy("dve", qs[:, c2, :, h], pl[:, 0:NS])
        if "A" not in os.environ.get("KSKIP", ""):
            self.run_units([("mla_u1", h, 1280, (lambda wb, h=h: passA(wb, h))) for h in range(8)])
        S.release(mA)

        if "D" not in os.environ.get("KSKIP", ""):
            self.mla_decode(qs, ols, ckvb, kpeb, vnew)
        else:
            S.memset("pool", ols[:], 0.0)

        mB = S.mark()
        qn = S.sb("qn", [128, T], BF16)
        olat = [S.sb(f"olat{k}", [128, T], BF16) for k in range(2)]
        PT = [S.sb(f"PT{i}", [128, 512], BF16) for i in range(3)]
        rden = S.sb("rden", [128, 512], F32)
        QT = [(0, 512), (512, 512), (1024, 512), (1536, 512), (2048, 16)]

        def head_q(wb, h):
            wq3 = wb[:, 0:1024].rearrange("p (k f) -> p k f", k=4)
            wuk = wb[:, 1024:1280]

            def ev(ti, t0, n, acc):
                S.copy("act", qn[:, t0:t0 + n], acc[:, 0:n])
            self.proj_chunk(lambda k: wq3[:, k, 0:128], cqb, ev)
            for ti, (t0, n) in enumerate(TT):
                p1 = self.bank()
                p2 = self.bank()
                for k in range(4):
                    S.mm(p1[0:64, 0:n], wq3[:, k, 128:192], cqb[k][:, t0:t0 + n], start=(k == 0), stop=(k == 3))
                for k in range(4):
                    S.mm(p2[0:64, 0:n], wq3[:, k, 192:256], cqb[k][:, t0:t0 + n], start=(k == 0), stop=(k == 3))
                self.rope_tile(qaug[2][0:64, t0:t0 + n], p1[0:64, 0:n], p2[0:64, 0:n], t0, n, cs, rt1, rt2)
            for c2 in range(2):
                def ev2(ti, t0, n, acc, c2=c2):
                    S.copy("act", qaug[c2][:, t0:t0 + n], acc[:, 0:n])
                self.proj_chunk(lambda k, c2=c2: wuk[:, c2 * 128:(c2 + 1) * 128], [qn], ev2)
            a0, a1, dn_ = self.rbank(4), self.rbank(5), self.rbank(6)
            pairs = []
            for (q0, qn_) in QT:
                nb = (q0 + qn_ - 1) // 128 + 1
                for jb in range(nb):
                    k0 = jb * 128
                    kn = min(128, TP - k0)
                    qs0 = max(q0, k0)
                    pairs.append(dict(q0=q0, qn=qn_, jb=jb, k0=k0, kn=kn, qs0=qs0, nc=q0 + qn_ - qs0, off=qs0 - q0,
                                      first=(jb == 0), last=(jb == nb - 1), idx=len(pairs)))

            def scores(p):
                kn, nc_, k0, qs0 = p["kn"], p["nc"], p["k0"], p["qs0"]
                sp = self.bank()
                S.mm(sp[0:kn, 0:nc_], ckvb[0][:, k0:k0 + kn], qaug[0][:, qs0:qs0 + nc_], start=True, stop=False)
                S.mm(sp[0:kn, 0:nc_], ckvb[1][:, k0:k0 + kn], qaug[1][:, qs0:qs0 + nc_], start=False, stop=False)
                S.mm(sp[0:kn, 0:nc_], kpeb[0:64, k0:k0 + kn], qaug[2][0:64, qs0:qs0 + nc_], start=False, stop=True)
                pt_ = PT[p["idx"] % len(PT)]
                S.act(pt_[0:kn, 0:nc_], sp[0:kn, 0:nc_], AF.Exp, scale=MLA_SCALE)
                if k0 >= p["q0"]:
                    dnn = min(128, nc_)
                    S.tt("pool", pt_[0:kn, 0:dnn], pt_[0:kn, 0:dnn], trib[0:kn, 0:dnn], ALU.mult)

            def pv(p):
                kn, nc_, off, jb = p["kn"], p["nc"], p["off"], p["jb"]
                pt_ = PT[p["idx"] % len(PT)]
                S.mm(a0[:, off:off + nc_], vtok[0:kn, jb, 0:128], pt_[0:kn, 0:nc_], start=p["first"], stop=p["last"])
                S.mm(a1[:, off:off + nc_], vtok[0:kn, jb, 128:256], pt_[0:kn, 0:nc_], start=p["first"], stop=p["last"])
                S.mm(dn_[:, off:off + nc_], self.ones_b[0:kn, :], pt_[0:kn, 0:nc_], start=p["first"], stop=p["last"])
                if p["last"]:
                    q0, qn_ = p["q0"], p["qn"]
                    S.recip(rden[:, 0:qn_], dn_[:, 0:qn_])
                    S.tt("dve", olat[0][:, q0:q0 + qn_], a0[:, 0:qn_], rden[:, 0:qn_], ALU.mult)
                    S.tt("dve", olat[1][:, q0:q0 + qn_], a1[:, 0:qn_], rden[:, 0:qn_], ALU.mult)
            scores(pairs[0])
            for i, p in enumerate(pairs):
                if i + 1 < len(pairs):
                    scores(pairs[i + 1])
                pv(p)
            for c2 in range(2):
                S.copy("pool", olat[c2][:, TP:T], ols[:, c2, h, :])

        def head_o(wb, h):
            wuv = wb[:, 0:256].rearrange("p (k v) -> p k v", k=2)
            wo = wb[:, 256:1280]

            def ev(ti, t0, n, acc):
                S.copy("act", oh[:, t0:t0 + n], acc[:, 0:n])
            self.proj_chunk(lambda k: wuv[:, k, :], olat, ev)
            for fo in range(KC):
                def ev2(ti, t0, n, acc, fo=fo):
                    S.tt("dve", self.x[fo][:, t0:t0 + n], acc[:, 0:n], self.x[fo][:, t0:t0 + n], ALU.add)
                self.proj_chunk(lambda k, fo=fo: wo[:, fo * 128:(fo + 1) * 128], [oh], ev2)
        units = []
        for h in range(8):
            units.append(("mla_u1", h, 1280, (lambda wb, h=h: head_q(wb, h))))
            units.append(("mla_u2", h, 1280, (lambda wb, h=h: head_o(wb, h))))
        if "B" not in os.environ.get("KSKIP", ""):
            self.run_units(units)
        S.release(m0)

    def mla_decode(self, qs, ols, ckvb, kpeb, vnew):
        S = self.S
        m = S.mark()
        NTK = 4
        NSUB = PAGE // NTK
        NBUF = 4
        ptab = S.sb("ptab", [128, NS], I32)
        S.dma(ptab[:], self.dram["pt"], sem="ptab")
        idx = S.sb("idx", [128, NS, NSUB], I32)
        for b in range(NS):
            for s_ in range(NSUB):
                S.ts("dve", idx[:, b, s_:s_ + 1], ptab[:, b:b + 1], float(NSUB), ALU.mult, float(s_), ALU.add)
        kvs = [S.sb(f"kvs{i}", [128, NTK * 320], F32) for i in range(NBUF)]
        kT = [S.sb(f"kT{i}", [128, 384], BF16) for i in range(2)]
        Vb = [S.sb(f"Vb{i}", [128, NTK, 257], BF16) for i in range(2)]
        for i in range(2):
            S.memset("pool", Vb[i][:], 1.0)
        PTd = [S.sb(f"PTd{i}", [128, NTK * 8], BF16) for i in range(2)]
        pnew = S.sb("pnew", [1, 8], BF16)
        osb = S.sb("osb", [8, 257], F32)
        rd = S.sb("rd", [8, 1], F32)
        onb = S.sb("onb", [8, 256], F32)
        accb = self.rbank(7)
        toks = []
        g = 0
        for b in range(NS):
            for s_ in range(NSUB):
                for tt_ in range(NTK):
                    toks.append((b, s_, tt_, g))
                g += 1
        pks = {}

        def start_chunk(b, s_, g):
            kv = kvs[g % NBUF]
            S.gather(kv[:], self.dram["poolkv"], idx[:, b, s_:s_ + 1], sem=f"kv{g % NBUF}")

        def vcast(g):
            kv = kvs[g % NBUF]
            S.copy("act", Vb[g % 2][:, :, 0:256], kv[:].rearrange("p (t c) -> p t c", t=NTK)[:, :, 0:256])

        def transposes(i):
            b, s_, tt_, g = toks[i]
            kv = kvs[g % NBUF]
            pk = self.bank()
            base = tt_ * 320
            S.tr(pk[:, 0:128], kv[:, base:base + 128], self.ident_f[:])
            S.tr(pk[:, 128:256], kv[:, base + 128:base + 256], self.ident_f[:])
            S.tr(pk[0:64, 256:384], kv[:, base + 256:base + 320], self.ident_f[:])
            kt = kT[i % 2]
            S.copy("dve", kt[:, 0:256], pk[:, 0:256])
            S.copy("dve", kt[0:64, 256:384], pk[0:64, 256:384])

        def qk(i):
            b, s_, tt_, g = toks[i]
            kt = kT[i % 2]
            sp = self.rbank(5 + (g % 2))
            o_ = sp[:, tt_ * 8:(tt_ + 1) * 8]
            S.mm(o_, kt[:, 0:128], qs[:, 0, b, :], start=True, stop=False)
            S.mm(o_, kt[:, 128:256], qs[:, 1, b, :], start=False, stop=False)
            S.mm(o_, kt[0:64, 256:384], qs[0:64, 2, b, :], start=False, stop=True)
            if tt_ == NTK - 1:
                S.act(PTd[g % 2][:], sp[:, 0:NTK * 8], AF.Exp, scale=MLA_SCALE)

        def pv(b, s_, g):
            for tt_ in range(NTK):
                S.mm(accb[0:8, 0:257], PTd[g % 2][:, tt_ * 8:(tt_ + 1) * 8], Vb[g % 2][:, tt_, :],
                     start=(s_ == 0 and tt_ == 0), stop=False)

        def finish(b):
            sp = self.bank()
            S.mm(sp[0:1, 0:8], ckvb[0][:, TP + b:TP + b + 1], qs[:, 0, b, :], start=True, stop=False)
            S.mm(sp[0:1, 0:8], ckvb[1][:, TP + b:TP + b + 1], qs[:, 1, b, :], start=False, stop=False)
            S.mm(sp[0:1, 0:8], kpeb[0:64, TP + b:TP + b + 1], qs[0:64, 2, b, :], start=False, stop=True)
            S.act(pnew[:], sp[0:1, 0:8], AF.Exp, scale=MLA_SCALE)
            S.mm(accb[0:8, 0:257], pnew[:], vnew[0:1, b, :], start=False, stop=True)
            S.copy("dve", osb[:], accb[0:8, 0:257])
            S.recip(rd[:], osb[:, 256:257])
            S.ts("dve", onb[:], osb[:, 0:256], rd[:, 0:1], ALU.mult)
            for c2 in range(2):
                po = self.bank()
                S.tr(po[:, 0:8], onb[:, c2 * 128:(c2 + 1) * 128], self.ident_f[0:8, 0:8])
                S.copy("dve", ols[:, c2, :, b], po[:, 0:8])

        n = len(toks)
        started = 0
        nchunks = NS * NSUB

        def ensure_started(upto):
            nonlocal started
            while started <= min(upto, nchunks - 1):
                bb, ss = divmod(started, NSUB)
                start_chunk(bb, ss, started)
                started += 1
        ensure_started(NBUF - 2)
        transposes(0)
        pending_pv = None
        for i in range(n):
            b, s_, tt_, g = toks[i]
            if tt_ == 0:
                ensure_started(g + NBUF - 2)
                vcast(g)
            if i + 1 < n:
                transposes(i + 1)
            qk(i)
            if tt_ == NTK - 1:
                if pending_pv is not None:
                    pv(*pending_pv)
                    if pending_pv[1] == NSUB - 1:
                        finish(pending_pv[0])
                pending_pv = (b, s_, g)
        pv(*pending_pv)
        finish(pending_pv[0])
        S.release(m)

    def mmf(self, out, lhsT, rhs, start=True, stop=True):
        return self.S.mm(out, lhsT, rhs, start, stop)

    def dn(self, li, j):
        S = self.S
        self.rmsnorm_to_xn(f"nmix{li}")
        m0 = S.mark()
        small2 = S.sb("small2", [128, 360], F32)
        S.memset("pool", small2[:], 0.0)
        p_cv = small2[:, 0:72].rearrange("p (k j) -> p k j", k=24)
        s_cv = small2[:, 72:360].rearrange("p (k b j) -> p k b j", k=24, b=NS)
        oS = self.out("o_dn_S", [8 + NS * 8, 128, 128])
        Gc = S.sb("Gc", [8, T], F32)
        GT = S.sb("GT", [64, 37, 8], F32)
        BT = S.sb("BT", [64, 37, 8], F32)
        sel = S.sb("sel", [8, 128], F32)
        mm_ = S.sb("mm_", [64, 128], F32)
        S.dma(mm_[:], self.dram["dn_mm"], sem="dnc")
        st_c = S.sb("dst_c", [128, 24, NS, 3], F32)
        S.dma(st_c[:], self.dram["s_dn_conv"], sem="dnc")
        chunks = [(0, 16, 4)] + [(16 + 64 * c, 64, 6) for c in range(32)] + [(TP + b, 1, 0) for b in range(NS)]
        xx = S.sb("dxx", [128, TP + 3], F32)
        xs = S.sb("dxs", [128, NS, 4], F32)
        ctmp = S.sb("dctmp", [128, T], F32)
        S.memset("pool", xx[:, 0:3], 0.0)
        qdec = S.sb("qdec", [128, T], BF16)
        qn = S.sb("dqn", [128, T], BF16)
        kn = S.sb("kn", [128, T], BF16)
        vb = S.sb("vb", [128, T], BF16)
        sz = S.sb("sz", [128, T], BF16)
        GB = S.sb("GB", [128, T], F32)
        oT = S.sb("oT", [128, T], BF16)
        Sf = S.sb("Sf", [128, 128], F32)
        Sb_ = S.sb("Sb", [128, 128], BF16)
        eg = self.rstd[1]
        wba = self.wload("dn_ba", 0, 128)
        wba3 = wba[:, 0:128].rearrange("p (k f) -> p k f", k=8)
        Ball = GB[0:8, 0:T]
        graw = ctmp[0:8, 0:T]
        sv_ = S.sb_ptr
        S.sb_ptr = S.bases[qdec.name]
        mrow_t = S.sb("mrow", [8, T], F32)
        S.sb_ptr = sv_
        mrow = mrow_t[:, :]
        S.dma(mrow, self.dram["dn_mask"], sem="dnc")
        nA = S.sb("nA", [8, 1], F32)
        ab = self.col("dn_ab", 0, 2)
        S.act(nA[:], ab[0:8, 0:1], AF.Exp)
        S.ts("dve", nA[:], nA[:], -1.0, ALU.mult)
        for ti, (t0, n) in enumerate(TT):
            pb_, pa_ = self.bank(), self.bank()
            for k in range(KC):
                S.mm(pb_[0:8, 0:n], wba3[:, k, 0:8], self.xn[k][:, t0:t0 + n], start=(k == 0), stop=(k == KC - 1))
            for k in range(KC):
                S.mm(pa_[0:8, 0:n], wba3[:, k, 8:16], self.xn[k][:, t0:t0 + n], start=(k == 0), stop=(k == KC - 1))
            S.act(Ball[:, t0:t0 + n], pb_[0:8, 0:n], AF.Sigmoid)
            S.act(graw[:, t0:t0 + n], pa_[0:8, 0:n], AF.Exp, bias=ab[0:8, 1:2])
            S.act(graw[:, t0:t0 + n], graw[:, t0:t0 + n], AF.Ln, bias=self.one_col[0:8, 0:1])
        S.ts("dve", graw, graw, nA[:, 0:1], ALU.mult)
        S.scan(Gc[:], mrow, graw, 0.0)
        for ci, (t0, C, L) in enumerate(chunks):
            pt_ = self.bank()
            S.tr(pt_[0:C, 0:8], Gc[:, t0:t0 + C], self.ident_f[0:8, 0:8])
            S.tr(pt_[0:C, 8:16], Ball[:, t0:t0 + C], self.ident_f[0:8, 0:8])
            S.copy("dve", GT[0:C, ci, :], pt_[0:C, 0:8])
            S.copy("dve", BT[0:C, ci, :], pt_[0:C, 8:16])
        sv_ = S.sb_ptr
        S.sb_ptr = S.bases[xx.name]
        F1 = S.sb("gF1", [64, 8, 64], F32)
        F2 = S.sb("gF2", [64, 8, 64], F32)
        gbf = lambda nm: S.sb(nm, [64, 8, 64], BF16)
        A1, A2, A3, B1, B2, B3 = (gbf(nm) for nm in ("gA1", "gA2", "gA3", "gB1", "gB2", "gB3"))
        usb0 = S.sb("usb", [64, 8, 128], F32)
        attnT0 = S.sb("attnT", [64, 8, 64], BF16)
        wT0 = S.sb("wT", [128, 8, 64], BF16)
        assert S.sb_ptr <= S.bases[ctmp.name] + T * 4, "group overlay overflow"
        S.sb_ptr = sv_
        Vb_ = S.sb("Vbt", [64, 8, 128], BF16)
        Kb_ = S.sb("Kbt", [64, 8, 128], BF16)
        delta = S.sb("delta", [64, 128], BF16)
        cols_ = S.sb("ccols", [64, 8, 4], F32)
        usbs = [usb0, S.sb("usb1", [64, 8, 128], F32)]
        attnTs = [attnT0, S.sb("attnT1", [64, 8, 64], BF16)]
        wTs = [wT0, S.sb("wT1", [128, 8, 64], BF16)]
        kdecs = [S.sb(f"kdec{i}", [64, 8, 128], BF16) for i in range(2)]
        egls = [S.sb(f"egl{i}", [128, 8], F32) for i in range(2)]
        mmax, mmin = mm_[:, 0:64], mm_[:, 64:128]

        def conv_silu(psrc_list, kc, dst_f32):
            for (ti, t0, n, acc) in psrc_list:
                if t0 + n <= TP:
                    S.copy("act", xx[:, 3 + t0:3 + t0 + n], acc[:, 0:n])
                else:
                    npz = TP - t0
                    S.copy("act", xx[:, 3 + t0:3 + TP], acc[:, 0:npz])
                    S.copy("act", xs[:, :, 3], acc[:, npz:npz + NS])

        def conv_finish(kc, dst):
            S.memset("pool", xx[:, 0:3], 0.0)
            S.copy("pool", xs[:, :, 0:3], st_c[:, kc, :, :])
            S.copy("pool", p_cv[:, kc, :], xx[:, TP:TP + 3])
            S.copy("pool", s_cv[:, kc, :, :], xs[:, :, 1:4])
            cw = lambda q: self.col(f"dn_cw{q}", kc)
            S.ts("dve", ctmp[:, 0:TP], xx[:, 0:TP], cw(0), ALU.mult)
            for q in range(1, 4):
                S.stt(ctmp[:, 0:TP], xx[:, q:q + TP], cw(q), ctmp[:, 0:TP], ALU.mult, ALU.add)
            S.ts("dve", ctmp[:, TP:T], xs[:, :, 0], cw(0), ALU.mult)
            for q in range(1, 4):
                S.stt(ctmp[:, TP:T], xs[:, :, q], cw(q), ctmp[:, TP:T], ALU.mult, ALU.add)
            S.act(dst, ctmp[:], AF.Silu)

        def l2n(src, dst_bf, scale):
            for ti, (t0, n) in enumerate(TT):
                sq = self.sq[ti % 2]
                S.act(sq[:, 0:n], src[:, t0:t0 + n], AF.Square)
                acc = self.bank()
                S.mm(acc[:, 0:n], self.ones_b[:], sq[:, 0:n], start=True, stop=True)
                r = self.rstd[ti % 2]
                S.act(r[:, 0:n], acc[:, 0:n], AF.Sqrt, bias=self.eps_col[:, 0:1], scale=1.0 / (scale * scale))
                S.recip(r[:, 0:n], r[:, 0:n])
                S.tt("pool", dst_bf[:, t0:t0 + n], src[:, t0:t0 + n], r[:, 0:n], ALU.mult)

        def proj2(wb, c, kc):
            w3 = wb[:, 0:2048].rearrange("p (k f) -> p k f", k=8)
            lst = []
            for ti, (t0, n) in enumerate(TT):
                acc = self.bank()
                for k in range(KC):
                    S.mm(acc[:, 0:n], w3[:, k, c * 128:(c + 1) * 128], self.xn[k][:, t0:t0 + n], start=(k == 0), stop=(k == KC - 1))
                conv_silu([(ti, t0, n, acc)], kc, None)

        def head_qk(wb, h):
            proj2(wb, 0, h)
            conv_finish(h, ctmp[:])
            l2n(ctmp, qn, 128.0 ** -0.5)
            S.dma(sel[:], self.dram["dn_sel"][:, h * 128:(h + 1) * 128], sem="dnsel")
            for ti, (t0, n) in enumerate(TT):
                acc = self.bank()
                S.mm(acc[:, 0:n], sel[:, :], Gc[:, t0:t0 + n], start=True, stop=True)
                S.copy("act", GB[:, t0:t0 + n], acc[:, 0:n])
                S.act(eg[:, 0:n], acc[:, 0:n], AF.Exp)
                S.tt("pool", qdec[:, t0:t0 + n], qn[:, t0:t0 + n], eg[:, 0:n], ALU.mult)
            proj2(wb, 1, 8 + h)
            conv_finish(8 + h, ctmp[:])
            l2n(ctmp, kn, 1.0)

        def head_vz(wb, h):
            proj2(wb, 0, 16 + h)
            conv_finish(16 + h, ctmp[:])
            S.copy("pool", vb[:], ctmp[:])
            w3 = wb[:, 0:2048].rearrange("p (k f) -> p k f", k=8)

            def ev(ti, t0, n, acc):
                S.act(sz[:, t0:t0 + n], acc[:, 0:n], AF.Silu)
            self.proj_chunk(lambda k: w3[:, k, 128:256], self.xn, ev)
            S.memset("pool", Sf[:], 0.0)
            S.memset("pool", Sb_[:], 0.0)
            groups = [[0]] + [list(range(1 + 8 * q, 9 + 8 * q)) for q in range(4)] + [[33, 34, 35, 36]]

            def phaseA(grp, bi):
                usb, wT, kdec, attnT, egl = usbs[bi], wTs[bi], kdecs[bi], attnTs[bi], egls[bi]
                ng = len(grp)
                C, L = chunks[grp[0]][1], chunks[grp[0]][2]
                R = slice(0, C)
                pk, pq = self.rbank(0), self.rbank(1)
                ci0 = grp[0]
                tg0 = chunks[ci0][0]
                GR = (R, slice(0, ng), slice(0, C))
                bc = lambda ap: ap.to_broadcast([C, ng, C])
                GBg = GB[0:C, tg0:tg0 + ng * C].rearrange("p (g c) -> p g c", g=ng)
                gcolg = GT[0:C, ci0:ci0 + ng, h:h + 1]
                bcolg = BT[0:C, ci0:ci0 + ng, h:h + 1]
                glastg = GB[0:C, tg0 + C - 1:tg0 + ng * C:C].unsqueeze(2)
                S.act(cols_[R, 0:ng, 0:1], gcolg, AF.Exp)
                S.tt("dve", cols_[R, 0:ng, 1:2], cols_[R, 0:ng, 0:1], bcolg, ALU.mult)
                S.tt("dve", cols_[R, 0:ng, 2:3], glastg, gcolg, ALU.subtract)
                S.act(cols_[R, 0:ng, 2:3], cols_[R, 0:ng, 2:3], AF.Exp)
                S.act(egl[:, 0:ng], GB[:, tg0 + C - 1:tg0 + ng * C:C], AF.Exp)
                for g, ci in enumerate(grp):
                    t0 = chunks[ci][0]
                    cs_ = slice(t0, t0 + C)
                    S.mm(pk[R, g * 64:g * 64 + C], kn[:, cs_], kn[:, cs_], start=True, stop=True)
                    S.mm(pq[R, g * 64:g * 64 + C], kn[:, cs_], qn[:, cs_], start=True, stop=True)
                yield
                S.tt("dve", F2[GR], GBg, bc(gcolg), ALU.subtract)
                S.tt("dve", F1[GR], F2[GR], bc(mmax[0:C, 0:C].unsqueeze(1)), ALU.max)
                S.tt("dve", F2[GR], F2[GR], bc(mmin[0:C, 0:C].unsqueeze(1)), ALU.min)
                pk3 = pk[:, 0:512].rearrange("p (g c) -> p g c", g=8)
                pq3 = pq[:, 0:512].rearrange("p (g c) -> p g c", g=8)
                yield
                S.act(B1[GR], F1[GR], AF.Exp, scale=-1.0)
                S.act(B2[GR], F2[GR], AF.Exp)
                yield
                S.tt("dve", F1[GR], pk3[GR], bc(bcolg), ALU.mult)
                S.stt(F1[GR], F1[GR], -1.0, B1[GR], ALU.mult, ALU.mult)
                S.copy("act", A1[GR], F1[GR])
                S.tt("dve", attnT[GR], pq3[GR], B2[GR], ALU.mult)
                yield
                if C > 1:
                    ptr_ = self.rbank(2)
                    ptr3 = ptr_[:, 0:512].rearrange("p (g c) -> p g c", g=8)
                    for g in range(ng):
                        S.tr(ptr_[R, g * 64:g * 64 + C], F1[R, g, 0:C], self.ident_f[0:C, 0:C])
                    S.copy("dve", A2[GR], ptr3[GR])
                else:
                    S.copy("dve", A2[GR], A1[GR])
                yield
                S.tt("pool", A3[GR], A2[GR], bc(self.ident_b[0:C, 0:C].unsqueeze(1)), ALU.add)
                P_, PT_, TT_ = A1, A2, A3
                P2, PT2, TT2 = B1, B2, B3
                for lv in range(1, L):
                    pl, plT, pl2 = self.rbank(2), self.rbank(3), self.rbank(4)
                    pl3 = pl[:, 0:512].rearrange("p (g c) -> p g c", g=8)
                    plT3 = plT[:, 0:512].rearrange("p (g c) -> p g c", g=8)
                    pl23 = pl2[:, 0:512].rearrange("p (g c) -> p g c", g=8)
                    for g in range(ng):
                        S.mm(pl[R, g * 64:g * 64 + C], PT_[R, g, 0:C], P_[R, g, 0:C], start=True, stop=True)
                    for g in range(ng):
                        S.mm(plT[R, g * 64:g * 64 + C], P_[R, g, 0:C], PT_[R, g, 0:C], start=True, stop=True)
                    yield
                    S.copy("dve", P2[GR], pl3[GR])
                    S.copy("act", PT2[GR], plT3[GR])
                    yield
                    for g in range(ng):
                        S.mm(pl2[R, g * 64:g * 64 + C], P2[R, g, 0:C], TT_[R, g, 0:C], start=True, stop=True)
                    yield
                    S.tt("dve", TT2[GR], pl23[GR], TT_[GR], ALU.add)
                    yield
                    P_, P2 = P2, P_
                    PT_, PT2 = PT2, PT_
                    TT_, TT2 = TT2, TT_
                TTbf = TT_
                for half in range((ng + 3) // 4):
                    pkk, pvv = self.rbank(0 + half), self.rbank(2 + half)
                    for g in range(half * 4, min(ng, half * 4 + 4)):
                        ci = grp[g]
                        t0 = chunks[ci][0]
                        cs_ = slice(t0, t0 + C)
                        o0 = (g % 4) * 128
                        S.mm(pkk[R, o0:o0 + 128], kn[:, cs_], self.ident_b[:], start=True, stop=True)
                        S.mm(pvv[R, o0:o0 + 128], vb[:, cs_], self.ident_b[:], start=True, stop=True)
                    yield
                    h4 = half * 4
                    n4 = min(ng, h4 + 4) - h4
                    pkk3 = pkk[:, 0:512].rearrange("p (g c) -> p g c", g=4)
                    pvv3 = pvv[:, 0:512].rearrange("p (g c) -> p g c", g=4)
                    b4 = lambda ap: ap.to_broadcast([C, n4, 128])
                    S.tt("dve", Kb_[R, h4:h4 + n4, :], pkk3[R, 0:n4, :], b4(cols_[R, h4:h4 + n4, 1:2]), ALU.mult)
                    S.tt("dve", kdec[R, h4:h4 + n4, :], pkk3[R, 0:n4, :], b4(cols_[R, h4:h4 + n4, 2:3]), ALU.mult)
                    S.tt("dve", Vb_[R, h4:h4 + n4, :], pvv3[R, 0:n4, :], b4(BT[0:C, ci0 + h4:ci0 + h4 + n4, h:h + 1]), ALU.mult)
                yield
                pw = self.rbank(2)
                for half in range((ng + 3) // 4):
                    pu = self.rbank(4)
                    for g in range(half * 4, min(ng, half * 4 + 4)):
                        o0 = (g % 4) * 128
                        S.mm(pu[R, o0:o0 + 128], TTbf[R, g, 0:C], Vb_[R, g, :], start=True, stop=True)
                    n4 = min(ng, half * 4 + 4) - half * 4
                    S.copy("act", usb[R, half * 4:half * 4 + n4, :],
                           pu[:, 0:512].rearrange("p (g c) -> p g c", g=4)[R, 0:n4, :])
                yield
                for g in range(ng):
                    S.mm(pw[:, g * 64:g * 64 + C], Kb_[R, g, :], TTbf[R, g, 0:C], start=True, stop=True)
                S.copy("dve", wT[:, 0:ng, 0:C], pw[:, 0:512].rearrange("p (g c) -> p g c", g=8)[:, 0:ng, 0:C])
                yield

            def phaseB(grp, bi):
                usb, wT, kdec, attnT, egl = usbs[bi], wTs[bi], kdecs[bi], attnTs[bi], egls[bi]
                ng = len(grp)
                C, L = chunks[grp[0]][1], chunks[grp[0]][2]
                R = slice(0, C)
                for g, ci in enumerate(grp):
                    t0 = chunks[ci][0]
                    cs_ = slice(t0, t0 + C)
                    sample = t0 >= TP
                    if sample:
                        b = t0 - TP
                        S.dma(Sf[:], self.dram["s_dn_S"][b * 8 + h], sem="dnS")
                        S.copy("dve", Sb_[:], Sf[:])
                    pd, po, ps_ = self.rbank(7), self.rbank(5), self.rbank(6)
                    S.mm(pd[R, 0:128], wT[:, g, 0:C], Sb_[:], start=True, stop=True)
                    yield
                    S.tt("dve", delta[R, :], usb[R, g, :], pd[R, 0:128], ALU.subtract)
                    yield
                    S.mm(ps_[:, 0:128], kdec[R, g, :], delta[R, :], start=True, stop=True)
                    S.mm(po[:, 0:C], Sb_[:], qdec[:, cs_], start=True, stop=False)
                    S.mm(po[:, 0:C], delta[R, :], attnT[R, g, 0:C], start=False, stop=True)
                    yield
                    S.stt(Sf[:], Sf[:], egl[:, g:g + 1], ps_[:, 0:128], ALU.mult, ALU.add)
                    S.copy("act", Sb_[:], Sf[:])
                    S.copy("act", oT[:, cs_], po[:, 0:C])
                    yield
                    if ci == 32:
                        S.dma(oS[h], Sf[:], sem="oS")
                    if sample:
                        S.dma(oS[8 + (t0 - TP) * 8 + h], Sf[:], sem="oS")
                yield

            for _ in phaseA(groups[0], 0):
                pass
            for gi in range(len(groups)):
                gB = phaseB(groups[gi], gi % 2)
                gA = phaseA(groups[gi + 1], (gi + 1) % 2) if gi + 1 < len(groups) else None
                doneA = gA is None
                doneB = False
                while not (doneA and doneB):
                    if not doneA:
                        try:
                            next(gA)
                        except StopIteration:
                            doneA = True
                    if not doneB:
                        try:
                            next(gB)
                        except StopIteration:
                            doneB = True

        def head_o(wb, h):
            for ti, (t0, n) in enumerate(TT):
                sq = self.sq[ti % 2]
                S.act(sq[:, 0:n], oT[:, t0:t0 + n], AF.Square)
                acc = self.bank()
                S.mm(acc[:, 0:n], self.ones_b[:], sq[:, 0:n], start=True, stop=True)
                r = self.rstd[0]
                S.act(r[:, 0:n], acc[:, 0:n], AF.Sqrt, bias=self.eps_col[:, 0:1], scale=1.0 / 128)
                S.recip(r[:, 0:n], r[:, 0:n])
                S.stt(eg[:, 0:n], oT[:, t0:t0 + n], self.col("dn_norm", 0), r[:, 0:n], ALU.mult, ALU.mult)
                S.tt("pool", oT[:, t0:t0 + n], eg[:, 0:n], sz[:, t0:t0 + n], ALU.mult)
            for fo in range(KC):
                def ev2(ti, t0, n, acc, fo=fo):
                    S.tt("dve", self.x[fo][:, t0:t0 + n], acc[:, 0:n], self.x[fo][:, t0:t0 + n], ALU.add)
                self.proj_chunk(lambda k, fo=fo: wb[:, fo * 128:(fo + 1) * 128], [oT], ev2)
        units = []
        for h in range(8):
            units.append(("dn_qk", h, 2048, (lambda wb, h=h: head_qk(wb, h))))
            units.append(("dn_vz", h, 2048, (lambda wb, h=h: head_vz(wb, h))))
            units.append(("dn_wo", h, 1024, (lambda wb, h=h: head_o(wb, h))))
        self.run_units(units)
        sm2 = self.out("small2", [128, 360])
        S.dma(sm2[:, :], small2[:], sem="o1")
        S.release(m0)

    def consts(self):
        S = self.S
        self.eps_col = S.sb("eps_col", [128, 1], F32)
        self.one_col = S.sb("one_col", [128, 1], F32)
        S.memset("pool", self.eps_col[:], EPS)
        S.memset("pool", self.one_col[:], 1.0)

    def final(self):
        S = self.S
        yT = self.out("yT", [128, KC, T])
        for ti, (t0, n) in enumerate(TT):
            r = self.rmsnorm_stats(ti)
            for k in range(KC):
                S.stt(self.x[k][:, t0:t0 + n], self.x[k][:, t0:t0 + n], self.col("nfinal", k), r[:, 0:n],
                      ALU.mult, ALU.mult)
        for k in range(KC):
            S.dma(yT[:, k, :], self.x[k][:], sem=f"o{k % 2}")
        sm = self.out("small", [128, 320])
        S.dma(sm[:, :], self.small[:], sem="o0")

    def dump_x(self):
        S = self.S
        dbg = self.out("dbg", [128, KC, T])
        for k in range(KC):
            S.dma(dbg[:, k, :], self.x[k][:], sem=f"o{k % 2}")


def build_program(shapes, colidx, stop_after=None, only=None):
    nc = bass.Bass("TRN2", target_bir_lowering=False)
    S = Sched(nc)
    B = Builder(nc, S, shapes, colidx, stop_after)
    B.setup()
    B.consts()
    S.memset("pool", B.small[:], 0.0)
    layers = [("lru", 0), ("dn", 0), ("mla", 0), ("lru", 1)]
    done = False
    for li, (kind, j) in enumerate(layers):
        if only is not None and li != only:
            continue
        if kind == "lru":
            B.lru(li, j)
        elif kind == "dn":
            B.dn(li, j)
        else:
            B.mla(li, j)
        if stop_after == (li, "mix"):
            done = True
            break
        B.ffn(li)
        if stop_after == (li, "ffn"):
            done = True
            break
    if done:
        B.dump_x()
        sm = B.out("small", [128, 320])
        S.dma(sm[:, :], B.small[:], sem="o0")
    else:
        B.final()
    S.emit()
    return nc, B


_STOP_AFTER = None
_DEBUG = {}


def _run(inputs, stop_after=None, cores=NCORES, only=None, x_override=None):
    inp = {k: np.asarray(v) for k, v in inputs.items()}
    sh = _prep_shared(inp)
    colidx = sh.pop("_colidx")
    per_core = [_prep_core(inp, c) for c in range(cores)]
    shapes = {k: v.shape for k, v in sh.items()}
    shapes.update({k: v.shape for k, v in per_core[0].items()})
    if x_override is not None:
        for c in range(cores):
            per_core[c]["xT"] = np.ascontiguousarray(x_override[c].reshape(T, KC, 128).transpose(2, 1, 0))
    nc, B = build_program(shapes, colidx, stop_after, only)
    in_maps = []
    for c in range(cores):
        m = dict(sh)
        m.update(per_core[c])
        in_maps.append(m)
    res = run_bass_kernel_spmd(nc, in_maps, core_ids=list(range(cores)))
    return res.results, B


def kernel(**inputs):
    results, B = _run(inputs, None)
    f = np.float32
    y_prompt = np.zeros((8, SEQ, D), f)
    y_sample = np.zeros((32, 1, D), f)
    p_lru_h = np.zeros((2, 8, D), f)
    p_lru_conv = np.zeros((2, 8, 3, D), f)
    p_dn_S = np.zeros((1, 8, 8, 128, 128), f)
    p_dn_conv = np.zeros((1, 8, 3, 3072), f)
    p_ckv = np.zeros((1, 8, TP, 256), f)
    p_kpe = np.zeros((1, 8, TP, 64), f)
    s_lru_h = np.zeros((2, 32, D), f)
    s_lru_conv = np.zeros((2, 32, 3, D), f)
    s_dn_S = np.zeros((1, 32, 8, 128, 128), f)
    s_dn_conv = np.zeros((1, 32, 3, 3072), f)
    s_ckv = np.zeros((1, 32, 1, 256), f)
    s_kpe = np.zeros((1, 32, 1, 64), f)
    for c in range(NCORES):
        r = results[c]
        y = r["yT"].transpose(2, 1, 0).reshape(T, D)
        y_prompt[c] = y[NMETA:TP]
        y_sample[NS * c:NS * (c + 1), 0] = y[TP:]
        sm = r["small"]
        for j in range(2):
            i, n = B.small_idx[f"p_lru_h{j}"]
            p_lru_h[j, c] = sm[:, i:i + n].T.reshape(D)
            i, n = B.small_idx[f"p_lru_conv{j}"]
            p_lru_conv[j, c] = sm[:, i:i + n].reshape(128, 8, 3).transpose(2, 1, 0).reshape(3, D)
            i, n = B.small_idx[f"s_lru_h{j}"]
            s_lru_h[j, NS * c:NS * (c + 1)] = sm[:, i:i + n].reshape(128, 8, NS).transpose(2, 1, 0).reshape(NS, D)
            i, n = B.small_idx[f"s_lru_conv{j}"]
            s_lru_conv[j, NS * c:NS * (c + 1)] = sm[:, i:i + n].reshape(128, 8, NS, 3).transpose(2, 3, 1, 0).reshape(NS, 3, D)
        s2 = r["small2"]
        p_dn_conv[0, c] = s2[:, 0:72].reshape(128, 24, 3).transpose(2, 1, 0).reshape(3, 3072)
        s_dn_conv[0, NS * c:NS * (c + 1)] = s2[:, 72:360].reshape(128, 24, NS, 3).transpose(2, 3, 1, 0).reshape(NS, 3, 3072)
        oS = r["o_dn_S"]
        p_dn_S[0, c] = oS[0:8]
        s_dn_S[0, NS * c:NS * (c + 1)] = oS[8:].reshape(NS, 8, 128, 128)
        kv = r["p_kv"]
        p_ckv[0, c] = kv[:TP, :256]
        p_kpe[0, c] = kv[:TP, 256:]
        s_ckv[0, NS * c:NS * (c + 1), 0] = kv[TP:, :256]
        s_kpe[0, NS * c:NS * (c + 1), 0] = kv[TP:, 256:]
    return (y_prompt, y_sample, p_lru_h, p_lru_conv, p_dn_S, p_dn_conv, p_ckv, p_kpe,
            s_lru_h, s_lru_conv, s_dn_S, s_dn_conv, s_ckv, s_kpe)
```

```python
import bisect
import os
from contextlib import ExitStack

import numpy as np
import concourse.bass as bass
import concourse.mybir as mybir
from concourse.bass_utils import run_bass_kernel_spmd

F32 = mybir.dt.float32
BF16 = mybir.dt.bfloat16
I32 = mybir.dt.int32
AF = mybir.ActivationFunctionType
ALU = mybir.AluOpType
AX = mybir.AxisListType

NCORES = 8
D = 1024
KC = 8
SEQ = 2048
NMETA = 16
TP = SEQ + NMETA
NS = 4
T = TP + NS
DFF = 2816
FC = DFF // 128
NPAGES = 128
PAGE = 128
NPOOL = 5120
EPS = 1e-6
MLA_SCALE = (128 + 64) ** -0.5
TT = [(0, 512), (512, 512), (1024, 512), (1536, 512), (2048, 20)]
WSLOT = 2048


class _Op:
    __slots__ = ("eng", "fn", "deps", "dma_sem", "dma_val", "idx", "milestone", "mval", "waits", "dma_deps")


class _IMap:
    def __init__(self, size):
        self.b = [0, size]
        self.r = [[None, {}]]

    def _split(self, x):
        i = bisect.bisect_left(self.b, x)
        if self.b[i] == x:
            return i
        w, rd = self.r[i - 1]
        self.b.insert(i, x)
        self.r.insert(i, [w, dict(rd)])
        return i

    def read(self, lo, hi, op, key, deps):
        i = self._split(lo)
        j = self._split(hi)
        for k in range(i, j):
            rec = self.r[k]
            if rec[0] is not None:
                deps.add(rec[0])
            rec[1][key] = op

    def write(self, lo, hi, op, deps):
        i = self._split(lo)
        j = self._split(hi)
        for k in range(i, j):
            rec = self.r[k]
            if rec[0] is not None:
                deps.add(rec[0])
            deps.update(rec[1].values())
        self.b[i:j + 1] = [lo, hi]
        self.r[i:j] = [[op, {}]]


class Sched:
    ENGS = ("pe", "act", "dve", "pool", "sp")

    def __init__(self, nc):
        self.nc = nc
        self.ops = {e: [] for e in self.ENGS}
        self.maps = {"SB": _IMap(1 << 20), "PSUM": _IMap(1 << 16)}
        self.dma_cnt = {}
        self.total_sems = set()
        self.bases = {}
        self.sb_ptr = (nc.sbuf_base + 63) // 64 * 64
        self.sb_top = nc.sbuf_top
        self.nalloc = 0

    def sb(self, name, shape, dtype):
        esz = 2 if dtype == BF16 else 4
        n = 1
        for s in shape[1:]:
            n *= s
        nbytes = (n * esz + 63) // 64 * 64
        off = self.sb_ptr
        self.sb_ptr += nbytes
        assert self.sb_ptr <= self.sb_top, f"SBUF overflow at {name}: {self.sb_ptr} > {self.sb_top}"
        self.nalloc += 1
        t = self.nc.alloc_sbuf_tensor_at(f"{name}_{self.nalloc}", list(shape), dtype, offset=off)
        self.bases[t.name] = off
        return t

    def mark(self):
        return self.sb_ptr

    def release(self, m):
        self.sb_ptr = m

    def _range(self, ap):
        sp = str(ap.space)
        if "SB" in sp:
            m = self.maps["SB"]
        elif "PSUM" in sp:
            m = self.maps["PSUM"]
        else:
            return None
        esz = 2 if ap.dtype == BF16 else 4
        pat = ap.ap
        pstride = pat[0][0]
        off = ap.offset % pstride if pstride > 0 else ap.offset
        ext = 1
        for st, cnt in pat[1:]:
            ext += (cnt - 1) * abs(st)
        base = self.bases.get(ap.tensor.name, 0)
        lo = base + off * esz
        hi = lo + ext * esz
        if m is self.maps["PSUM"]:
            lo = lo // 2048 * 2048
            hi = (hi + 2047) // 2048 * 2048
        return m, lo, hi

    def rec(self, eng, fn, reads=(), writes=(), dma_sem=None):
        op = _Op()
        op.eng = eng
        op.fn = fn
        op.dma_sem = dma_sem
        op.milestone = False
        op.mval = 0
        key = eng if dma_sem is None else ("dma", dma_sem)
        deps = set()
        for ap in reads:
            if ap is None or isinstance(ap, (int, float)):
                continue
            r = self._range(ap)
            if r:
                if r[0] is self.maps["PSUM"]:
                    r[0].write(r[1], r[2], op, deps)
                else:
                    r[0].read(r[1], r[2], op, key, deps)
        for ap in writes:
            r = self._range(ap)
            if r:
                r[0].write(r[1], r[2], op, deps)
        deps.discard(op)
        op.deps = []
        op.dma_deps = {}
        for d in deps:
            if d.dma_sem is not None:
                s = d.dma_sem
                v = self.dma_cnt[s]
                if op.dma_deps.get(s, 0) < v:
                    op.dma_deps[s] = v
            else:
                op.deps.append(d)
        if dma_sem is not None:
            self.dma_cnt[dma_sem] = self.dma_cnt.get(dma_sem, 0) + 16
            op.dma_val = self.dma_cnt[dma_sem]
        op.idx = len(self.ops[eng])
        self.ops[eng].append(op)
        return op

    def mm(self, out, lhsT, rhs, start=True, stop=True):
        return self.rec("pe", lambda e: e.matmul(out, lhsT=lhsT, rhs=rhs, start=start, stop=stop),
                        [lhsT, rhs], [out])

    def tr(self, out, in_, ident):
        return self.rec("pe", lambda e: e.transpose(out=out, in_=in_, identity=ident), [in_, ident], [out])

    def act(self, out, in_, func, bias=None, scale=1.0, accum_out=None):
        kw = {}
        if bias is not None:
            kw["bias"] = bias
        if accum_out is not None:
            kw["accum_out"] = accum_out
        w = [out] + ([accum_out] if accum_out is not None else [])
        return self.rec("act", lambda e: e.activation(out=out, in_=in_, func=func, scale=scale, **kw),
                        [in_, bias, scale], w)

    def tt(self, eng, out, in0, in1, op):
        return self.rec(eng, lambda e: e.tensor_tensor(out=out, in0=in0, in1=in1, op=op), [in0, in1], [out])

    def ts(self, eng, out, in0, s1, op0, s2=None, op1=None, accum_out=None):
        kw = {}
        if op1 is not None:
            kw["op1"] = op1
        if accum_out is not None:
            kw["accum_out"] = accum_out
        w = [out] + ([accum_out] if accum_out is not None else [])
        return self.rec(eng, lambda e: e.tensor_scalar(out=out, in0=in0, scalar1=s1, scalar2=s2, op0=op0, **kw),
                        [in0, s1, s2], w)

    def stt(self, out, in0, scalar, in1, op0, op1, eng="dve"):
        return self.rec(eng, lambda e: e.scalar_tensor_tensor(out=out, in0=in0, scalar=scalar, in1=in1,
                                                              op0=op0, op1=op1), [in0, scalar, in1], [out])

    def copy(self, eng, out, in_):
        if eng == "act":
            return self.rec("act", lambda e: e.copy(out=out, in_=in_), [in_], [out])
        return self.rec(eng, lambda e: e.tensor_copy(out=out, in_=in_), [in_], [out])

    def memset(self, eng, ap, val):
        return self.rec(eng, lambda e: e.memset(ap, val), [], [ap])

    def recip(self, out, in_):
        return self.rec("dve", lambda e: e.reciprocal(out=out, in_=in_), [in_], [out])

    def scan(self, out, d0, d1, initial, op0=ALU.mult, op1=ALU.add):
        return self.rec("dve", lambda e: e.tensor_tensor_scan(out=out, data0=d0, data1=d1, initial=initial,
                                                              op0=op0, op1=op1), [d0, d1, initial], [out])

    def reduce(self, out, in_, op, axis=AX.X):
        return self.rec("dve", lambda e: e.tensor_reduce(out=out, in_=in_, axis=axis, op=op), [in_], [out])

    def dma(self, out, in_, sem, eng="sp"):
        return self.rec(eng, lambda e: e.dma_start(out=out, in_=in_), [in_], [out], dma_sem=sem)

    def gather(self, out, in_, idx_ap, sem):
        return self.rec("pool", lambda e: e.indirect_dma_start(
            out=out, out_offset=None, in_=in_, in_offset=bass.IndirectOffsetOnAxis(ap=idx_ap, axis=0)),
            [idx_ap], [out], dma_sem=sem)

    def emit(self):
        nc = self.nc
        ops = self.ops
        for e in self.ENGS:
            seen = {f: -1 for f in self.ENGS}
            seen_dma = {}
            for op in ops[e]:
                keep = {}
                for d in op.deps:
                    f = d.eng
                    if f == e and e in ("pe", "sp"):
                        continue
                    if d.idx > seen[f] and d.idx > keep.get(f, (-1, None))[0]:
                        keep[f] = (d.idx, d)
                op.waits = []
                for f, (i, d) in keep.items():
                    seen[f] = i
                    d.milestone = True
                    op.waits.append(d)
                dw = []
                for s, v in op.dma_deps.items():
                    if s in self.total_sems:
                        v = -1
                    if seen_dma.get(s, 0) < v or v == -1:
                        if v == -1 and seen_dma.get(s, 0) == -1:
                            continue
                        seen_dma[s] = v
                        dw.append((s, v))
                op.dma_deps = dw
        for e in self.ENGS:
            c = 0
            for op in ops[e]:
                if op.milestone:
                    c += 1
                    op.mval = c
        self.nmil = {e: sum(1 for o in ops[e] if o.milestone) for e in self.ENGS}
        with ExitStack() as st:
            esem = {e: st.enter_context(nc.semaphore(f"e_{e}")) for e in self.ENGS}
            dsem = {s: st.enter_context(nc.semaphore(f"d_{s}")) for s in self.dma_cnt}
            block = st.enter_context(nc.Block())

            def run(e, eng):
                for op in ops[e]:
                    for d in op.waits:
                        eng.wait_ge(esem[d.eng], d.mval)
                    for s, v in op.dma_deps:
                        eng.wait_ge(dsem[s], self.dma_cnt[s] if v == -1 else v)
                    ins = op.fn(eng)
                    if op.dma_sem is not None:
                        ins.then_inc(dsem[op.dma_sem], 16)
                    elif op.milestone:
                        ins.then_inc(esem[e], 1)
                if e == "sp":
                    for s, v in self.dma_cnt.items():
                        eng.wait_ge(dsem[s], v)

            @block.tensor
            def _(eng):
                run("pe", eng)

            @block.scalar
            def _(eng):
                run("act", eng)

            @block.vector
            def _(eng):
                run("dve", eng)

            @block.gpsimd
            def _(eng):
                run("pool", eng)

            @block.sync
            def _(eng):
                run("sp", eng)


def _units_proj(W, gf):
    K, N = W.shape
    kc = K // 128
    return np.ascontiguousarray(W.reshape(kc, 128, N // gf, gf).transpose(2, 1, 0, 3).reshape(N // gf, 128, kc * gf))


def _cols(v):
    v = np.asarray(v, np.float32).reshape(-1, 128)
    return np.ascontiguousarray(v.T)


class _ColPack:
    def __init__(self):
        self.parts = []
        self.n = 0
        self.idx = {}

    def add(self, name, arr):
        arr = np.asarray(arr, np.float32)
        assert arr.shape[0] == 128
        self.idx[name] = self.n
        self.parts.append(arr)
        self.n += arr.shape[1]

    def build(self):
        return np.ascontiguousarray(np.concatenate(self.parts, axis=1))


def _prep_shared(inp):
    sh = {}
    cp = _ColPack()
    for i in range(4):
        cp.add(f"nmix{i}", _cols(inp["norm_mix"][i]))
        cp.add(f"nffn{i}", _cols(inp["norm_ffn"][i]))
    cp.add("nfinal", _cols(inp["norm_final"]))
    for j in range(2):
        for k in range(4):
            cp.add(f"lru_cw{j}_{k}", _cols(inp["lru_conv_w"][j, k]))
        cp.add(f"lru_cb{j}", _cols(inp["lru_conv_b"][j]))
        cp.add(f"lru_ba{j}", _cols(inp["lru_b_a"][j]))
        cp.add(f"lru_bi{j}", _cols(inp["lru_b_i"][j]))
        cp.add(f"lru_lam{j}", _cols(inp["lru_lambda"][j]))
        w_in = inp["lru_w_in"][j]
        u = []
        for n in range(4):
            u.append(_units_proj(w_in[:, n * 256:(n + 1) * 256], 256)[0])
            u.append(_units_proj(w_in[:, 1024 + n * 256:1024 + (n + 1) * 256], 256)[0])
        sh[f"lru_win{j}"] = np.stack(u)
        wa, wi = inp["lru_w_a"][j], inp["lru_w_i"][j]
        g = []
        for n in range(4):
            a = _units_proj(wa[n], 256)[0]
            b = _units_proj(wi[n], 256)[0]
            g.append(np.concatenate([a, b], axis=1))
        sh[f"lru_wg{j}"] = np.stack(g)
        wo = inp["lru_w_out"][j]
        sh[f"lru_wout{j}"] = np.stack([_units_proj(wo[n * 256:(n + 1) * 256], 1024)[0] for n in range(4)])
    for i in range(4):
        wgu = inp["ffn_w_gu"][i]
        g = _units_proj(wgu[:, :DFF], 128)
        u = _units_proj(wgu[:, DFF:], 128)
        sh[f"ffn_gu{i}"] = np.ascontiguousarray(
            np.stack([g.reshape(FC, 128, 8, 128), u.reshape(FC, 128, 8, 128)], axis=3).reshape(FC, 128, 2048))
        wd = inp["ffn_w_down"][i]
        hv = []
        for half in range(2):
            hv.append(_units_proj(wd[half * 1408:(half + 1) * 1408], 128))
        sh[f"ffn_dn{i}"] = np.ascontiguousarray(np.stack(hv).reshape(16, 128, 1408))
    _prep_mla(inp, sh, cp)
    _prep_dn(inp, sh, cp)
    sh["cols"] = cp.build()
    sh["_colidx"] = cp.idx
    sh["ones_bf"] = np.ones((128, 128), np.float32)
    sh["ident"] = np.eye(128, dtype=np.float32)
    return sh


def _prep_mla(inp, sh, cp):
    cp.add("mla_qn", _cols(inp["mla_q_norm"][0]))
    cp.add("mla_kvn", _cols(inp["mla_kv_norm"][0]))
    wdkv = inp["mla_w_dkv"][0]
    sh["mla_dkv_c"] = _units_proj(wdkv[:, :256], 256)
    perm = np.concatenate([np.arange(32, 64), np.arange(0, 32)])
    kr = np.concatenate([wdkv[:, 256:320], wdkv[:, 256 + perm]], axis=1)
    sh["mla_dkv_r"] = _units_proj(kr, 128)
    sh["mla_dq"] = _units_proj(inp["mla_w_dq"][0], 256)
    wuq = inp["mla_w_uq"][0].reshape(512, 8, 192)
    wuk = inp["mla_w_uk"][0]
    wuv = inp["mla_w_uv"][0]
    wo = inp["mla_w_o"][0]
    u1, u2 = [], []
    for h in range(8):
        q = np.concatenate([wuq[:, h, :128], wuq[:, h, 128:192], wuq[:, h, 128 + perm]], axis=1)
        a = _units_proj(q, 256)[0]
        b = np.ascontiguousarray(wuk[:, h, :].T)
        u1.append(np.concatenate([a, b], axis=1))
        v = _units_proj(wuv[:, h, :], 128)[0]
        o = wo[h * 128:(h + 1) * 128, :]
        u2.append(np.concatenate([v, o], axis=1))
    sh["mla_u1"] = np.stack(u1)
    sh["mla_u2"] = np.stack(u2)
    half = 32
    freqs = (10000.0 ** (-np.arange(half, dtype=np.float32) / half)).astype(np.float32)
    pos = np.concatenate([np.arange(TP), np.full(NS, NPAGES * PAGE)]).astype(np.float32)
    ang = pos[None, :] * freqs[:, None]
    c, sn = np.cos(ang).astype(np.float32), np.sin(ang).astype(np.float32)
    rope = np.stack([np.concatenate([c, c], axis=0), np.concatenate([-sn, sn], axis=0)], axis=1)
    sh["rope"] = np.ascontiguousarray(rope.astype(np.float32))
    sh["tri"] = np.triu(np.ones((128, 128), np.float32))
    pool = np.concatenate([inp["cache_mla_ckv"][0], inp["cache_mla_kpe"][0]], axis=-1)
    sh["poolkv"] = pool.reshape(NPOOL * 32, 4 * 320)


def _prep_dn(inp, sh, cp):
    w = inp["dn_w_in"][0]
    qk, vz = [], []
    for h in range(8):
        qk.append(_units_proj(np.concatenate([w[:, h * 128:(h + 1) * 128], w[:, 1024 + h * 128:1024 + (h + 1) * 128]], axis=1), 256)[0])
        vz.append(_units_proj(np.concatenate([w[:, 2048 + h * 128:2048 + (h + 1) * 128], w[:, 3072 + h * 128:3072 + (h + 1) * 128]], axis=1), 256)[0])
    sh["dn_qk"] = np.stack(qk)
    sh["dn_vz"] = np.stack(vz)
    sh["dn_ba"] = _units_proj(w[:, 4096:4112], 16)
    for q in range(4):
        cp.add(f"dn_cw{q}", _cols(inp["dn_conv_w"][0, q]))
    cp.add("dn_norm", _cols(inp["dn_norm"][0]))
    pad = np.zeros((128, 2), np.float32)
    pad[:8, 0] = inp["dn_a_log"][0]
    pad[:8, 1] = inp["dn_dt_bias"][0]
    cp.add("dn_ab", pad)
    wo = inp["dn_w_out"][0]
    sh["dn_wo"] = np.ascontiguousarray(wo.reshape(8, 128, 1024))
    sel = np.zeros((8, 8, 128), np.float32)
    for h in range(8):
        sel[h, h, :] = 1.0
    sh["dn_sel"] = sel.reshape(8, 1024)
    mask = np.ones((8, T), np.float32)
    mask[:, 0] = 0.0
    mask[:, 16:TP:64] = 0.0
    mask[:, TP:] = 0.0
    sh["dn_mask"] = mask
    r = np.arange(64)[:, None]
    c = np.arange(64)[None, :]
    mmax = np.where(c < r, 0.0, 30000.0).astype(np.float32)
    mmin = np.where(c >= r, 0.0, -30000.0).astype(np.float32)
    sh["dn_mm"] = np.ascontiguousarray(np.concatenate([mmax, mmin], axis=1))


def _prep_core(inp, c):
    x_full = np.concatenate([inp["meta_tokens"], inp["x_prompt"][c], inp["x_sample"][NS * c:NS * (c + 1), 0]], axis=0)
    pc = {}
    pc["xT"] = np.ascontiguousarray(x_full.reshape(T, KC, 128).transpose(2, 1, 0))
    lh = inp["state_lru_h"][:, NS * c:NS * (c + 1)]
    pc["s_lru_h"] = np.ascontiguousarray(lh.reshape(2, NS, KC, 128).transpose(3, 0, 2, 1))
    lc = inp["state_lru_conv"][:, NS * c:NS * (c + 1)]
    pc["s_lru_conv"] = np.ascontiguousarray(lc.reshape(2, NS, 3, KC, 128).transpose(4, 0, 3, 1, 2))
    pc["s_dn_S"] = np.ascontiguousarray(inp["state_dn_S"][0, NS * c:NS * (c + 1)].reshape(NS * 8, 128, 128))
    dc = inp["state_dn_conv"][0, NS * c:NS * (c + 1)]
    pc["s_dn_conv"] = np.ascontiguousarray(dc.reshape(NS, 3, 24, 128).transpose(3, 2, 0, 1))
    pc["pt"] = np.ascontiguousarray(inp["page_table"][NS * c:NS * (c + 1)].T.astype(np.int32))
    return pc


class Builder:
    def __init__(self, nc, S, shapes, colidx, stop_after=None):
        self.nc = nc
        self.S = S
        self.colidx = colidx
        self.stop_after = stop_after
        self.dram = {}
        for name, shp in shapes.items():
            self.dram[name] = nc.dram_tensor(name, list(shp), I32 if name == "pt" else F32, kind="ExternalInput").ap()
        self.ps = nc.alloc_psum_tensor("ps", [128, 4096], F32)
        self.ps_next = 0
        self.wq = []
        self.wi = 0
        self.outs = {}

    def out(self, name, shape):
        ap = self.nc.dram_tensor(name, list(shape), F32, kind="ExternalOutput").ap()
        self.outs[name] = ap
        return ap

    def bank(self):
        b = self.ps_next
        self.ps_next = (self.ps_next + 1) % 4
        return self.ps[:, b * 512:(b + 1) * 512]

    def col(self, name, k=0, n=1):
        i = self.colidx[name] + k
        return self.cols[:, i:i + n]

    def wload(self, name, u, nel):
        S = self.S
        slot = self.wi % self.nws
        ss = self.wi % self.nst
        self.wi += 1
        stg = self.wstage[ss]
        wb = self.wbf[slot]
        src = self.dram[name][u]
        S.dma(stg[:, 0:nel], src, sem=f"w{ss}")
        S.copy("pool", wb[:, 0:nel], stg[:, 0:nel])
        return wb

    def run_units(self, units, depth=2):
        loaded = []
        n = len(units)
        for i in range(n + depth):
            if i < n:
                nm, u, nel, _ = units[i]
                loaded.append(self.wload(nm, u, nel))
            j = i - depth
            if j >= 0:
                units[j][3](loaded[j])

    def setup(self):
        S = self.S
        nc = self.nc
        ncol = self.dram["cols"].shape[1]
        self.cols = S.sb("cols", [128, ncol], F32)
        S.dma(self.cols[:], self.dram["cols"], sem="init")
        S.total_sems.add("init")
        self.ones_f = S.sb("ones_f", [128, 128], F32)
        self.ident_f = S.sb("ident_f", [128, 128], F32)
        S.dma(self.ones_f[:], self.dram["ones_bf"], sem="init")
        S.dma(self.ident_f[:], self.dram["ident"], sem="init")
        self.ones_b = S.sb("ones_b", [128, 128], BF16)
        self.ident_b = S.sb("ident_b", [128, 128], BF16)
        S.copy("pool", self.ones_b[:], self.ones_f[:])
        S.copy("pool", self.ident_b[:], self.ident_f[:])
        self.x = [S.sb(f"x{k}", [128, T], F32) for k in range(KC)]
        for k in range(KC):
            S.dma(self.x[k][:], self.dram["xT"][:, k, :], sem="init")
        self.xn = [S.sb(f"xn{k}", [128, T], BF16) for k in range(KC)]
        self.nws = 3
        self.nst = 1
        self.wstage = [S.sb(f"wst{i}", [128, WSLOT], F32) for i in range(self.nst)]
        self.wbf = [S.sb(f"wbf{i}", [128, WSLOT], BF16) for i in range(self.nws)]
        self.sq = [S.sb(f"sq{i}", [128, 512], BF16) for i in range(2)]
        self.rstd = [S.sb(f"rstd{i}", [128, 512], F32) for i in range(2)]
        self.small = S.sb("small", [128, 320], F32)
        self.small_n = 0
        self.small_idx = {}

    def small_alloc(self, name, n):
        i = self.small_n
        self.small_idx[name] = (i, n)
        self.small_n += n
        assert self.small_n <= 320
        return self.small[:, i:i + n]

    def rmsnorm_stats(self, ti):
        S = self.S
        t0, n = TT[ti]
        acc = self.bank()
        for k in range(KC):
            sq = self.sq[k % 2]
            S.act(sq[:, 0:n], self.x[k][:, t0:t0 + n], AF.Square)
            S.mm(acc[:, 0:n], self.ones_b[:], sq[:, 0:n], start=(k == 0), stop=(k == KC - 1))
        r = self.rstd[ti % 2]
        S.act(r[:, 0:n], acc[:, 0:n], AF.Sqrt, bias=self.eps_col[:, 0:1], scale=1.0 / D)
        S.recip(r[:, 0:n], r[:, 0:n])
        return r

    def rmsnorm_to_xn(self, gname):
        S = self.S
        for ti, (t0, n) in enumerate(TT):
            r = self.rmsnorm_stats(ti)
            for k in range(KC):
                S.stt(self.xn[k][:, t0:t0 + n], self.x[k][:, t0:t0 + n], self.col(gname, k), r[:, 0:n],
                      ALU.mult, ALU.mult)

    def proj_chunk(self, wb_lhsT, rhs_list, evac):
        S = self.S
        nk = len(rhs_list)
        for ti, (t0, n) in enumerate(TT):
            acc = self.bank()
            for k in range(nk):
                S.mm(acc[:, 0:n], wb_lhsT(k), rhs_list[k][:, t0:t0 + n], start=(k == 0), stop=(k == nk - 1))
            evac(ti, t0, n, acc)

    def ffn(self, li):
        S = self.S
        self.rmsnorm_to_xn(f"nffn{li}")
        m = S.mark()
        h = [S.sb(f"h{j}", [128, T], BF16) for j in range(11)]
        sg = [S.sb(f"sg{j}", [128, 512], BF16) for j in range(2)]
        for half in range(2):
            units = []
            for jj in range(11):
                j = half * 11 + jj

                def fn(wb, jj=jj):
                    w4 = wb[:, 0:2048].rearrange("p (k g f) -> p k g f", k=8, g=2)
                    for ti, (t0, n) in enumerate(TT):
                        pg = self.bank()
                        pu = self.bank()
                        for k in range(KC):
                            S.mm(pg[:, 0:n], w4[:, k, 0, :], self.xn[k][:, t0:t0 + n], start=(k == 0), stop=(k == KC - 1))
                        for k in range(KC):
                            S.mm(pu[:, 0:n], w4[:, k, 1, :], self.xn[k][:, t0:t0 + n], start=(k == 0), stop=(k == KC - 1))
                        s = sg[ti % 2]
                        S.act(s[:, 0:n], pg[:, 0:n], AF.Silu)
                        S.tt("dve", h[jj][:, t0:t0 + n], pu[:, 0:n], s[:, 0:n], ALU.mult)
                units.append((f"ffn_gu{li}", j, 2048, fn))
            for fo in range(KC):
                def fn2(wb, fo=fo):
                    w3 = wb[:, 0:1408].rearrange("p (k f) -> p k f", k=11)

                    def ev(ti, t0, n, acc):
                        S.tt("dve", self.x[fo][:, t0:t0 + n], acc[:, 0:n], self.x[fo][:, t0:t0 + n], ALU.add)
                    self.proj_chunk(lambda k: w3[:, k, :], h, ev)
                units.append((f"ffn_dn{li}", half * 8 + fo, 1408, fn2))
            self.run_units(units)
        S.release(m)

    def lru(self, li, j):
        S = self.S
        self.rmsnorm_to_xn(f"nmix{li}")
        m = S.mark()
        HALF = [(0, 1024, (0, 1)), (1024, T - 1024, (2, 3, 4))]
        HN = T - 1024
        cA = S.sb("cA", [128, 8], F32)
        ncA = S.sb("ncA", [128, 8], F32)
        lam = self.col(f"lru_lam{j}", 0, 8)
        S.act(cA[:], lam, AF.Exp, scale=-1.0)
        S.act(cA[:], cA[:], AF.Ln, bias=self.one_col[:, 0:1])
        S.ts("dve", ncA[:], cA[:], 8.0, ALU.mult)
        S.ts("dve", cA[:], cA[:], -8.0, ALU.mult)
        hg = [S.sb(f"hg{k}", [128, T], BF16) for k in range(2)]
        gate = [S.sb(f"gate{k}", [128, T], BF16) for k in range(2)]
        xx = [S.sb(f"xx{k}", [128, TP + 3], F32) for k in range(2)]
        xs = [S.sb(f"xs{k}", [128, NS, 4], F32) for k in range(2)]
        xcb = [S.sb(f"xcb{k}", [128, T], BF16) for k in range(2)]
        ctmp = S.sb("ctmp", [128, T], F32)
        ra = S.sb("ra", [128, HN], F32)
        ri = S.sb("ri", [128, HN], F32)
        av = S.sb("av", [128, HN], F32)
        tmp = S.sb("tmp", [128, HN], F32)
        carry = S.sb("carry", [128, 1], F32)
        p_h = self.small_alloc(f"p_lru_h{j}", 8)
        p_cv = self.small_alloc(f"p_lru_conv{j}", 24)
        s_h = self.small_alloc(f"s_lru_h{j}", 32)
        s_cv = self.small_alloc(f"s_lru_conv{j}", 96)
        s_cv4 = s_cv.rearrange("p (k b j) -> p k b j", k=8, b=NS)
        s_h3 = s_h.rearrange("p (k b) -> p k b", k=8)
        st_h = S.sb("st_h", [128, 8, NS], F32)
        S.dma(st_h[:], self.dram["s_lru_h"][:, j], sem=f"st{j}")
        st_c = S.sb("st_c", [128, 8, NS, 3], F32)
        S.dma(st_c[:], self.dram["s_lru_conv"][:, j], sem=f"st{j}")
        for k in range(2):
            S.memset("pool", xx[k][:, 0:3], 0.0)

        units = []
        for n in range(4):
            def f_gate(wb, n=n):
                w3 = wb[:, 0:2048].rearrange("p (k f) -> p k f", k=8)
                for c in range(2):
                    def ev(ti, t0, nn, acc, c=c):
                        S.act(gate[c][:, t0:t0 + nn], acc[:, 0:nn], AF.Gelu)
                    self.proj_chunk(lambda k, c=c: w3[:, k, c * 128:(c + 1) * 128], self.xn, ev)
            units.append((f"lru_win{j}", 2 * n, 2048, f_gate))

            def f_x(wb, n=n):
                w3 = wb[:, 0:2048].rearrange("p (k f) -> p k f", k=8)
                for c in range(2):
                    kc = 2 * n + c

                    def ev(ti, t0, nn, acc, c=c, kc=kc):
                        if t0 + nn <= TP:
                            S.copy("act", xx[c][:, 3 + t0:3 + t0 + nn], acc[:, 0:nn])
                        else:
                            npz = TP - t0
                            S.copy("act", xx[c][:, 3 + t0:3 + TP], acc[:, 0:npz])
                            S.copy("act", xs[c][:, :, 3], acc[:, npz:npz + NS])
                    self.proj_chunk(lambda k, c=c: w3[:, k, c * 128:(c + 1) * 128], self.xn, ev)
                    S.copy("pool", xs[c][:, :, 0:3], st_c[:, kc, :, :])
                    S.copy("pool", p_cv[:, kc * 3:(kc + 1) * 3], xx[c][:, TP:TP + 3])
                    S.copy("pool", s_cv4[:, kc, :, :], xs[c][:, :, 1:4])
                    cw = lambda q, kc=kc: self.col(f"lru_cw{j}_{q}", kc)
                    cb = self.col(f"lru_cb{j}", kc)
                    S.ts("dve", ctmp[:, 0:TP], xx[c][:, 0:TP], cw(0), ALU.mult, cb, ALU.add)
                    for q in range(1, 3):
                        S.stt(ctmp[:, 0:TP], xx[c][:, q:q + TP], cw(q), ctmp[:, 0:TP], ALU.mult, ALU.add)
                    S.stt(xcb[c][:, 0:TP], xx[c][:, 3:3 + TP], cw(3), ctmp[:, 0:TP], ALU.mult, ALU.add)
                    S.ts("dve", ctmp[:, TP:T], xs[c][:, :, 0], cw(0), ALU.mult, cb, ALU.add)
                    for q in range(1, 3):
                        S.stt(ctmp[:, TP:T], xs[c][:, :, q], cw(q), ctmp[:, TP:T], ALU.mult, ALU.add)
                    S.stt(xcb[c][:, TP:T], xs[c][:, :, 3], cw(3), ctmp[:, TP:T], ALU.mult, ALU.add)
            units.append((f"lru_win{j}", 2 * n + 1, 2048, f_x))

            def f_g(wb, n=n):
                w4 = wb[:, 0:1024].rearrange("p (g k f) -> p g k f", g=2, k=2)
                for c in range(2):
                    kc = 2 * n + c
                    for (h0, hn, tiles) in HALF:
                        for ti in tiles:
                            t0, nn = TT[ti]
                            pa = self.bank()
                            pi = self.bank()
                            for k in range(2):
                                S.mm(pa[:, 0:nn], w4[:, 0, k, c * 128:(c + 1) * 128], xcb[k][:, t0:t0 + nn],
                                     start=(k == 0), stop=(k == 1))
                            for k in range(2):
                                S.mm(pi[:, 0:nn], w4[:, 1, k, c * 128:(c + 1) * 128], xcb[k][:, t0:t0 + nn],
                                     start=(k == 0), stop=(k == 1))
                            S.act(ra[:, t0 - h0:t0 - h0 + nn], pa[:, 0:nn], AF.Sigmoid, bias=self.col(f"lru_ba{j}", kc))
                            S.act(ri[:, t0 - h0:t0 - h0 + nn], pi[:, 0:nn], AF.Sigmoid, bias=self.col(f"lru_bi{j}", kc))
                        R = slice(0, hn)
                        G = slice(h0, h0 + hn)
                        S.act(av[:, R], ra[:, R], AF.Exp, scale=cA[:, kc:kc + 1])
                        S.act(tmp[:, R], ra[:, R], AF.Tanh, scale=ncA[:, kc:kc + 1])
                        S.tt("pool", ra[:, R], av[:, R], av[:, R], ALU.mult)
                        S.stt(tmp[:, R], ra[:, R], 1.0, tmp[:, R], ALU.add, ALU.mult)
                        S.act(tmp[:, R], tmp[:, R], AF.Sqrt)
                        S.tt("pool", ri[:, R], ri[:, R], xcb[c][:, G], ALU.mult)
                        S.tt("dve", ri[:, R], ri[:, R], tmp[:, R], ALU.mult)
                        if h0 == 0:
                            S.scan(tmp[:, R], av[:, R], ri[:, R], 0.0)
                            S.copy("pool", carry[:], tmp[:, hn - 1:hn])
                        else:
                            npr = TP - h0
                            S.scan(tmp[:, 0:npr], av[:, 0:npr], ri[:, 0:npr], carry[:, 0:1])
                            S.tt("dve", tmp[:, npr:hn], av[:, npr:hn], st_h[:, kc, :], ALU.mult)
                            S.tt("dve", tmp[:, npr:hn], tmp[:, npr:hn], ri[:, npr:hn], ALU.add)
                            S.copy("pool", p_h[:, kc:kc + 1], tmp[:, npr - 1:npr])
                            S.copy("pool", s_h3[:, kc, :], tmp[:, npr:hn])
                        S.tt("dve", hg[c][:, G], tmp[:, R], gate[c][:, G], ALU.mult)
            units.append((f"lru_wg{j}", n, 1024, f_g))

            def f_o(wb, n=n):
                w3 = wb[:, 0:2048].rearrange("p (k f) -> p k f", k=2)
                for fo in range(KC):
                    def ev(ti, t0, nn, acc, fo=fo):
                        S.tt("dve", self.x[fo][:, t0:t0 + nn], acc[:, 0:nn], self.x[fo][:, t0:t0 + nn], ALU.add)
                    self.proj_chunk(lambda k, fo=fo: w3[:, k, fo * 128:(fo + 1) * 128], hg, ev)
            units.append((f"lru_wout{j}", n, 2048, f_o))
        self.run_units(units)
        S.release(m)

    def rbank(self, i):
        return self.ps[:, i * 512:(i + 1) * 512]

    def rope_tile(self, dst, p_raw, p_swp, t0, n, cs, t1, t2):
        S = self.S
        S.dma(cs[:, :, 0:n], self.dram["rope"][:, :, t0:t0 + n], sem="cs")
        S.tt("dve", t1[:, 0:n], p_raw, cs[:, 0, 0:n], ALU.mult)
        S.tt("dve", t2[:, 0:n], p_swp, cs[:, 1, 0:n], ALU.mult)
        S.tt("pool", dst, t1[:, 0:n], t2[:, 0:n], ALU.add)

    def mla(self, li, j):
        S = self.S
        self.rmsnorm_to_xn(f"nmix{li}")
        xn_base = S.bases[self.xn[0].name]
        m0 = S.mark()
        ckvb = [S.sb(f"ckvb{k}", [128, T], BF16) for k in range(2)]
        kpeb = S.sb("kpeb", [64, T], BF16)
        cqb = [S.sb(f"cqb{k}", [128, T], BF16) for k in range(4)]
        qs = S.sb("qs", [128, 3, NS, 8], BF16)
        ols = S.sb("ols", [128, 2, 8, NS], BF16)
        trib = S.sb("trib", [128, 128], BF16)
        trif = S.sb("trif", [128, 128], F32)
        S.dma(trif[:], self.dram["tri"], sem="tri")
        S.copy("pool", trib[:], trif[:])
        cs = S.sb("cs", [64, 2, 512], F32)
        rt1 = S.sb("rt1", [64, 512], F32)
        rt2 = S.sb("rt2", [64, 512], F32)
        m1 = S.mark()
        kpef = S.sb("kpef", [64, T], F32)
        ckvT = [S.sb(f"ckvT{k}", [128, T], F32) for k in range(2)]

        wq = [self.wload("mla_dq", u, 2048) for u in range(2)]
        for ti, (t0, n) in enumerate(TT):
            pb = [self.bank() for _ in range(4)]
            for c4 in range(4):
                w3 = wq[c4 // 2][:, 0:2048].rearrange("p (k f) -> p k f", k=8)
                for k in range(KC):
                    S.mm(pb[c4][:, 0:n], w3[:, k, (c4 % 2) * 128:(c4 % 2 + 1) * 128], self.xn[k][:, t0:t0 + n],
                         start=(k == 0), stop=(k == KC - 1))
            acc = self.rbank(4)
            for c4 in range(4):
                sq = self.sq[c4 % 2]
                S.act(sq[:, 0:n], pb[c4][:, 0:n], AF.Square)
                S.mm(acc[:, 0:n], self.ones_b[:], sq[:, 0:n], start=(c4 == 0), stop=(c4 == 3))
            r = self.rstd[ti % 2]
            S.act(r[:, 0:n], acc[:, 0:n], AF.Sqrt, bias=self.eps_col[:, 0:1], scale=1.0 / 512)
            S.recip(r[:, 0:n], r[:, 0:n])
            for c4 in range(4):
                S.stt(cqb[c4][:, t0:t0 + n], pb[c4][:, 0:n], self.col("mla_qn", c4), r[:, 0:n], ALU.mult, ALU.mult)

        wr = self.wload("mla_dkv_r", 0, 1024)
        wr3 = wr[:, 0:1024].rearrange("p (k f) -> p k f", k=8)
        for ti, (t0, n) in enumerate(TT):
            p1 = self.bank()
            p2 = self.bank()
            for k in range(KC):
                S.mm(p1[0:64, 0:n], wr3[:, k, 0:64], self.xn[k][:, t0:t0 + n], start=(k == 0), stop=(k == KC - 1))
            for k in range(KC):
                S.mm(p2[0:64, 0:n], wr3[:, k, 64:128], self.xn[k][:, t0:t0 + n], start=(k == 0), stop=(k == KC - 1))
            self.rope_tile(kpef[:, t0:t0 + n], p1[0:64, 0:n], p2[0:64, 0:n], t0, n, cs, rt1, rt2)
        S.copy("pool", kpeb[:], kpef[:])

        wc = self.wload("mla_dkv_c", 0, 2048)
        wc3 = wc[:, 0:2048].rearrange("p (k f) -> p k f", k=8)
        for ti, (t0, n) in enumerate(TT):
            pb = [self.bank() for _ in range(2)]
            for c2 in range(2):
                for k in range(KC):
                    S.mm(pb[c2][:, 0:n], wc3[:, k, c2 * 128:(c2 + 1) * 128], self.xn[k][:, t0:t0 + n],
                         start=(k == 0), stop=(k == KC - 1))
            acc = self.rbank(4)
            for c2 in range(2):
                sq = self.sq[c2 % 2]
                S.act(sq[:, 0:n], pb[c2][:, 0:n], AF.Square)
                S.mm(acc[:, 0:n], self.ones_b[:], sq[:, 0:n], start=(c2 == 0), stop=(c2 == 1))
            r = self.rstd[ti % 2]
            S.act(r[:, 0:n], acc[:, 0:n], AF.Sqrt, bias=self.eps_col[:, 0:1], scale=1.0 / 256)
            S.recip(r[:, 0:n], r[:, 0:n])
            for c2 in range(2):
                S.stt(ckvT[c2][:, t0:t0 + n], pb[c2][:, 0:n], self.col("mla_kvn", c2), r[:, 0:n], ALU.mult, ALU.mult)
        for c2 in range(2):
            S.copy("pool", ckvb[c2][:], ckvT[c2][:])

        sv = S.sb_ptr
        S.sb_ptr = xn_base
        vtok = S.sb("vtok", [128, 17, 256], BF16)
        ostg = [S.sb(f"ostg{i}", [128, 320], F32) for i in range(2)]
        qaug = [S.sb(f"qaug{k}", [128, T], BF16) for k in range(3)]
        oh = S.sb("oh", [128, T], BF16)
        vnew = S.sb("vnew", [1, NS, 257], BF16)
        assert S.sb_ptr <= xn_base + 8 * ((T * 2 + 63) // 64 * 64), "xn overlay overflow"
        S.sb_ptr = sv

        okv = self.out("p_kv", [T, 320])
        for bi in range(17 if "T" not in os.environ.get("KSKIP", "") else 0):
            t0 = bi * 128
            n = min(128, T - t0)
            pt_ = self.bank()
            for c2 in range(2):
                S.tr(pt_[0:n, c2 * 128:(c2 + 1) * 128], ckvT[c2][:, t0:t0 + n], self.ident_f[:])
            S.tr(pt_[0:n, 256:320], kpef[:, t0:t0 + n], self.ident_f[0:64, 0:64])
            og = ostg[bi % 2]
            KS = os.environ.get("KSKIP", "")
            if "1" not in KS:
                S.copy("act", og[0:n, :], pt_[0:n, 0:320])
            if "2" not in KS:
                S.copy("dve", vtok[0:n, bi, :], pt_[0:n, 0:256])
            if "3" not in KS:
                S.dma(okv[t0:t0 + n, :], og[0:n, :], sem=f"okv{bi % 2}")
        S.memset("pool", vnew[:], 1.0)
        for b in range(NS if "V" not in os.environ.get("KSKIP", "") else 0):
            pt_ = self.bank()
            for c2 in range(2):
                S.tr(pt_[0:1, c2 * 128:(c2 + 1) * 128], ckvT[c2][:, TP + b:TP + b + 1], self.ident_f[:])
            S.copy("dve", vnew[0:1, b, 0:256], pt_[0:1, 0:256])
        S.release(m1)

        mA = S.mark()
        qn_s = S.sb("qn_s", [128, NS], BF16)
        u1 = []

        def passA(wb, h):
            wq3 = wb[:, 0:1024].rearrange("p (k f) -> p k f", k=4)
            wuk = wb[:, 1024:1280]
            pn = self.bank()
            for k in range(4):
                S.mm(pn[:, 0:NS], wq3[:, k, 0:128], cqb[k][:, TP:T], start=(k == 0), stop=(k == 3))
            S.copy("dve", qn_s[:], pn[:, 0:NS])
            p1 = self.bank()
            p2 = self.bank()
            for k in range(4):
                S.mm(p1[0:64, 0:NS], wq3[:, k, 128:192], cqb[k][:, TP:T], start=(k == 0), stop=(k == 3))
            for k in range(4):
                S.mm(p2[0:64, 0:NS], wq3[:, k, 192:256], cqb[k][:, TP:T], start=(k == 0), stop=(k == 3))
            self.rope_tile(qs[0:64, 2, :, h], p1[0:64, 0:NS], p2[0:64, 0:NS], TP, NS, cs, rt1, rt2)
            for c2 in range(2):
                pl = self.bank()
                S.mm(pl[:, 0:NS], wuk[:, c2 * 128:(c2 + 1) * 128], qn_s[:], start=True, stop=True)
                S.copy("dve", qs[:, c2, :, h], pl[:, 0:NS])
        if "A" not in os.environ.get("KSKIP", ""):
            self.run_units([("mla_u1", h, 1280, (lambda wb, h=h: passA(wb, h))) for h in range(8)])
        S.release(mA)

        if "D" not in os.environ.get("KSKIP", ""):
            self.mla_decode(qs, ols, ckvb, kpeb, vnew)
        else:
            S.memset("pool", ols[:], 0.0)

        mB = S.mark()
        qn = S.sb("qn", [128, T], BF16)
        olat = [S.sb(f"olat{k}", [128, T], BF16) for k in range(2)]
        PT = [S.sb(f"PT{i}", [128, 512], BF16) for i in range(3)]
        rden = S.sb("rden", [128, 512], F32)
        QT = [(0, 512), (512, 512), (1024, 512), (1536, 512), (2048, 16)]

        def head_q(wb, h):
            wq3 = wb[:, 0:1024].rearrange("p (k f) -> p k f", k=4)
            wuk = wb[:, 1024:1280]

            def ev(ti, t0, n, acc):
                S.copy("act", qn[:, t0:t0 + n], acc[:, 0:n])
            self.proj_chunk(lambda k: wq3[:, k, 0:128], cqb, ev)
            for ti, (t0, n) in enumerate(TT):
                p1 = self.bank()
                p2 = self.bank()
                for k in range(4):
                    S.mm(p1[0:64, 0:n], wq3[:, k, 128:192], cqb[k][:, t0:t0 + n], start=(k == 0), stop=(k == 3))
                for k in range(4):
                    S.mm(p2[0:64, 0:n], wq3[:, k, 192:256], cqb[k][:, t0:t0 + n], start=(k == 0), stop=(k == 3))
                self.rope_tile(qaug[2][0:64, t0:t0 + n], p1[0:64, 0:n], p2[0:64, 0:n], t0, n, cs, rt1, rt2)
            for c2 in range(2):
                def ev2(ti, t0, n, acc, c2=c2):
                    S.copy("act", qaug[c2][:, t0:t0 + n], acc[:, 0:n])
                self.proj_chunk(lambda k, c2=c2: wuk[:, c2 * 128:(c2 + 1) * 128], [qn], ev2)
            a0, a1, dn_ = self.rbank(4), self.rbank(5), self.rbank(6)
            pairs = []
            for (q0, qn_) in QT:
                nb = (q0 + qn_ - 1) // 128 + 1
                for jb in range(nb):
                    k0 = jb * 128
                    kn = min(128, TP - k0)
                    qs0 = max(q0, k0)
                    pairs.append(dict(q0=q0, qn=qn_, jb=jb, k0=k0, kn=kn, qs0=qs0, nc=q0 + qn_ - qs0, off=qs0 - q0,
                                      first=(jb == 0), last=(jb == nb - 1), idx=len(pairs)))

            def scores(p):
                kn, nc_, k0, qs0 = p["kn"], p["nc"], p["k0"], p["qs0"]
                sp = self.bank()
                S.mm(sp[0:kn, 0:nc_], ckvb[0][:, k0:k0 + kn], qaug[0][:, qs0:qs0 + nc_], start=True, stop=False)
                S.mm(sp[0:kn, 0:nc_], ckvb[1][:, k0:k0 + kn], qaug[1][:, qs0:qs0 + nc_], start=False, stop=False)
                S.mm(sp[0:kn, 0:nc_], kpeb[0:64, k0:k0 + kn], qaug[2][0:64, qs0:qs0 + nc_], start=False, stop=True)
                pt_ = PT[p["idx"] % len(PT)]
                S.act(pt_[0:kn, 0:nc_], sp[0:kn, 0:nc_], AF.Exp, scale=MLA_SCALE)
                if k0 >= p["q0"]:
                    dnn = min(128, nc_)
                    S.tt("pool", pt_[0:kn, 0:dnn], pt_[0:kn, 0:dnn], trib[0:kn, 0:dnn], ALU.mult)

            def pv(p):
                kn, nc_, off, jb = p["kn"], p["nc"], p["off"], p["jb"]
                pt_ = PT[p["idx"] % len(PT)]
                S.mm(a0[:, off:off + nc_], vtok[0:kn, jb, 0:128], pt_[0:kn, 0:nc_], start=p["first"], stop=p["last"])
                S.mm(a1[:, off:off + nc_], vtok[0:kn, jb, 128:256], pt_[0:kn, 0:nc_], start=p["first"], stop=p["last"])
                S.mm(dn_[:, off:off + nc_], self.ones_b[0:kn, :], pt_[0:kn, 0:nc_], start=p["first"], stop=p["last"])
                if p["last"]:
                    q0, qn_ = p["q0"], p["qn"]
                    S.recip(rden[:, 0:qn_], dn_[:, 0:qn_])
                    S.tt("dve", olat[0][:, q0:q0 + qn_], a0[:, 0:qn_], rden[:, 0:qn_], ALU.mult)
                    S.tt("dve", olat[1][:, q0:q0 + qn_], a1[:, 0:qn_], rden[:, 0:qn_], ALU.mult)
            scores(pairs[0])
            for i, p in enumerate(pairs):
                if i + 1 < len(pairs):
                    scores(pairs[i + 1])
                pv(p)
            for c2 in range(2):
                S.copy("pool", olat[c2][:, TP:T], ols[:, c2, h, :])

        def head_o(wb, h):
            wuv = wb[:, 0:256].rearrange("p (k v) -> p k v", k=2)
            wo = wb[:, 256:1280]

            def ev(ti, t0, n, acc):
                S.copy("act", oh[:, t0:t0 + n], acc[:, 0:n])
            self.proj_chunk(lambda k: wuv[:, k, :], olat, ev)
            for fo in range(KC):
                def ev2(ti, t0, n, acc, fo=fo):
                    S.tt("dve", self.x[fo][:, t0:t0 + n], acc[:, 0:n], self.x[fo][:, t0:t0 + n], ALU.add)
                self.proj_chunk(lambda k, fo=fo: wo[:, fo * 128:(fo + 1) * 128], [oh], ev2)
        units = []
        for h in range(8):
            units.append(("mla_u1", h, 1280, (lambda wb, h=h: head_q(wb, h))))
            units.append(("mla_u2", h, 1280, (lambda wb, h=h: head_o(wb, h))))
        if "B" not in os.environ.get("KSKIP", ""):
            self.run_units(units)
        S.release(m0)

    def mla_decode(self, qs, ols, ckvb, kpeb, vnew):
        S = self.S
        m = S.mark()
        NTK = 4
        NSUB = PAGE // NTK
        NBUF = 4
        ptab = S.sb("ptab", [128, NS], I32)
        S.dma(ptab[:], self.dram["pt"], sem="ptab")
        idx = S.sb("idx", [128, NS, NSUB], I32)
        for b in range(NS):
            for s_ in range(NSUB):
                S.ts("dve", idx[:, b, s_:s_ + 1], ptab[:, b:b + 1], float(NSUB), ALU.mult, float(s_), ALU.add)
        kvs = [S.sb(f"kvs{i}", [128, NTK * 320], F32) for i in range(NBUF)]
        kT = [S.sb(f"kT{i}", [128, 384], BF16) for i in range(2)]
        NVB = 3
        Vb = [S.sb(f"Vb{i}", [128, NTK, 336], BF16) for i in range(NVB)]
        for i in range(NVB):
            S.memset("pool", Vb[i][:], 1.0)
        PTd = [S.sb(f"PTd{i}", [128, NTK * 8], BF16) for i in range(2)]
        pnew = S.sb("pnew", [1, 8], BF16)
        osb = S.sb("osb", [8, 257], F32)
        rd = S.sb("rd", [8, 1], F32)
        onb = S.sb("onb", [8, 256], F32)
        accb = self.rbank(7)
        toks = []
        g = 0
        for b in range(NS):
            for s_ in range(NSUB):
                for tt_ in range(NTK):
                    toks.append((b, s_, tt_, g))
                g += 1
        pks = {}

        def start_chunk(b, s_, g):
            kv = kvs[g % NBUF]
            S.gather(kv[:], self.dram["poolkv"], idx[:, b, s_:s_ + 1], sem=f"kv{g % NBUF}")

        def vcast(g):
            kv3 = kvs[g % NBUF][:].rearrange("p (t c) -> p t c", t=NTK)
            S.copy("act", Vb[g % NVB][:, :, 0:256], kv3[:, :, 0:256])
            S.copy("act", Vb[g % NVB][:, :, 264:328], kv3[:, :, 256:320])

        def transposes(i):
            b, s_, tt_, g = toks[i]
            if tt_ == 0:
                vcast(g)
            vb = Vb[g % NVB]
            pk = self.bank().bitcast(BF16)
            S.tr(pk[:, 0:128], vb[:, tt_, 0:128], self.ident_b[:])
            S.tr(pk[:, 128:256], vb[:, tt_, 128:256], self.ident_b[:])
            S.tr(pk[0:64, 256:384], vb[:, tt_, 264:328], self.ident_b[:])
            kt = kT[i % 2]
            S.copy("dve", kt[:, 0:256], pk[:, 0:256])
            S.copy("dve", kt[0:64, 256:384], pk[0:64, 256:384])

        def qk(i):
            b, s_, tt_, g = toks[i]
            kt = kT[i % 2]
            sp = self.rbank(5 + (g % 2))
            o_ = sp[:, tt_ * 8:(tt_ + 1) * 8]
            S.mm(o_, kt[:, 0:128], qs[:, 0, b, :], start=True, stop=False)
            S.mm(o_, kt[:, 128:256], qs[:, 1, b, :], start=False, stop=False)
            S.mm(o_, kt[0:64, 256:384], qs[0:64, 2, b, :], start=False, stop=True)
            if tt_ == NTK - 1:
                S.act(PTd[g % 2][:], sp[:, 0:NTK * 8], AF.Exp, scale=MLA_SCALE)

        def pv(b, s_, g):
            for tt_ in range(NTK):
                S.mm(accb[0:8, 0:257], PTd[g % 2][:, tt_ * 8:(tt_ + 1) * 8], Vb[g % NVB][:, tt_, 0:257],
                     start=(s_ == 0 and tt_ == 0), stop=False)

        def finish(b):
            sp = self.bank()
            S.mm(sp[0:1, 0:8], ckvb[0][:, TP + b:TP + b + 1], qs[:, 0, b, :], start=True, stop=False)
            S.mm(sp[0:1, 0:8], ckvb[1][:, TP + b:TP + b + 1], qs[:, 1, b, :], start=False, stop=False)
            S.mm(sp[0:1, 0:8], kpeb[0:64, TP + b:TP + b + 1], qs[0:64, 2, b, :], start=False, stop=True)
            S.act(pnew[:], sp[0:1, 0:8], AF.Exp, scale=MLA_SCALE)
            S.mm(accb[0:8, 0:257], pnew[:], vnew[0:1, b, :], start=False, stop=True)
            S.copy("dve", osb[:], accb[0:8, 0:257])
            S.recip(rd[:], osb[:, 256:257])
            S.ts("dve", onb[:], osb[:, 0:256], rd[:, 0:1], ALU.mult)
            for c2 in range(2):
                po = self.bank()
                S.tr(po[:, 0:8], onb[:, c2 * 128:(c2 + 1) * 128], self.ident_f[0:8, 0:8])
                S.copy("dve", ols[:, c2, :, b], po[:, 0:8])

        n = len(toks)
        started = 0
        nchunks = NS * NSUB

        def ensure_started(upto):
            nonlocal started
            while started <= min(upto, nchunks - 1):
                bb, ss = divmod(started, NSUB)
                start_chunk(bb, ss, started)
                started += 1
        ensure_started(NBUF - 2)
        transposes(0)
        pending_pv = None
        for i in range(n):
            b, s_, tt_, g = toks[i]
            if tt_ == 0:
                ensure_started(g + NBUF - 2)
            if i + 1 < n:
                transposes(i + 1)
            qk(i)
            if tt_ == NTK - 1:
                if pending_pv is not None:
                    pv(*pending_pv)
                    if pending_pv[1] == NSUB - 1:
                        finish(pending_pv[0])
                pending_pv = (b, s_, g)
        pv(*pending_pv)
        finish(pending_pv[0])
        S.release(m)

    def mmf(self, out, lhsT, rhs, start=True, stop=True):
        return self.S.mm(out, lhsT, rhs, start, stop)

    def dn(self, li, j):
        S = self.S
        self.rmsnorm_to_xn(f"nmix{li}")
        m0 = S.mark()
        small2 = S.sb("small2", [128, 360], F32)
        S.memset("pool", small2[:], 0.0)
        p_cv = small2[:, 0:72].rearrange("p (k j) -> p k j", k=24)
        s_cv = small2[:, 72:360].rearrange("p (k b j) -> p k b j", k=24, b=NS)
        oS = self.out("o_dn_S", [8 + NS * 8, 128, 128])
        Gc = S.sb("Gc", [8, T], F32)
        GT = S.sb("GT", [64, 37, 8], F32)
        BT = S.sb("BT", [64, 37, 8], F32)
        sel = S.sb("sel", [8, 128], F32)
        mm_ = S.sb("mm_", [64, 128], F32)
        S.dma(mm_[:], self.dram["dn_mm"], sem="dnc")
        st_c = S.sb("dst_c", [128, 24, NS, 3], F32)
        S.dma(st_c[:], self.dram["s_dn_conv"], sem="dnc")
        chunks = [(0, 16, 4)] + [(16 + 64 * c, 64, 6) for c in range(32)] + [(TP + b, 1, 0) for b in range(NS)]
        xx = S.sb("dxx", [128, TP + 3], F32)
        xs = S.sb("dxs", [128, NS, 4], F32)
        ctmp = S.sb("dctmp", [128, T], F32)
        S.memset("pool", xx[:, 0:3], 0.0)
        qdec = S.sb("qdec", [128, T], BF16)
        qn = S.sb("dqn", [128, T], BF16)
        kn = S.sb("kn", [128, T], BF16)
        vb = S.sb("vb", [128, T], BF16)
        sz = S.sb("sz", [128, T], BF16)
        GB = S.sb("GB", [128, T], F32)
        oT = S.sb("oT", [128, T], BF16)
        Sf = S.sb("Sf", [128, 128], F32)
        Sb_ = S.sb("Sb", [128, 128], BF16)
        eg = self.rstd[1]
        wba = self.wload("dn_ba", 0, 128)
        wba3 = wba[:, 0:128].rearrange("p (k f) -> p k f", k=8)
        Ball = GB[0:8, 0:T]
        graw = ctmp[0:8, 0:T]
        sv_ = S.sb_ptr
        S.sb_ptr = S.bases[qdec.name]
        mrow_t = S.sb("mrow", [8, T], F32)
        S.sb_ptr = sv_
        mrow = mrow_t[:, :]
        S.dma(mrow, self.dram["dn_mask"], sem="dnc")
        nA = S.sb("nA", [8, 1], F32)
        ab = self.col("dn_ab", 0, 2)
        S.act(nA[:], ab[0:8, 0:1], AF.Exp)
        S.ts("dve", nA[:], nA[:], -1.0, ALU.mult)
        for ti, (t0, n) in enumerate(TT):
            pb_, pa_ = self.bank(), self.bank()
            for k in range(KC):
                S.mm(pb_[0:8, 0:n], wba3[:, k, 0:8], self.xn[k][:, t0:t0 + n], start=(k == 0), stop=(k == KC - 1))
            for k in range(KC):
                S.mm(pa_[0:8, 0:n], wba3[:, k, 8:16], self.xn[k][:, t0:t0 + n], start=(k == 0), stop=(k == KC - 1))
            S.act(Ball[:, t0:t0 + n], pb_[0:8, 0:n], AF.Sigmoid)
            S.act(graw[:, t0:t0 + n], pa_[0:8, 0:n], AF.Exp, bias=ab[0:8, 1:2])
            S.act(graw[:, t0:t0 + n], graw[:, t0:t0 + n], AF.Ln, bias=self.one_col[0:8, 0:1])
        S.ts("dve", graw, graw, nA[:, 0:1], ALU.mult)
        S.scan(Gc[:], mrow, graw, 0.0)
        for ci, (t0, C, L) in enumerate(chunks):
            pt_ = self.bank()
            S.tr(pt_[0:C, 0:8], Gc[:, t0:t0 + C], self.ident_f[0:8, 0:8])
            S.tr(pt_[0:C, 8:16], Ball[:, t0:t0 + C], self.ident_f[0:8, 0:8])
            S.copy("dve", GT[0:C, ci, :], pt_[0:C, 0:8])
            S.copy("dve", BT[0:C, ci, :], pt_[0:C, 8:16])
        sv_ = S.sb_ptr
        S.sb_ptr = S.bases[xx.name]
        F1 = S.sb("gF1", [64, 8, 64], F32)
        F2 = S.sb("gF2", [64, 8, 64], F32)
        gbf = lambda nm: S.sb(nm, [64, 8, 64], BF16)
        A1, A2, A3, B1, B2, B3 = (gbf(nm) for nm in ("gA1", "gA2", "gA3", "gB1", "gB2", "gB3"))
        usb0 = S.sb("usb", [64, 8, 128], F32)
        attnT0 = S.sb("attnT", [64, 8, 64], BF16)
        wT0 = S.sb("wT", [128, 8, 64], BF16)
        assert S.sb_ptr <= S.bases[ctmp.name] + T * 4, "group overlay overflow"
        S.sb_ptr = sv_
        Vb_ = S.sb("Vbt", [64, 8, 128], BF16)
        Kb_ = S.sb("Kbt", [64, 8, 128], BF16)
        delta = S.sb("delta", [64, 128], BF16)
        cols_ = S.sb("ccols", [64, 8, 4], F32)
        usbs = [usb0, S.sb("usb1", [64, 8, 128], F32)]
        attnTs = [attnT0, S.sb("attnT1", [64, 8, 64], BF16)]
        wTs = [wT0, S.sb("wT1", [128, 8, 64], BF16)]
        kdecs = [S.sb(f"kdec{i}", [64, 8, 128], BF16) for i in range(2)]
        egls = [S.sb(f"egl{i}", [128, 8], F32) for i in range(2)]
        mmax, mmin = mm_[:, 0:64], mm_[:, 64:128]

        def conv_silu(psrc_list, kc, dst_f32):
            for (ti, t0, n, acc) in psrc_list:
                if t0 + n <= TP:
                    S.copy("act", xx[:, 3 + t0:3 + t0 + n], acc[:, 0:n])
                else:
                    npz = TP - t0
                    S.copy("act", xx[:, 3 + t0:3 + TP], acc[:, 0:npz])
                    S.copy("act", xs[:, :, 3], acc[:, npz:npz + NS])

        def conv_finish(kc, dst):
            S.memset("pool", xx[:, 0:3], 0.0)
            S.copy("pool", xs[:, :, 0:3], st_c[:, kc, :, :])
            S.copy("pool", p_cv[:, kc, :], xx[:, TP:TP + 3])
            S.copy("pool", s_cv[:, kc, :, :], xs[:, :, 1:4])
            cw = lambda q: self.col(f"dn_cw{q}", kc)
            S.ts("dve", ctmp[:, 0:TP], xx[:, 0:TP], cw(0), ALU.mult)
            for q in range(1, 4):
                S.stt(ctmp[:, 0:TP], xx[:, q:q + TP], cw(q), ctmp[:, 0:TP], ALU.mult, ALU.add)
            S.ts("dve", ctmp[:, TP:T], xs[:, :, 0], cw(0), ALU.mult)
            for q in range(1, 4):
                S.stt(ctmp[:, TP:T], xs[:, :, q], cw(q), ctmp[:, TP:T], ALU.mult, ALU.add)
            S.act(dst, ctmp[:], AF.Silu)

        def l2n(src, dst_bf, scale):
            for ti, (t0, n) in enumerate(TT):
                sq = self.sq[ti % 2]
                S.act(sq[:, 0:n], src[:, t0:t0 + n], AF.Square)
                acc = self.bank()
                S.mm(acc[:, 0:n], self.ones_b[:], sq[:, 0:n], start=True, stop=True)
                r = self.rstd[ti % 2]
                S.act(r[:, 0:n], acc[:, 0:n], AF.Sqrt, bias=self.eps_col[:, 0:1], scale=1.0 / (scale * scale))
                S.recip(r[:, 0:n], r[:, 0:n])
                S.tt("pool", dst_bf[:, t0:t0 + n], src[:, t0:t0 + n], r[:, 0:n], ALU.mult)

        def proj2(wb, c, kc):
            w3 = wb[:, 0:2048].rearrange("p (k f) -> p k f", k=8)
            lst = []
            for ti, (t0, n) in enumerate(TT):
                acc = self.bank()
                for k in range(KC):
                    S.mm(acc[:, 0:n], w3[:, k, c * 128:(c + 1) * 128], self.xn[k][:, t0:t0 + n], start=(k == 0), stop=(k == KC - 1))
                conv_silu([(ti, t0, n, acc)], kc, None)

        def head_qk(wb, h):
            proj2(wb, 0, h)
            conv_finish(h, ctmp[:])
            l2n(ctmp, qn, 128.0 ** -0.5)
            S.dma(sel[:], self.dram["dn_sel"][:, h * 128:(h + 1) * 128], sem="dnsel")
            for ti, (t0, n) in enumerate(TT):
                acc = self.bank()
                S.mm(acc[:, 0:n], sel[:, :], Gc[:, t0:t0 + n], start=True, stop=True)
                S.copy("act", GB[:, t0:t0 + n], acc[:, 0:n])
                S.act(eg[:, 0:n], acc[:, 0:n], AF.Exp)
                S.tt("pool", qdec[:, t0:t0 + n], qn[:, t0:t0 + n], eg[:, 0:n], ALU.mult)
            proj2(wb, 1, 8 + h)
            conv_finish(8 + h, ctmp[:])
            l2n(ctmp, kn, 1.0)

        def head_vz(wb, h):
            proj2(wb, 0, 16 + h)
            conv_finish(16 + h, ctmp[:])
            S.copy("pool", vb[:], ctmp[:])
            w3 = wb[:, 0:2048].rearrange("p (k f) -> p k f", k=8)

            def ev(ti, t0, n, acc):
                S.act(sz[:, t0:t0 + n], acc[:, 0:n], AF.Silu)
            self.proj_chunk(lambda k: w3[:, k, 128:256], self.xn, ev)
            S.memset("pool", Sf[:], 0.0)
            S.memset("pool", Sb_[:], 0.0)
            groups = [[0]] + [list(range(1 + 8 * q, 9 + 8 * q)) for q in range(4)] + [[33, 34, 35, 36]]

            def phaseA(grp, bi):
                usb, wT, kdec, attnT, egl = usbs[bi], wTs[bi], kdecs[bi], attnTs[bi], egls[bi]
                ng = len(grp)
                C, L = chunks[grp[0]][1], chunks[grp[0]][2]
                R = slice(0, C)
                pk, pq = self.rbank(0), self.rbank(1)
                ci0 = grp[0]
                tg0 = chunks[ci0][0]
                GR = (R, slice(0, ng), slice(0, C))
                bc = lambda ap: ap.to_broadcast([C, ng, C])
                GBg = GB[0:C, tg0:tg0 + ng * C].rearrange("p (g c) -> p g c", g=ng)
                gcolg = GT[0:C, ci0:ci0 + ng, h:h + 1]
                bcolg = BT[0:C, ci0:ci0 + ng, h:h + 1]
                glastg = GB[0:C, tg0 + C - 1:tg0 + ng * C:C].unsqueeze(2)
                S.act(cols_[R, 0:ng, 0:1], gcolg, AF.Exp)
                S.tt("dve", cols_[R, 0:ng, 1:2], cols_[R, 0:ng, 0:1], bcolg, ALU.mult)
                S.tt("dve", cols_[R, 0:ng, 2:3], glastg, gcolg, ALU.subtract)
                S.act(cols_[R, 0:ng, 2:3], cols_[R, 0:ng, 2:3], AF.Exp)
                S.act(egl[:, 0:ng], GB[:, tg0 + C - 1:tg0 + ng * C:C], AF.Exp)
                for g, ci in enumerate(grp):
                    t0 = chunks[ci][0]
                    cs_ = slice(t0, t0 + C)
                    S.mm(pk[R, g * 64:g * 64 + C], kn[:, cs_], kn[:, cs_], start=True, stop=True)
                    S.mm(pq[R, g * 64:g * 64 + C], kn[:, cs_], qn[:, cs_], start=True, stop=True)
                yield
                S.tt("dve", F2[GR], GBg, bc(gcolg), ALU.subtract)
                S.tt("dve", F1[GR], F2[GR], bc(mmax[0:C, 0:C].unsqueeze(1)), ALU.max)
                S.tt("dve", F2[GR], F2[GR], bc(mmin[0:C, 0:C].unsqueeze(1)), ALU.min)
                pk3 = pk[:, 0:512].rearrange("p (g c) -> p g c", g=8)
                pq3 = pq[:, 0:512].rearrange("p (g c) -> p g c", g=8)
                yield
                S.act(B1[GR], F1[GR], AF.Exp, scale=-1.0)
                S.act(B2[GR], F2[GR], AF.Exp)
                yield
                S.tt("dve", F1[GR], pk3[GR], bc(bcolg), ALU.mult)
                S.stt(F1[GR], F1[GR], -1.0, B1[GR], ALU.mult, ALU.mult)
                S.copy("act", A1[GR], F1[GR])
                S.tt("dve", attnT[GR], pq3[GR], B2[GR], ALU.mult)
                yield
                if C > 1:
                    ptr_ = self.rbank(2)
                    ptr3 = ptr_[:, 0:512].rearrange("p (g c) -> p g c", g=8)
                    for g in range(ng):
                        S.tr(ptr_[R, g * 64:g * 64 + C], F1[R, g, 0:C], self.ident_f[0:C, 0:C])
                    S.copy("dve", A2[GR], ptr3[GR])
                else:
                    S.copy("dve", A2[GR], A1[GR])
                yield
                S.tt("pool", A3[GR], A2[GR], bc(self.ident_b[0:C, 0:C].unsqueeze(1)), ALU.add)
                P_, PT_, TT_ = A1, A2, A3
                P2, PT2, TT2 = B1, B2, B3
                for lv in range(1, L):
                    pl, plT, pl2 = self.rbank(2), self.rbank(3), self.rbank(4)
                    pl3 = pl[:, 0:512].rearrange("p (g c) -> p g c", g=8)
                    plT3 = plT[:, 0:512].rearrange("p (g c) -> p g c", g=8)
                    pl23 = pl2[:, 0:512].rearrange("p (g c) -> p g c", g=8)
                    for g in range(ng):
                        S.mm(pl[R, g * 64:g * 64 + C], PT_[R, g, 0:C], P_[R, g, 0:C], start=True, stop=True)
                    for g in range(ng):
                        S.mm(plT[R, g * 64:g * 64 + C], P_[R, g, 0:C], PT_[R, g, 0:C], start=True, stop=True)
                    yield
                    S.copy("dve", P2[GR], pl3[GR])
                    S.copy("act", PT2[GR], plT3[GR])
                    yield
                    for g in range(ng):
                        S.mm(pl2[R, g * 64:g * 64 + C], P2[R, g, 0:C], TT_[R, g, 0:C], start=True, stop=True)
                    yield
                    S.tt("dve", TT2[GR], pl23[GR], TT_[GR], ALU.add)
                    yield
                    P_, P2 = P2, P_
                    PT_, PT2 = PT2, PT_
                    TT_, TT2 = TT2, TT_
                TTbf = TT_
                for half in range((ng + 3) // 4):
                    pkk, pvv = self.rbank(0 + half), self.rbank(2 + half)
                    for g in range(half * 4, min(ng, half * 4 + 4)):
                        ci = grp[g]
                        t0 = chunks[ci][0]
                        cs_ = slice(t0, t0 + C)
                        o0 = (g % 4) * 128
                        S.mm(pkk[R, o0:o0 + 128], kn[:, cs_], self.ident_b[:], start=True, stop=True)
                        S.mm(pvv[R, o0:o0 + 128], vb[:, cs_], self.ident_b[:], start=True, stop=True)
                    yield
                    h4 = half * 4
                    n4 = min(ng, h4 + 4) - h4
                    pkk3 = pkk[:, 0:512].rearrange("p (g c) -> p g c", g=4)
                    pvv3 = pvv[:, 0:512].rearrange("p (g c) -> p g c", g=4)
                    b4 = lambda ap: ap.to_broadcast([C, n4, 128])
                    S.tt("dve", Kb_[R, h4:h4 + n4, :], pkk3[R, 0:n4, :], b4(cols_[R, h4:h4 + n4, 1:2]), ALU.mult)
                    S.tt("dve", kdec[R, h4:h4 + n4, :], pkk3[R, 0:n4, :], b4(cols_[R, h4:h4 + n4, 2:3]), ALU.mult)
                    S.tt("dve", Vb_[R, h4:h4 + n4, :], pvv3[R, 0:n4, :], b4(BT[0:C, ci0 + h4:ci0 + h4 + n4, h:h + 1]), ALU.mult)
                yield
                pw = self.rbank(2)
                for half in range((ng + 3) // 4):
                    pu = self.rbank(4)
                    for g in range(half * 4, min(ng, half * 4 + 4)):
                        o0 = (g % 4) * 128
                        S.mm(pu[R, o0:o0 + 128], TTbf[R, g, 0:C], Vb_[R, g, :], start=True, stop=True)
                    n4 = min(ng, half * 4 + 4) - half * 4
                    S.copy("act", usb[R, half * 4:half * 4 + n4, :],
                           pu[:, 0:512].rearrange("p (g c) -> p g c", g=4)[R, 0:n4, :])
                yield
                for g in range(ng):
                    S.mm(pw[:, g * 64:g * 64 + C], Kb_[R, g, :], TTbf[R, g, 0:C], start=True, stop=True)
                S.copy("dve", wT[:, 0:ng, 0:C], pw[:, 0:512].rearrange("p (g c) -> p g c", g=8)[:, 0:ng, 0:C])
                yield

            def phaseB(grp, bi):
                usb, wT, kdec, attnT, egl = usbs[bi], wTs[bi], kdecs[bi], attnTs[bi], egls[bi]
                ng = len(grp)
                C, L = chunks[grp[0]][1], chunks[grp[0]][2]
                R = slice(0, C)
                for g, ci in enumerate(grp):
                    t0 = chunks[ci][0]
                    cs_ = slice(t0, t0 + C)
                    sample = t0 >= TP
                    if sample:
                        b = t0 - TP
                        S.dma(Sf[:], self.dram["s_dn_S"][b * 8 + h], sem="dnS")
                        S.copy("dve", Sb_[:], Sf[:])
                    pd, po, ps_ = self.rbank(7), self.rbank(5), self.rbank(6)
                    S.mm(pd[R, 0:128], wT[:, g, 0:C], Sb_[:], start=True, stop=True)
                    yield
                    S.tt("dve", delta[R, :], usb[R, g, :], pd[R, 0:128], ALU.subtract)
                    yield
                    S.mm(ps_[:, 0:128], kdec[R, g, :], delta[R, :], start=True, stop=True)
                    S.mm(po[:, 0:C], Sb_[:], qdec[:, cs_], start=True, stop=False)
                    S.mm(po[:, 0:C], delta[R, :], attnT[R, g, 0:C], start=False, stop=True)
                    yield
                    S.stt(Sf[:], Sf[:], egl[:, g:g + 1], ps_[:, 0:128], ALU.mult, ALU.add)
                    S.copy("act", Sb_[:], Sf[:])
                    S.copy("act", oT[:, cs_], po[:, 0:C])
                    yield
                    if ci == 32:
                        S.dma(oS[h], Sf[:], sem="oS")
                    if sample:
                        S.dma(oS[8 + (t0 - TP) * 8 + h], Sf[:], sem="oS")
                yield

            for _ in phaseA(groups[0], 0):
                pass
            for gi in range(len(groups)):
                gB = phaseB(groups[gi], gi % 2)
                gA = phaseA(groups[gi + 1], (gi + 1) % 2) if gi + 1 < len(groups) else None
                doneA = gA is None
                doneB = False
                while not (doneA and doneB):
                    if not doneA:
                        try:
                            next(gA)
                        except StopIteration:
                            doneA = True
                    if not doneB:
                        try:
                            next(gB)
                        except StopIteration:
                            doneB = True

        def head_o(wb, h):
            for ti, (t0, n) in enumerate(TT):
                sq = self.sq[ti % 2]
                S.act(sq[:, 0:n], oT[:, t0:t0 + n], AF.Square)
                acc = self.bank()
                S.mm(acc[:, 0:n], self.ones_b[:], sq[:, 0:n], start=True, stop=True)
                r = self.rstd[0]
                S.act(r[:, 0:n], acc[:, 0:n], AF.Sqrt, bias=self.eps_col[:, 0:1], scale=1.0 / 128)
                S.recip(r[:, 0:n], r[:, 0:n])
                S.stt(eg[:, 0:n], oT[:, t0:t0 + n], self.col("dn_norm", 0), r[:, 0:n], ALU.mult, ALU.mult)
                S.tt("pool", oT[:, t0:t0 + n], eg[:, 0:n], sz[:, t0:t0 + n], ALU.mult)
            for fo in range(KC):
                def ev2(ti, t0, n, acc, fo=fo):
                    S.tt("dve", self.x[fo][:, t0:t0 + n], acc[:, 0:n], self.x[fo][:, t0:t0 + n], ALU.add)
                self.proj_chunk(lambda k, fo=fo: wb[:, fo * 128:(fo + 1) * 128], [oT], ev2)
        units = []
        for h in range(8):
            units.append(("dn_qk", h, 2048, (lambda wb, h=h: head_qk(wb, h))))
            units.append(("dn_vz", h, 2048, (lambda wb, h=h: head_vz(wb, h))))
            units.append(("dn_wo", h, 1024, (lambda wb, h=h: head_o(wb, h))))
        self.run_units(units)
        sm2 = self.out("small2", [128, 360])
        S.dma(sm2[:, :], small2[:], sem="o1")
        S.release(m0)

    def consts(self):
        S = self.S
        self.eps_col = S.sb("eps_col", [128, 1], F32)
        self.one_col = S.sb("one_col", [128, 1], F32)
        S.memset("pool", self.eps_col[:], EPS)
        S.memset("pool", self.one_col[:], 1.0)

    def final(self):
        S = self.S
        yT = self.out("yT", [128, KC, T])
        for ti, (t0, n) in enumerate(TT):
            r = self.rmsnorm_stats(ti)
            for k in range(KC):
                S.stt(self.x[k][:, t0:t0 + n], self.x[k][:, t0:t0 + n], self.col("nfinal", k), r[:, 0:n],
                      ALU.mult, ALU.mult)
        for k in range(KC):
            S.dma(yT[:, k, :], self.x[k][:], sem=f"o{k % 2}")
        sm = self.out("small", [128, 320])
        S.dma(sm[:, :], self.small[:], sem="o0")

    def dump_x(self):
        S = self.S
        dbg = self.out("dbg", [128, KC, T])
        for k in range(KC):
            S.dma(dbg[:, k, :], self.x[k][:], sem=f"o{k % 2}")


def build_program(shapes, colidx, stop_after=None, only=None):
    nc = bass.Bass("TRN2", target_bir_lowering=False)
    S = Sched(nc)
    B = Builder(nc, S, shapes, colidx, stop_after)
    B.setup()
    B.consts()
    S.memset("pool", B.small[:], 0.0)
    layers = [("lru", 0), ("dn", 0), ("mla", 0), ("lru", 1)]
    done = False
    for li, (kind, j) in enumerate(layers):
        if only is not None and li != only:
            continue
        if kind == "lru":
            B.lru(li, j)
        elif kind == "dn":
            B.dn(li, j)
        else:
            B.mla(li, j)
        if stop_after == (li, "mix"):
            done = True
            break
        B.ffn(li)
        if stop_after == (li, "ffn"):
            done = True
            break
    if done:
        B.dump_x()
        sm = B.out("small", [128, 320])
        S.dma(sm[:, :], B.small[:], sem="o0")
    else:
        B.final()
    S.emit()
    return nc, B


_STOP_AFTER = None
_DEBUG = {}


def _run(inputs, stop_after=None, cores=NCORES, only=None, x_override=None):
    inp = {k: np.asarray(v) for k, v in inputs.items()}
    sh = _prep_shared(inp)
    colidx = sh.pop("_colidx")
    per_core = [_prep_core(inp, c) for c in range(cores)]
    shapes = {k: v.shape for k, v in sh.items()}
    shapes.update({k: v.shape for k, v in per_core[0].items()})
    if x_override is not None:
        for c in range(cores):
            per_core[c]["xT"] = np.ascontiguousarray(x_override[c].reshape(T, KC, 128).transpose(2, 1, 0))
    nc, B = build_program(shapes, colidx, stop_after, only)
    in_maps = []
    for c in range(cores):
        m = dict(sh)
        m.update(per_core[c])
        in_maps.append(m)
    res = run_bass_kernel_spmd(nc, in_maps, core_ids=list(range(cores)))
    return res.results, B


def kernel(**inputs):
    results, B = _run(inputs, None)
    f = np.float32
    y_prompt = np.zeros((8, SEQ, D), f)
    y_sample = np.zeros((32, 1, D), f)
    p_lru_h = np.zeros((2, 8, D), f)
    p_lru_conv = np.zeros((2, 8, 3, D), f)
    p_dn_S = np.zeros((1, 8, 8, 128, 128), f)
    p_dn_conv = np.zeros((1, 8, 3, 3072), f)
    p_ckv = np.zeros((1, 8, TP, 256), f)
    p_kpe = np.zeros((1, 8, TP, 64), f)
    s_lru_h = np.zeros((2, 32, D), f)
    s_lru_conv = np.zeros((2, 32, 3, D), f)
    s_dn_S = np.zeros((1, 32, 8, 128, 128), f)
    s_dn_conv = np.zeros((1, 32, 3, 3072), f)
    s_ckv = np.zeros((1, 32, 1, 256), f)
    s_kpe = np.zeros((1, 32, 1, 64), f)
    for c in range(NCORES):
        r = results[c]
        y = r["yT"].transpose(2, 1, 0).reshape(T, D)
        y_prompt[c] = y[NMETA:TP]
        y_sample[NS * c:NS * (c + 1), 0] = y[TP:]
        sm = r["small"]
        for j in range(2):
            i, n = B.small_idx[f"p_lru_h{j}"]
            p_lru_h[j, c] = sm[:, i:i + n].T.reshape(D)
            i, n = B.small_idx[f"p_lru_conv{j}"]
            p_lru_conv[j, c] = sm[:, i:i + n].reshape(128, 8, 3).transpose(2, 1, 0).reshape(3, D)
            i, n = B.small_idx[f"s_lru_h{j}"]
            s_lru_h[j, NS * c:NS * (c + 1)] = sm[:, i:i + n].reshape(128, 8, NS).transpose(2, 1, 0).reshape(NS, D)
            i, n = B.small_idx[f"s_lru_conv{j}"]
            s_lru_conv[j, NS * c:NS * (c + 1)] = sm[:, i:i + n].reshape(128, 8, NS, 3).transpose(2, 3, 1, 0).reshape(NS, 3, D)
        s2 = r["small2"]
        p_dn_conv[0, c] = s2[:, 0:72].reshape(128, 24, 3).transpose(2, 1, 0).reshape(3, 3072)
        s_dn_conv[0, NS * c:NS * (c + 1)] = s2[:, 72:360].reshape(128, 24, NS, 3).transpose(2, 3, 1, 0).reshape(NS, 3, 3072)
        oS = r["o_dn_S"]
        p_dn_S[0, c] = oS[0:8]
        s_dn_S[0, NS * c:NS * (c + 1)] = oS[8:].reshape(NS, 8, 128, 128)
        kv = r["p_kv"]
        p_ckv[0, c] = kv[:TP, :256]
        p_kpe[0, c] = kv[:TP, 256:]
        s_ckv[0, NS * c:NS * (c + 1), 0] = kv[TP:, :256]
        s_kpe[0, NS * c:NS * (c + 1), 0] = kv[TP:, 256:]
    return (y_prompt, y_sample, p_lru_h, p_lru_conv, p_dn_S, p_dn_conv, p_ckv, p_kpe,
            s_lru_h, s_lru_conv, s_dn_S, s_dn_conv, s_ckv, s_kpe)
```
